# Optimizing a Trainium2 kernel written in Bass

```python
import math
import jax
import jax.numpy as jnp
from jax import lax
import numpy as np

D_MODEL = 1024
BATCH = 4
SEQ = 4096
DEPTH = 2

SSD_HEADS = 8
SSD_HEAD_DIM = 64
SSD_INNER = SSD_HEADS * SSD_HEAD_DIM
SSD_GROUPS = 2
SSD_STATE = 128
SSD_CONV = 5
SSD_CHUNK = 256
SSD_CONV_DIM = SSD_INNER + 2 * SSD_GROUPS * SSD_STATE
SWA_HEADS = 4
SWA_KV_HEADS = 2
SWA_HEAD_DIM = 64
SWA_WINDOW = 128
SWA_BLOCK = 128
SWA_WIDTH = SWA_HEADS * SWA_HEAD_DIM
MLA_HEADS = 4
MLA_Q_RANK = 256
MLA_KV_RANK = 128
MLA_NOPE = 64
MLA_ROPE = 32
MLA_QK = MLA_NOPE + MLA_ROPE
MLA_V = 64
MLA_WIDTH = MLA_HEADS * MLA_V
MLA_BLOCK = 128
D_MIX = SSD_INNER + SWA_WIDTH + MLA_WIDTH
IN_SIZES = (SSD_INNER, SSD_CONV_DIM, 2 * SSD_HEADS,
            SWA_HEADS * SWA_HEAD_DIM, 2 * SWA_KV_HEADS * SWA_HEAD_DIM,
            MLA_Q_RANK, MLA_KV_RANK, MLA_ROPE)
D_IN_PROJ = sum(IN_SIZES)
D_FF = 2816
ROPE_THETA = 10000.0
EPS = 1e-6

kernel_name = 'hybrid_ssd_swa_mla_macaron_encoder'


def rms_norm(x, w):
    xf = x.astype(jnp.float32)
    y = xf * lax.rsqrt(jnp.mean(xf * xf, axis=-1, keepdims=True) + EPS)
    return (y * w.astype(jnp.float32)).astype(x.dtype)


def split_columns(t, sizes):
    idx, acc = [], 0
    for sz in sizes[:-1]:
        acc += sz
        idx.append(acc)
    return jnp.split(t, idx, axis=-1)


def rope_tables(n, dim):
    inv = 1.0 / jnp.power(ROPE_THETA, jnp.arange(0, dim, 2, dtype=jnp.float32) / dim)
    ang = jnp.arange(n, dtype=jnp.float32)[:, None] * inv[None, :]
    return jnp.cos(ang), jnp.sin(ang)


def apply_rope(x, cos, sin):
    x1, x2 = jnp.split(x, 2, axis=-1)
    c = cos[None, :, None, :]
    sn = sin[None, :, None, :]
    return jnp.concatenate([x1 * c - x2 * sn, x1 * sn + x2 * c], axis=-1).astype(x.dtype)


def swiglu(x, w_gate, w_up, w_down):
    return (jax.nn.silu(x @ w_gate) * (x @ w_up)) @ w_down


def centred_depthwise_conv(x, w, b):
    k = w.shape[0]
    y = lax.conv_general_dilated(x, w[:, None, :], window_strides=(1,),
                                 padding=[(k // 2, k // 2)],
                                 dimension_numbers=('NWC', 'WIO', 'NWC'),
                                 feature_group_count=x.shape[-1])
    return y + b


def segsum(a):
    t = a.shape[-1]
    a_rep = jnp.broadcast_to(a[..., :, None], a.shape + (t,))
    strict = jnp.tril(jnp.ones((t, t), dtype=bool), -1)
    cs = jnp.cumsum(jnp.where(strict, a_rep, 0.0), axis=-2)
    return jnp.where(jnp.tril(jnp.ones((t, t), dtype=bool)), cs, -jnp.inf)


def ssd_scan(x, dt, a_decay, bm, cm):
    b, l, h, p = x.shape
    g, n = bm.shape[-2:]
    r = h // g
    xdt = x.astype(jnp.float32) * dt[..., None]
    a = dt * a_decay
    bf = bm.astype(jnp.float32)
    cf = cm.astype(jnp.float32)
    pad = (-l) % SSD_CHUNK
    if pad:
        pw = lambda t: jnp.pad(t, [(0, 0), (0, pad)] + [(0, 0)] * (t.ndim - 2))
        xdt, a, bf, cf = pw(xdt), pw(a), pw(bf), pw(cf)
    t = SSD_CHUNK
    nc = (l + pad) // t
    xc = xdt.reshape(b, nc, t, g, r, p)
    ac = a.reshape(b, nc, t, g, r).transpose(0, 3, 4, 1, 2)
    bc = bf.reshape(b, nc, t, g, n)
    cc = cf.reshape(b, nc, t, g, n)
    a_cum = jnp.cumsum(ac, axis=-1)
    l_mat = jnp.exp(segsum(ac))
    cb = jnp.einsum('bclgn,bcsgn->bgcls', cc, bc)
    y_diag = jnp.einsum('bgcls,bgrcls,bcsgrp->bclgrp', cb, l_mat, xc)
    decay_states = jnp.exp(a_cum[..., -1:] - a_cum)
    states = jnp.einsum('bclgn,bgrcl,bclgrp->bcgrpn', bc, decay_states, xc)
    states = jnp.concatenate([jnp.zeros_like(states[:, :1]), states], axis=1)
    chunk_tot = jnp.pad(a_cum[..., -1], [(0, 0), (0, 0), (0, 0), (1, 0)])
    decay_chunk = jnp.exp(segsum(chunk_tot))
    states = jnp.einsum('bgrzc,bcgrpn->bzgrpn', decay_chunk, states)[:, :-1]
    y_off = jnp.einsum('bclgn,bcgrpn,bgrcl->bclgrp', cc, states, jnp.exp(a_cum))
    return (y_diag + y_off).reshape(b, nc * t, h, p)[:, :l]


def ssd_mixer(z, xbc, dt_raw, conv_w, conv_b, dt_bias, a_log, d_skip, norm_w):
    b, s, _ = xbc.shape
    xbc = jax.nn.silu(centred_depthwise_conv(xbc, conv_w, conv_b))
    xs, bm, cm = jnp.split(xbc, [SSD_INNER, SSD_INNER + SSD_GROUPS * SSD_STATE], axis=-1)
    xs = xs.reshape(b, s, SSD_HEADS, SSD_HEAD_DIM)
    bm = bm.reshape(b, s, SSD_GROUPS, SSD_STATE)
    cm = cm.reshape(b, s, SSD_GROUPS, SSD_STATE)
    dt = jax.nn.softplus(dt_raw.astype(jnp.float32).reshape(b, s, 2, SSD_HEADS)
                         + dt_bias.astype(jnp.float32))
    a_decay = -jnp.exp(a_log.astype(jnp.float32))
    flip = lambda t: jnp.flip(t, axis=1)
    y_fwd = ssd_scan(xs, dt[:, :, 0], a_decay[0], bm, cm)
    y_bwd = flip(ssd_scan(flip(xs), flip(dt[:, :, 1]), a_decay[1], flip(bm), flip(cm)))
    y = y_fwd + y_bwd + d_skip.astype(jnp.float32)[:, None] * xs.astype(jnp.float32)
    y = y.reshape(b, s, SSD_INNER) * jax.nn.silu(z.astype(jnp.float32))
    y = y.reshape(b, s, SSD_GROUPS, SSD_INNER // SSD_GROUPS)
    y = y * lax.rsqrt(jnp.mean(y * y, axis=-1, keepdims=True) + EPS)
    return (y.reshape(b, s, SSD_INNER) * norm_w.astype(jnp.float32)).astype(z.dtype)


def swa_mixer(q, kv, cos, sin, q_norm_w, k_norm_w, sink, out_norm_w):
    b, s, _ = q.shape
    hd, kvh, blk = SWA_HEAD_DIM, SWA_KV_HEADS, SWA_BLOCK
    rep = SWA_HEADS // kvh
    q = q.reshape(b, s, SWA_HEADS, hd)
    k, v = jnp.split(kv, 2, axis=-1)
    k = k.reshape(b, s, kvh, hd)
    v = v.reshape(b, s, kvh, hd)
    q = apply_rope(rms_norm(q, q_norm_w), cos, sin)
    k = apply_rope(rms_norm(k, k_norm_w), cos, sin)
    nb = s // blk
    qb = q.reshape(b, nb, blk, kvh, rep, hd)

    def band(t):
        tp = jnp.pad(t, [(0, 0), (blk, blk), (0, 0), (0, 0)]).reshape(b, nb + 2, blk, kvh, hd)
        return jnp.concatenate([tp[:, :-2], tp[:, 1:-1], tp[:, 2:]], axis=2)

    kb, vb = band(k), band(v)
    scores = jnp.einsum('bnqkrd,bnjkd->bnkrqj', qb, kb).astype(jnp.float32) * (hd ** -0.5)
    qpos = jnp.arange(nb)[:, None] * blk + jnp.arange(blk)[None, :]
    kpos = (jnp.arange(nb)[:, None] - 1) * blk + jnp.arange(3 * blk)[None, :]
    rel = kpos[:, None, :] - qpos[:, :, None]
    valid = (jnp.abs(rel) <= SWA_WINDOW) & (kpos[:, None, :] >= 0) & (kpos[:, None, :] < s)
    scores = jnp.where(valid[None, :, None, None], scores, -jnp.inf)
    sink_col = jnp.broadcast_to(sink.astype(jnp.float32).reshape(kvh, rep)[None, None, :, :, None, None],
                                scores.shape[:-1] + (1,))
    probs = jax.nn.softmax(jnp.concatenate([scores, sink_col], axis=-1), axis=-1)[..., :-1]
    out = jnp.einsum('bnkrqj,bnjkd->bnqkrd', probs.astype(vb.dtype), vb)
    return rms_norm(out.reshape(b, s, SWA_WIDTH), out_norm_w)


def mla_mixer(q_lat, c_kv, k_rope, cos_r, sin_r, q_lat_norm_w, w_uq, kv_norm_w, w_ukv,
              q_norm_w, k_norm_w, out_norm_w):
    b, s, _ = q_lat.shape
    h = MLA_HEADS
    q = (rms_norm(q_lat, q_lat_norm_w) @ w_uq).reshape(b, s, h, MLA_QK)
    kv = (rms_norm(c_kv, kv_norm_w) @ w_ukv).reshape(b, s, h, MLA_NOPE + MLA_V)
    k_nope, v = jnp.split(kv, [MLA_NOPE], axis=-1)
    k = jnp.concatenate([k_nope, jnp.broadcast_to(k_rope[:, :, None, :], (b, s, h, MLA_ROPE))], axis=-1)
    q = rms_norm(q, q_norm_w)
    k = rms_norm(k, k_norm_w)
    q = jnp.concatenate([q[..., :MLA_NOPE], apply_rope(q[..., MLA_NOPE:], cos_r, sin_r)], axis=-1)
    k = jnp.concatenate([k[..., :MLA_NOPE], apply_rope(k[..., MLA_NOPE:], cos_r, sin_r)], axis=-1)
    scale = MLA_QK ** -0.5
    nb = s // MLA_BLOCK
    qb = q.reshape(b, nb, MLA_BLOCK, h, MLA_QK).transpose(1, 0, 2, 3, 4)

    def attend(q_blk):
        sc = jnp.einsum('bqhd,bkhd->bhqk', q_blk, k).astype(jnp.float32) * scale
        p = jax.nn.softmax(sc, axis=-1)
        return jnp.einsum('bhqk,bkhd->bqhd', p.astype(v.dtype), v)

    out = lax.map(attend, qb)
    out = out.transpose(1, 0, 2, 3, 4).reshape(b, s, MLA_WIDTH)
    return rms_norm(out, out_norm_w)


def setup_inputs(seed: int = 0) -> dict:
    key = jax.random.key(seed)
    ks = iter(jax.random.split(key, 48))
    L = DEPTH

    def nrm(shape, scale):
        return jax.random.normal(next(ks), shape, jnp.float32) * scale

    def gain(shape):
        return 1.0 + nrm(shape, 0.02)

    x = jax.random.normal(next(ks), (BATCH, SEQ, D_MODEL), jnp.float32)
    ffn1_norm = gain((L, D_MODEL))
    ffn1_gate = nrm((L, D_MODEL, D_FF), D_MODEL ** -0.5)
    ffn1_up = nrm((L, D_MODEL, D_FF), D_MODEL ** -0.5)
    ffn1_down = nrm((L, D_FF, D_MODEL), D_FF ** -0.5)
    mix_norm = gain((L, D_MODEL))
    w_in = nrm((L, D_MODEL, D_IN_PROJ), D_MODEL ** -0.5)
    ssd_conv_w = nrm((L, SSD_CONV, SSD_CONV_DIM), SSD_CONV ** -0.5)
    ssd_conv_b = nrm((L, SSD_CONV_DIM), 0.02)
    dt0 = jnp.exp(jax.random.uniform(next(ks), (L, 2, SSD_HEADS), jnp.float32,
                                     minval=math.log(1e-3), maxval=math.log(1e-1)))
    ssd_dt_bias = dt0 + jnp.log(-jnp.expm1(-dt0))
    ssd_a_log = jnp.log(jax.random.uniform(next(ks), (L, 2, SSD_HEADS), jnp.float32,
                                           minval=1.0, maxval=16.0))
    ssd_d = gain((L, SSD_HEADS))
    ssd_norm = gain((L, SSD_INNER))
    swa_q_norm = gain((L, SWA_HEAD_DIM))
    swa_k_norm = gain((L, SWA_HEAD_DIM))
    swa_sink = nrm((L, SWA_HEADS), 0.5)
    swa_out_norm = gain((L, SWA_WIDTH))
    mla_q_lat_norm = gain((L, MLA_Q_RANK))
    mla_w_uq = nrm((L, MLA_Q_RANK, MLA_HEADS * MLA_QK), MLA_Q_RANK ** -0.5)
    mla_kv_norm = gain((L, MLA_KV_RANK))
    mla_w_ukv = nrm((L, MLA_KV_RANK, MLA_HEADS * (MLA_NOPE + MLA_V)), MLA_KV_RANK ** -0.5)
    mla_q_norm = gain((L, MLA_QK))
    mla_k_norm = gain((L, MLA_QK))
    mla_out_norm = gain((L, MLA_WIDTH))
    w_out = nrm((L, D_MIX, D_MODEL), D_MIX ** -0.5)
    ffn2_norm = gain((L, D_MODEL))
    ffn2_gate = nrm((L, D_MODEL, D_FF), D_MODEL ** -0.5)
    ffn2_up = nrm((L, D_MODEL, D_FF), D_MODEL ** -0.5)
    ffn2_down = nrm((L, D_FF, D_MODEL), D_FF ** -0.5)
    return {'x': x, 'ffn1_norm': ffn1_norm, 'ffn1_gate': ffn1_gate, 'ffn1_up': ffn1_up,
            'ffn1_down': ffn1_down, 'mix_norm': mix_norm, 'w_in': w_in,
            'ssd_conv_w': ssd_conv_w, 'ssd_conv_b': ssd_conv_b, 'ssd_dt_bias': ssd_dt_bias,
            'ssd_a_log': ssd_a_log, 'ssd_d': ssd_d, 'ssd_norm': ssd_norm,
            'swa_q_norm': swa_q_norm, 'swa_k_norm': swa_k_norm, 'swa_sink': swa_sink,
            'swa_out_norm': swa_out_norm, 'mla_q_lat_norm': mla_q_lat_norm,
            'mla_w_uq': mla_w_uq, 'mla_kv_norm': mla_kv_norm, 'mla_w_ukv': mla_w_ukv,
            'mla_q_norm': mla_q_norm, 'mla_k_norm': mla_k_norm, 'mla_out_norm': mla_out_norm,
            'w_out': w_out, 'ffn2_norm': ffn2_norm, 'ffn2_gate': ffn2_gate,
            'ffn2_up': ffn2_up, 'ffn2_down': ffn2_down}


def reference(x, ffn1_norm, ffn1_gate, ffn1_up, ffn1_down, mix_norm, w_in,
              ssd_conv_w, ssd_conv_b, ssd_dt_bias, ssd_a_log, ssd_d, ssd_norm,
              swa_q_norm, swa_k_norm, swa_sink, swa_out_norm,
              mla_q_lat_norm, mla_w_uq, mla_kv_norm, mla_w_ukv, mla_q_norm, mla_k_norm,
              mla_out_norm, w_out, ffn2_norm, ffn2_gate, ffn2_up, ffn2_down):
    s = x.shape[1]
    cos, sin = rope_tables(s, SWA_HEAD_DIM)
    cos_r, sin_r = rope_tables(s, MLA_ROPE)
    for l in range(DEPTH):
        x = x + 0.5 * swiglu(rms_norm(x, ffn1_norm[l]), ffn1_gate[l], ffn1_up[l], ffn1_down[l])
        h = rms_norm(x, mix_norm[l])
        proj = h @ w_in[l]
        z, xbc, dt_raw, swa_q, swa_kv, mla_q, mla_ckv, mla_kr = split_columns(proj, IN_SIZES)
        y_ssd = ssd_mixer(z, xbc, dt_raw, ssd_conv_w[l], ssd_conv_b[l], ssd_dt_bias[l],
                          ssd_a_log[l], ssd_d[l], ssd_norm[l])
        y_swa = swa_mixer(swa_q, swa_kv, cos, sin, swa_q_norm[l], swa_k_norm[l],
                          swa_sink[l], swa_out_norm[l])
        y_mla = mla_mixer(mla_q, mla_ckv, mla_kr, cos_r, sin_r, mla_q_lat_norm[l], mla_w_uq[l],
                          mla_kv_norm[l], mla_w_ukv[l], mla_q_norm[l], mla_k_norm[l],
                          mla_out_norm[l])
        y = jnp.concatenate([y_ssd, y_swa, y_mla], axis=-1)
        x = x + y @ w_out[l]
        x = x + 0.5 * swiglu(rms_norm(x, ffn2_norm[l]), ffn2_gate[l], ffn2_up[l], ffn2_down[l])
    return x
```

```python
import contextlib
import numpy as np
import concourse.bass as bass
import concourse.mybir as mybir
from concourse.bass_utils import run_bass_kernel_spmd

F32 = mybir.dt.float32
BF16 = mybir.dt.bfloat16
AF = mybir.ActivationFunctionType
ALU = mybir.AluOpType
AX = mybir.AxisListType

D = 1024
DFF = 2816
NPROJ = 2480
NTM = 1456
EPS = 1e-6
BIG = 1.0e4
DEBUG = False
ALT_DVE_ONLY = False
N_DMA_SEMS = 40

VOFF = {}
_o = 0
for _n, _s in (("ffn1_norm", 1024), ("mix_norm", 1024), ("ffn2_norm", 1024), ("ssd_norm", 512),
               ("swa_q_norm", 64), ("swa_k_norm", 64), ("swa_out_norm", 256), ("mla_q_lat_norm", 256),
               ("mla_kv_norm", 128), ("mla_q_norm", 96), ("mla_k_norm", 96), ("mla_out_norm", 256),
               ("ssd_dt_bias", 16), ("ssd_a_log", 16), ("ssd_d", 8), ("swa_sink", 4)):
    VOFF[_n] = (_o, _s)
    _o += _s
NV = _o

C_ID = 0
C_R0F = 128
C_R1F = 384
C_R0B = 640
C_R1B = 896
C_NEGF = 1152
C_POSB = 1408
C_GE = 1664
C_SEL = 1792
NCST = C_SEL + 1024


def make_consts():
    c = np.zeros((128, NCST), np.float32)
    i = np.arange(128)
    le = (i[:, None] <= i[None, :]).astype(np.float32)
    lt = (i[:, None] < i[None, :]).astype(np.float32)
    ge = (i[:, None] >= i[None, :]).astype(np.float32)
    gt = (i[:, None] > i[None, :]).astype(np.float32)
    c[:, C_ID:C_ID + 128] = np.eye(128)
    c[:, C_R0F:C_R0F + 128] = le
    c[:, C_R0F + 128:C_R0F + 256] = 1.0
    c[:, C_R1F + 128:C_R1F + 256] = le
    c[:, C_R0B:C_R0B + 128] = lt
    c[:, C_R0B + 128:C_R0B + 256] = 1.0
    c[:, C_R1B + 128:C_R1B + 256] = lt
    c[:, C_NEGF:C_NEGF + 128] = -BIG * gt
    c[:, C_POSB + 128:C_POSB + 256] = BIG * lt
    c[:, C_GE:C_GE + 128] = ge
    for h in range(8):
        c[h, C_SEL + h * 128:C_SEL + (h + 1) * 128] = 1.0
    return c


def rope_tables(n, dim):
    inv = 1.0 / np.power(np.float32(10000.0), np.arange(0, dim, 2, dtype=np.float32) / np.float32(dim))
    ang = np.arange(n, dtype=np.float32)[:, None] * inv[None, :].astype(np.float32)
    return np.concatenate([np.cos(ang), np.sin(ang)], axis=1).astype(np.float32)


def _freeze(fn):
    import types
    if fn.__closure__ is None:
        return fn
    cells = []
    for c in fn.__closure__:
        try:
            cells.append(types.CellType(c.cell_contents))
        except ValueError:
            cells.append(c)
    return types.FunctionType(fn.__code__, fn.__globals__, fn.__name__, fn.__defaults__, tuple(cells))


class Op:
    __slots__ = ("eng", "fn", "waits", "signal", "sem", "val", "is_dma")

    def __init__(self, eng, fn, is_dma=False):
        self.eng = eng
        self.fn = fn
        self.waits = []
        self.signal = False
        self.sem = None
        self.val = None
        self.is_dma = is_dma


class KB:
    ENGS = ("pe", "act", "dve", "pool", "sp")

    def __init__(self, nc):
        self.nc = nc
        self.prog = {e: [] for e in self.ENGS}
        self.last_w = {}
        self.readers = {}
        self.dma_rr = 0
        self.dma_last = [None] * N_DMA_SEMS
        self.dma_uses = [0] * N_DMA_SEMS
        self.bar = None
        self.nops = 0
        self.carrier = None

    def _deps(self, op, reads, writes):
        deps = []
        if self.bar is not None:
            deps.append(self.bar)
        for k in reads:
            w = self.last_w.get(k)
            if w is not None:
                deps.append(w)
        for k in writes:
            w = self.last_w.get(k)
            if w is not None:
                deps.append(w)
            deps.extend(self.readers.get(k, ()))
        for k in reads:
            self.readers.setdefault(k, []).append(op)
        for k in writes:
            self.last_w[k] = op
            self.readers[k] = []
        seen = set()
        for d in deps:
            if d is op or id(d) in seen:
                continue
            seen.add(id(d))
            if op.eng == "pe" and d.eng == "pe" and not d.is_dma and not op.is_dma:
                continue
            op.waits.append(d)

    def op(self, eng, fn, reads=(), writes=()):
        o = Op(eng, _freeze(fn))
        self._deps(o, reads, writes)
        self.prog[eng].append(o)
        self.nops += 1
        return o

    def dma(self, out, in_, reads=(), writes=(), q="sp", **kw):
        o = Op(q, None, is_dma=True)
        o.fn = lambda e, out=out, in_=in_, kw=kw: e.dma_start(out=out, in_=in_, **kw)
        self._deps(o, reads, writes)
        i = self.dma_rr
        self.dma_rr = (self.dma_rr + 1) % N_DMA_SEMS
        prev = self.dma_last[i]
        if prev is not None:
            o.waits.append(prev)
        self.dma_last[i] = o
        self.dma_uses[i] += 1
        o.sem = i
        o.val = 16 * self.dma_uses[i]
        o.signal = True
        self.prog[q].append(o)
        self.nops += 1
        return o

    def barrier(self, scratch_ap):
        o = Op("dve", lambda e: e.memset(scratch_ap, 0.0))
        if self.bar is not None:
            o.waits.append(self.bar)
        for e in self.ENGS:
            for p in reversed(self.prog[e]):
                if not p.is_dma:
                    o.waits.append(p)
                    break
        for d in self.dma_last:
            if d is not None:
                o.waits.append(d)
        self.prog["dve"].append(o)
        self.bar = o
        self.last_w = {}
        self.readers = {}
        return o

    def emit(self, final_wait_ops=()):
        nc = self.nc
        for e in self.ENGS:
            for o in self.prog[e]:
                for w in o.waits:
                    w.signal = True
        for o in final_wait_ops:
            o.signal = True
        for e in self.ENGS:
            c = 0
            for o in self.prog[e]:
                if o.is_dma:
                    continue
                if o.signal:
                    c += 1
                    o.sem = e
                    o.val = c
        with contextlib.ExitStack() as st:
            esem = {e: st.enter_context(nc.semaphore("s_" + e)) for e in self.ENGS}
            dsem = [st.enter_context(nc.semaphore("d_%d" % i)) for i in range(N_DMA_SEMS)]
            block = st.enter_context(nc.Block())

            def sem_of(o):
                return dsem[o.sem] if o.is_dma else esem[o.sem]

            def run(e, h):
                waited = {}
                for o in self.prog[e]:
                    need = {}
                    for w in o.waits:
                        key = ("d", w.sem) if w.is_dma else ("e", w.sem)
                        if waited.get(key, 0) >= w.val:
                            continue
                        waited[key] = w.val
                        need[key] = w
                    need = list(need.values())
                    if e == "pe" and not o.is_dma:
                        for w in need[:-1]:
                            h.ldweights(self.carrier)._wait_ge(sem_of(w), w.val)
                        ins = o.fn(h)
                        if need:
                            ins._wait_ge(sem_of(need[-1]), need[-1].val)
                    else:
                        for w in need:
                            h.wait_ge(sem_of(w), w.val)
                        ins = o.fn(h)
                    if o.is_dma:
                        ins.then_inc(dsem[o.sem], 16)
                    elif o.signal:
                        ins.then_inc(esem[e], 1)
                if e == "sp":
                    for o in final_wait_ops:
                        h.wait_ge(sem_of(o), o.val)

            @block.tensor
            def _(t):
                run("pe", t)

            @block.scalar
            def _(s):
                run("act", s)

            @block.vector
            def _(v):
                run("dve", v)

            @block.gpsimd
            def _(g):
                run("pool", g)

            @block.sync
            def _(s):
                run("sp", s)


class Arena:
    def __init__(self, apf, nf, apb, nb):
        self.apf, self.nf, self.apb, self.nb = apf, nf, apb, nb
        self.p = 0
        self.pb = 0
        self.hi = 0
        self.hib = 0

    def reset(self, to=0, tob=0):
        self.p = to
        self.pb = tob

    def f32(self, cols):
        a = self.apf[:, self.p:self.p + cols]
        self.p += (cols + 7) // 8 * 8
        self.hi = max(self.hi, self.p)
        assert self.p <= self.nf, ("f32 arena overflow", self.p, self.nf)
        return a

    def bf16(self, cols):
        a = self.apb[:, self.pb:self.pb + cols]
        self.pb += (cols + 15) // 16 * 16
        self.hib = max(self.hib, self.pb)
        assert self.pb <= self.nb, ("bf16 arena overflow", self.pb, self.nb)
        return a


def build(S, L, stop_after=None):
    assert S % 512 == 0
    NT = S // 128
    TB = min(1024, S)
    NBLK = S // TB
    NTB = TB // 128
    NCH = S // 256

    nc = bass.Bass("TRN2", target_bir_lowering=False)
    dr = lambda name, shape, dt=F32, kind="ExternalInput": nc.dram_tensor(name, list(shape), dt, kind=kind).ap()
    x_d = dr("x", [S, D])
    wg_d = [dr("ffn1_gate", [L, D, DFF]), dr("ffn2_gate", [L, D, DFF])]
    wu_d = [dr("ffn1_up", [L, D, DFF]), dr("ffn2_up", [L, D, DFF])]
    wd_d = [dr("ffn1_down", [L, DFF, D]), dr("ffn2_down", [L, DFF, D])]
    win_d = dr("w_in", [L, D, NPROJ])
    wout_d = dr("w_out", [L, D, D])
    wuq_d = dr("mla_w_uq", [L, 256, 384])
    wukv_d = dr("mla_w_ukv", [L, 128, 512])
    vec_d = dr("vecs", [L, NV])
    convp_d = dr("convp", [L, 128, 48])
    cst_d = dr("cst", [128, NCST])
    cs_swa_d = dr("cs_swa", [S, 64])
    cs_mla_d = dr("cs_mla", [S, 32])
    out_d = dr("out", [S, D], kind="ExternalOutput")
    xbc_d = dr("xbc_scr", [1024, S], BF16, kind="Internal")
    ptok_d = dr("ptok_scr", [S, NTM], F32, kind="Internal")
    y_d = dr("y_scr", [S, D], BF16, kind="Internal")

    ARENA_F = 17 * 1024
    ARENA_B = 56 * 1024
    st = contextlib.ExitStack()
    with st:
        arena_f = st.enter_context(nc.sbuf_tensor("arena_f", [128, ARENA_F], F32))
        arena_b = st.enter_context(nc.sbuf_tensor("arena_b", [128, ARENA_B], BF16))
        cst = st.enter_context(nc.sbuf_tensor("cst_sb", [128, NCST], F32))
        cstb = st.enter_context(nc.sbuf_tensor("cstb_sb", [128, 128 * 5], BF16))
        barsc = st.enter_context(nc.sbuf_tensor("barsc", [128, 8], F32))
        banks = [st.enter_context(nc.psum_tensor("bank%d" % i, [128, 1024] if i == 6 else [128, 512], BF16 if i == 6 else F32))
                 for i in range(8)]
        k = KB(nc)
        A = Arena(arena_f[:], ARENA_F, arena_b[:], ARENA_B)

        k.carrier = cstb[:, 0:1]
        identf = cst[:, C_ID:C_ID + 128]
        identb = cstb[:, 0:128]
        ones_f = cst[:, C_R0F + 128:C_R0F + 256]

        k.dma(cst[:], cst_d, writes=["cst"])
        k.dma(cstb[:, 0:128], cst_d[:, C_ID:C_ID + 128], writes=["cstb"], q="pool")
        for j in range(2):
            k.dma(cstb[:, 128 + j * 128:256 + j * 128], cst_d[:, C_GE:C_GE + 128], writes=["cstb"], q="pool")
            k.dma(cstb[:, 384 + j * 128:512 + j * 128], cst_d[:, C_R0F:C_R0F + 128], writes=["cstb"], q="pool")

        def PS(i):
            return banks[i], "bank%d" % i

        rr = {"n": 0}
        dbg_outs = []
        dbg_names = set()

        def dbg_dump(name, ap, reads, dt=F32, big=False):
            if not DEBUG or ("dbg_" + name) in dbg_names:
                return
            dbg_names.add("dbg_" + name)
            t = nc.dram_tensor("dbg_" + name, list(ap.shape), dt, kind="ExternalOutput").ap()
            if big:
                for r0 in range(0, ap.shape[0], 256):
                    dbg_outs.append(k.dma(t[r0:r0 + 256], ap[r0:r0 + 256], reads=reads))
            else:
                dbg_outs.append(k.dma(t, ap, reads=reads))

        def alt():
            rr["n"] += 1
            return "dve" if (rr["n"] % 2 or ALT_DVE_ONLY) else "act"

        def copy_op(eng, out, in_, reads, writes):
            if eng == "act":
                return k.op("act", lambda e: e.copy(out, in_), reads, writes)
            return k.op(eng, lambda e: e.tensor_copy(out, in_), reads, writes)

        def vrow(vbc, name, lo=0, n=None):
            o, s = VOFF[name]
            n = s - lo if n is None else n
            return vbc[:, o + lo:o + lo + n]

        def block_phase(pidx, l_prev, l_next):
            A.reset()
            vbcA = A.f32(3072) if l_prev is not None else None
            vbcB = A.f32(3072) if l_next is not None else None
            xb = A.f32(NTB * D).rearrange("p (t d) -> p t d", t=NTB)
            hT = A.bf16(8 * TB).rearrange("p (c t) -> p c t", c=8)
            hb = [A.bf16(D) for _ in range(2)]
            junk = A.f32(D)
            ss = A.f32(NTB)
            sq = A.f32(NTB)
            rstd = A.f32(NTB)
            wgu = [A.bf16(2 * 8 * 256).rearrange("p (w c f) -> p w c f", w=2, c=8) for _ in range(2)]
            wdn = [A.bf16(2 * D).rearrange("p (c d) -> p c d", c=2) for _ in range(2)]
            sg = [A.bf16(512) for _ in range(2)]
            aT = [A.bf16(2 * 512).rearrange("p (c t) -> p c t", c=2) for _ in range(2)]
            wpj = [A.bf16(8 * 512).rearrange("p (c f) -> p c f", c=8) for _ in range(2)]
            evf = [A.bf16(512) for _ in range(2)]
            evt = [A.f32(512) for _ in range(2)]
            cnt = {"w": 0, "p": 0, "a": 0, "d": 0, "t": 0, "e": 0, "g": 0}
            if l_prev is not None:
                k.dma(vbcA, vec_d[l_prev:l_prev + 1, 0:3072].partition_broadcast(128), writes=["vbcA"])
            if l_next is not None:
                k.dma(vbcB, vec_d[l_next:l_next + 1, 0:3072].partition_broadcast(128), writes=["vbcB"])

            def norm_to_hT(vbc, vkey, nname, tag):
                k.op("dve", lambda e: e.memset(ss, 0.0), writes=["ss"])
                for tt in range(NTB):
                    k.op("act", lambda e, tt=tt: e.activation(out=junk, in_=xb[:, tt, :], func=AF.Square,
                                                              accum_out=ss[:, tt:tt + 1]),
                         reads=[("xb", tt)], writes=["junk", "ss"])
                k.op("act", lambda e: e.activation(out=sq, in_=ss, func=AF.Sqrt, scale=1.0 / D, bias=EPS),
                     reads=["ss"], writes=["sq"])
                k.op("dve", lambda e: e.reciprocal(out=rstd, in_=sq), reads=["sq"], writes=["rstd"])
                wv = vrow(vbc, nname)
                if tag == "f" and not cnt.get("dbg1"):
                    cnt["dbg1"] = 1
                    dbg_dump("ss", ss, ["ss"]); dbg_dump("rstd", rstd, ["rstd"]); dbg_dump("wv", wv, [vkey])
                    dbg_dump("xb0", xb[:, 0, :], [("xb", 0)])
                for tt in range(NTB):
                    hbi = tt % 2
                    k.op("dve", lambda e, tt=tt, hbi=hbi: e.scalar_tensor_tensor(
                        out=hb[hbi], in0=xb[:, tt, :], scalar=rstd[:, tt:tt + 1], in1=wv,
                        op0=ALU.mult, op1=ALU.mult), reads=[("xb", tt), "rstd", vkey], writes=[("hb", hbi)])
                    transpose_rows(hb[hbi], ("hb", hbi), 8, hT, tt, "hT")

            def transpose_rows(src, skey, nchunks, dstT, tt, dkey):
                for h0 in range(0, nchunks, 4):
                    nn = min(4, nchunks - h0)
                    pb, pk = PS(6)
                    pv = pb[:, (cnt["t"] % 2) * 512:(cnt["t"] % 2) * 512 + 512].rearrange("p (c t) -> p c t", c=4)
                    cnt["t"] += 1
                    for j in range(nn):
                        k.op("pe", lambda e, j=j, h0=h0: e.transpose(pv[:, j, :], src[:, (h0 + j) * 128:(h0 + j + 1) * 128], identb),
                             reads=[skey, "cstb"], writes=[pk])
                    copy_op(alt(), dstT[:, h0:h0 + nn, tt * 128:(tt + 1) * 128], pv[:, 0:nn, :],
                            reads=[], writes=[pk, (dkey, tt, h0)])

            def dump_hT(nm):
                dbg_dump(nm, hT[:, :, 0:128], [("hT", 0, 0), ("hT", 0, 4)], BF16)

            def hT_keys(t4, key="hT"):
                return [(key, tt, h0) for tt in range(t4 * 4, t4 * 4 + 4) for h0 in (0, 4)]

            def ffn(l, which, vbc, vkey):
                norm_to_hT(vbc, vkey, "ffn1_norm" if which == 0 else "ffn2_norm", "f")
                if not cnt.get("dbg2"):
                    cnt["dbg2"] = 1
                    dump_hT("hT")
                wg_l = wg_d[which][l].rearrange("(c p) f -> p c f", p=128)
                wu_l = wu_d[which][l].rearrange("(c p) f -> p c f", p=128)
                wd_l = wd_d[which][l]
                for g in range(DFF // 256):
                    wi = cnt["w"] % 2
                    cnt["w"] += 1
                    f0 = g * 256
                    k.dma(wgu[wi][:, 0], wg_l[:, :, f0:f0 + 256], writes=[("wgu", wi, 0)], q="pool")
                    k.dma(wgu[wi][:, 1], wu_l[:, :, f0:f0 + 256], writes=[("wgu", wi, 1)], q="pool")
                    k.dma(wdn[wi], wd_l[f0:f0 + 256, :].rearrange("(c p) d -> p c d", p=128), writes=[("wdn", wi)], q="pool")
                    for t4 in range(TB // 512):
                        ai = cnt["a"] % 2
                        cnt["a"] += 1
                        hk = hT_keys(t4)
                        for fc in range(2):
                            gi = cnt["g"] % 2
                            cnt["g"] += 1
                            pg, pgk = PS(gi)
                            pu, puk = PS(2 + gi)
                            for w, (pp, ppk) in enumerate(((pg, pgk), (pu, puk))):
                                for dc in range(8):
                                    k.op("pe", lambda e, pp=pp, w=w, dc=dc, fc=fc, wi=wi, t4=t4: e.matmul(
                                        pp[:], lhsT=wgu[wi][:, w, dc, fc * 128:(fc + 1) * 128],
                                        rhs=hT[:, dc, t4 * 512:(t4 + 1) * 512], start=(dc == 0), stop=(dc == 7)),
                                        reads=[("wgu", wi, w)] + hk, writes=[ppk])
                            k.op("act", lambda e, pg=pg, gi=gi: e.activation(out=sg[gi], in_=pg[:], func=AF.Silu),
                                 reads=[], writes=[pgk, ("sg", gi)])
                            k.op("dve", lambda e, pu=pu, gi=gi, ai=ai, fc=fc: e.tensor_tensor(
                                out=aT[ai][:, fc, :], in0=pu[:], in1=sg[gi], op=ALU.mult),
                                reads=[("sg", gi)], writes=[puk, ("aT", ai, fc)])
                        for ts in range(4):
                            tt = t4 * 4 + ts
                            for dh in range(2):
                                di = 4 + cnt["d"] % 2
                                cnt["d"] += 1
                                pd, pdk = PS(di)
                                for fc in range(2):
                                    k.op("pe", lambda e, pd=pd, fc=fc, ai=ai, ts=ts, dh=dh, wi=wi: e.matmul(
                                        pd[:], lhsT=aT[ai][:, fc, ts * 128:(ts + 1) * 128],
                                        rhs=wdn[wi][:, fc, dh * 512:(dh + 1) * 512], start=(fc == 0), stop=(fc == 1)),
                                        reads=[("aT", ai, fc), ("wdn", wi)], writes=[pdk])
                                k.op("dve", lambda e, pd=pd, tt=tt, dh=dh: e.scalar_tensor_tensor(
                                    out=xb[:, tt, dh * 512:(dh + 1) * 512], in0=pd[:], scalar=0.5,
                                    in1=xb[:, tt, dh * 512:(dh + 1) * 512], op0=ALU.mult, op1=ALU.add),
                                    reads=[], writes=[pdk, ("xb", tt)])

            def inproj(l, blk, vbc, vkey):
                norm_to_hT(vbc, vkey, "mix_norm", "m")
                win_l = win_d[l].rearrange("(c p) f -> p c f", p=128)
                t0 = blk * TB
                for cg in range(8):
                    wi = cnt["p"] % 2
                    cnt["p"] += 1
                    c0 = 512 + cg * 128
                    k.dma(wpj[wi][:, :, 0:128], win_l[:, :, c0:c0 + 128], writes=[("wpj", wi)], q="pool")
                    for t4 in range(TB // 512):
                        gi = cnt["g"] % 2
                        cnt["g"] += 1
                        pg, pgk = PS(gi)
                        hk = hT_keys(t4)
                        for dc in range(8):
                            k.op("pe", lambda e, pg=pg, dc=dc, wi=wi, t4=t4: e.matmul(
                                pg[:], lhsT=wpj[wi][:, dc, 0:128], rhs=hT[:, dc, t4 * 512:(t4 + 1) * 512],
                                start=(dc == 0), stop=(dc == 7)), reads=[("wpj", wi)] + hk, writes=[pgk])
                        ei = cnt["e"] % 2
                        cnt["e"] += 1
                        copy_op(alt(), evf[ei], pg[:], reads=[], writes=[pgk, ("evf", ei)])
                        k.dma(xbc_d[cg * 128:(cg + 1) * 128, t0 + t4 * 512:t0 + (t4 + 1) * 512], evf[ei],
                              reads=[("evf", ei)], writes=[("xbc_d", cg)])
                for (c0, ncol, j0) in ((0, 512, 0), (1536, 512, 512), (2048, 432, 1024)):
                    wi = cnt["p"] % 2
                    cnt["p"] += 1
                    k.dma(wpj[wi][:, :, 0:ncol], win_l[:, :, c0:c0 + ncol], writes=[("wpj", wi)], q="pool")
                    for tt in range(NTB):
                        gi = cnt["g"] % 2
                        cnt["g"] += 1
                        pu, puk = PS(2 + gi)
                        for dc in range(8):
                            k.op("pe", lambda e, pu=pu, dc=dc, wi=wi, tt=tt, ncol=ncol: e.matmul(
                                pu[:, 0:ncol], lhsT=hT[:, dc, tt * 128:(tt + 1) * 128], rhs=wpj[wi][:, dc, 0:ncol],
                                start=(dc == 0), stop=(dc == 7)),
                                reads=[("wpj", wi), ("hT", tt, 0), ("hT", tt, 4)], writes=[puk])
                        ei = cnt["e"] % 2
                        cnt["e"] += 1
                        copy_op(alt(), evt[ei][:, 0:ncol], pu[:, 0:ncol], reads=[], writes=[puk, ("evt", ei)])
                        k.dma(ptok_d[t0 + tt * 128:t0 + (tt + 1) * 128, j0:j0 + ncol], evt[ei][:, 0:ncol],
                              reads=[("evt", ei)], writes=[("ptok_d", blk)])

            def outproj(l, blk):
                t0 = blk * TB
                wout_l = wout_d[l].rearrange("(c p) f -> p c f", p=128)
                for tt in range(NTB):
                    hbi = tt % 2
                    k.dma(hb[hbi], y_d[t0 + tt * 128:t0 + (tt + 1) * 128, :], reads=[("y_d",)], writes=[("hb", hbi)])
                    transpose_rows(hb[hbi], ("hb", hbi), 8, hT, tt, "hT")
                for dh in range(2):
                    wi = cnt["p"] % 2
                    cnt["p"] += 1
                    k.dma(wpj[wi], wout_l[:, :, dh * 512:(dh + 1) * 512], writes=[("wpj", wi)], q="pool")
                    for tt in range(NTB):
                        di = 4 + cnt["d"] % 2
                        cnt["d"] += 1
                        pd, pdk = PS(di)
                        for dc in range(8):
                            k.op("pe", lambda e, pd=pd, dc=dc, wi=wi, tt=tt: e.matmul(
                                pd[:], lhsT=hT[:, dc, tt * 128:(tt + 1) * 128], rhs=wpj[wi][:, dc, :],
                                start=(dc == 0), stop=(dc == 7)),
                                reads=[("wpj", wi), ("hT", tt, 0), ("hT", tt, 4)], writes=[pdk])
                        k.op("dve", lambda e, pd=pd, tt=tt, dh=dh: e.tensor_tensor(
                            out=xb[:, tt, dh * 512:(dh + 1) * 512], in0=pd[:], in1=xb[:, tt, dh * 512:(dh + 1) * 512],
                            op=ALU.add), reads=[], writes=[pdk, ("xb", tt)])

            stores = []
            for blk in range(NBLK):
                t0 = blk * TB
                src = x_d if pidx == 0 else out_d
                for tt in range(NTB):
                    k.dma(xb[:, tt, :], src[t0 + tt * 128:t0 + (tt + 1) * 128, :],
                          reads=[("out_d", blk)], writes=[("xb", tt)])
                if l_prev is not None:
                    outproj(l_prev, blk)
                    ffn(l_prev, 1, vbcA, "vbcA")
                if l_next is not None:
                    ffn(l_next, 0, vbcB, "vbcB")
                    inproj(l_next, blk, vbcB, "vbcB")
                for tt in range(NTB):
                    stores.append(k.dma(out_d[t0 + tt * 128:t0 + (tt + 1) * 128, :], xb[:, tt, :],
                                        reads=[("xb", tt)], writes=[("out_d", blk)]))
            return stores

        def headnorm_rope(src, H, Dh, wrow, rlo, rhalf, cs, dst, tmp, keys_r, keys_w, cskey):
            sqv = tmp["sq"][:, 0:H * Dh].rearrange("p (h d) -> p h d", h=H)
            ssv = tmp["ss"][:, 0:H]
            rs = tmp["rs"][:, 0:H]
            qn = tmp["qn"][:, 0:H * Dh].rearrange("p (h d) -> p h d", h=H)
            t1 = tmp["t1"][:, 0:H * rhalf].rearrange("p (h d) -> p h d", h=H)
            t2 = tmp["t2"][:, 0:H * rhalf].rearrange("p (h d) -> p h d", h=H)
            T = ["T_sq", "T_ss", "T_rs", "T_qn", "T_t1", "T_t2"]
            k.op("dve", lambda e: e.tensor_tensor(out=sqv, in0=src, in1=src, op=ALU.mult), reads=keys_r, writes=[T[0]])
            k.op("dve", lambda e: e.tensor_reduce(out=ssv, in_=sqv, axis=AX.X, op=ALU.add), reads=[T[0]], writes=[T[1]])
            k.op("act", lambda e: e.activation(out=rs, in_=ssv, func=AF.Sqrt, scale=1.0 / Dh, bias=EPS), reads=[T[1]], writes=[T[2]])
            k.op("dve", lambda e: e.reciprocal(out=ssv, in_=rs), reads=[T[2]], writes=[T[1]])
            k.op("dve", lambda e: e.tensor_tensor(out=qn, in0=src, in1=ssv.unsqueeze(2).to_broadcast([128, H, Dh]), op=ALU.mult),
                 reads=keys_r + [T[1]], writes=[T[3]])
            k.op("dve", lambda e: e.tensor_tensor(out=qn, in0=qn, in1=wrow.unsqueeze(1).to_broadcast([128, H, Dh]), op=ALU.mult),
                 reads=["vbc"], writes=[T[3]])
            if rlo > 0:
                k.op("dve", lambda e: e.tensor_copy(dst[:, :, 0:rlo], qn[:, :, 0:rlo]), reads=[T[3]], writes=keys_w)
            x1 = qn[:, :, rlo:rlo + rhalf]
            x2 = qn[:, :, rlo + rhalf:rlo + 2 * rhalf]
            cb = cs[:, 0:rhalf].unsqueeze(1).to_broadcast([128, H, rhalf])
            sb = cs[:, rhalf:2 * rhalf].unsqueeze(1).to_broadcast([128, H, rhalf])
            k.op("dve", lambda e: e.tensor_tensor(out=t1, in0=x1, in1=cb, op=ALU.mult), reads=[T[3], cskey], writes=[T[4]])
            k.op("dve", lambda e: e.tensor_tensor(out=t2, in0=x2, in1=sb, op=ALU.mult), reads=[T[3], cskey], writes=[T[5]])
            k.op("dve", lambda e: e.tensor_tensor(out=dst[:, :, rlo:rlo + rhalf], in0=t1, in1=t2, op=ALU.subtract),
                 reads=[T[4], T[5]], writes=keys_w)
            k.op("dve", lambda e: e.tensor_tensor(out=t1, in0=x1, in1=sb, op=ALU.mult), reads=[T[3], cskey], writes=[T[4]])
            k.op("dve", lambda e: e.tensor_tensor(out=t2, in0=x2, in1=cb, op=ALU.mult), reads=[T[3], cskey], writes=[T[5]])
            k.op("dve", lambda e: e.tensor_tensor(out=dst[:, :, rlo + rhalf:rlo + 2 * rhalf], in0=t1, in1=t2, op=ALU.add),
                 reads=[T[4], T[5]], writes=keys_w)

        def heads_to_T(src, skey, H, Dh, dstT, tt, dkey, bank):
            pb, pk = PS(6)
            pv = pb[:, 0:512].rearrange("p (c t) -> p c t", c=4)
            for h in range(H):
                k.op("pe", lambda e, h=h: e.transpose(pv[0:Dh, h, :], src[:, h, :], identb), reads=[skey, "cstb"], writes=[pk])
            copy_op(alt(), dstT[0:Dh, 0:H, tt * 128:(tt + 1) * 128], pv[0:Dh, 0:H, :], reads=[], writes=[pk, (dkey, tt)])

        def out_norm_store(osrc, okeys, width, wrow, col0, tmp, tt, ring, tag):
            k.op("dve", lambda e: e.memset(tmp["ss"][:, 0:1], 0.0), writes=["T_ss"])
            k.op("act", lambda e: e.activation(out=tmp["sq"][:, 0:width], in_=osrc, func=AF.Square, accum_out=tmp["ss"][:, 0:1]),
                 reads=okeys, writes=["T_sq", "T_ss"])
            k.op("act", lambda e: e.activation(out=tmp["rs"][:, 0:1], in_=tmp["ss"][:, 0:1], func=AF.Sqrt, scale=1.0 / width, bias=EPS),
                 reads=["T_ss"], writes=["T_rs"])
            k.op("dve", lambda e: e.reciprocal(out=tmp["ss"][:, 1:2], in_=tmp["rs"][:, 0:1]), reads=["T_rs"], writes=["T_ss"])
            yi = ring % 2
            yb = tmp["yb"][yi][:, 0:width]
            k.op("dve", lambda e: e.scalar_tensor_tensor(out=yb, in0=osrc, scalar=tmp["ss"][:, 1:2], in1=wrow,
                                                         op0=ALU.mult, op1=ALU.mult),
                 reads=list(okeys) + ["T_ss", "vbc"], writes=[(tag + "yb", yi)])
            k.dma(y_d[tt * 128:(tt + 1) * 128, col0:col0 + width], yb, reads=[(tag + "yb", yi)], writes=[("y_d",)])

        def attn_finish(acc_bank, ncols_q, nheads, tmp, odst_fn, extra_den, tag):
            pa, pak = PS(acc_bank)
            oT = tmp["oT"]
            k.op("act", lambda e: e.copy(oT[0:65, 0:ncols_q], pa[0:65, 0:ncols_q]), reads=[], writes=[pak, "T_oT"])
            for j in range(ncols_q // 128):
                pb, pk = PS(7)
                k.op("pe", lambda e, j=j: e.transpose(pb[:, 0:65], oT[0:65, j * 128:(j + 1) * 128], identf[0:65, 0:65]),
                     reads=["T_oT", "cst"], writes=[pk])
                den = tmp["den"]
                if extra_den is not None:
                    ed = extra_den(j)
                    k.op("dve", lambda e, ed=ed: e.tensor_tensor(out=den[:, 0:1], in0=pb[:, 64:65], in1=ed, op=ALU.add),
                         reads=["sinkexp"], writes=[pk, "T_den"])
                else:
                    k.op("dve", lambda e: e.tensor_copy(den[:, 0:1], pb[:, 64:65]), reads=[], writes=[pk, "T_den"])
                k.op("dve", lambda e: e.reciprocal(out=den[:, 1:2], in_=den[:, 0:1]), reads=["T_den"], writes=["T_rden"])
                dst, dkey = odst_fn(j)
                k.op("dve", lambda e, dst=dst: e.tensor_scalar_mul(out=dst, in0=pb[:, 0:64], scalar1=den[:, 1:2]), reads=["T_rden"], writes=[pk, dkey])

        def mla_phase(l, vbc):
            fbase, bbase = A.p, A.pb
            qT = A.bf16(4 * S).rearrange("p (h t) -> p h t", h=4)
            kT = A.bf16(4 * S).rearrange("p (h t) -> p h t", h=4)
            vx = A.bf16(NT * 4 * 65).rearrange("p (t h d) -> p t h d", t=NT, h=4)
            oall = A.f32(NT * 256).rearrange("p (t d) -> p t d", t=NT)
            wuq = A.bf16(2 * 384).rearrange("p (c f) -> p c f", c=2)
            wukv = A.bf16(512)
            pin = [A.f32(416) for _ in range(2)]
            csb = [A.f32(32) for _ in range(2)]
            nb = [A.bf16(384) for _ in range(2)]
            nT = A.bf16(3 * 128).rearrange("p (c t) -> p c t", c=3)
            qf = A.f32(384).rearrange("p (h d) -> p h d", h=4)
            kf = A.f32(384).rearrange("p (h d) -> p h d", h=4)
            qb = A.bf16(384).rearrange("p (h d) -> p h d", h=4)
            kb_ = A.bf16(384).rearrange("p (h d) -> p h d", h=4)
            tmp = {"sq": A.f32(384), "ss": A.f32(8), "rs": A.f32(8), "qn": A.f32(384), "t1": A.f32(64), "t2": A.f32(64),
                   "oT": A.f32(512), "den": A.f32(2), "yb": [A.bf16(256) for _ in range(2)]}
            eb = [A.bf16(512) for _ in range(3)]
            k.dma(wuq, wuq_d[l].rearrange("(c p) f -> p c f", p=128), writes=["wuq"], q="pool")
            k.dma(wukv, wukv_d[l], writes=["wukv"], q="pool")
            k.op("dve", lambda e: e.memset(vx[:, :, :, 64:65], 1.0), writes=["vx1"])
            for tt in range(NT):
                pi = tt % 2
                k.dma(pin[pi], ptok_d[tt * 128:(tt + 1) * 128, 1040:1456], reads=[("ptok_d",)], writes=[("pin", pi)])
                k.dma(csb[pi], cs_mla_d[tt * 128:(tt + 1) * 128, :], writes=[("cs", pi)])
                for (c0, n, wn, tagn) in ((0, 256, "mla_q_lat_norm", "a"), (256, 128, "mla_kv_norm", "b")):
                    k.op("dve", lambda e: e.memset(tmp["ss"][:, 0:1], 0.0), writes=["T_ss"])
                    k.op("act", lambda e, c0=c0, n=n, pi=pi: e.activation(out=tmp["sq"][:, 0:n], in_=pin[pi][:, c0:c0 + n],
                                                                        func=AF.Square, accum_out=tmp["ss"][:, 0:1]),
                         reads=[("pin", pi)], writes=["T_sq", "T_ss"])
                    k.op("act", lambda e, n=n: e.activation(out=tmp["rs"][:, 0:1], in_=tmp["ss"][:, 0:1], func=AF.Sqrt,
                                                            scale=1.0 / n, bias=EPS), reads=["T_ss"], writes=["T_rs"])
                    k.op("dve", lambda e: e.reciprocal(out=tmp["ss"][:, 1:2], in_=tmp["rs"][:, 0:1]), reads=["T_rs"], writes=["T_ss"])
                    k.op("dve", lambda e, c0=c0, n=n, pi=pi, wn=wn: e.scalar_tensor_tensor(
                        out=nb[pi][:, c0:c0 + n], in0=pin[pi][:, c0:c0 + n], scalar=tmp["ss"][:, 1:2], in1=vrow(vbc, wn),
                        op0=ALU.mult, op1=ALU.mult), reads=[("pin", pi), "T_ss", "vbc"], writes=[("nb", pi, tagn)])
                pb, pk = PS(6)
                pv = pb[:, 512:1024].rearrange("p (c t) -> p c t", c=4)
                for c in range(3):
                    k.op("pe", lambda e, c=c, pi=pi: e.transpose(pv[:, c, :], nb[pi][:, c * 128:(c + 1) * 128], identb),
                         reads=[("nb", pi, "a"), ("nb", pi, "b"), "cstb"], writes=[pk])
                copy_op("act", nT, pv[:, 0:3, :], reads=[], writes=[pk, "nT"])
                pq, pqk = PS(0)
                for c in range(2):
                    k.op("pe", lambda e, c=c: e.matmul(pq[:, 0:384], lhsT=nT[:, c, :], rhs=wuq[:, c, :], start=(c == 0), stop=(c == 1)),
                         reads=["nT", "wuq"], writes=[pqk])
                pkv, pkvk = PS(1)
                k.op("pe", lambda e: e.matmul(pkv[:], lhsT=nT[:, 2, :], rhs=wukv, start=True, stop=True), reads=["nT", "wukv"], writes=[pkvk])
                copy_op("act", qf, pq[:, 0:384].rearrange("p (h d) -> p h d", h=4), reads=[], writes=[pqk, "qf"])
                kvv = pkv[:].rearrange("p (h d) -> p h d", h=4)
                copy_op("act", kf[:, :, 0:64], kvv[:, :, 0:64], reads=[], writes=[pkvk, "kf"])
                copy_op("dve", vx[:, tt, :, 0:64], kvv[:, :, 64:128], reads=[], writes=[pkvk, ("vx", tt)])
                k.op("dve", lambda e, pi=pi: e.tensor_copy(kf[:, :, 64:96], pin[pi][:, 384:416].unsqueeze(1).to_broadcast([128, 4, 32])),
                     reads=[("pin", pi)], writes=["kf"])
                headnorm_rope(qf, 4, 96, vrow(vbc, "mla_q_norm"), 64, 16, csb[pi], qb, tmp, ["qf"], ["qb"], ("cs", pi))
                headnorm_rope(kf, 4, 96, vrow(vbc, "mla_k_norm"), 64, 16, csb[pi], kb_, tmp, ["kf"], ["kb"], ("cs", pi))
                heads_to_T(qb, "qb", 4, 96, qT, tt, "qT", 6)
                heads_to_T(kb_, "kb", 4, 96, kT, tt, "kT", 6)
            scale = 96.0 ** -0.5
            qkeys = lambda qi: [("qT", tt) for tt in range(qi * 4, qi * 4 + 4)]
            ei = 0
            for h in range(4):
                for qi in range(S // 512):
                    pa, pak = PS(4 + (qi % 2))
                    for kt in range(NT):
                        psn, psk = PS(kt % 3)
                        k.op("pe", lambda e, psn=psn, h=h, kt=kt, qi=qi: e.matmul(
                            psn[:], lhsT=kT[0:96, h, kt * 128:(kt + 1) * 128], rhs=qT[0:96, h, qi * 512:(qi + 1) * 512],
                            start=True, stop=True), reads=[("kT", kt)] + qkeys(qi), writes=[psk])
                        ebi = ei % 3
                        ei += 1
                        k.op("act", lambda e, psn=psn, ebi=ebi: e.activation(out=eb[ebi], in_=psn[:], func=AF.Exp, scale=scale),
                             reads=[], writes=[psk, ("eb", ebi)])
                        k.op("pe", lambda e, pa=pa, kt=kt, h=h, ebi=ebi: e.matmul(
                            pa[0:65, :], lhsT=vx[:, kt, h, :], rhs=eb[ebi], start=(kt == 0), stop=(kt == NT - 1)),
                            reads=[("vx", kt), "vx1", ("eb", ebi)], writes=[pak])
                    attn_finish(4 + (qi % 2), 512, 1, tmp,
                                lambda j, h=h, qi=qi: (oall[:, qi * 4 + j, h * 64:(h + 1) * 64], ("oall", qi * 4 + j, h)),
                                None, "m")
            for tt in range(NT):
                okeys = [("oall", tt, h) for h in range(4)]
                out_norm_store(oall[:, tt, :], okeys, 256, vrow(vbc, "mla_out_norm"), 768, tmp, tt, tt, "mo")
            A.reset(fbase, bbase)

        def swa_phase(l, vbc):
            fbase, bbase = A.p, A.pb
            qT = A.bf16(4 * S).rearrange("p (h t) -> p h t", h=4)
            kT = A.bf16(2 * S).rearrange("p (h t) -> p h t", h=2)
            vx = A.bf16(NT * 2 * 65).rearrange("p (t h d) -> p t h d", t=NT, h=2)
            oall = A.f32(NT * 256).rearrange("p (t d) -> p t d", t=NT)
            pin = [A.f32(512) for _ in range(2)]
            csb = [A.f32(64) for _ in range(2)]
            qb = A.bf16(256).rearrange("p (h d) -> p h d", h=4)
            kb_ = A.bf16(128).rearrange("p (h d) -> p h d", h=2)
            tmp = {"sq": A.f32(256), "ss": A.f32(8), "rs": A.f32(8), "qn": A.f32(256), "t1": A.f32(128), "t2": A.f32(128),
                   "oT": A.f32(512), "den": A.f32(2), "yb": [A.bf16(256) for _ in range(2)]}
            sinkexp = A.f32(4)
            eb = [A.bf16(256) for _ in range(3)]
            k.op("act", lambda e: e.activation(out=sinkexp, in_=vrow(vbc, "swa_sink"), func=AF.Exp), reads=["vbc"], writes=["sinkexp"])
            k.op("dve", lambda e: e.memset(vx[:, :, :, 64:65], 1.0), writes=["vx1"])
            for tt in range(NT):
                pi = tt % 2
                k.dma(pin[pi], ptok_d[tt * 128:(tt + 1) * 128, 528:1040], reads=[("ptok_d",)], writes=[("pin", pi)])
                k.dma(csb[pi], cs_swa_d[tt * 128:(tt + 1) * 128, :], writes=[("cs", pi)])
                qsrc = pin[pi][:, 0:256].rearrange("p (h d) -> p h d", h=4)
                ksrc = pin[pi][:, 256:384].rearrange("p (h d) -> p h d", h=2)
                headnorm_rope(qsrc, 4, 64, vrow(vbc, "swa_q_norm"), 0, 32, csb[pi], qb, tmp, [("pin", pi)], ["qb"], ("cs", pi))
                headnorm_rope(ksrc, 2, 64, vrow(vbc, "swa_k_norm"), 0, 32, csb[pi], kb_, tmp, [("pin", pi)], ["kb"], ("cs", pi))
                k.op("dve", lambda e, pi=pi, tt=tt: e.tensor_copy(vx[:, tt, :, 0:64], pin[pi][:, 384:512].rearrange("p (h d) -> p h d", h=2)),
                     reads=[("pin", pi)], writes=[("vx", tt)])
                heads_to_T(qb, "qb", 4, 64, qT, tt, "qT", 6)
                heads_to_T(kb_, "kb", 2, 64, kT, tt, "kT", 6)
            scale = 64.0 ** -0.5
            ei = 0
            for n in range(NT):
                for kvh in range(2):
                    pa, pak = PS(4 + (kvh % 2))
                    js = [j for j in (n - 1, n, n + 1) if 0 <= j < NT]
                    for ji, j in enumerate(js):
                        psn, psk = PS(ei % 3)
                        k.op("pe", lambda e, psn=psn, kvh=kvh, j=j, n=n: e.matmul(
                            psn[:, 0:256].rearrange("p (a b) -> p a b", a=2), lhsT=kT[0:64, kvh, j * 128:(j + 1) * 128],
                            rhs=qT[0:64, 2 * kvh:2 * kvh + 2, n * 128:(n + 1) * 128], start=True, stop=True),
                            reads=[("kT", j), ("qT", n)], writes=[psk])
                        ebi = ei % 3
                        ei += 1
                        k.op("act", lambda e, psn=psn, ebi=ebi: e.activation(out=eb[ebi], in_=psn[:, 0:256], func=AF.Exp, scale=scale),
                             reads=[], writes=[psk, ("eb", ebi)])
                        if j != n:
                            mk = cstb[:, 128:384] if j < n else cstb[:, 384:640]
                            k.op("dve", lambda e, ebi=ebi, mk=mk: e.tensor_tensor(out=eb[ebi], in0=eb[ebi], in1=mk, op=ALU.mult),
                                 reads=["cstb"], writes=[("eb", ebi)])
                        k.op("pe", lambda e, pa=pa, j=j, kvh=kvh, ebi=ebi, ji=ji, js=js: e.matmul(
                            pa[0:65, 0:256], lhsT=vx[:, j, kvh, :], rhs=eb[ebi], start=(ji == 0), stop=(ji == len(js) - 1)),
                            reads=[("vx", j), "vx1", ("eb", ebi)], writes=[pak])
                    attn_finish(4 + (kvh % 2), 256, 2, tmp,
                                lambda jj, n=n, kvh=kvh: (oall[:, n, (2 * kvh + jj) * 64:(2 * kvh + jj + 1) * 64], ("oall", n, 2 * kvh + jj)),
                                lambda jj, kvh=kvh: sinkexp[:, 2 * kvh + jj:2 * kvh + jj + 1], "s")
            for tt in range(NT):
                okeys = [("oall", tt, h) for h in range(4)]
                out_norm_store(oall[:, tt, :], okeys, 256, vrow(vbc, "swa_out_norm"), 512, tmp, tt, tt, "so")
            A.reset(fbase, bbase)

        def ssd_phase(l, vbc):
            fbase, bbase = A.p, A.pb
            convw = A.f32(48).rearrange("p (c k) -> p c k", c=8)
            BT = A.bf16(2 * S).rearrange("p (g t) -> p g t", g=2)
            CT = A.bf16(2 * S).rearrange("p (g t) -> p g t", g=2)
            dt = A.f32(NT * 16).rearrange("p (t d) -> p t d", t=NT)
            av = A.f32(NT * 16).rearrange("p (t d) -> p t d", t=NT)
            cum = A.f32(NT * 16).rearrange("p (t d) -> p t d", t=NT)
            ncf = A.f32(NT * 8).rearrange("p (t d) -> p t d", t=NT)
            sc1 = A.f32(NT * 16).rearrange("p (t d) -> p t d", t=NT)
            sc2 = A.f32(NT * 16).rearrange("p (t d) -> p t d", t=NT)
            dw = A.f32(NT * 16).rearrange("p (t d) -> p t d", t=NT)
            tot = A.f32(NCH * 16).rearrange("p (c d) -> p c d", c=NCH)
            etot = A.f32(NCH * 16).rearrange("p (c d) -> p c d", c=NCH)
            Abc = A.f32(16)
            t16 = [A.f32(NT * 16).rearrange("p (t d) -> p t d", t=NT) for _ in range(3)]
            cumT = [A.f32(512).rearrange("p (d l) -> p d l", d=2) for _ in range(2)]
            Rst = [A.f32(256) for _ in range(2)]
            Rtmp = A.f32(256)
            Eb = [A.f32(256) for _ in range(4)]
            ya = [A.f32(256) for _ in range(2)]
            yg = [A.f32(256) for _ in range(2)]
            zb = [A.f32(256) for _ in range(2)]
            szb = [A.f32(256) for _ in range(2)]
            tmp = {"sq": A.f32(256), "ss": A.f32(8), "rs": A.f32(8), "yb": [A.bf16(256) for _ in range(2)]}
            xin = [A.bf16(S + 4)] * 2
            dg = A.bf16(5 * 128).rearrange("p (k c) -> p k c", k=5)
            xTf = A.bf16(2 * S).rearrange("p (c t) -> p c t", c=2)
            xs = A.bf16(NT * 256).rearrange("p (t d) -> p t d", t=NT)
            Btok = A.bf16(NT * 128).rearrange("p (t d) -> p t d", t=NT)
            Sinb = [A.bf16((NCH + 1) * 256).rearrange("p (c d) -> p c d", c=NCH + 1) for _ in range(2)]
            Mb = [A.bf16(256) for _ in range(4)]
            GT = [A.bf16(256) for _ in range(2)]
            xdt = [A.bf16(2 * 2 * 256).rearrange("p (r t d) -> p r t d", r=2, t=2) for _ in range(2)]
            xdw = [A.bf16(2 * 256).rearrange("p (t d) -> p t d", t=2) for _ in range(2)]
            k.dma(convw, convp_d[l].rearrange("p (c k) -> p c k", c=8), writes=["convw"])
            ci = [0]

            def conv_chunk(cc, dst_fn):
                xi = 0
                k.op("dve", lambda e, xi=xi: e.memset(xin[xi][:, 0:2], 0.0), writes=[("xin", xi)])
                k.op("dve", lambda e, xi=xi: e.memset(xin[xi][:, S + 2:S + 4], 0.0), writes=[("xin", xi)])
                k.dma(xin[xi][:, 2:S + 2], xbc_d[cc * 128:(cc + 1) * 128, :], reads=[("xbc_d",)], writes=[("xin", xi)])
                for kk in range(5):
                    k.op("dve", lambda e, kk=kk, cc=cc: e.tensor_scalar_mul(out=dg[:, kk, :], in0=identf, scalar1=convw[:, cc, kk:kk + 1]),
                         reads=["cst", "convw"], writes=["dg"])
                for t4 in range(S // 512):
                    pb, pk = PS(ci[0] % 2)
                    ci[0] += 1
                    for kk in range(5):
                        k.op("pe", lambda e, pb=pb, kk=kk, xi=xi, t4=t4: e.matmul(
                            pb[:], lhsT=dg[:, kk, :], rhs=xin[xi][:, t4 * 512 + kk:t4 * 512 + kk + 512],
                            start=(kk == 0), stop=(kk == 4)), reads=["dg", ("xin", xi)], writes=[pk])
                    dst, dkey = dst_fn(t4)
                    k.op("act", lambda e, pb=pb, dst=dst, cc=cc: e.activation(out=dst, in_=pb[:], func=AF.Silu, bias=convw[:, cc, 5:6]),
                         reads=["convw"], writes=[pk, dkey])
                ci[0] += 1

            for g in range(2):
                conv_chunk(4 + g, lambda t4, g=g: (BT[:, g, t4 * 512:(t4 + 1) * 512], ("BT", g, t4)))
                conv_chunk(6 + g, lambda t4, g=g: (CT[:, g, t4 * 512:(t4 + 1) * 512], ("CT", g, t4)))
            for tt in range(NT):
                k.dma(dt[:, tt, :], ptok_d[tt * 128:(tt + 1) * 128, 512:528], reads=[("ptok_d",)], writes=["dt"])
            bias_bc = vrow(vbc, "ssd_dt_bias").unsqueeze(1).to_broadcast([128, NT, 16])
            k.op("dve", lambda e: e.tensor_tensor(out=dt, in0=dt, in1=bias_bc, op=ALU.add), reads=["vbc"], writes=["dt"])
            k.op("dve", lambda e: e.tensor_scalar_mul(out=t16[0], in0=dt, scalar1=-1.0), reads=["dt"], writes=["t16a"])
            k.op("dve", lambda e: e.tensor_tensor(out=t16[0], in0=t16[0], in1=dt, op=ALU.max), reads=["dt"], writes=["t16a"])
            k.op("act", lambda e: e.activation(out=t16[1], in_=t16[0], func=AF.Exp, scale=-1.0), reads=["t16a"], writes=["t16b"])
            k.op("act", lambda e: e.activation(out=t16[0], in_=t16[1], func=AF.Ln, bias=1.0), reads=["t16b"], writes=["t16a"])
            k.op("dve", lambda e: e.tensor_scalar_max(out=t16[1], in0=dt, scalar1=0.0), reads=["dt"], writes=["t16b"])
            k.op("dve", lambda e: e.tensor_tensor(out=dt, in0=t16[0], in1=t16[1], op=ALU.add), reads=["t16a", "t16b"], writes=["dt"])
            k.op("act", lambda e: e.activation(out=Abc, in_=vrow(vbc, "ssd_a_log"), func=AF.Exp), reads=["vbc"], writes=["Abc"])
            k.op("dve", lambda e: e.scalar_tensor_tensor(out=av, in0=dt, scalar=-1.0, in1=Abc.unsqueeze(1).to_broadcast([128, NT, 16]),
                                                         op0=ALU.mult, op1=ALU.mult), reads=["dt", "Abc"], writes=["av"])
            dbg_dump("ssd_dt", dt, ["dt"])
            TRI_LE = cst[:, C_R0F:C_R0F + 128]
            TRI_LT = cst[:, C_R0B:C_R0B + 128]
            for c in range(NCH):
                t0_, t1_ = 2 * c, 2 * c + 1
                pb, pk = PS(c % 2)
                pv = pb[:, 0:48].rearrange("p (t d) -> p t d", t=3)
                for d, tri in ((0, TRI_LE), (1, TRI_LT)):
                    cs_ = slice(d * 8, d * 8 + 8)
                    k.op("pe", lambda e, pv=pv, tri=tri, cs_=cs_, t0_=t0_: e.matmul(pv[:, 0, cs_], lhsT=tri, rhs=av[:, t0_, cs_], start=True, stop=True),
                         reads=["av", "cst"], writes=[pk])
                    k.op("pe", lambda e, pv=pv, cs_=cs_, t0_=t0_: e.matmul(pv[:, 1, cs_], lhsT=ones_f, rhs=av[:, t0_, cs_], start=False, stop=False,
                                                                          skip_group_check=True), reads=["av", "cst"], writes=[pk])
                    k.op("pe", lambda e, pv=pv, tri=tri, cs_=cs_, t1_=t1_: e.matmul(pv[:, 1, cs_], lhsT=tri, rhs=av[:, t1_, cs_], start=False, stop=True,
                                                                                  skip_group_check=True), reads=["av", "cst"], writes=[pk])
                k.op("pe", lambda e, pv=pv, t0_=t0_: e.matmul(pv[:, 2, :], lhsT=ones_f, rhs=av[:, t0_, :], start=False, stop=False, skip_group_check=True),
                     reads=["av", "cst"], writes=[pk])
                k.op("pe", lambda e, pv=pv, t1_=t1_: e.matmul(pv[:, 2, :], lhsT=ones_f, rhs=av[:, t1_, :], start=False, stop=True, skip_group_check=True),
                     reads=["av", "cst"], writes=[pk])
                copy_op("dve", cum[:, t0_:t0_ + 2, :], pv[:, 0:2, :], reads=[], writes=[pk, "cum"])
                copy_op("dve", tot[:, c, :], pv[:, 2, :], reads=[], writes=[pk, "tot"])
            totb = lambda d: tot[:, :, d * 8:d * 8 + 8].unsqueeze(2).to_broadcast([128, NCH, 2, 8])
            v4 = lambda a, d: a[:, :, d * 8:d * 8 + 8].rearrange("p (c t) h -> p c t h", t=2)
            k.op("dve", lambda e: e.tensor_scalar_mul(out=ncf, in0=cum[:, :, 0:8], scalar1=-1.0), reads=["cum"], writes=["ncf"])
            k.op("act", lambda e: e.activation(out=sc1[:, :, 0:8], in_=cum[:, :, 0:8], func=AF.Exp), reads=["cum"], writes=["sc1a"])
            k.op("act", lambda e: e.activation(out=sc2[:, :, 8:16], in_=cum[:, :, 8:16], func=AF.Exp), reads=["cum"], writes=["sc2b"])
            k.op("dve", lambda e: e.tensor_tensor(out=v4(t16[2], 0), in0=totb(0), in1=v4(cum, 0), op=ALU.subtract), reads=["cum", "tot"], writes=["t16c0"])
            k.op("dve", lambda e: e.tensor_tensor(out=v4(t16[2], 1), in0=totb(1), in1=v4(cum, 1), op=ALU.subtract), reads=["cum", "tot"], writes=["t16c1"])
            k.op("act", lambda e: e.activation(out=sc2[:, :, 0:8], in_=t16[2][:, :, 0:8], func=AF.Exp), reads=["t16c0"], writes=["sc2a"])
            k.op("act", lambda e: e.activation(out=sc1[:, :, 8:16], in_=t16[2][:, :, 8:16], func=AF.Exp), reads=["t16c1"], writes=["sc1b"])
            k.op("act", lambda e: e.activation(out=etot, in_=tot, func=AF.Exp), reads=["tot"], writes=["etot"])
            k.op("dve", lambda e: e.tensor_tensor(out=dw, in0=dt, in1=sc2, op=ALU.mult), reads=["dt", "sc2a", "sc2b"], writes=["dw"])
            dbg_dump("ssd_cum", cum, ["cum"])
            Dbc = vrow(vbc, "ssd_d")
            v3 = lambda a: a.rearrange("p (h d) -> p h d", h=4)
            for g in range(2):
                for j in range(2):
                    conv_chunk(2 * g + j, lambda t4, j=j: (xTf[:, j, t4 * 512:(t4 + 1) * 512], ("xTf", t4)))
                for tt in range(NT):
                    pb, pk = PS(6)
                    pv = pb[:, 0:256].rearrange("p (c t) -> p c t", c=2)
                    for j in range(2):
                        k.op("pe", lambda e, pv=pv, j=j, tt=tt: e.transpose(pv[:, j, :], xTf[:, j, tt * 128:(tt + 1) * 128], identb),
                             reads=[("xTf", tt // 4), "cstb"], writes=[pk])
                    k.op("pe", lambda e, pb=pb, tt=tt: e.transpose(pb[:, 256:384], BT[:, g, tt * 128:(tt + 1) * 128], identb),
                         reads=[("BT", g, tt // 4), "cstb"], writes=[pk])
                    copy_op("act", xs[:, tt, :], pb[:, 0:256], reads=[], writes=[pk, ("xs", tt)])
                    copy_op("dve", Btok[:, tt, :], pb[:, 256:384], reads=[], writes=[pk, ("Btok", tt)])
                if g == 0:
                    dbg_dump("ssd_xs", xs, [("xs", tt) for tt in range(NT)], BF16)
                xs4 = xs.rearrange("p t (h d) -> p t h d", h=4)
                wi = 0
                for d in range(2):
                    k.op("dve", lambda e, d=d: e.memset(Rst[d], 0.0), writes=[("Rst", d)])
                    order = list(range(NCH)) if d == 0 else list(range(NCH - 1, -1, -1))
                    first_slot = 0 if d == 0 else NCH
                    k.op("dve", lambda e, d=d, first_slot=first_slot: e.memset(Sinb[d][:, first_slot, :], 0.0), writes=[("Sinb", d, first_slot)])
                    for c in order:
                        wr = wi % 2
                        wi += 1
                        dwb = dw[:, 2 * c:2 * c + 2, d * 8 + 4 * g:d * 8 + 4 * g + 4].unsqueeze(3).to_broadcast([128, 2, 4, 64])
                        k.op("dve", lambda e, wr=wr, dwb=dwb, c=c: e.tensor_tensor(out=xdw[wr].rearrange("p t (h d) -> p t h d", h=4),
                                                                                in0=xs4[:, 2 * c:2 * c + 2, :, :], in1=dwb, op=ALU.mult),
                             reads=[("xs", 2 * c), ("xs", 2 * c + 1), "dw"], writes=[("xdw", wr)])
                        pb, pk = PS(c % 2)
                        for ti in range(2):
                            tt = 2 * c + ti
                            k.op("pe", lambda e, pb=pb, tt=tt, ti=ti, wr=wr: e.matmul(
                                pb[:, 0:256], lhsT=Btok[:, tt, :], rhs=xdw[wr][:, ti, :], start=(ti == 0), stop=(ti == 1)),
                                reads=[("Btok", tt), ("xdw", wr)], writes=[pk])
                        etb = etot[:, c, d * 8 + 4 * g:d * 8 + 4 * g + 4].unsqueeze(2).to_broadcast([128, 4, 64])
                        k.op("dve", lambda e, d=d, etb=etb: e.tensor_tensor(out=v3(Rtmp), in0=v3(Rst[d]), in1=etb, op=ALU.mult),
                             reads=[("Rst", d), "etot"], writes=["Rtmp"])
                        k.op("dve", lambda e, pb=pb, d=d: e.tensor_tensor(out=Rst[d], in0=pb[:, 0:256], in1=Rtmp, op=ALU.add),
                             reads=["Rtmp"], writes=[pk, ("Rst", d)])
                        slot = c + 1 if d == 0 else c
                        k.op("act", lambda e, d=d, slot=slot: e.copy(Sinb[d][:, slot, :], Rst[d]), reads=[("Rst", d)], writes=[("Sinb", d, slot)])
                for c in range(NCH):
                    t0_, t1_ = 2 * c, 2 * c + 1
                    cr = c % 2
                    pb2, pk2 = PS(7)
                    for d, (r0, r1) in ((0, (C_R0F, C_R1F)), (1, (C_R0B, C_R1B))):
                        cs_ = slice(d * 8, d * 8 + 8)
                        o_ = pb2[0:8, d * 256:(d + 1) * 256]
                        k.op("pe", lambda e, o_=o_, cs_=cs_, t0_=t0_, r0=r0, d=d: e.matmul(o_, lhsT=av[:, t0_, cs_], rhs=cst[:, r0:r0 + 256],
                                                                                       start=(d == 0), stop=False, skip_group_check=True),
                             reads=["av", "cst"], writes=[pk2])
                        k.op("pe", lambda e, o_=o_, cs_=cs_, t1_=t1_, r1=r1: e.matmul(o_, lhsT=av[:, t1_, cs_], rhs=cst[:, r1:r1 + 256],
                                                                                     start=False, stop=True, skip_group_check=True),
                             reads=["av", "cst"], writes=[pk2])
                    copy_op("act", cumT[cr][0:8, :, :], pb2[0:8, 0:512].rearrange("p (d l) -> p d l", d=2), reads=[], writes=[pk2, ("cumT", cr)])
                    for d in range(2):
                        dtb = dt[:, 2 * c:2 * c + 2, d * 8 + 4 * g:d * 8 + 4 * g + 4].unsqueeze(3).to_broadcast([128, 2, 4, 64])
                        k.op("dve", lambda e, d=d, cr=cr, dtb=dtb, c=c: e.tensor_tensor(out=xdt[d][:, cr].rearrange("p t (h d) -> p t h d", h=4),
                                                                                     in0=xs4[:, 2 * c:2 * c + 2, :, :], in1=dtb, op=ALU.mult),
                             reads=[("xs", 2 * c), ("xs", 2 * c + 1), "dt"], writes=[("xdt", d, cr)])
                    for si in range(2):
                        pb, pk = PS(si)
                        k.op("pe", lambda e, pb=pb, si=si, c=c: e.matmul(pb[:, 0:256], lhsT=BT[:, g, (2 * c + si) * 128:(2 * c + si + 1) * 128],
                                                                         rhs=CT[:, g, c * 256:(c + 1) * 256], start=True, stop=True),
                             reads=[("BT", g, (2 * c + si) // 4), ("CT", g, c // 2)], writes=[pk])
                        copy_op("act", GT[si], pb[:, 0:256], reads=[], writes=[pk, ("GT", si)])
                    yb0, yk0 = PS(4)
                    yb1, yk1 = PS(5)
                    ybank = (yb0, yb1)
                    ykey = (yk0, yk1)
                    started = [False, False]
                    for h in range(4):
                        hh = 4 * g + h
                        sel = cst[0:8, C_SEL + hh * 128:C_SEL + (hh + 1) * 128]
                        for d in range(2):
                            mms = []
                            for si in range(2):
                                pb, pk = PS(2 + si)
                                if d == 0:
                                    lo, hi = (0, 256) if si == 0 else (128, 256)
                                    mask = cst[:, C_NEGF:C_NEGF + 256] if si == 0 else cst[:, C_NEGF:C_NEGF + 128]
                                else:
                                    lo, hi = (0, 128) if si == 0 else (0, 256)
                                    mask = cst[:, C_POSB + 128:C_POSB + 256] if si == 0 else cst[:, C_POSB:C_POSB + 256]
                                w = hi - lo
                                k.op("pe", lambda e, pb=pb, w=w, lo=lo, hi=hi, d=d, cr=cr, sel=sel: e.matmul(
                                    pb[:, 0:w], lhsT=sel, rhs=cumT[cr][0:8, d, lo:hi], start=True, stop=False),
                                    reads=[("cumT", cr), "cst"], writes=[pk])
                                k.op("pe", lambda e, pb=pb, w=w, mask=mask: e.matmul(pb[:, 0:w], lhsT=identf, rhs=mask, start=False, stop=True),
                                     reads=["cst"], writes=[pk])
                                ebi = 2 * d + si
                                tt_s = 2 * c + si
                                if d == 0:
                                    k.op("act", lambda e, pb=pb, w=w, ebi=ebi, tt_s=tt_s, hh=hh: e.activation(
                                        out=Eb[ebi][:, 0:w], in_=pb[:, 0:w], func=AF.Exp, bias=ncf[:, tt_s, hh:hh + 1], scale=1.0),
                                        reads=["ncf"], writes=[pk, ("Eb", ebi)])
                                else:
                                    k.op("act", lambda e, pb=pb, w=w, ebi=ebi, tt_s=tt_s, hh=hh: e.activation(
                                        out=Eb[ebi][:, 0:w], in_=pb[:, 0:w], func=AF.Exp, bias=cum[:, tt_s, 8 + hh:9 + hh], scale=-1.0),
                                        reads=["cum"], writes=[pk, ("Eb", ebi)])
                                k.op("dve", lambda e, ebi=ebi, w=w, si=si, lo=lo, hi=hi: e.tensor_tensor(
                                    out=Mb[ebi][:, 0:w], in0=Eb[ebi][:, 0:w], in1=GT[si][:, lo:hi], op=ALU.mult),
                                    reads=[("Eb", ebi), ("GT", si)], writes=[("Mb", ebi)])
                                for li in range(2):
                                    if not (lo <= li * 128 < hi):
                                        continue
                                    mms.append((ebi, li * 128 - lo, li, si))
                            for (ebi, off, li, si) in mms:
                                st_ = not started[li]
                                started[li] = True
                                k.op("pe", lambda e, ebi=ebi, off=off, li=li, si=si, d=d, h=h, st_=st_, cr=cr: e.matmul(
                                    ybank[li][:, h * 64:(h + 1) * 64], lhsT=Mb[ebi][:, off:off + 128], rhs=xdt[d][:, cr, si, h * 64:(h + 1) * 64],
                                    start=st_, stop=False, skip_group_check=True),
                                    reads=[("Mb", ebi), ("xdt", d, cr)], writes=[ykey[li]])
                    for li in range(2):
                        tt = 2 * c + li
                        pf7, pfk = PS(7)
                        pf = pf7[:, 0:256]
                        pbw = pf7[:, 256:512]
                        k.op("pe", lambda e, pf=pf, tt=tt, c=c: e.matmul(pf, lhsT=CT[:, g, tt * 128:(tt + 1) * 128], rhs=Sinb[0][:, c, :],
                                                                         start=True, stop=True), reads=[("CT", g, tt // 4), ("Sinb", 0, c)], writes=[pfk])
                        k.op("pe", lambda e, pbw=pbw, tt=tt, c=c: e.matmul(pbw, lhsT=CT[:, g, tt * 128:(tt + 1) * 128], rhs=Sinb[1][:, c + 1, :],
                                                                           start=False, stop=True, skip_group_check=True),
                             reads=[("CT", g, tt // 4), ("Sinb", 1, c + 1)], writes=[pfk])
                        ecfb = sc1[:, tt, 4 * g:4 * g + 4].unsqueeze(2).to_broadcast([128, 4, 64])
                        erbb = sc1[:, tt, 8 + 4 * g:12 + 4 * g].unsqueeze(2).to_broadcast([128, 4, 64])
                        dbb = Dbc[:, 4 * g:4 * g + 4].unsqueeze(2).to_broadcast([128, 4, 64])
                        yi = li
                        k.op("dve", lambda e, pf=pf, yi=yi, ecfb=ecfb: e.tensor_tensor(out=v3(ya[yi]), in0=v3(pf), in1=ecfb, op=ALU.mult),
                             reads=["sc1a"], writes=[pfk, ("ya", yi)])
                        k.op("dve", lambda e, pbw=pbw, erbb=erbb: e.tensor_tensor(out=v3(Rtmp), in0=v3(pbw), in1=erbb, op=ALU.mult),
                             reads=["sc1b"], writes=[pfk, "Rtmp"])
                        k.op("dve", lambda e, yi=yi: e.tensor_tensor(out=ya[yi], in0=ya[yi], in1=Rtmp, op=ALU.add), reads=["Rtmp"], writes=[("ya", yi)])
                        k.op("dve", lambda e, tt=tt, dbb=dbb: e.tensor_tensor(out=v3(Rtmp), in0=xs4[:, tt, :, :], in1=dbb, op=ALU.mult),
                             reads=[("xs", tt), "vbc"], writes=["Rtmp"])
                        k.op("dve", lambda e, yi=yi: e.tensor_tensor(out=ya[yi], in0=ya[yi], in1=Rtmp, op=ALU.add), reads=["Rtmp"], writes=[("ya", yi)])
                        k.op("dve", lambda e, yi=yi, li=li: e.tensor_tensor(out=ya[yi], in0=ybank[li][:, 0:256], in1=ya[yi], op=ALU.add),
                             reads=[], writes=[ykey[li], ("ya", yi)])
                        zi = li
                        k.dma(zb[zi], ptok_d[tt * 128:(tt + 1) * 128, g * 256:(g + 1) * 256], reads=[("ptok_d",)], writes=[("zb", zi)])
                        k.op("act", lambda e, zi=zi: e.activation(out=szb[zi], in_=zb[zi], func=AF.Silu), reads=[("zb", zi)], writes=[("szb", zi)])
                        k.op("dve", lambda e, yi=yi, zi=zi: e.tensor_tensor(out=yg[yi], in0=ya[yi], in1=szb[zi], op=ALU.mult),
                             reads=[("szb", zi), ("ya", yi)], writes=[("yg", yi)])
                        out_norm_store(yg[yi], [("yg", yi)], 256, vrow(vbc, "ssd_norm", g * 256, 256), g * 256, tmp, tt, tt, "do")
            A.reset(fbase, bbase)

        def mixer_phase(l):
            A.reset()
            vbc = A.f32(NV)
            k.dma(vbc, vec_d[l:l + 1, :].partition_broadcast(128), writes=["vbc"])
            ssd_phase(l, vbc)
            k.barrier(barsc[:, 1:2])
            swa_phase(l, vbc)
            k.barrier(barsc[:, 2:3])
            mla_phase(l, vbc)
            if l == 0:
                k.barrier(barsc[:, 3:4])
                dbg_dump("ptok", ptok_d, [("ptok_d",)], big=True)
                dbg_dump("xbc", xbc_d, [("xbc_d",)], BF16, big=True)
                dbg_dump("y", y_d, [("y_d",)], BF16, big=True)

        finals = []
        for p in range(L + 1):
            l_prev = p - 1 if p > 0 else None
            l_next = p if p < L else None
            if stop_after is not None and p > stop_after[0]:
                break
            finals = block_phase(p, l_prev, l_next)
            k.barrier(barsc[:, 0:1])
            if l_next is not None:
                if stop_after is not None and stop_after == (p, "block"):
                    break
                mixer_phase(l_next)
                k.barrier(barsc[:, 0:1])
                if stop_after is not None and stop_after == (p, "mix"):
                    break
        k.emit(list(finals) + dbg_outs)
        print("ops recorded:", k.nops, "arena hi (KiB):", A.hi * 4 / 1024, A.hib * 2 / 1024)
    return nc


def prep_common(inp, S):
    L = inp["ffn1_norm"].shape[0]
    vec = np.zeros((L, NV), np.float32)
    for n, (o, s) in VOFF.items():
        vec[:, o:o + s] = np.asarray(inp[n], np.float32).reshape(L, s)
    cw = np.asarray(inp["ssd_conv_w"], np.float32)
    cb = np.asarray(inp["ssd_conv_b"], np.float32)
    convp = np.zeros((L, 128, 8, 6), np.float32)
    for c in range(8):
        convp[:, :, c, 0:5] = cw[:, :, c * 128:(c + 1) * 128].transpose(0, 2, 1)
        convp[:, :, c, 5] = cb[:, c * 128:(c + 1) * 128]
    com = {"vecs": vec, "convp": np.ascontiguousarray(convp.reshape(L, 128, 48)), "cst": make_consts(),
           "cs_swa": rope_tables(S, 64), "cs_mla": rope_tables(S, 32)}
    for n in ("ffn1_gate", "ffn1_up", "ffn1_down", "ffn2_gate", "ffn2_up", "ffn2_down", "w_in", "w_out",
              "mla_w_uq", "mla_w_ukv"):
        com[n] = np.ascontiguousarray(np.asarray(inp[n], np.float32))
    return com, L


_NC_CACHE = {}


def kernel(**inputs):
    x = np.asarray(inputs["x"], np.float32)
    B, S, _ = x.shape
    com, L = prep_common(inputs, S)
    key = (S, L)
    if key not in _NC_CACHE:
        _NC_CACHE[key] = build(S, L)
    nc = _NC_CACHE[key]
    ncores = 8
    in_maps = []
    for c in range(ncores):
        m = dict(com)
        m["x"] = np.ascontiguousarray(x[c % B])
        in_maps.append(m)
    res = run_bass_kernel_spmd(nc, in_maps, core_ids=list(range(ncores)))
    out = np.stack([np.asarray(res.results[b]["out"], np.float32) for b in range(B)], axis=0)
    return out
```

```python
import contextlib
import numpy as np
import concourse.bass as bass
import concourse.mybir as mybir
from concourse.bass_utils import run_bass_kernel_spmd

F32 = mybir.dt.float32
BF16 = mybir.dt.bfloat16
AF = mybir.ActivationFunctionType
ALU = mybir.AluOpType
AX = mybir.AxisListType

D = 1024
DFF = 2816
NPROJ = 2480
NTM = 1456
EPS = 1e-6
BIG = 1.0e4
DEBUG = False
ALT_DVE_ONLY = False
N_DMA_SEMS = 40

VOFF = {}
_o = 0
for _n, _s in (("ffn1_norm", 1024), ("mix_norm", 1024), ("ffn2_norm", 1024), ("ssd_norm", 512),
               ("swa_q_norm", 64), ("swa_k_norm", 64), ("swa_out_norm", 256), ("mla_q_lat_norm", 256),
               ("mla_kv_norm", 128), ("mla_q_norm", 96), ("mla_k_norm", 96), ("mla_out_norm", 256),
               ("ssd_dt_bias", 16), ("ssd_a_log", 16), ("ssd_d", 8), ("swa_sink", 4)):
    VOFF[_n] = (_o, _s)
    _o += _s
NV = _o

C_ID = 0
C_R0F = 128
C_R1F = 384
C_R0B = 640
C_R1B = 896
C_NEGF = 1152
C_POSB = 1408
C_GE = 1664
C_SEL = 1792
NCST = C_SEL + 1024


def make_consts():
    c = np.zeros((128, NCST), np.float32)
    i = np.arange(128)
    le = (i[:, None] <= i[None, :]).astype(np.float32)
    lt = (i[:, None] < i[None, :]).astype(np.float32)
    ge = (i[:, None] >= i[None, :]).astype(np.float32)
    gt = (i[:, None] > i[None, :]).astype(np.float32)
    c[:, C_ID:C_ID + 128] = np.eye(128)
    c[:, C_R0F:C_R0F + 128] = le
    c[:, C_R0F + 128:C_R0F + 256] = 1.0
    c[:, C_R1F + 128:C_R1F + 256] = le
    c[:, C_R0B:C_R0B + 128] = lt
    c[:, C_R0B + 128:C_R0B + 256] = 1.0
    c[:, C_R1B + 128:C_R1B + 256] = lt
    c[:, C_NEGF:C_NEGF + 128] = -BIG * gt
    c[:, C_POSB + 128:C_POSB + 256] = BIG * lt
    c[:, C_GE:C_GE + 128] = ge
    for h in range(8):
        c[h, C_SEL + h * 128:C_SEL + (h + 1) * 128] = 1.0
    return c


def rope_tables(n, dim):
    inv = 1.0 / np.power(np.float32(10000.0), np.arange(0, dim, 2, dtype=np.float32) / np.float32(dim))
    ang = np.arange(n, dtype=np.float32)[:, None] * inv[None, :].astype(np.float32)
    return np.concatenate([np.cos(ang), np.sin(ang)], axis=1).astype(np.float32)


def _freeze(fn):
    import types
    if fn.__closure__ is None:
        return fn
    cells = []
    for c in fn.__closure__:
        try:
            cells.append(types.CellType(c.cell_contents))
        except ValueError:
            cells.append(c)
    return types.FunctionType(fn.__code__, fn.__globals__, fn.__name__, fn.__defaults__, tuple(cells))


class Op:
    __slots__ = ("eng", "fn", "waits", "signal", "sem", "val", "is_dma")

    def __init__(self, eng, fn, is_dma=False):
        self.eng = eng
        self.fn = fn
        self.waits = []
        self.signal = False
        self.sem = None
        self.val = None
        self.is_dma = is_dma


class KB:
    ENGS = ("pe", "act", "dve", "pool", "sp")

    def __init__(self, nc):
        self.nc = nc
        self.prog = {e: [] for e in self.ENGS}
        self.last_w = {}
        self.readers = {}
        self.dma_rr = 0
        self.dma_last = [None] * N_DMA_SEMS
        self.dma_uses = [0] * N_DMA_SEMS
        self.bar = None
        self.nops = 0
        self.carrier = None

    def _deps(self, op, reads, writes):
        deps = []
        if self.bar is not None:
            deps.append(self.bar)
        for k in reads:
            w = self.last_w.get(k)
            if w is not None:
                deps.append(w)
        for k in writes:
            w = self.last_w.get(k)
            if w is not None:
                deps.append(w)
            deps.extend(self.readers.get(k, ()))
        for k in reads:
            self.readers.setdefault(k, []).append(op)
        for k in writes:
            self.last_w[k] = op
            self.readers[k] = []
        seen = set()
        for d in deps:
            if d is op or id(d) in seen:
                continue
            seen.add(id(d))
            if op.eng == "pe" and d.eng == "pe" and not d.is_dma and not op.is_dma:
                continue
            op.waits.append(d)

    def op(self, eng, fn, reads=(), writes=()):
        o = Op(eng, _freeze(fn))
        self._deps(o, reads, writes)
        self.prog[eng].append(o)
        self.nops += 1
        return o

    def dma(self, out, in_, reads=(), writes=(), q="sp", **kw):
        o = Op(q, None, is_dma=True)
        o.fn = lambda e, out=out, in_=in_, kw=kw: e.dma_start(out=out, in_=in_, **kw)
        self._deps(o, reads, writes)
        i = self.dma_rr
        self.dma_rr = (self.dma_rr + 1) % N_DMA_SEMS
        prev = self.dma_last[i]
        if prev is not None:
            o.waits.append(prev)
        self.dma_last[i] = o
        self.dma_uses[i] += 1
        o.sem = i
        o.val = 16 * self.dma_uses[i]
        o.signal = True
        self.prog[q].append(o)
        self.nops += 1
        return o

    def barrier(self, scratch_ap):
        o = Op("dve", lambda e: e.memset(scratch_ap, 0.0))
        if self.bar is not None:
            o.waits.append(self.bar)
        for e in self.ENGS:
            for p in reversed(self.prog[e]):
                if not p.is_dma:
                    o.waits.append(p)
                    break
        for d in self.dma_last:
            if d is not None:
                o.waits.append(d)
        self.prog["dve"].append(o)
        self.bar = o
        self.last_w = {}
        self.readers = {}
        return o

    def emit(self, final_wait_ops=()):
        nc = self.nc
        for e in self.ENGS:
            for o in self.prog[e]:
                for w in o.waits:
                    w.signal = True
        for o in final_wait_ops:
            o.signal = True
        for e in self.ENGS:
            c = 0
            for o in self.prog[e]:
                if o.is_dma:
                    continue
                if o.signal:
                    c += 1
                    o.sem = e
                    o.val = c
        with contextlib.ExitStack() as st:
            esem = {e: st.enter_context(nc.semaphore("s_" + e)) for e in self.ENGS}
            dsem = [st.enter_context(nc.semaphore("d_%d" % i)) for i in range(N_DMA_SEMS)]
            block = st.enter_context(nc.Block())

            def sem_of(o):
                return dsem[o.sem] if o.is_dma else esem[o.sem]

            def run(e, h):
                waited = {}
                for o in self.prog[e]:
                    need = {}
                    for w in o.waits:
                        key = ("d", w.sem) if w.is_dma else ("e", w.sem)
                        if waited.get(key, 0) >= w.val:
                            continue
                        waited[key] = w.val
                        need[key] = w
                    need = list(need.values())
                    if e == "pe" and not o.is_dma:
                        for w in need[:-1]:
                            h.ldweights(self.carrier)._wait_ge(sem_of(w), w.val)
                        ins = o.fn(h)
                        if need:
                            ins._wait_ge(sem_of(need[-1]), need[-1].val)
                    else:
                        for w in need:
                            h.wait_ge(sem_of(w), w.val)
                        ins = o.fn(h)
                    if o.is_dma:
                        ins.then_inc(dsem[o.sem], 16)
                    elif o.signal:
                        ins.then_inc(esem[e], 1)
                if e == "sp":
                    for o in final_wait_ops:
                        h.wait_ge(sem_of(o), o.val)

            @block.tensor
            def _(t):
                run("pe", t)

            @block.scalar
            def _(s):
                run("act", s)

            @block.vector
            def _(v):
                run("dve", v)

            @block.gpsimd
            def _(g):
                run("pool", g)

            @block.sync
            def _(s):
                run("sp", s)


class Arena:
    def __init__(self, apf, nf, apb, nb):
        self.apf, self.nf, self.apb, self.nb = apf, nf, apb, nb
        self.p = 0
        self.pb = 0
        self.hi = 0
        self.hib = 0

    def reset(self, to=0, tob=0):
        self.p = to
        self.pb = tob

    def f32(self, cols):
        a = self.apf[:, self.p:self.p + cols]
        self.p += (cols + 7) // 8 * 8
        self.hi = max(self.hi, self.p)
        assert self.p <= self.nf, ("f32 arena overflow", self.p, self.nf)
        return a

    def bf16(self, cols):
        a = self.apb[:, self.pb:self.pb + cols]
        self.pb += (cols + 15) // 16 * 16
        self.hib = max(self.hib, self.pb)
        assert self.pb <= self.nb, ("bf16 arena overflow", self.pb, self.nb)
        return a


def build(S, L, stop_after=None):
    assert S % 512 == 0
    NT = S // 128
    TB = min(1024, S)
    NBLK = S // TB
    NTB = TB // 128
    NCH = S // 256

    nc = bass.Bass("TRN2", target_bir_lowering=False)
    dr = lambda name, shape, dt=F32, kind="ExternalInput": nc.dram_tensor(name, list(shape), dt, kind=kind).ap()
    x_d = dr("x", [S, D])
    wg_d = [dr("ffn1_gate", [L, D, DFF]), dr("ffn2_gate", [L, D, DFF])]
    wu_d = [dr("ffn1_up", [L, D, DFF]), dr("ffn2_up", [L, D, DFF])]
    wd_d = [dr("ffn1_down", [L, DFF, D]), dr("ffn2_down", [L, DFF, D])]
    win_d = dr("w_in", [L, D, NPROJ])
    wout_d = dr("w_out", [L, D, D])
    wuq_d = dr("mla_w_uq", [L, 256, 384])
    wukv_d = dr("mla_w_ukv", [L, 128, 512])
    vec_d = dr("vecs", [L, NV])
    convp_d = dr("convp", [L, 128, 48])
    cst_d = dr("cst", [128, NCST])
    cs_swa_d = dr("cs_swa", [S, 64])
    cs_mla_d = dr("cs_mla", [S, 32])
    out_d = dr("out", [S, D], kind="ExternalOutput")
    xbc_d = dr("xbc_scr", [1024, S], BF16, kind="Internal")
    ptok_d = dr("ptok_scr", [S, NTM], F32, kind="Internal")
    y_d = dr("y_scr", [S, D], BF16, kind="Internal")

    ARENA_F = 17 * 1024
    ARENA_B = 56 * 1024
    st = contextlib.ExitStack()
    with st:
        arena_f = st.enter_context(nc.sbuf_tensor("arena_f", [128, ARENA_F], F32))
        arena_b = st.enter_context(nc.sbuf_tensor("arena_b", [128, ARENA_B], BF16))
        cst = st.enter_context(nc.sbuf_tensor("cst_sb", [128, NCST], F32))
        cstb = st.enter_context(nc.sbuf_tensor("cstb_sb", [128, 128 * 5], BF16))
        barsc = st.enter_context(nc.sbuf_tensor("barsc", [128, 8], F32))
        banks = [st.enter_context(nc.psum_tensor("bank%d" % i, [128, 1024] if i == 6 else [128, 512], BF16 if i == 6 else F32))
                 for i in range(8)]
        k = KB(nc)
        A = Arena(arena_f[:], ARENA_F, arena_b[:], ARENA_B)

        k.carrier = cstb[:, 0:1]
        identf = cst[:, C_ID:C_ID + 128]
        identb = cstb[:, 0:128]
        ones_f = cst[:, C_R0F + 128:C_R0F + 256]

        k.dma(cst[:], cst_d, writes=["cst"])
        k.dma(cstb[:, 0:128], cst_d[:, C_ID:C_ID + 128], writes=["cstb"], q="pool")
        for j in range(2):
            k.dma(cstb[:, 128 + j * 128:256 + j * 128], cst_d[:, C_GE:C_GE + 128], writes=["cstb"], q="pool")
            k.dma(cstb[:, 384 + j * 128:512 + j * 128], cst_d[:, C_R0F:C_R0F + 128], writes=["cstb"], q="pool")

        def PS(i):
            return banks[i], "bank%d" % i

        rr = {"n": 0}
        dbg_outs = []
        dbg_names = set()

        def dbg_dump(name, ap, reads, dt=F32, big=False):
            if not DEBUG or ("dbg_" + name) in dbg_names:
                return
            dbg_names.add("dbg_" + name)
            t = nc.dram_tensor("dbg_" + name, list(ap.shape), dt, kind="ExternalOutput").ap()
            if big:
                for r0 in range(0, ap.shape[0], 256):
                    dbg_outs.append(k.dma(t[r0:r0 + 256], ap[r0:r0 + 256], reads=reads))
            else:
                dbg_outs.append(k.dma(t, ap, reads=reads))

        def alt():
            rr["n"] += 1
            return "dve" if (rr["n"] % 2 or ALT_DVE_ONLY) else "act"

        def copy_op(eng, out, in_, reads, writes):
            if eng == "act":
                return k.op("act", lambda e: e.copy(out, in_), reads, writes)
            return k.op(eng, lambda e: e.tensor_copy(out, in_), reads, writes)

        def vrow(vbc, name, lo=0, n=None):
            o, s = VOFF[name]
            n = s - lo if n is None else n
            return vbc[:, o + lo:o + lo + n]

        def block_phase(pidx, l_prev, l_next):
            A.reset()
            vbcA = A.f32(3072) if l_prev is not None else None
            vbcB = A.f32(3072) if l_next is not None else None
            xb = A.f32(NTB * D).rearrange("p (t d) -> p t d", t=NTB)
            hT = A.bf16(8 * TB).rearrange("p (c t) -> p c t", c=8)
            hb = [A.bf16(D) for _ in range(2)]
            junk = A.f32(D)
            ss = A.f32(NTB)
            sq = A.f32(NTB)
            rstd = A.f32(NTB)
            wgu = [A.bf16(2 * 8 * 256).rearrange("p (w c f) -> p w c f", w=2, c=8) for _ in range(2)]
            wdn = [A.bf16(2 * D).rearrange("p (c d) -> p c d", c=2) for _ in range(2)]
            sg = [A.bf16(512) for _ in range(2)]
            aT = [A.bf16(2 * 512).rearrange("p (c t) -> p c t", c=2) for _ in range(2)]
            wpj = [A.bf16(8 * 512).rearrange("p (c f) -> p c f", c=8) for _ in range(2)]
            evf = [A.bf16(512) for _ in range(2)]
            evt = [A.f32(512) for _ in range(2)]
            cnt = {"w": 0, "p": 0, "a": 0, "d": 0, "t": 0, "e": 0, "g": 0}
            if l_prev is not None:
                k.dma(vbcA, vec_d[l_prev:l_prev + 1, 0:3072].partition_broadcast(128), writes=["vbcA"])
            if l_next is not None:
                k.dma(vbcB, vec_d[l_next:l_next + 1, 0:3072].partition_broadcast(128), writes=["vbcB"])

            def norm_to_hT(vbc, vkey, nname, tag):
                k.op("dve", lambda e: e.memset(ss, 0.0), writes=["ss"])
                for tt in range(NTB):
                    k.op("act", lambda e, tt=tt: e.activation(out=junk, in_=xb[:, tt, :], func=AF.Square,
                                                              accum_out=ss[:, tt:tt + 1]),
                         reads=[("xb", tt)], writes=["junk", "ss"])
                k.op("act", lambda e: e.activation(out=sq, in_=ss, func=AF.Sqrt, scale=1.0 / D, bias=EPS),
                     reads=["ss"], writes=["sq"])
                k.op("dve", lambda e: e.reciprocal(out=rstd, in_=sq), reads=["sq"], writes=["rstd"])
                wv = vrow(vbc, nname)
                if tag == "f" and not cnt.get("dbg1"):
                    cnt["dbg1"] = 1
                    dbg_dump("ss", ss, ["ss"]); dbg_dump("rstd", rstd, ["rstd"]); dbg_dump("wv", wv, [vkey])
                    dbg_dump("xb0", xb[:, 0, :], [("xb", 0)])
                for tt in range(NTB):
                    hbi = tt % 2
                    k.op("dve", lambda e, tt=tt, hbi=hbi: e.scalar_tensor_tensor(
                        out=hb[hbi], in0=xb[:, tt, :], scalar=rstd[:, tt:tt + 1], in1=wv,
                        op0=ALU.mult, op1=ALU.mult), reads=[("xb", tt), "rstd", vkey], writes=[("hb", hbi)])
                    transpose_rows(hb[hbi], ("hb", hbi), 8, hT, tt, "hT")

            def transpose_rows(src, skey, nchunks, dstT, tt, dkey):
                for h0 in range(0, nchunks, 4):
                    nn = min(4, nchunks - h0)
                    pb, pk = PS(6)
                    pv = pb[:, (cnt["t"] % 2) * 512:(cnt["t"] % 2) * 512 + 512].rearrange("p (c t) -> p c t", c=4)
                    cnt["t"] += 1
                    for j in range(nn):
                        k.op("pe", lambda e, j=j, h0=h0: e.transpose(pv[:, j, :], src[:, (h0 + j) * 128:(h0 + j + 1) * 128], identb),
                             reads=[skey, "cstb"], writes=[pk])
                    copy_op(alt(), dstT[:, h0:h0 + nn, tt * 128:(tt + 1) * 128], pv[:, 0:nn, :],
                            reads=[], writes=[pk, (dkey, tt, h0)])

            def dump_hT(nm):
                dbg_dump(nm, hT[:, :, 0:128], [("hT", 0, 0), ("hT", 0, 4)], BF16)

            def hT_keys(t4, key="hT"):
                return [(key, tt, h0) for tt in range(t4 * 4, t4 * 4 + 4) for h0 in (0, 4)]

            def ffn(l, which, vbc, vkey):
                norm_to_hT(vbc, vkey, "ffn1_norm" if which == 0 else "ffn2_norm", "f")
                if not cnt.get("dbg2"):
                    cnt["dbg2"] = 1
                    dump_hT("hT")
                wg_l = wg_d[which][l].rearrange("(c p) f -> p c f", p=128)
                wu_l = wu_d[which][l].rearrange("(c p) f -> p c f", p=128)
                wd_l = wd_d[which][l]
                for g in range(DFF // 256):
                    wi = cnt["w"] % 2
                    cnt["w"] += 1
                    f0 = g * 256
                    k.dma(wgu[wi][:, 0], wg_l[:, :, f0:f0 + 256], writes=[("wgu", wi, 0)], q="pool")
                    k.dma(wgu[wi][:, 1], wu_l[:, :, f0:f0 + 256], writes=[("wgu", wi, 1)], q="pool")
                    k.dma(wdn[wi], wd_l[f0:f0 + 256, :].rearrange("(c p) d -> p c d", p=128), writes=[("wdn", wi)], q="pool")
                    for t4 in range(TB // 512):
                        ai = cnt["a"] % 2
                        cnt["a"] += 1
                        hk = hT_keys(t4)
                        for fc in range(2):
                            gi = cnt["g"] % 2
                            cnt["g"] += 1
                            pg, pgk = PS(gi)
                            pu, puk = PS(2 + gi)
                            for w, (pp, ppk) in enumerate(((pg, pgk), (pu, puk))):
                                for dc in range(8):
                                    k.op("pe", lambda e, pp=pp, w=w, dc=dc, fc=fc, wi=wi, t4=t4: e.matmul(
                                        pp[:], lhsT=wgu[wi][:, w, dc, fc * 128:(fc + 1) * 128],
                                        rhs=hT[:, dc, t4 * 512:(t4 + 1) * 512], start=(dc == 0), stop=(dc == 7)),
                                        reads=[("wgu", wi, w)] + hk, writes=[ppk])
                            k.op("act", lambda e, pg=pg, gi=gi: e.activation(out=sg[gi], in_=pg[:], func=AF.Silu),
                                 reads=[], writes=[pgk, ("sg", gi)])
                            k.op("dve", lambda e, pu=pu, gi=gi, ai=ai, fc=fc: e.tensor_tensor(
                                out=aT[ai][:, fc, :], in0=pu[:], in1=sg[gi], op=ALU.mult),
                                reads=[("sg", gi)], writes=[puk, ("aT", ai, fc)])
                        for ts in range(4):
                            tt = t4 * 4 + ts
                            for dh in range(2):
                                di = 4 + cnt["d"] % 2
                                cnt["d"] += 1
                                pd, pdk = PS(di)
                                for fc in range(2):
                                    k.op("pe", lambda e, pd=pd, fc=fc, ai=ai, ts=ts, dh=dh, wi=wi: e.matmul(
                                        pd[:], lhsT=aT[ai][:, fc, ts * 128:(ts + 1) * 128],
                                        rhs=wdn[wi][:, fc, dh * 512:(dh + 1) * 512], start=(fc == 0), stop=(fc == 1)),
                                        reads=[("aT", ai, fc), ("wdn", wi)], writes=[pdk])
                                k.op("dve", lambda e, pd=pd, tt=tt, dh=dh: e.scalar_tensor_tensor(
                                    out=xb[:, tt, dh * 512:(dh + 1) * 512], in0=pd[:], scalar=0.5,
                                    in1=xb[:, tt, dh * 512:(dh + 1) * 512], op0=ALU.mult, op1=ALU.add),
                                    reads=[], writes=[pdk, ("xb", tt)])

            def inproj(l, blk, vbc, vkey):
                norm_to_hT(vbc, vkey, "mix_norm", "m")
                win_l = win_d[l].rearrange("(c p) f -> p c f", p=128)
                t0 = blk * TB
                for cg in range(8):
                    wi = cnt["p"] % 2
                    cnt["p"] += 1
                    c0 = 512 + cg * 128
                    k.dma(wpj[wi][:, :, 0:128], win_l[:, :, c0:c0 + 128], writes=[("wpj", wi)], q="pool")
                    for t4 in range(TB // 512):
                        gi = cnt["g"] % 2
                        cnt["g"] += 1
                        pg, pgk = PS(gi)
                        hk = hT_keys(t4)
                        for dc in range(8):
                            k.op("pe", lambda e, pg=pg, dc=dc, wi=wi, t4=t4: e.matmul(
                                pg[:], lhsT=wpj[wi][:, dc, 0:128], rhs=hT[:, dc, t4 * 512:(t4 + 1) * 512],
                                start=(dc == 0), stop=(dc == 7)), reads=[("wpj", wi)] + hk, writes=[pgk])
                        ei = cnt["e"] % 2
                        cnt["e"] += 1
                        copy_op(alt(), evf[ei], pg[:], reads=[], writes=[pgk, ("evf", ei)])
                        k.dma(xbc_d[cg * 128:(cg + 1) * 128, t0 + t4 * 512:t0 + (t4 + 1) * 512], evf[ei],
                              reads=[("evf", ei)], writes=[("xbc_d", cg)])
                for (c0, ncol, j0) in ((0, 512, 0), (1536, 512, 512), (2048, 432, 1024)):
                    wi = cnt["p"] % 2
                    cnt["p"] += 1
                    k.dma(wpj[wi][:, :, 0:ncol], win_l[:, :, c0:c0 + ncol], writes=[("wpj", wi)], q="pool")
                    for tt in range(NTB):
                        gi = cnt["g"] % 2
                        cnt["g"] += 1
                        pu, puk = PS(2 + gi)
                        for dc in range(8):
                            k.op("pe", lambda e, pu=pu, dc=dc, wi=wi, tt=tt, ncol=ncol: e.matmul(
                                pu[:, 0:ncol], lhsT=hT[:, dc, tt * 128:(tt + 1) * 128], rhs=wpj[wi][:, dc, 0:ncol],
                                start=(dc == 0), stop=(dc == 7)),
                                reads=[("wpj", wi), ("hT", tt, 0), ("hT", tt, 4)], writes=[puk])
                        ei = cnt["e"] % 2
                        cnt["e"] += 1
                        copy_op(alt(), evt[ei][:, 0:ncol], pu[:, 0:ncol], reads=[], writes=[puk, ("evt", ei)])
                        k.dma(ptok_d[t0 + tt * 128:t0 + (tt + 1) * 128, j0:j0 + ncol], evt[ei][:, 0:ncol],
                              reads=[("evt", ei)], writes=[("ptok_d", blk)])

            def outproj(l, blk):
                t0 = blk * TB
                wout_l = wout_d[l].rearrange("(c p) f -> p c f", p=128)
                for tt in range(NTB):
                    hbi = tt % 2
                    k.dma(hb[hbi], y_d[t0 + tt * 128:t0 + (tt + 1) * 128, :], reads=[("y_d",)], writes=[("hb", hbi)])
                    transpose_rows(hb[hbi], ("hb", hbi), 8, hT, tt, "hT")
                for dh in range(2):
                    wi = cnt["p"] % 2
                    cnt["p"] += 1
                    k.dma(wpj[wi], wout_l[:, :, dh * 512:(dh + 1) * 512], writes=[("wpj", wi)], q="pool")
                    for tt in range(NTB):
                        di = 4 + cnt["d"] % 2
                        cnt["d"] += 1
                        pd, pdk = PS(di)
                        for dc in range(8):
                            k.op("pe", lambda e, pd=pd, dc=dc, wi=wi, tt=tt: e.matmul(
                                pd[:], lhsT=hT[:, dc, tt * 128:(tt + 1) * 128], rhs=wpj[wi][:, dc, :],
                                start=(dc == 0), stop=(dc == 7)),
                                reads=[("wpj", wi), ("hT", tt, 0), ("hT", tt, 4)], writes=[pdk])
                        k.op("dve", lambda e, pd=pd, tt=tt, dh=dh: e.tensor_tensor(
                            out=xb[:, tt, dh * 512:(dh + 1) * 512], in0=pd[:], in1=xb[:, tt, dh * 512:(dh + 1) * 512],
                            op=ALU.add), reads=[], writes=[pdk, ("xb", tt)])

            stores = []
            for blk in range(NBLK):
                t0 = blk * TB
                src = x_d if pidx == 0 else out_d
                for tt in range(NTB):
                    k.dma(xb[:, tt, :], src[t0 + tt * 128:t0 + (tt + 1) * 128, :],
                          reads=[("out_d", blk)], writes=[("xb", tt)])
                if l_prev is not None:
                    outproj(l_prev, blk)
                    ffn(l_prev, 1, vbcA, "vbcA")
                if l_next is not None:
                    ffn(l_next, 0, vbcB, "vbcB")
                    inproj(l_next, blk, vbcB, "vbcB")
                for tt in range(NTB):
                    stores.append(k.dma(out_d[t0 + tt * 128:t0 + (tt + 1) * 128, :], xb[:, tt, :],
                                        reads=[("xb", tt)], writes=[("out_d", blk)]))
            return stores

        def headnorm_rope(src, H, Dh, wrow, rlo, rhalf, cs, dst, tmp, keys_r, keys_w, cskey):
            sqv = tmp["sq"][:, 0:H * Dh].rearrange("p (h d) -> p h d", h=H)
            ssv = tmp["ss"][:, 0:H]
            rs = tmp["rs"][:, 0:H]
            qn = tmp["qn"][:, 0:H * Dh].rearrange("p (h d) -> p h d", h=H)
            t1 = tmp["t1"][:, 0:H * rhalf].rearrange("p (h d) -> p h d", h=H)
            t2 = tmp["t2"][:, 0:H * rhalf].rearrange("p (h d) -> p h d", h=H)
            T = ["T_sq", "T_ss", "T_rs", "T_qn", "T_t1", "T_t2"]
            k.op("dve", lambda e: e.tensor_tensor(out=sqv, in0=src, in1=src, op=ALU.mult), reads=keys_r, writes=[T[0]])
            k.op("dve", lambda e: e.tensor_reduce(out=ssv, in_=sqv, axis=AX.X, op=ALU.add), reads=[T[0]], writes=[T[1]])
            k.op("act", lambda e: e.activation(out=rs, in_=ssv, func=AF.Sqrt, scale=1.0 / Dh, bias=EPS), reads=[T[1]], writes=[T[2]])
            k.op("dve", lambda e: e.reciprocal(out=ssv, in_=rs), reads=[T[2]], writes=[T[1]])
            k.op("dve", lambda e: e.tensor_tensor(out=qn, in0=src, in1=ssv.unsqueeze(2).to_broadcast([128, H, Dh]), op=ALU.mult),
                 reads=keys_r + [T[1]], writes=[T[3]])
            k.op("dve", lambda e: e.tensor_tensor(out=qn, in0=qn, in1=wrow.unsqueeze(1).to_broadcast([128, H, Dh]), op=ALU.mult),
                 reads=["vbc"], writes=[T[3]])
            if rlo > 0:
                k.op("dve", lambda e: e.tensor_copy(dst[:, :, 0:rlo], qn[:, :, 0:rlo]), reads=[T[3]], writes=keys_w)
            x1 = qn[:, :, rlo:rlo + rhalf]
            x2 = qn[:, :, rlo + rhalf:rlo + 2 * rhalf]
            cb = cs[:, 0:rhalf].unsqueeze(1).to_broadcast([128, H, rhalf])
            sb = cs[:, rhalf:2 * rhalf].unsqueeze(1).to_broadcast([128, H, rhalf])
            k.op("dve", lambda e: e.tensor_tensor(out=t1, in0=x1, in1=cb, op=ALU.mult), reads=[T[3], cskey], writes=[T[4]])
            k.op("dve", lambda e: e.tensor_tensor(out=t2, in0=x2, in1=sb, op=ALU.mult), reads=[T[3], cskey], writes=[T[5]])
            k.op("dve", lambda e: e.tensor_tensor(out=dst[:, :, rlo:rlo + rhalf], in0=t1, in1=t2, op=ALU.subtract),
                 reads=[T[4], T[5]], writes=keys_w)
            k.op("dve", lambda e: e.tensor_tensor(out=t1, in0=x1, in1=sb, op=ALU.mult), reads=[T[3], cskey], writes=[T[4]])
            k.op("dve", lambda e: e.tensor_tensor(out=t2, in0=x2, in1=cb, op=ALU.mult), reads=[T[3], cskey], writes=[T[5]])
            k.op("dve", lambda e: e.tensor_tensor(out=dst[:, :, rlo + rhalf:rlo + 2 * rhalf], in0=t1, in1=t2, op=ALU.add),
                 reads=[T[4], T[5]], writes=keys_w)

        def heads_to_T(src, skey, H, Dh, dstT, tt, dkey, bank):
            pb, pk = PS(6)
            pv = pb[:, 0:512].rearrange("p (c t) -> p c t", c=4)
            for h in range(H):
                k.op("pe", lambda e, h=h: e.transpose(pv[0:Dh, h, :], src[:, h, :], identb), reads=[skey, "cstb"], writes=[pk])
            copy_op(alt(), dstT[0:Dh, 0:H, tt * 128:(tt + 1) * 128], pv[0:Dh, 0:H, :], reads=[], writes=[pk, (dkey, tt)])

        def out_norm_store(osrc, okeys, width, wrow, col0, tmp, tt, ring, tag):
            k.op("dve", lambda e: e.memset(tmp["ss"][:, 0:1], 0.0), writes=["T_ss"])
            k.op("act", lambda e: e.activation(out=tmp["sq"][:, 0:width], in_=osrc, func=AF.Square, accum_out=tmp["ss"][:, 0:1]),
                 reads=okeys, writes=["T_sq", "T_ss"])
            k.op("act", lambda e: e.activation(out=tmp["rs"][:, 0:1], in_=tmp["ss"][:, 0:1], func=AF.Sqrt, scale=1.0 / width, bias=EPS),
                 reads=["T_ss"], writes=["T_rs"])
            k.op("dve", lambda e: e.reciprocal(out=tmp["ss"][:, 1:2], in_=tmp["rs"][:, 0:1]), reads=["T_rs"], writes=["T_ss"])
            yi = ring % 2
            yb = tmp["yb"][yi][:, 0:width]
            k.op("dve", lambda e: e.scalar_tensor_tensor(out=yb, in0=osrc, scalar=tmp["ss"][:, 1:2], in1=wrow,
                                                         op0=ALU.mult, op1=ALU.mult),
                 reads=list(okeys) + ["T_ss", "vbc"], writes=[(tag + "yb", yi)])
            k.dma(y_d[tt * 128:(tt + 1) * 128, col0:col0 + width], yb, reads=[(tag + "yb", yi)], writes=[("y_d",)])

        def attn_finish(acc_bank, ncols_q, nheads, tmp, odst_fn, extra_den, tag):
            pa, pak = PS(acc_bank)
            oT = tmp["oT"]
            k.op("act", lambda e: e.copy(oT[0:65, 0:ncols_q], pa[0:65, 0:ncols_q]), reads=[], writes=[pak, "T_oT"])
            for j in range(ncols_q // 128):
                pb, pk = PS(7)
                k.op("pe", lambda e, j=j: e.transpose(pb[:, 0:65], oT[0:65, j * 128:(j + 1) * 128], identf[0:65, 0:65]),
                     reads=["T_oT", "cst"], writes=[pk])
                den = tmp["den"]
                if extra_den is not None:
                    ed = extra_den(j)
                    k.op("dve", lambda e, ed=ed: e.tensor_tensor(out=den[:, 0:1], in0=pb[:, 64:65], in1=ed, op=ALU.add),
                         reads=["sinkexp"], writes=[pk, "T_den"])
                else:
                    k.op("dve", lambda e: e.tensor_copy(den[:, 0:1], pb[:, 64:65]), reads=[], writes=[pk, "T_den"])
                k.op("dve", lambda e: e.reciprocal(out=den[:, 1:2], in_=den[:, 0:1]), reads=["T_den"], writes=["T_rden"])
                dst, dkey = odst_fn(j)
                k.op("dve", lambda e, dst=dst: e.tensor_scalar_mul(out=dst, in0=pb[:, 0:64], scalar1=den[:, 1:2]), reads=["T_rden"], writes=[pk, dkey])

        def mla_phase(l, vbc):
            fbase, bbase = A.p, A.pb
            qT = A.bf16(4 * S).rearrange("p (h t) -> p h t", h=4)
            kT = A.bf16(4 * S).rearrange("p (h t) -> p h t", h=4)
            vx = A.bf16(NT * 4 * 65).rearrange("p (t h d) -> p t h d", t=NT, h=4)
            oall = A.f32(NT * 256).rearrange("p (t d) -> p t d", t=NT)
            wuq = A.bf16(2 * 384).rearrange("p (c f) -> p c f", c=2)
            wukv = A.bf16(512)
            pin = [A.f32(416) for _ in range(2)]
            csb = [A.f32(32) for _ in range(2)]
            nb = [A.bf16(384) for _ in range(2)]
            nT = A.bf16(3 * 128).rearrange("p (c t) -> p c t", c=3)
            qf = A.f32(384).rearrange("p (h d) -> p h d", h=4)
            kf = A.f32(384).rearrange("p (h d) -> p h d", h=4)
            qb = A.bf16(384).rearrange("p (h d) -> p h d", h=4)
            kb_ = A.bf16(384).rearrange("p (h d) -> p h d", h=4)
            tmp = {"sq": A.f32(384), "ss": A.f32(8), "rs": A.f32(8), "qn": A.f32(384), "t1": A.f32(64), "t2": A.f32(64),
                   "oT": A.f32(512), "den": A.f32(2), "yb": [A.bf16(256) for _ in range(2)]}
            eb = [A.bf16(512) for _ in range(3)]
            k.dma(wuq, wuq_d[l].rearrange("(c p) f -> p c f", p=128), writes=["wuq"], q="pool")
            k.dma(wukv, wukv_d[l], writes=["wukv"], q="pool")
            k.op("dve", lambda e: e.memset(vx[:, :, :, 64:65], 1.0), writes=["vx1"])
            for tt in range(NT):
                pi = tt % 2
                k.dma(pin[pi], ptok_d[tt * 128:(tt + 1) * 128, 1040:1456], reads=[("ptok_d",)], writes=[("pin", pi)])
                k.dma(csb[pi], cs_mla_d[tt * 128:(tt + 1) * 128, :], writes=[("cs", pi)])
                for (c0, n, wn, tagn) in ((0, 256, "mla_q_lat_norm", "a"), (256, 128, "mla_kv_norm", "b")):
                    k.op("dve", lambda e: e.memset(tmp["ss"][:, 0:1], 0.0), writes=["T_ss"])
                    k.op("act", lambda e, c0=c0, n=n, pi=pi: e.activation(out=tmp["sq"][:, 0:n], in_=pin[pi][:, c0:c0 + n],
                                                                        func=AF.Square, accum_out=tmp["ss"][:, 0:1]),
                         reads=[("pin", pi)], writes=["T_sq", "T_ss"])
                    k.op("act", lambda e, n=n: e.activation(out=tmp["rs"][:, 0:1], in_=tmp["ss"][:, 0:1], func=AF.Sqrt,
                                                            scale=1.0 / n, bias=EPS), reads=["T_ss"], writes=["T_rs"])
                    k.op("dve", lambda e: e.reciprocal(out=tmp["ss"][:, 1:2], in_=tmp["rs"][:, 0:1]), reads=["T_rs"], writes=["T_ss"])
                    k.op("dve", lambda e, c0=c0, n=n, pi=pi, wn=wn: e.scalar_tensor_tensor(
                        out=nb[pi][:, c0:c0 + n], in0=pin[pi][:, c0:c0 + n], scalar=tmp["ss"][:, 1:2], in1=vrow(vbc, wn),
                        op0=ALU.mult, op1=ALU.mult), reads=[("pin", pi), "T_ss", "vbc"], writes=[("nb", pi, tagn)])
                pb, pk = PS(6)
                pv = pb[:, 512:1024].rearrange("p (c t) -> p c t", c=4)
                for c in range(3):
                    k.op("pe", lambda e, c=c, pi=pi: e.transpose(pv[:, c, :], nb[pi][:, c * 128:(c + 1) * 128], identb),
                         reads=[("nb", pi, "a"), ("nb", pi, "b"), "cstb"], writes=[pk])
                copy_op("act", nT, pv[:, 0:3, :], reads=[], writes=[pk, "nT"])
                pq, pqk = PS(0)
                for c in range(2):
                    k.op("pe", lambda e, c=c: e.matmul(pq[:, 0:384], lhsT=nT[:, c, :], rhs=wuq[:, c, :], start=(c == 0), stop=(c == 1)),
                         reads=["nT", "wuq"], writes=[pqk])
                pkv, pkvk = PS(1)
                k.op("pe", lambda e: e.matmul(pkv[:], lhsT=nT[:, 2, :], rhs=wukv, start=True, stop=True), reads=["nT", "wukv"], writes=[pkvk])
                copy_op("act", qf, pq[:, 0:384].rearrange("p (h d) -> p h d", h=4), reads=[], writes=[pqk, "qf"])
                kvv = pkv[:].rearrange("p (h d) -> p h d", h=4)
                copy_op("act", kf[:, :, 0:64], kvv[:, :, 0:64], reads=[], writes=[pkvk, "kf"])
                copy_op("dve", vx[:, tt, :, 0:64], kvv[:, :, 64:128], reads=[], writes=[pkvk, ("vx", tt)])
                k.op("dve", lambda e, pi=pi: e.tensor_copy(kf[:, :, 64:96], pin[pi][:, 384:416].unsqueeze(1).to_broadcast([128, 4, 32])),
                     reads=[("pin", pi)], writes=["kf"])
                headnorm_rope(qf, 4, 96, vrow(vbc, "mla_q_norm"), 64, 16, csb[pi], qb, tmp, ["qf"], ["qb"], ("cs", pi))
                headnorm_rope(kf, 4, 96, vrow(vbc, "mla_k_norm"), 64, 16, csb[pi], kb_, tmp, ["kf"], ["kb"], ("cs", pi))
                heads_to_T(qb, "qb", 4, 96, qT, tt, "qT", 6)
                heads_to_T(kb_, "kb", 4, 96, kT, tt, "kT", 6)
            scale = 96.0 ** -0.5
            qkeys = lambda qi: [("qT", tt) for tt in range(qi * 4, qi * 4 + 4)]
            items = [(h, qi, kt) for h in range(4) for qi in range(S // 512) for kt in range(NT)]
            LA = 2

            def stA(i):
                h, qi, kt = items[i]
                psn, psk = PS(i % 3)
                k.op("pe", lambda e: e.matmul(
                    psn[:], lhsT=kT[0:96, h, kt * 128:(kt + 1) * 128], rhs=qT[0:96, h, qi * 512:(qi + 1) * 512],
                    start=True, stop=True), reads=[("kT", kt)] + qkeys(qi), writes=[psk])

            def stBC(i):
                h, qi, kt = items[i]
                psn, psk = PS(i % 3)
                ebi = i % 3
                blk = h * (S // 512) + qi
                pa, pak = PS(4 + (blk % 2))
                k.op("act", lambda e: e.activation(out=eb[ebi], in_=psn[:], func=AF.Exp, scale=scale),
                     reads=[], writes=[psk, ("eb", ebi)])
                k.op("pe", lambda e: e.matmul(
                    pa[0:65, :], lhsT=vx[:, kt, h, :], rhs=eb[ebi], start=(kt == 0), stop=(kt == NT - 1)),
                    reads=[("vx", kt), "vx1", ("eb", ebi)], writes=[pak])
                if kt == NT - 1:
                    attn_finish(4 + (blk % 2), 512, 1, tmp,
                                lambda j, h=h, qi=qi: (oall[:, qi * 4 + j, h * 64:(h + 1) * 64], ("oall", qi * 4 + j, h)),
                                None, "m")

            for i in range(min(LA, len(items))):
                stA(i)
            for i in range(len(items)):
                if i + LA < len(items):
                    stA(i + LA)
                stBC(i)
            for tt in range(NT):
                okeys = [("oall", tt, h) for h in range(4)]
                out_norm_store(oall[:, tt, :], okeys, 256, vrow(vbc, "mla_out_norm"), 768, tmp, tt, tt, "mo")
            A.reset(fbase, bbase)

        def swa_phase(l, vbc):
            fbase, bbase = A.p, A.pb
            qT = A.bf16(4 * S).rearrange("p (h t) -> p h t", h=4)
            kT = A.bf16(2 * S).rearrange("p (h t) -> p h t", h=2)
            vx = A.bf16(NT * 2 * 65).rearrange("p (t h d) -> p t h d", t=NT, h=2)
            oall = A.f32(NT * 256).rearrange("p (t d) -> p t d", t=NT)
            pin = [A.f32(512) for _ in range(2)]
            csb = [A.f32(64) for _ in range(2)]
            qb = A.bf16(256).rearrange("p (h d) -> p h d", h=4)
            kb_ = A.bf16(128).rearrange("p (h d) -> p h d", h=2)
            tmp = {"sq": A.f32(256), "ss": A.f32(8), "rs": A.f32(8), "qn": A.f32(256), "t1": A.f32(128), "t2": A.f32(128),
                   "oT": A.f32(512), "den": A.f32(2), "yb": [A.bf16(256) for _ in range(2)]}
            sinkexp = A.f32(4)
            eb = [A.bf16(256) for _ in range(3)]
            k.op("act", lambda e: e.activation(out=sinkexp, in_=vrow(vbc, "swa_sink"), func=AF.Exp), reads=["vbc"], writes=["sinkexp"])
            k.op("dve", lambda e: e.memset(vx[:, :, :, 64:65], 1.0), writes=["vx1"])
            for tt in range(NT):
                pi = tt % 2
                k.dma(pin[pi], ptok_d[tt * 128:(tt + 1) * 128, 528:1040], reads=[("ptok_d",)], writes=[("pin", pi)])
                k.dma(csb[pi], cs_swa_d[tt * 128:(tt + 1) * 128, :], writes=[("cs", pi)])
                qsrc = pin[pi][:, 0:256].rearrange("p (h d) -> p h d", h=4)
                ksrc = pin[pi][:, 256:384].rearrange("p (h d) -> p h d", h=2)
                headnorm_rope(qsrc, 4, 64, vrow(vbc, "swa_q_norm"), 0, 32, csb[pi], qb, tmp, [("pin", pi)], ["qb"], ("cs", pi))
                headnorm_rope(ksrc, 2, 64, vrow(vbc, "swa_k_norm"), 0, 32, csb[pi], kb_, tmp, [("pin", pi)], ["kb"], ("cs", pi))
                k.op("dve", lambda e, pi=pi, tt=tt: e.tensor_copy(vx[:, tt, :, 0:64], pin[pi][:, 384:512].rearrange("p (h d) -> p h d", h=2)),
                     reads=[("pin", pi)], writes=[("vx", tt)])
                heads_to_T(qb, "qb", 4, 64, qT, tt, "qT", 6)
                heads_to_T(kb_, "kb", 2, 64, kT, tt, "kT", 6)
            scale = 64.0 ** -0.5
            items = []
            for n in range(NT):
                for kvh in range(2):
                    js = [j for j in (n - 1, n, n + 1) if 0 <= j < NT]
                    for ji, j in enumerate(js):
                        items.append((n, kvh, j, ji, len(js)))
            LA = 2
            pend = []

            def stA(i):
                n, kvh, j, ji, nj = items[i]
                psn, psk = PS(i % 3)
                k.op("pe", lambda e: e.matmul(
                    psn[:, 0:256].rearrange("p (a b) -> p a b", a=2), lhsT=kT[0:64, kvh, j * 128:(j + 1) * 128],
                    rhs=qT[0:64, 2 * kvh:2 * kvh + 2, n * 128:(n + 1) * 128], start=True, stop=True),
                    reads=[("kT", j), ("qT", n)], writes=[psk])

            def stBC(i):
                n, kvh, j, ji, nj = items[i]
                psn, psk = PS(i % 3)
                ebi = i % 3
                blk = n * 2 + kvh
                pa, pak = PS(4 + (blk % 2))
                k.op("act", lambda e: e.activation(out=eb[ebi], in_=psn[:, 0:256], func=AF.Exp, scale=scale),
                     reads=[], writes=[psk, ("eb", ebi)])
                if j != n:
                    mk = cstb[:, 128:384] if j < n else cstb[:, 384:640]
                    k.op("dve", lambda e: e.tensor_tensor(out=eb[ebi], in0=eb[ebi], in1=mk, op=ALU.mult),
                         reads=["cstb"], writes=[("eb", ebi)])
                k.op("pe", lambda e: e.matmul(
                    pa[0:65, 0:256], lhsT=vx[:, j, kvh, :], rhs=eb[ebi], start=(ji == 0), stop=(ji == nj - 1)),
                    reads=[("vx", j), "vx1", ("eb", ebi)], writes=[pak])
                if ji == 0 and pend:
                    pend.pop(0)()
                if ji == nj - 1:
                    pend.append(lambda n=n, kvh=kvh, blk=blk: attn_finish(
                        4 + (blk % 2), 256, 2, tmp,
                        lambda jj: (oall[:, n, (2 * kvh + jj) * 64:(2 * kvh + jj + 1) * 64], ("oall", n, 2 * kvh + jj)),
                        lambda jj: sinkexp[:, 2 * kvh + jj:2 * kvh + jj + 1], "s"))

            for i in range(min(LA, len(items))):
                stA(i)
            for i in range(len(items)):
                if i + LA < len(items):
                    stA(i + LA)
                stBC(i)
            while pend:
                pend.pop(0)()
            for tt in range(NT):
                okeys = [("oall", tt, h) for h in range(4)]
                out_norm_store(oall[:, tt, :], okeys, 256, vrow(vbc, "swa_out_norm"), 512, tmp, tt, tt, "so")
            A.reset(fbase, bbase)

        def ssd_phase(l, vbc):
            fbase, bbase = A.p, A.pb
            convw = A.f32(48).rearrange("p (c k) -> p c k", c=8)
            BT = A.bf16(2 * S).rearrange("p (g t) -> p g t", g=2)
            CT = A.bf16(2 * S).rearrange("p (g t) -> p g t", g=2)
            dt = A.f32(NT * 16).rearrange("p (t d) -> p t d", t=NT)
            av = A.f32(NT * 16).rearrange("p (t d) -> p t d", t=NT)
            cum = A.f32(NT * 16).rearrange("p (t d) -> p t d", t=NT)
            ncf = A.f32(NT * 8).rearrange("p (t d) -> p t d", t=NT)
            sc1 = A.f32(NT * 16).rearrange("p (t d) -> p t d", t=NT)
            sc2 = A.f32(NT * 16).rearrange("p (t d) -> p t d", t=NT)
            dw = A.f32(NT * 16).rearrange("p (t d) -> p t d", t=NT)
            tot = A.f32(NCH * 16).rearrange("p (c d) -> p c d", c=NCH)
            etot = A.f32(NCH * 16).rearrange("p (c d) -> p c d", c=NCH)
            Abc = A.f32(16)
            t16 = [A.f32(NT * 16).rearrange("p (t d) -> p t d", t=NT) for _ in range(3)]
            cumT = [A.f32(512).rearrange("p (d l) -> p d l", d=2) for _ in range(2)]
            Rst = [A.f32(256) for _ in range(2)]
            Rtmp = A.f32(256)
            Eb = [A.f32(256) for _ in range(4)]
            ya = [A.f32(256) for _ in range(2)]
            yg = [A.f32(256) for _ in range(2)]
            zb = [A.f32(256) for _ in range(2)]
            szb = [A.f32(256) for _ in range(2)]
            tmp = {"sq": A.f32(256), "ss": A.f32(8), "rs": A.f32(8), "yb": [A.bf16(256) for _ in range(2)]}
            xin = [A.bf16(S + 4)] * 2
            dg = A.bf16(5 * 128).rearrange("p (k c) -> p k c", k=5)
            xTf = A.bf16(2 * S).rearrange("p (c t) -> p c t", c=2)
            xs = A.bf16(NT * 256).rearrange("p (t d) -> p t d", t=NT)
            Btok = A.bf16(NT * 128).rearrange("p (t d) -> p t d", t=NT)
            Sinb = [A.bf16((NCH + 1) * 256).rearrange("p (c d) -> p c d", c=NCH + 1) for _ in range(2)]
            Mb = [A.bf16(256) for _ in range(4)]
            GT = [A.bf16(512) for _ in range(2)]
            xdt = [A.bf16(2 * 2 * 256).rearrange("p (r t d) -> p r t d", r=2, t=2) for _ in range(2)]
            xdw = [A.bf16(2 * 256).rearrange("p (t d) -> p t d", t=2) for _ in range(2)]
            k.dma(convw, convp_d[l].rearrange("p (c k) -> p c k", c=8), writes=["convw"])
            ci = [0]

            def conv_chunk(cc, dst_fn):
                xi = 0
                k.op("dve", lambda e, xi=xi: e.memset(xin[xi][:, 0:2], 0.0), writes=[("xin", xi)])
                k.op("dve", lambda e, xi=xi: e.memset(xin[xi][:, S + 2:S + 4], 0.0), writes=[("xin", xi)])
                k.dma(xin[xi][:, 2:S + 2], xbc_d[cc * 128:(cc + 1) * 128, :], reads=[("xbc_d",)], writes=[("xin", xi)])
                for kk in range(5):
                    k.op("dve", lambda e, kk=kk, cc=cc: e.tensor_scalar_mul(out=dg[:, kk, :], in0=identf, scalar1=convw[:, cc, kk:kk + 1]),
                         reads=["cst", "convw"], writes=["dg"])
                for t4 in range(S // 512):
                    pb, pk = PS(ci[0] % 2)
                    ci[0] += 1
                    for kk in range(5):
                        k.op("pe", lambda e, pb=pb, kk=kk, xi=xi, t4=t4: e.matmul(
                            pb[:], lhsT=dg[:, kk, :], rhs=xin[xi][:, t4 * 512 + kk:t4 * 512 + kk + 512],
                            start=(kk == 0), stop=(kk == 4)), reads=["dg", ("xin", xi)], writes=[pk])
                    dst, dkey = dst_fn(t4)
                    k.op("act", lambda e, pb=pb, dst=dst, cc=cc: e.activation(out=dst, in_=pb[:], func=AF.Silu, bias=convw[:, cc, 5:6]),
                         reads=["convw"], writes=[pk, dkey])
                ci[0] += 1

            for g in range(2):
                conv_chunk(4 + g, lambda t4, g=g: (BT[:, g, t4 * 512:(t4 + 1) * 512], ("BT", g, t4)))
                conv_chunk(6 + g, lambda t4, g=g: (CT[:, g, t4 * 512:(t4 + 1) * 512], ("CT", g, t4)))
            for tt in range(NT):
                k.dma(dt[:, tt, :], ptok_d[tt * 128:(tt + 1) * 128, 512:528], reads=[("ptok_d",)], writes=["dt"])
            bias_bc = vrow(vbc, "ssd_dt_bias").unsqueeze(1).to_broadcast([128, NT, 16])
            k.op("dve", lambda e: e.tensor_tensor(out=dt, in0=dt, in1=bias_bc, op=ALU.add), reads=["vbc"], writes=["dt"])
            k.op("dve", lambda e: e.tensor_scalar_mul(out=t16[0], in0=dt, scalar1=-1.0), reads=["dt"], writes=["t16a"])
            k.op("dve", lambda e: e.tensor_tensor(out=t16[0], in0=t16[0], in1=dt, op=ALU.max), reads=["dt"], writes=["t16a"])
            k.op("act", lambda e: e.activation(out=t16[1], in_=t16[0], func=AF.Exp, scale=-1.0), reads=["t16a"], writes=["t16b"])
            k.op("act", lambda e: e.activation(out=t16[0], in_=t16[1], func=AF.Ln, bias=1.0), reads=["t16b"], writes=["t16a"])
            k.op("dve", lambda e: e.tensor_scalar_max(out=t16[1], in0=dt, scalar1=0.0), reads=["dt"], writes=["t16b"])
            k.op("dve", lambda e: e.tensor_tensor(out=dt, in0=t16[0], in1=t16[1], op=ALU.add), reads=["t16a", "t16b"], writes=["dt"])
            k.op("act", lambda e: e.activation(out=Abc, in_=vrow(vbc, "ssd_a_log"), func=AF.Exp), reads=["vbc"], writes=["Abc"])
            k.op("dve", lambda e: e.scalar_tensor_tensor(out=av, in0=dt, scalar=-1.0, in1=Abc.unsqueeze(1).to_broadcast([128, NT, 16]),
                                                         op0=ALU.mult, op1=ALU.mult), reads=["dt", "Abc"], writes=["av"])
            dbg_dump("ssd_dt", dt, ["dt"])
            TRI_LE = cst[:, C_R0F:C_R0F + 128]
            TRI_LT = cst[:, C_R0B:C_R0B + 128]
            for c in range(NCH):
                t0_, t1_ = 2 * c, 2 * c + 1
                pb, pk = PS(c % 2)
                pv = pb[:, 0:48].rearrange("p (t d) -> p t d", t=3)
                for d, tri in ((0, TRI_LE), (1, TRI_LT)):
                    cs_ = slice(d * 8, d * 8 + 8)
                    k.op("pe", lambda e, pv=pv, tri=tri, cs_=cs_, t0_=t0_: e.matmul(pv[:, 0, cs_], lhsT=tri, rhs=av[:, t0_, cs_], start=True, stop=True),
                         reads=["av", "cst"], writes=[pk])
                    k.op("pe", lambda e, pv=pv, cs_=cs_, t0_=t0_: e.matmul(pv[:, 1, cs_], lhsT=ones_f, rhs=av[:, t0_, cs_], start=False, stop=False,
                                                                          skip_group_check=True), reads=["av", "cst"], writes=[pk])
                    k.op("pe", lambda e, pv=pv, tri=tri, cs_=cs_, t1_=t1_: e.matmul(pv[:, 1, cs_], lhsT=tri, rhs=av[:, t1_, cs_], start=False, stop=True,
                                                                                  skip_group_check=True), reads=["av", "cst"], writes=[pk])
                k.op("pe", lambda e, pv=pv, t0_=t0_: e.matmul(pv[:, 2, :], lhsT=ones_f, rhs=av[:, t0_, :], start=False, stop=False, skip_group_check=True),
                     reads=["av", "cst"], writes=[pk])
                k.op("pe", lambda e, pv=pv, t1_=t1_: e.matmul(pv[:, 2, :], lhsT=ones_f, rhs=av[:, t1_, :], start=False, stop=True, skip_group_check=True),
                     reads=["av", "cst"], writes=[pk])
                copy_op("dve", cum[:, t0_:t0_ + 2, :], pv[:, 0:2, :], reads=[], writes=[pk, "cum"])
                copy_op("dve", tot[:, c, :], pv[:, 2, :], reads=[], writes=[pk, "tot"])
            totb = lambda d: tot[:, :, d * 8:d * 8 + 8].unsqueeze(2).to_broadcast([128, NCH, 2, 8])
            v4 = lambda a, d: a[:, :, d * 8:d * 8 + 8].rearrange("p (c t) h -> p c t h", t=2)
            k.op("dve", lambda e: e.tensor_scalar_mul(out=ncf, in0=cum[:, :, 0:8], scalar1=-1.0), reads=["cum"], writes=["ncf"])
            k.op("act", lambda e: e.activation(out=sc1[:, :, 0:8], in_=cum[:, :, 0:8], func=AF.Exp), reads=["cum"], writes=["sc1a"])
            k.op("act", lambda e: e.activation(out=sc2[:, :, 8:16], in_=cum[:, :, 8:16], func=AF.Exp), reads=["cum"], writes=["sc2b"])
            k.op("dve", lambda e: e.tensor_tensor(out=v4(t16[2], 0), in0=totb(0), in1=v4(cum, 0), op=ALU.subtract), reads=["cum", "tot"], writes=["t16c0"])
            k.op("dve", lambda e: e.tensor_tensor(out=v4(t16[2], 1), in0=totb(1), in1=v4(cum, 1), op=ALU.subtract), reads=["cum", "tot"], writes=["t16c1"])
            k.op("act", lambda e: e.activation(out=sc2[:, :, 0:8], in_=t16[2][:, :, 0:8], func=AF.Exp), reads=["t16c0"], writes=["sc2a"])
            k.op("act", lambda e: e.activation(out=sc1[:, :, 8:16], in_=t16[2][:, :, 8:16], func=AF.Exp), reads=["t16c1"], writes=["sc1b"])
            k.op("act", lambda e: e.activation(out=etot, in_=tot, func=AF.Exp), reads=["tot"], writes=["etot"])
            k.op("dve", lambda e: e.tensor_tensor(out=dw, in0=dt, in1=sc2, op=ALU.mult), reads=["dt", "sc2a", "sc2b"], writes=["dw"])
            dbg_dump("ssd_cum", cum, ["cum"])
            Dbc = vrow(vbc, "ssd_d")
            v3 = lambda a: a.rearrange("p (h d) -> p h d", h=4)
            for g in range(2):
                for j in range(2):
                    conv_chunk(2 * g + j, lambda t4, j=j: (xTf[:, j, t4 * 512:(t4 + 1) * 512], ("xTf", t4)))
                for tt in range(NT):
                    pb, pk = PS(6)
                    pv = pb[:, 0:256].rearrange("p (c t) -> p c t", c=2)
                    for j in range(2):
                        k.op("pe", lambda e, pv=pv, j=j, tt=tt: e.transpose(pv[:, j, :], xTf[:, j, tt * 128:(tt + 1) * 128], identb),
                             reads=[("xTf", tt // 4), "cstb"], writes=[pk])
                    k.op("pe", lambda e, pb=pb, tt=tt: e.transpose(pb[:, 256:384], BT[:, g, tt * 128:(tt + 1) * 128], identb),
                         reads=[("BT", g, tt // 4), "cstb"], writes=[pk])
                    copy_op("act", xs[:, tt, :], pb[:, 0:256], reads=[], writes=[pk, ("xs", tt)])
                    copy_op("dve", Btok[:, tt, :], pb[:, 256:384], reads=[], writes=[pk, ("Btok", tt)])
                if g == 0:
                    dbg_dump("ssd_xs", xs, [("xs", tt) for tt in range(NT)], BF16)
                xs4 = xs.rearrange("p t (h d) -> p t h d", h=4)
                wi = 0
                for d in range(2):
                    k.op("dve", lambda e, d=d: e.memset(Rst[d], 0.0), writes=[("Rst", d)])
                    order = list(range(NCH)) if d == 0 else list(range(NCH - 1, -1, -1))
                    first_slot = 0 if d == 0 else NCH
                    k.op("dve", lambda e, d=d, first_slot=first_slot: e.memset(Sinb[d][:, first_slot, :], 0.0), writes=[("Sinb", d, first_slot)])
                    for c in order:
                        wr = wi % 2
                        wi += 1
                        dwb = dw[:, 2 * c:2 * c + 2, d * 8 + 4 * g:d * 8 + 4 * g + 4].unsqueeze(3).to_broadcast([128, 2, 4, 64])
                        k.op("dve", lambda e, wr=wr, dwb=dwb, c=c: e.tensor_tensor(out=xdw[wr].rearrange("p t (h d) -> p t h d", h=4),
                                                                                in0=xs4[:, 2 * c:2 * c + 2, :, :], in1=dwb, op=ALU.mult),
                             reads=[("xs", 2 * c), ("xs", 2 * c + 1), "dw"], writes=[("xdw", wr)])
                        pb, pk = PS(c % 2)
                        for ti in range(2):
                            tt = 2 * c + ti
                            k.op("pe", lambda e, pb=pb, tt=tt, ti=ti, wr=wr: e.matmul(
                                pb[:, 0:256], lhsT=Btok[:, tt, :], rhs=xdw[wr][:, ti, :], start=(ti == 0), stop=(ti == 1)),
                                reads=[("Btok", tt), ("xdw", wr)], writes=[pk])
                        etb = etot[:, c, d * 8 + 4 * g:d * 8 + 4 * g + 4].unsqueeze(2).to_broadcast([128, 4, 64])
                        k.op("dve", lambda e, d=d, etb=etb: e.tensor_tensor(out=v3(Rtmp), in0=v3(Rst[d]), in1=etb, op=ALU.mult),
                             reads=[("Rst", d), "etot"], writes=["Rtmp"])
                        k.op("dve", lambda e, pb=pb, d=d: e.tensor_tensor(out=Rst[d], in0=pb[:, 0:256], in1=Rtmp, op=ALU.add),
                             reads=["Rtmp"], writes=[pk, ("Rst", d)])
                        slot = c + 1 if d == 0 else c
                        k.op("act", lambda e, d=d, slot=slot: e.copy(Sinb[d][:, slot, :], Rst[d]), reads=[("Rst", d)], writes=[("Sinb", d, slot)])
                def prologue(c):
                    t0_, t1_ = 2 * c, 2 * c + 1
                    cr = c % 2
                    pb2, pk2 = PS(7)
                    for d, (r0, r1) in ((0, (C_R0F, C_R1F)), (1, (C_R0B, C_R1B))):
                        cs_ = slice(d * 8, d * 8 + 8)
                        o_ = pb2[0:8, d * 256:(d + 1) * 256]
                        k.op("pe", lambda e: e.matmul(o_, lhsT=av[:, t0_, cs_], rhs=cst[:, r0:r0 + 256],
                                                      start=(d == 0), stop=False, skip_group_check=True),
                             reads=["av", "cst"], writes=[pk2])
                        k.op("pe", lambda e: e.matmul(o_, lhsT=av[:, t1_, cs_], rhs=cst[:, r1:r1 + 256],
                                                      start=False, stop=True, skip_group_check=True),
                             reads=["av", "cst"], writes=[pk2])
                    copy_op("act", cumT[cr][0:8, :, :], pb2[0:8, 0:512].rearrange("p (d l) -> p d l", d=2), reads=[], writes=[pk2, ("cumT", cr)])
                    for d in range(2):
                        dtb = dt[:, 2 * c:2 * c + 2, d * 8 + 4 * g:d * 8 + 4 * g + 4].unsqueeze(3).to_broadcast([128, 2, 4, 64])
                        k.op("dve", lambda e: e.tensor_tensor(out=xdt[d][:, cr].rearrange("p t (h d) -> p t h d", h=4),
                                                              in0=xs4[:, 2 * c:2 * c + 2, :, :], in1=dtb, op=ALU.mult),
                             reads=[("xs", 2 * c), ("xs", 2 * c + 1), "dt"], writes=[("xdt", d, cr)])
                    pb, pk = PS(7)
                    for si in range(2):
                        k.op("pe", lambda e: e.matmul(pb[:, si * 256:(si + 1) * 256], lhsT=BT[:, g, (2 * c + si) * 128:(2 * c + si + 1) * 128],
                                                      rhs=CT[:, g, c * 256:(c + 1) * 256], start=(si == 0), stop=True, skip_group_check=True),
                             reads=[("BT", g, (2 * c + si) // 4), ("CT", g, c // 2)], writes=[pk])
                    copy_op("act", GT[cr], pb[:, 0:512], reads=[], writes=[pk, ("GT", cr)])

                def item_front(c, it):
                    h, d = it // 2, it % 2
                    hh = 4 * g + h
                    cr = c % 2
                    sel = cst[0:8, C_SEL + hh * 128:C_SEL + (hh + 1) * 128]
                    for si in range(2):
                        pb, pk = PS(2 * (it % 2) + si)
                        if d == 0:
                            lo, hi = (0, 256) if si == 0 else (128, 256)
                            mask = cst[:, C_NEGF:C_NEGF + 256] if si == 0 else cst[:, C_NEGF:C_NEGF + 128]
                        else:
                            lo, hi = (0, 128) if si == 0 else (0, 256)
                            mask = cst[:, C_POSB + 128:C_POSB + 256] if si == 0 else cst[:, C_POSB:C_POSB + 256]
                        w = hi - lo
                        k.op("pe", lambda e: e.matmul(pb[:, 0:w], lhsT=sel, rhs=cumT[cr][0:8, d, lo:hi], start=True, stop=False),
                             reads=[("cumT", cr), "cst"], writes=[pk])
                        k.op("pe", lambda e: e.matmul(pb[:, 0:w], lhsT=identf, rhs=mask, start=False, stop=True),
                             reads=["cst"], writes=[pk])

                def item_back(c, it, started, ybank, ykey):
                    h, d = it // 2, it % 2
                    hh = 4 * g + h
                    cr = c % 2
                    mms = []
                    for si in range(2):
                        pb, pk = PS(2 * (it % 2) + si)
                        if d == 0:
                            lo, hi = (0, 256) if si == 0 else (128, 256)
                        else:
                            lo, hi = (0, 128) if si == 0 else (0, 256)
                        w = hi - lo
                        ebi = 2 * d + si
                        tt_s = 2 * c + si
                        if d == 0:
                            k.op("act", lambda e: e.activation(out=Eb[ebi][:, 0:w], in_=pb[:, 0:w], func=AF.Exp, bias=ncf[:, tt_s, hh:hh + 1], scale=1.0),
                                 reads=["ncf"], writes=[pk, ("Eb", ebi)])
                        else:
                            k.op("act", lambda e: e.activation(out=Eb[ebi][:, 0:w], in_=pb[:, 0:w], func=AF.Exp, bias=cum[:, tt_s, 8 + hh:9 + hh], scale=-1.0),
                                 reads=["cum"], writes=[pk, ("Eb", ebi)])
                        k.op("dve", lambda e: e.tensor_tensor(out=Mb[ebi][:, 0:w], in0=Eb[ebi][:, 0:w], in1=GT[cr][:, si * 256 + lo:si * 256 + hi], op=ALU.mult),
                             reads=[("Eb", ebi), ("GT", cr)], writes=[("Mb", ebi)])
                        for li in range(2):
                            if lo <= li * 128 < hi:
                                mms.append((ebi, li * 128 - lo, li, si))
                    for (ebi, off, li, si) in mms:
                        st_ = not started[li]
                        started[li] = True
                        k.op("pe", lambda e: e.matmul(
                            ybank[li][:, h * 64:(h + 1) * 64], lhsT=Mb[ebi][:, off:off + 128], rhs=xdt[d][:, cr, si, h * 64:(h + 1) * 64],
                            start=st_, stop=False, skip_group_check=True),
                            reads=[("Mb", ebi), ("xdt", d, cr)], writes=[ykey[li]])

                def epilogue(c, ybank, ykey):
                    for li in range(2):
                        tt = 2 * c + li
                        pf7, pfk = PS(6 if False else 7)
                        pf = pf7[:, 0:256]
                        pbw = pf7[:, 256:512]
                        k.op("pe", lambda e: e.matmul(pf, lhsT=CT[:, g, tt * 128:(tt + 1) * 128], rhs=Sinb[0][:, c, :],
                                                      start=True, stop=True), reads=[("CT", g, tt // 4), ("Sinb", 0, c)], writes=[pfk])
                        k.op("pe", lambda e: e.matmul(pbw, lhsT=CT[:, g, tt * 128:(tt + 1) * 128], rhs=Sinb[1][:, c + 1, :],
                                                      start=False, stop=True, skip_group_check=True),
                             reads=[("CT", g, tt // 4), ("Sinb", 1, c + 1)], writes=[pfk])
                        ecfb = sc1[:, tt, 4 * g:4 * g + 4].unsqueeze(2).to_broadcast([128, 4, 64])
                        erbb = sc1[:, tt, 8 + 4 * g:12 + 4 * g].unsqueeze(2).to_broadcast([128, 4, 64])
                        dbb = Dbc[:, 4 * g:4 * g + 4].unsqueeze(2).to_broadcast([128, 4, 64])
                        yi = li
                        k.op("dve", lambda e: e.tensor_tensor(out=v3(ya[yi]), in0=v3(pf), in1=ecfb, op=ALU.mult),
                             reads=["sc1a"], writes=[pfk, ("ya", yi)])
                        k.op("dve", lambda e: e.tensor_tensor(out=v3(Rtmp), in0=v3(pbw), in1=erbb, op=ALU.mult),
                             reads=["sc1b"], writes=[pfk, "Rtmp"])
                        k.op("dve", lambda e: e.tensor_tensor(out=ya[yi], in0=ya[yi], in1=Rtmp, op=ALU.add), reads=["Rtmp"], writes=[("ya", yi)])
                        k.op("dve", lambda e: e.tensor_tensor(out=v3(Rtmp), in0=xs4[:, tt, :, :], in1=dbb, op=ALU.mult),
                             reads=[("xs", tt), "vbc"], writes=["Rtmp"])
                        k.op("dve", lambda e: e.tensor_tensor(out=ya[yi], in0=ya[yi], in1=Rtmp, op=ALU.add), reads=["Rtmp"], writes=[("ya", yi)])
                        k.op("dve", lambda e: e.tensor_tensor(out=ya[yi], in0=ybank[li][:, 0:256], in1=ya[yi], op=ALU.add),
                             reads=[], writes=[ykey[li], ("ya", yi)])
                        zi = li
                        k.dma(zb[zi], ptok_d[tt * 128:(tt + 1) * 128, g * 256:(g + 1) * 256], reads=[("ptok_d",)], writes=[("zb", zi)])
                        k.op("act", lambda e: e.activation(out=szb[zi], in_=zb[zi], func=AF.Silu), reads=[("zb", zi)], writes=[("szb", zi)])
                        k.op("dve", lambda e: e.tensor_tensor(out=yg[yi], in0=ya[yi], in1=szb[zi], op=ALU.mult),
                             reads=[("szb", zi), ("ya", yi)], writes=[("yg", yi)])
                        out_norm_store(yg[yi], [("yg", yi)], 256, vrow(vbc, "ssd_norm", g * 256, 256), g * 256, tmp, tt, tt, "do")

                prologue(0)
                for c in range(NCH):
                    yb0, yk0 = PS(4)
                    yb1, yk1 = PS(5)
                    ybank = (yb0, yb1)
                    ykey = (yk0, yk1)
                    started = [False, False]
                    item_front(c, 0)
                    for it in range(8):
                        if it + 1 < 8:
                            item_front(c, it + 1)
                        item_back(c, it, started, ybank, ykey)
                    if c + 1 < NCH:
                        prologue(c + 1)
                    epilogue(c, ybank, ykey)
            A.reset(fbase, bbase)

        def mixer_phase(l):
            A.reset()
            vbc = A.f32(NV)
            k.dma(vbc, vec_d[l:l + 1, :].partition_broadcast(128), writes=["vbc"])
            ssd_phase(l, vbc)
            k.barrier(barsc[:, 1:2])
            swa_phase(l, vbc)
            k.barrier(barsc[:, 2:3])
            mla_phase(l, vbc)
            if l == 0:
                k.barrier(barsc[:, 3:4])
                dbg_dump("ptok", ptok_d, [("ptok_d",)], big=True)
                dbg_dump("xbc", xbc_d, [("xbc_d",)], BF16, big=True)
                dbg_dump("y", y_d, [("y_d",)], BF16, big=True)

        finals = []
        for p in range(L + 1):
            l_prev = p - 1 if p > 0 else None
            l_next = p if p < L else None
            if stop_after is not None and p > stop_after[0]:
                break
            finals = block_phase(p, l_prev, l_next)
            k.barrier(barsc[:, 0:1])
            if l_next is not None:
                if stop_after is not None and stop_after == (p, "block"):
                    break
                mixer_phase(l_next)
                k.barrier(barsc[:, 0:1])
                if stop_after is not None and stop_after == (p, "mix"):
                    break
        k.emit(list(finals) + dbg_outs)
        print("ops recorded:", k.nops, "arena hi (KiB):", A.hi * 4 / 1024, A.hib * 2 / 1024)
    return nc


def prep_common(inp, S):
    L = inp["ffn1_norm"].shape[0]
    vec = np.zeros((L, NV), np.float32)
    for n, (o, s) in VOFF.items():
        vec[:, o:o + s] = np.asarray(inp[n], np.float32).reshape(L, s)
    cw = np.asarray(inp["ssd_conv_w"], np.float32)
    cb = np.asarray(inp["ssd_conv_b"], np.float32)
    convp = np.zeros((L, 128, 8, 6), np.float32)
    for c in range(8):
        convp[:, :, c, 0:5] = cw[:, :, c * 128:(c + 1) * 128].transpose(0, 2, 1)
        convp[:, :, c, 5] = cb[:, c * 128:(c + 1) * 128]
    com = {"vecs": vec, "convp": np.ascontiguousarray(convp.reshape(L, 128, 48)), "cst": make_consts(),
           "cs_swa": rope_tables(S, 64), "cs_mla": rope_tables(S, 32)}
    for n in ("ffn1_gate", "ffn1_up", "ffn1_down", "ffn2_gate", "ffn2_up", "ffn2_down", "w_in", "w_out",
              "mla_w_uq", "mla_w_ukv"):
        com[n] = np.ascontiguousarray(np.asarray(inp[n], np.float32))
    return com, L


_NC_CACHE = {}


def kernel(**inputs):
    x = np.asarray(inputs["x"], np.float32)
    B, S, _ = x.shape
    com, L = prep_common(inputs, S)
    key = (S, L)
    if key not in _NC_CACHE:
        _NC_CACHE[key] = build(S, L)
    nc = _NC_CACHE[key]
    ncores = 8
    in_maps = []
    for c in range(ncores):
        m = dict(com)
        m["x"] = np.ascontiguousarray(x[c % B])
        in_maps.append(m)
    res = run_bass_kernel_spmd(nc, in_maps, core_ids=list(range(ncores)))
    out = np.stack([np.asarray(res.results[b]["out"], np.float32) for b in range(B)], axis=0)
    return out
```

```python
import contextlib
import numpy as np
import concourse.bass as bass
import concourse.mybir as mybir
from concourse.bass_utils import run_bass_kernel_spmd

F32 = mybir.dt.float32
BF16 = mybir.dt.bfloat16
AF = mybir.ActivationFunctionType
ALU = mybir.AluOpType
AX = mybir.AxisListType

D = 1024
DFF = 2816
NPROJ = 2480
NTM = 1456
EPS = 1e-6
BIG = 1.0e4
DEBUG = False
ALT_DVE_ONLY = False
N_DMA_SEMS = 40

VOFF = {}
_o = 0
for _n, _s in (("ffn1_norm", 1024), ("mix_norm", 1024), ("ffn2_norm", 1024), ("ssd_norm", 512),
               ("swa_q_norm", 64), ("swa_k_norm", 64), ("swa_out_norm", 256), ("mla_q_lat_norm", 256),
               ("mla_kv_norm", 128), ("mla_q_norm", 96), ("mla_k_norm", 96), ("mla_out_norm", 256),
               ("ssd_dt_bias", 16), ("ssd_a_log", 16), ("ssd_d", 8), ("swa_sink", 4)):
    VOFF[_n] = (_o, _s)
    _o += _s
NV = _o

C_ID = 0
C_R0F = 128
C_R1F = 384
C_R0B = 640
C_R1B = 896
C_NEGF = 1152
C_POSB = 1408
C_GE = 1664
C_SEL = 1792
NCST = C_SEL + 1024


def make_consts():
    c = np.zeros((128, NCST), np.float32)
    i = np.arange(128)
    le = (i[:, None] <= i[None, :]).astype(np.float32)
    lt = (i[:, None] < i[None, :]).astype(np.float32)
    ge = (i[:, None] >= i[None, :]).astype(np.float32)
    gt = (i[:, None] > i[None, :]).astype(np.float32)
    c[:, C_ID:C_ID + 128] = np.eye(128)
    c[:, C_R0F:C_R0F + 128] = le
    c[:, C_R0F + 128:C_R0F + 256] = 1.0
    c[:, C_R1F + 128:C_R1F + 256] = le
    c[:, C_R0B:C_R0B + 128] = lt
    c[:, C_R0B + 128:C_R0B + 256] = 1.0
    c[:, C_R1B + 128:C_R1B + 256] = lt
    c[:, C_NEGF:C_NEGF + 128] = -BIG * gt
    c[:, C_POSB + 128:C_POSB + 256] = BIG * lt
    c[:, C_GE:C_GE + 128] = ge
    for h in range(8):
        c[h, C_SEL + h * 128:C_SEL + (h + 1) * 128] = 1.0
    return c


def rope_tables(n, dim):
    inv = 1.0 / np.power(np.float32(10000.0), np.arange(0, dim, 2, dtype=np.float32) / np.float32(dim))
    ang = np.arange(n, dtype=np.float32)[:, None] * inv[None, :].astype(np.float32)
    return np.concatenate([np.cos(ang), np.sin(ang)], axis=1).astype(np.float32)


def _freeze(fn):
    import types
    if fn.__closure__ is None:
        return fn
    cells = []
    for c in fn.__closure__:
        try:
            cells.append(types.CellType(c.cell_contents))
        except ValueError:
            cells.append(c)
    return types.FunctionType(fn.__code__, fn.__globals__, fn.__name__, fn.__defaults__, tuple(cells))


class Op:
    __slots__ = ("eng", "fn", "waits", "signal", "sem", "val", "is_dma")

    def __init__(self, eng, fn, is_dma=False):
        self.eng = eng
        self.fn = fn
        self.waits = []
        self.signal = False
        self.sem = None
        self.val = None
        self.is_dma = is_dma


class KB:
    ENGS = ("pe", "act", "dve", "pool", "sp")

    def __init__(self, nc):
        self.nc = nc
        self.prog = {e: [] for e in self.ENGS}
        self.last_w = {}
        self.readers = {}
        self.dma_rr = 0
        self.dma_last = [None] * N_DMA_SEMS
        self.dma_uses = [0] * N_DMA_SEMS
        self.bar = None
        self.nops = 0
        self.carrier = None

    def _deps(self, op, reads, writes):
        deps = []
        if self.bar is not None:
            deps.append(self.bar)
        for k in reads:
            w = self.last_w.get(k)
            if w is not None:
                deps.append(w)
        for k in writes:
            w = self.last_w.get(k)
            if w is not None:
                deps.append(w)
            deps.extend(self.readers.get(k, ()))
        for k in reads:
            self.readers.setdefault(k, []).append(op)
        for k in writes:
            self.last_w[k] = op
            self.readers[k] = []
        seen = set()
        for d in deps:
            if d is op or id(d) in seen:
                continue
            seen.add(id(d))
            if op.eng == "pe" and d.eng == "pe" and not d.is_dma and not op.is_dma:
                continue
            op.waits.append(d)

    def op(self, eng, fn, reads=(), writes=()):
        o = Op(eng, _freeze(fn))
        self._deps(o, reads, writes)
        self.prog[eng].append(o)
        self.nops += 1
        return o

    def dma(self, out, in_, reads=(), writes=(), q="sp", **kw):
        o = Op(q, None, is_dma=True)
        o.fn = lambda e, out=out, in_=in_, kw=kw: e.dma_start(out=out, in_=in_, **kw)
        self._deps(o, reads, writes)
        i = self.dma_rr
        self.dma_rr = (self.dma_rr + 1) % N_DMA_SEMS
        prev = self.dma_last[i]
        if prev is not None:
            o.waits.append(prev)
        self.dma_last[i] = o
        self.dma_uses[i] += 1
        o.sem = i
        o.val = 16 * self.dma_uses[i]
        o.signal = True
        self.prog[q].append(o)
        self.nops += 1
        return o

    def barrier(self, scratch_ap):
        o = Op("dve", lambda e: e.memset(scratch_ap, 0.0))
        if self.bar is not None:
            o.waits.append(self.bar)
        for e in self.ENGS:
            for p in reversed(self.prog[e]):
                if not p.is_dma:
                    o.waits.append(p)
                    break
        for d in self.dma_last:
            if d is not None:
                o.waits.append(d)
        self.prog["dve"].append(o)
        self.bar = o
        self.last_w = {}
        self.readers = {}
        return o

    def emit(self, final_wait_ops=()):
        nc = self.nc
        for e in self.ENGS:
            for o in self.prog[e]:
                for w in o.waits:
                    w.signal = True
        for o in final_wait_ops:
            o.signal = True
        for e in self.ENGS:
            c = 0
            for o in self.prog[e]:
                if o.is_dma:
                    continue
                if o.signal:
                    c += 1
                    o.sem = e
                    o.val = c
        with contextlib.ExitStack() as st:
            esem = {e: st.enter_context(nc.semaphore("s_" + e)) for e in self.ENGS}
            dsem = [st.enter_context(nc.semaphore("d_%d" % i)) for i in range(N_DMA_SEMS)]
            block = st.enter_context(nc.Block())

            def sem_of(o):
                return dsem[o.sem] if o.is_dma else esem[o.sem]

            def run(e, h):
                waited = {}
                for o in self.prog[e]:
                    need = {}
                    for w in o.waits:
                        key = ("d", w.sem) if w.is_dma else ("e", w.sem)
                        if waited.get(key, 0) >= w.val:
                            continue
                        waited[key] = w.val
                        need[key] = w
                    need = list(need.values())
                    if e == "pe" and not o.is_dma:
                        for w in need[:-1]:
                            h.ldweights(self.carrier)._wait_ge(sem_of(w), w.val)
                        ins = o.fn(h)
                        if need:
                            ins._wait_ge(sem_of(need[-1]), need[-1].val)
                    else:
                        for w in need:
                            h.wait_ge(sem_of(w), w.val)
                        ins = o.fn(h)
                    if o.is_dma:
                        ins.then_inc(dsem[o.sem], 16)
                    elif o.signal:
                        ins.then_inc(esem[e], 1)
                if e == "sp":
                    for o in final_wait_ops:
                        h.wait_ge(sem_of(o), o.val)

            @block.tensor
            def _(t):
                run("pe", t)

            @block.scalar
            def _(s):
                run("act", s)

            @block.vector
            def _(v):
                run("dve", v)

            @block.gpsimd
            def _(g):
                run("pool", g)

            @block.sync
            def _(s):
                run("sp", s)


class Arena:
    def __init__(self, apf, nf, apb, nb):
        self.apf, self.nf, self.apb, self.nb = apf, nf, apb, nb
        self.p = 0
        self.pb = 0
        self.hi = 0
        self.hib = 0

    def reset(self, to=0, tob=0):
        self.p = to
        self.pb = tob

    def f32(self, cols):
        a = self.apf[:, self.p:self.p + cols]
        self.p += (cols + 7) // 8 * 8
        self.hi = max(self.hi, self.p)
        assert self.p <= self.nf, ("f32 arena overflow", self.p, self.nf)
        return a

    def bf16(self, cols):
        a = self.apb[:, self.pb:self.pb + cols]
        self.pb += (cols + 15) // 16 * 16
        self.hib = max(self.hib, self.pb)
        assert self.pb <= self.nb, ("bf16 arena overflow", self.pb, self.nb)
        return a


def build(S, L, stop_after=None):
    assert S % 512 == 0
    NT = S // 128
    TB = min(1024, S)
    NBLK = S // TB
    NTB = TB // 128
    NCH = S // 256

    nc = bass.Bass("TRN2", target_bir_lowering=False)
    dr = lambda name, shape, dt=F32, kind="ExternalInput": nc.dram_tensor(name, list(shape), dt, kind=kind).ap()
    x_d = dr("x", [S, D])
    wg_d = [dr("ffn1_gate", [L, D, DFF]), dr("ffn2_gate", [L, D, DFF])]
    wu_d = [dr("ffn1_up", [L, D, DFF]), dr("ffn2_up", [L, D, DFF])]
    wd_d = [dr("ffn1_down", [L, DFF, D]), dr("ffn2_down", [L, DFF, D])]
    win_d = dr("w_in", [L, D, NPROJ])
    wout_d = dr("w_out", [L, D, D])
    wuq_d = dr("mla_w_uq", [L, 256, 384])
    wukv_d = dr("mla_w_ukv", [L, 128, 512])
    vec_d = dr("vecs", [L, NV])
    convp_d = dr("convp", [L, 128, 48])
    cst_d = dr("cst", [128, NCST])
    cs_swa_d = dr("cs_swa", [S, 64])
    cs_mla_d = dr("cs_mla", [S, 32])
    out_d = dr("out", [S, D], kind="ExternalOutput")
    xbc_d = dr("xbc_scr", [1024, S], BF16, kind="Internal")
    ptok_d = dr("ptok_scr", [S, NTM], F32, kind="Internal")
    y_d = dr("y_scr", [S, D], BF16, kind="Internal")

    ARENA_F = 17 * 1024
    ARENA_B = 56 * 1024
    st = contextlib.ExitStack()
    with st:
        arena_f = st.enter_context(nc.sbuf_tensor("arena_f", [128, ARENA_F], F32))
        arena_b = st.enter_context(nc.sbuf_tensor("arena_b", [128, ARENA_B], BF16))
        cst = st.enter_context(nc.sbuf_tensor("cst_sb", [128, NCST], F32))
        cstb = st.enter_context(nc.sbuf_tensor("cstb_sb", [128, 128 * 5 + 512], BF16))
        barsc = st.enter_context(nc.sbuf_tensor("barsc", [128, 8], F32))
        banks = [st.enter_context(nc.psum_tensor("bank%d" % i, [128, 1024] if i == 6 else [128, 512], BF16 if i == 6 else F32))
                 for i in range(8)]
        k = KB(nc)
        A = Arena(arena_f[:], ARENA_F, arena_b[:], ARENA_B)

        k.carrier = cstb[:, 0:1]
        identf = cst[:, C_ID:C_ID + 128]
        identb = cstb[:, 0:128]
        ones_f = cst[:, C_R0F + 128:C_R0F + 256]

        k.dma(cst[:], cst_d, writes=["cst"])
        k.dma(cstb[:, 0:128], cst_d[:, C_ID:C_ID + 128], writes=["cstb"], q="pool")
        for j in range(2):
            k.dma(cstb[:, 128 + j * 128:256 + j * 128], cst_d[:, C_GE:C_GE + 128], writes=["cstb"], q="pool")
            k.dma(cstb[:, 384 + j * 128:512 + j * 128], cst_d[:, C_R0F:C_R0F + 128], writes=["cstb"], q="pool")

        k.dma(cstb[:, 640:896], cst_d[:, C_NEGF:C_NEGF + 256], writes=["cstb"], q="pool")
        k.dma(cstb[:, 896:1152], cst_d[:, C_POSB:C_POSB + 256], writes=["cstb"], q="pool")

        def PS(i):
            return banks[i], "bank%d" % i

        rr = {"n": 0}
        dbg_outs = []
        dbg_names = set()

        def dbg_dump(name, ap, reads, dt=F32, big=False):
            if not DEBUG or ("dbg_" + name) in dbg_names:
                return
            dbg_names.add("dbg_" + name)
            t = nc.dram_tensor("dbg_" + name, list(ap.shape), dt, kind="ExternalOutput").ap()
            if big:
                for r0 in range(0, ap.shape[0], 256):
                    dbg_outs.append(k.dma(t[r0:r0 + 256], ap[r0:r0 + 256], reads=reads))
            else:
                dbg_outs.append(k.dma(t, ap, reads=reads))

        def alt():
            rr["n"] += 1
            return "dve" if (rr["n"] % 2 or ALT_DVE_ONLY) else "act"

        def copy_op(eng, out, in_, reads, writes):
            if eng == "act":
                return k.op("act", lambda e: e.copy(out, in_), reads, writes)
            return k.op(eng, lambda e: e.tensor_copy(out, in_), reads, writes)

        def vrow(vbc, name, lo=0, n=None):
            o, s = VOFF[name]
            n = s - lo if n is None else n
            return vbc[:, o + lo:o + lo + n]

        def block_phase(pidx, l_prev, l_next):
            A.reset()
            vbcA = A.f32(3072) if l_prev is not None else None
            vbcB = A.f32(3072) if l_next is not None else None
            xb = A.f32(NTB * D).rearrange("p (t d) -> p t d", t=NTB)
            hT = A.bf16(8 * TB).rearrange("p (c t) -> p c t", c=8)
            hb = [A.bf16(D) for _ in range(2)]
            junk = A.f32(D)
            ss = A.f32(NTB)
            sq = A.f32(NTB)
            rstd = A.f32(NTB)
            wgu = [A.bf16(2 * 8 * 256).rearrange("p (w c f) -> p w c f", w=2, c=8) for _ in range(2)]
            wdn = [A.bf16(2 * D).rearrange("p (c d) -> p c d", c=2) for _ in range(2)]
            sg = [A.bf16(512) for _ in range(2)]
            aT = [A.bf16(2 * 512).rearrange("p (c t) -> p c t", c=2) for _ in range(2)]
            wpj = [A.bf16(8 * 512).rearrange("p (c f) -> p c f", c=8) for _ in range(2)]
            evf = [A.bf16(512) for _ in range(2)]
            evt = [A.f32(512) for _ in range(2)]
            cnt = {"w": 0, "p": 0, "a": 0, "d": 0, "t": 0, "e": 0, "g": 0}
            if l_prev is not None:
                k.dma(vbcA, vec_d[l_prev:l_prev + 1, 0:3072].partition_broadcast(128), writes=["vbcA"])
            if l_next is not None:
                k.dma(vbcB, vec_d[l_next:l_next + 1, 0:3072].partition_broadcast(128), writes=["vbcB"])

            def norm_to_hT(vbc, vkey, nname, tag):
                k.op("dve", lambda e: e.memset(ss, 0.0), writes=["ss"])
                for tt in range(NTB):
                    k.op("act", lambda e, tt=tt: e.activation(out=junk, in_=xb[:, tt, :], func=AF.Square,
                                                              accum_out=ss[:, tt:tt + 1]),
                         reads=[("xb", tt)], writes=["junk", "ss"])
                k.op("act", lambda e: e.activation(out=sq, in_=ss, func=AF.Sqrt, scale=1.0 / D, bias=EPS),
                     reads=["ss"], writes=["sq"])
                k.op("dve", lambda e: e.reciprocal(out=rstd, in_=sq), reads=["sq"], writes=["rstd"])
                wv = vrow(vbc, nname)
                if tag == "f" and not cnt.get("dbg1"):
                    cnt["dbg1"] = 1
                    dbg_dump("ss", ss, ["ss"]); dbg_dump("rstd", rstd, ["rstd"]); dbg_dump("wv", wv, [vkey])
                    dbg_dump("xb0", xb[:, 0, :], [("xb", 0)])
                for tt in range(NTB):
                    hbi = tt % 2
                    k.op("dve", lambda e, tt=tt, hbi=hbi: e.scalar_tensor_tensor(
                        out=hb[hbi], in0=xb[:, tt, :], scalar=rstd[:, tt:tt + 1], in1=wv,
                        op0=ALU.mult, op1=ALU.mult), reads=[("xb", tt), "rstd", vkey], writes=[("hb", hbi)])
                    transpose_rows(hb[hbi], ("hb", hbi), 8, hT, tt, "hT")

            def transpose_rows(src, skey, nchunks, dstT, tt, dkey):
                for h0 in range(0, nchunks, 4):
                    nn = min(4, nchunks - h0)
                    pb, pk = PS(6)
                    pv = pb[:, (cnt["t"] % 2) * 512:(cnt["t"] % 2) * 512 + 512].rearrange("p (c t) -> p c t", c=4)
                    cnt["t"] += 1
                    for j in range(nn):
                        k.op("pe", lambda e, j=j, h0=h0: e.transpose(pv[:, j, :], src[:, (h0 + j) * 128:(h0 + j + 1) * 128], identb),
                             reads=[skey, "cstb"], writes=[pk])
                    copy_op(alt(), dstT[:, h0:h0 + nn, tt * 128:(tt + 1) * 128], pv[:, 0:nn, :],
                            reads=[], writes=[pk, (dkey, tt, h0)])

            def dump_hT(nm):
                dbg_dump(nm, hT[:, :, 0:128], [("hT", 0, 0), ("hT", 0, 4)], BF16)

            def hT_keys(t4, key="hT"):
                return [(key, tt, h0) for tt in range(t4 * 4, t4 * 4 + 4) for h0 in (0, 4)]

            def ffn(l, which, vbc, vkey):
                norm_to_hT(vbc, vkey, "ffn1_norm" if which == 0 else "ffn2_norm", "f")
                if not cnt.get("dbg2"):
                    cnt["dbg2"] = 1
                    dump_hT("hT")
                wg_l = wg_d[which][l].rearrange("(c p) f -> p c f", p=128)
                wu_l = wu_d[which][l].rearrange("(c p) f -> p c f", p=128)
                wd_l = wd_d[which][l]
                NG = DFF // 256
                NT4 = TB // 512
                units = [(g, t4) for g in range(NG) for t4 in range(NT4)]
                wis = {}

                def load_w(g):
                    wi = cnt["w"] % 2
                    cnt["w"] += 1
                    wis[g] = wi
                    f0 = g * 256
                    k.dma(wgu[wi][:, 0], wg_l[:, :, f0:f0 + 256], writes=[("wgu", wi, 0)], q="pool")
                    k.dma(wgu[wi][:, 1], wu_l[:, :, f0:f0 + 256], writes=[("wgu", wi, 1)], q="pool")
                    k.dma(wdn[wi], wd_l[f0:f0 + 256, :].rearrange("(c p) d -> p c d", p=128), writes=[("wdn", wi)], q="pool")

                ais = {}

                def front(u):
                    g, t4 = units[u]
                    if t4 == 0:
                        load_w(g)
                    wi = wis[g]
                    ai = cnt["a"] % 2
                    cnt["a"] += 1
                    ais[u] = ai
                    hk = hT_keys(t4)
                    for fc in range(2):
                        gi = cnt["g"] % 2
                        cnt["g"] += 1
                        pg, pgk = PS(gi)
                        pu, puk = PS(2 + gi)
                        for w, (pp, ppk) in enumerate(((pg, pgk), (pu, puk))):
                            for dc in range(8):
                                k.op("pe", lambda e: e.matmul(
                                    pp[:], lhsT=wgu[wi][:, w, dc, fc * 128:(fc + 1) * 128],
                                    rhs=hT[:, dc, t4 * 512:(t4 + 1) * 512], start=(dc == 0), stop=(dc == 7)),
                                    reads=[("wgu", wi, w)] + hk, writes=[ppk])
                        k.op("act", lambda e: e.activation(out=sg[gi], in_=pg[:], func=AF.Silu),
                             reads=[], writes=[pgk, ("sg", gi)])
                        k.op("dve", lambda e: e.tensor_tensor(
                            out=aT[ai][:, fc, :], in0=pu[:], in1=sg[gi], op=ALU.mult),
                            reads=[("sg", gi)], writes=[puk, ("aT", ai, fc)])

                def back(u):
                    g, t4 = units[u]
                    wi = wis[g]
                    ai = ais[u]
                    for ts in range(4):
                        tt = t4 * 4 + ts
                        for dh in range(2):
                            di = 4 + cnt["d"] % 2
                            cnt["d"] += 1
                            pd, pdk = PS(di)
                            for fc in range(2):
                                k.op("pe", lambda e: e.matmul(
                                    pd[:], lhsT=aT[ai][:, fc, ts * 128:(ts + 1) * 128],
                                    rhs=wdn[wi][:, fc, dh * 512:(dh + 1) * 512], start=(fc == 0), stop=(fc == 1)),
                                    reads=[("aT", ai, fc), ("wdn", wi)], writes=[pdk])
                            k.op("dve", lambda e: e.scalar_tensor_tensor(
                                out=xb[:, tt, dh * 512:(dh + 1) * 512], in0=pd[:], scalar=0.5,
                                in1=xb[:, tt, dh * 512:(dh + 1) * 512], op0=ALU.mult, op1=ALU.add),
                                reads=[], writes=[pdk, ("xb", tt)])

                front(0)
                for u in range(len(units)):
                    if u + 1 < len(units):
                        front(u + 1)
                    back(u)

            def inproj(l, blk, vbc, vkey):
                norm_to_hT(vbc, vkey, "mix_norm", "m")
                win_l = win_d[l].rearrange("(c p) f -> p c f", p=128)
                t0 = blk * TB
                for cg in range(8):
                    wi = cnt["p"] % 2
                    cnt["p"] += 1
                    c0 = 512 + cg * 128
                    k.dma(wpj[wi][:, :, 0:128], win_l[:, :, c0:c0 + 128], writes=[("wpj", wi)], q="pool")
                    for t4 in range(TB // 512):
                        gi = cnt["g"] % 2
                        cnt["g"] += 1
                        pg, pgk = PS(gi)
                        hk = hT_keys(t4)
                        for dc in range(8):
                            k.op("pe", lambda e, pg=pg, dc=dc, wi=wi, t4=t4: e.matmul(
                                pg[:], lhsT=wpj[wi][:, dc, 0:128], rhs=hT[:, dc, t4 * 512:(t4 + 1) * 512],
                                start=(dc == 0), stop=(dc == 7)), reads=[("wpj", wi)] + hk, writes=[pgk])
                        ei = cnt["e"] % 2
                        cnt["e"] += 1
                        copy_op(alt(), evf[ei], pg[:], reads=[], writes=[pgk, ("evf", ei)])
                        k.dma(xbc_d[cg * 128:(cg + 1) * 128, t0 + t4 * 512:t0 + (t4 + 1) * 512], evf[ei],
                              reads=[("evf", ei)], writes=[("xbc_d", cg)])
                for (c0, ncol, j0) in ((0, 512, 0), (1536, 512, 512), (2048, 432, 1024)):
                    wi = cnt["p"] % 2
                    cnt["p"] += 1
                    k.dma(wpj[wi][:, :, 0:ncol], win_l[:, :, c0:c0 + ncol], writes=[("wpj", wi)], q="pool")
                    for tt in range(NTB):
                        gi = cnt["g"] % 2
                        cnt["g"] += 1
                        pu, puk = PS(2 + gi)
                        for dc in range(8):
                            k.op("pe", lambda e, pu=pu, dc=dc, wi=wi, tt=tt, ncol=ncol: e.matmul(
                                pu[:, 0:ncol], lhsT=hT[:, dc, tt * 128:(tt + 1) * 128], rhs=wpj[wi][:, dc, 0:ncol],
                                start=(dc == 0), stop=(dc == 7)),
                                reads=[("wpj", wi), ("hT", tt, 0), ("hT", tt, 4)], writes=[puk])
                        ei = cnt["e"] % 2
                        cnt["e"] += 1
                        copy_op(alt(), evt[ei][:, 0:ncol], pu[:, 0:ncol], reads=[], writes=[puk, ("evt", ei)])
                        k.dma(ptok_d[t0 + tt * 128:t0 + (tt + 1) * 128, j0:j0 + ncol], evt[ei][:, 0:ncol],
                              reads=[("evt", ei)], writes=[("ptok_d", blk)])

            def outproj(l, blk):
                t0 = blk * TB
                wout_l = wout_d[l].rearrange("(c p) f -> p c f", p=128)
                for tt in range(NTB):
                    hbi = tt % 2
                    k.dma(hb[hbi], y_d[t0 + tt * 128:t0 + (tt + 1) * 128, :], reads=[("y_d",)], writes=[("hb", hbi)])
                    transpose_rows(hb[hbi], ("hb", hbi), 8, hT, tt, "hT")
                for dh in range(2):
                    wi = cnt["p"] % 2
                    cnt["p"] += 1
                    k.dma(wpj[wi], wout_l[:, :, dh * 512:(dh + 1) * 512], writes=[("wpj", wi)], q="pool")
                    for tt in range(NTB):
                        di = 4 + cnt["d"] % 2
                        cnt["d"] += 1
                        pd, pdk = PS(di)
                        for dc in range(8):
                            k.op("pe", lambda e, pd=pd, dc=dc, wi=wi, tt=tt: e.matmul(
                                pd[:], lhsT=hT[:, dc, tt * 128:(tt + 1) * 128], rhs=wpj[wi][:, dc, :],
                                start=(dc == 0), stop=(dc == 7)),
                                reads=[("wpj", wi), ("hT", tt, 0), ("hT", tt, 4)], writes=[pdk])
                        k.op("dve", lambda e, pd=pd, tt=tt, dh=dh: e.tensor_tensor(
                            out=xb[:, tt, dh * 512:(dh + 1) * 512], in0=pd[:], in1=xb[:, tt, dh * 512:(dh + 1) * 512],
                            op=ALU.add), reads=[], writes=[pdk, ("xb", tt)])

            stores = []
            for blk in range(NBLK):
                t0 = blk * TB
                src = x_d if pidx == 0 else out_d
                for tt in range(NTB):
                    k.dma(xb[:, tt, :], src[t0 + tt * 128:t0 + (tt + 1) * 128, :],
                          reads=[("out_d", blk)], writes=[("xb", tt)])
                if l_prev is not None:
                    outproj(l_prev, blk)
                    ffn(l_prev, 1, vbcA, "vbcA")
                if l_next is not None:
                    ffn(l_next, 0, vbcB, "vbcB")
                    inproj(l_next, blk, vbcB, "vbcB")
                for tt in range(NTB):
                    stores.append(k.dma(out_d[t0 + tt * 128:t0 + (tt + 1) * 128, :], xb[:, tt, :],
                                        reads=[("xb", tt)], writes=[("out_d", blk)]))
            return stores

        def headnorm_rope(src, H, Dh, wrow, rlo, rhalf, cs, dst, tmp, keys_r, keys_w, cskey):
            sqv = tmp["sq"][:, 0:H * Dh].rearrange("p (h d) -> p h d", h=H)
            ssv = tmp["ss"][:, 0:H]
            rs = tmp["rs"][:, 0:H]
            qn = tmp["qn"][:, 0:H * Dh].rearrange("p (h d) -> p h d", h=H)
            t1 = tmp["t1"][:, 0:H * rhalf].rearrange("p (h d) -> p h d", h=H)
            t2 = tmp["t2"][:, 0:H * rhalf].rearrange("p (h d) -> p h d", h=H)
            T = ["T_sq", "T_ss", "T_rs", "T_qn", "T_t1", "T_t2"]
            k.op("dve", lambda e: e.tensor_tensor(out=sqv, in0=src, in1=src, op=ALU.mult), reads=keys_r, writes=[T[0]])
            k.op("dve", lambda e: e.tensor_reduce(out=ssv, in_=sqv, axis=AX.X, op=ALU.add), reads=[T[0]], writes=[T[1]])
            k.op("act", lambda e: e.activation(out=rs, in_=ssv, func=AF.Sqrt, scale=1.0 / Dh, bias=EPS), reads=[T[1]], writes=[T[2]])
            k.op("dve", lambda e: e.reciprocal(out=ssv, in_=rs), reads=[T[2]], writes=[T[1]])
            k.op("dve", lambda e: e.tensor_tensor(out=qn, in0=src, in1=ssv.unsqueeze(2).to_broadcast([128, H, Dh]), op=ALU.mult),
                 reads=keys_r + [T[1]], writes=[T[3]])
            k.op("dve", lambda e: e.tensor_tensor(out=qn, in0=qn, in1=wrow.unsqueeze(1).to_broadcast([128, H, Dh]), op=ALU.mult),
                 reads=["vbc"], writes=[T[3]])
            if rlo > 0:
                k.op("dve", lambda e: e.tensor_copy(dst[:, :, 0:rlo], qn[:, :, 0:rlo]), reads=[T[3]], writes=keys_w)
            x1 = qn[:, :, rlo:rlo + rhalf]
            x2 = qn[:, :, rlo + rhalf:rlo + 2 * rhalf]
            cb = cs[:, 0:rhalf].unsqueeze(1).to_broadcast([128, H, rhalf])
            sb = cs[:, rhalf:2 * rhalf].unsqueeze(1).to_broadcast([128, H, rhalf])
            k.op("dve", lambda e: e.tensor_tensor(out=t1, in0=x1, in1=cb, op=ALU.mult), reads=[T[3], cskey], writes=[T[4]])
            k.op("dve", lambda e: e.tensor_tensor(out=t2, in0=x2, in1=sb, op=ALU.mult), reads=[T[3], cskey], writes=[T[5]])
            k.op("dve", lambda e: e.tensor_tensor(out=dst[:, :, rlo:rlo + rhalf], in0=t1, in1=t2, op=ALU.subtract),
                 reads=[T[4], T[5]], writes=keys_w)
            k.op("dve", lambda e: e.tensor_tensor(out=t1, in0=x1, in1=sb, op=ALU.mult), reads=[T[3], cskey], writes=[T[4]])
            k.op("dve", lambda e: e.tensor_tensor(out=t2, in0=x2, in1=cb, op=ALU.mult), reads=[T[3], cskey], writes=[T[5]])
            k.op("dve", lambda e: e.tensor_tensor(out=dst[:, :, rlo + rhalf:rlo + 2 * rhalf], in0=t1, in1=t2, op=ALU.add),
                 reads=[T[4], T[5]], writes=keys_w)

        def heads_to_T(src, skey, H, Dh, dstT, tt, dkey, bank):
            pb, pk = PS(6)
            pv = pb[:, 0:512].rearrange("p (c t) -> p c t", c=4)
            for h in range(H):
                k.op("pe", lambda e, h=h: e.transpose(pv[0:Dh, h, :], src[:, h, :], identb), reads=[skey, "cstb"], writes=[pk])
            copy_op(alt(), dstT[0:Dh, 0:H, tt * 128:(tt + 1) * 128], pv[0:Dh, 0:H, :], reads=[], writes=[pk, (dkey, tt)])

        def out_norm_store(osrc, okeys, width, wrow, col0, tmp, tt, ring, tag):
            k.op("dve", lambda e: e.memset(tmp["ss"][:, 0:1], 0.0), writes=["T_ss"])
            k.op("act", lambda e: e.activation(out=tmp["sq"][:, 0:width], in_=osrc, func=AF.Square, accum_out=tmp["ss"][:, 0:1]),
                 reads=okeys, writes=["T_sq", "T_ss"])
            k.op("act", lambda e: e.activation(out=tmp["rs"][:, 0:1], in_=tmp["ss"][:, 0:1], func=AF.Sqrt, scale=1.0 / width, bias=EPS),
                 reads=["T_ss"], writes=["T_rs"])
            k.op("dve", lambda e: e.reciprocal(out=tmp["ss"][:, 1:2], in_=tmp["rs"][:, 0:1]), reads=["T_rs"], writes=["T_ss"])
            yi = ring % 2
            yb = tmp["yb"][yi][:, 0:width]
            k.op("dve", lambda e: e.scalar_tensor_tensor(out=yb, in0=osrc, scalar=tmp["ss"][:, 1:2], in1=wrow,
                                                         op0=ALU.mult, op1=ALU.mult),
                 reads=list(okeys) + ["T_ss", "vbc"], writes=[(tag + "yb", yi)])
            k.dma(y_d[tt * 128:(tt + 1) * 128, col0:col0 + width], yb, reads=[(tag + "yb", yi)], writes=[("y_d",)])

        def attn_finish(acc_bank, ncols_q, nheads, tmp, odst_fn, extra_den, tag):
            pa, pak = PS(acc_bank)
            oT = tmp["oT"]
            k.op("act", lambda e: e.copy(oT[0:65, 0:ncols_q], pa[0:65, 0:ncols_q]), reads=[], writes=[pak, "T_oT"])
            for j in range(ncols_q // 128):
                pb, pk = PS(7)
                k.op("pe", lambda e, j=j: e.transpose(pb[:, 0:65], oT[0:65, j * 128:(j + 1) * 128], identf[0:65, 0:65]),
                     reads=["T_oT", "cst"], writes=[pk])
                den = tmp["den"]
                if extra_den is not None:
                    ed = extra_den(j)
                    k.op("dve", lambda e, ed=ed: e.tensor_tensor(out=den[:, 0:1], in0=pb[:, 64:65], in1=ed, op=ALU.add),
                         reads=["sinkexp"], writes=[pk, "T_den"])
                else:
                    k.op("dve", lambda e: e.tensor_copy(den[:, 0:1], pb[:, 64:65]), reads=[], writes=[pk, "T_den"])
                k.op("dve", lambda e: e.reciprocal(out=den[:, 1:2], in_=den[:, 0:1]), reads=["T_den"], writes=["T_rden"])
                dst, dkey = odst_fn(j)
                k.op("dve", lambda e, dst=dst: e.tensor_scalar_mul(out=dst, in0=pb[:, 0:64], scalar1=den[:, 1:2]), reads=["T_rden"], writes=[pk, dkey])

        def mla_phase(l, vbc):
            fbase, bbase = A.p, A.pb
            qT = A.bf16(4 * S).rearrange("p (h t) -> p h t", h=4)
            kT = A.bf16(4 * S).rearrange("p (h t) -> p h t", h=4)
            vx = A.bf16(NT * 4 * 65).rearrange("p (t h d) -> p t h d", t=NT, h=4)
            oall = A.f32(NT * 256).rearrange("p (t d) -> p t d", t=NT)
            wuq = A.bf16(2 * 384).rearrange("p (c f) -> p c f", c=2)
            wukv = A.bf16(512)
            pin = [A.f32(416) for _ in range(2)]
            csb = [A.f32(32) for _ in range(2)]
            nb = [A.bf16(384) for _ in range(2)]
            nT = A.bf16(3 * 128).rearrange("p (c t) -> p c t", c=3)
            qf = A.f32(384).rearrange("p (h d) -> p h d", h=4)
            kf = A.f32(384).rearrange("p (h d) -> p h d", h=4)
            qb = A.bf16(384).rearrange("p (h d) -> p h d", h=4)
            kb_ = A.bf16(384).rearrange("p (h d) -> p h d", h=4)
            tmp = {"sq": A.f32(384), "ss": A.f32(8), "rs": A.f32(8), "qn": A.f32(384), "t1": A.f32(64), "t2": A.f32(64),
                   "oT": A.f32(512), "den": A.f32(2), "yb": [A.bf16(256) for _ in range(2)]}
            eb = [A.bf16(512) for _ in range(3)]
            k.dma(wuq, wuq_d[l].rearrange("(c p) f -> p c f", p=128), writes=["wuq"], q="pool")
            k.dma(wukv, wukv_d[l], writes=["wukv"], q="pool")
            k.op("dve", lambda e: e.memset(vx[:, :, :, 64:65], 1.0), writes=["vx1"])
            for tt in range(NT):
                pi = tt % 2
                k.dma(pin[pi], ptok_d[tt * 128:(tt + 1) * 128, 1040:1456], reads=[("ptok_d",)], writes=[("pin", pi)])
                k.dma(csb[pi], cs_mla_d[tt * 128:(tt + 1) * 128, :], writes=[("cs", pi)])
                for (c0, n, wn, tagn) in ((0, 256, "mla_q_lat_norm", "a"), (256, 128, "mla_kv_norm", "b")):
                    k.op("dve", lambda e: e.memset(tmp["ss"][:, 0:1], 0.0), writes=["T_ss"])
                    k.op("act", lambda e, c0=c0, n=n, pi=pi: e.activation(out=tmp["sq"][:, 0:n], in_=pin[pi][:, c0:c0 + n],
                                                                        func=AF.Square, accum_out=tmp["ss"][:, 0:1]),
                         reads=[("pin", pi)], writes=["T_sq", "T_ss"])
                    k.op("act", lambda e, n=n: e.activation(out=tmp["rs"][:, 0:1], in_=tmp["ss"][:, 0:1], func=AF.Sqrt,
                                                            scale=1.0 / n, bias=EPS), reads=["T_ss"], writes=["T_rs"])
                    k.op("dve", lambda e: e.reciprocal(out=tmp["ss"][:, 1:2], in_=tmp["rs"][:, 0:1]), reads=["T_rs"], writes=["T_ss"])
                    k.op("dve", lambda e, c0=c0, n=n, pi=pi, wn=wn: e.scalar_tensor_tensor(
                        out=nb[pi][:, c0:c0 + n], in0=pin[pi][:, c0:c0 + n], scalar=tmp["ss"][:, 1:2], in1=vrow(vbc, wn),
                        op0=ALU.mult, op1=ALU.mult), reads=[("pin", pi), "T_ss", "vbc"], writes=[("nb", pi, tagn)])
                pb, pk = PS(6)
                pv = pb[:, 512:1024].rearrange("p (c t) -> p c t", c=4)
                for c in range(3):
                    k.op("pe", lambda e, c=c, pi=pi: e.transpose(pv[:, c, :], nb[pi][:, c * 128:(c + 1) * 128], identb),
                         reads=[("nb", pi, "a"), ("nb", pi, "b"), "cstb"], writes=[pk])
                copy_op("act", nT, pv[:, 0:3, :], reads=[], writes=[pk, "nT"])
                pq, pqk = PS(0)
                for c in range(2):
                    k.op("pe", lambda e, c=c: e.matmul(pq[:, 0:384], lhsT=nT[:, c, :], rhs=wuq[:, c, :], start=(c == 0), stop=(c == 1)),
                         reads=["nT", "wuq"], writes=[pqk])
                pkv, pkvk = PS(1)
                k.op("pe", lambda e: e.matmul(pkv[:], lhsT=nT[:, 2, :], rhs=wukv, start=True, stop=True), reads=["nT", "wukv"], writes=[pkvk])
                copy_op("act", qf, pq[:, 0:384].rearrange("p (h d) -> p h d", h=4), reads=[], writes=[pqk, "qf"])
                kvv = pkv[:].rearrange("p (h d) -> p h d", h=4)
                copy_op("act", kf[:, :, 0:64], kvv[:, :, 0:64], reads=[], writes=[pkvk, "kf"])
                copy_op("dve", vx[:, tt, :, 0:64], kvv[:, :, 64:128], reads=[], writes=[pkvk, ("vx", tt)])
                k.op("dve", lambda e, pi=pi: e.tensor_copy(kf[:, :, 64:96], pin[pi][:, 384:416].unsqueeze(1).to_broadcast([128, 4, 32])),
                     reads=[("pin", pi)], writes=["kf"])
                headnorm_rope(qf, 4, 96, vrow(vbc, "mla_q_norm"), 64, 16, csb[pi], qb, tmp, ["qf"], ["qb"], ("cs", pi))
                headnorm_rope(kf, 4, 96, vrow(vbc, "mla_k_norm"), 64, 16, csb[pi], kb_, tmp, ["kf"], ["kb"], ("cs", pi))
                heads_to_T(qb, "qb", 4, 96, qT, tt, "qT", 6)
                heads_to_T(kb_, "kb", 4, 96, kT, tt, "kT", 6)
            scale = 96.0 ** -0.5
            qkeys = lambda qi: [("qT", tt) for tt in range(qi * 4, qi * 4 + 4)]
            items = [(h, qi, kt) for h in range(4) for qi in range(S // 512) for kt in range(NT)]
            LA = 2

            def stA(i):
                h, qi, kt = items[i]
                psn, psk = PS(i % 3)
                k.op("pe", lambda e: e.matmul(
                    psn[:], lhsT=kT[0:96, h, kt * 128:(kt + 1) * 128], rhs=qT[0:96, h, qi * 512:(qi + 1) * 512],
                    start=True, stop=True), reads=[("kT", kt)] + qkeys(qi), writes=[psk])

            def stBC(i):
                h, qi, kt = items[i]
                psn, psk = PS(i % 3)
                ebi = i % 3
                blk = h * (S // 512) + qi
                pa, pak = PS(4 + (blk % 2))
                k.op("act", lambda e: e.activation(out=eb[ebi], in_=psn[:], func=AF.Exp, scale=scale),
                     reads=[], writes=[psk, ("eb", ebi)])
                k.op("pe", lambda e: e.matmul(
                    pa[0:65, :], lhsT=vx[:, kt, h, :], rhs=eb[ebi], start=(kt == 0), stop=(kt == NT - 1)),
                    reads=[("vx", kt), "vx1", ("eb", ebi)], writes=[pak])
                if kt == NT - 1:
                    attn_finish(4 + (blk % 2), 512, 1, tmp,
                                lambda j, h=h, qi=qi: (oall[:, qi * 4 + j, h * 64:(h + 1) * 64], ("oall", qi * 4 + j, h)),
                                None, "m")

            for i in range(min(LA, len(items))):
                stA(i)
            for i in range(len(items)):
                if i + LA < len(items):
                    stA(i + LA)
                stBC(i)
            for tt in range(NT):
                okeys = [("oall", tt, h) for h in range(4)]
                out_norm_store(oall[:, tt, :], okeys, 256, vrow(vbc, "mla_out_norm"), 768, tmp, tt, tt, "mo")
            A.reset(fbase, bbase)

        def swa_phase(l, vbc):
            fbase, bbase = A.p, A.pb
            qT = A.bf16(4 * S).rearrange("p (h t) -> p h t", h=4)
            kT = A.bf16(2 * S).rearrange("p (h t) -> p h t", h=2)
            vx = A.bf16(NT * 2 * 65).rearrange("p (t h d) -> p t h d", t=NT, h=2)
            oall = A.f32(NT * 256).rearrange("p (t d) -> p t d", t=NT)
            pin = [A.f32(512) for _ in range(2)]
            csb = [A.f32(64) for _ in range(2)]
            qb = A.bf16(256).rearrange("p (h d) -> p h d", h=4)
            kb_ = A.bf16(128).rearrange("p (h d) -> p h d", h=2)
            tmp = {"sq": A.f32(256), "ss": A.f32(8), "rs": A.f32(8), "qn": A.f32(256), "t1": A.f32(128), "t2": A.f32(128),
                   "oT": A.f32(512), "den": A.f32(2), "yb": [A.bf16(256) for _ in range(2)]}
            sinkexp = A.f32(4)
            eb = [A.bf16(256) for _ in range(3)]
            k.op("act", lambda e: e.activation(out=sinkexp, in_=vrow(vbc, "swa_sink"), func=AF.Exp), reads=["vbc"], writes=["sinkexp"])
            k.op("dve", lambda e: e.memset(vx[:, :, :, 64:65], 1.0), writes=["vx1"])
            for tt in range(NT):
                pi = tt % 2
                k.dma(pin[pi], ptok_d[tt * 128:(tt + 1) * 128, 528:1040], reads=[("ptok_d",)], writes=[("pin", pi)])
                k.dma(csb[pi], cs_swa_d[tt * 128:(tt + 1) * 128, :], writes=[("cs", pi)])
                qsrc = pin[pi][:, 0:256].rearrange("p (h d) -> p h d", h=4)
                ksrc = pin[pi][:, 256:384].rearrange("p (h d) -> p h d", h=2)
                headnorm_rope(qsrc, 4, 64, vrow(vbc, "swa_q_norm"), 0, 32, csb[pi], qb, tmp, [("pin", pi)], ["qb"], ("cs", pi))
                headnorm_rope(ksrc, 2, 64, vrow(vbc, "swa_k_norm"), 0, 32, csb[pi], kb_, tmp, [("pin", pi)], ["kb"], ("cs", pi))
                k.op("dve", lambda e, pi=pi, tt=tt: e.tensor_copy(vx[:, tt, :, 0:64], pin[pi][:, 384:512].rearrange("p (h d) -> p h d", h=2)),
                     reads=[("pin", pi)], writes=[("vx", tt)])
                heads_to_T(qb, "qb", 4, 64, qT, tt, "qT", 6)
                heads_to_T(kb_, "kb", 2, 64, kT, tt, "kT", 6)
            scale = 64.0 ** -0.5
            items = []
            for n in range(NT):
                for kvh in range(2):
                    js = [j for j in (n - 1, n, n + 1) if 0 <= j < NT]
                    for ji, j in enumerate(js):
                        items.append((n, kvh, j, ji, len(js)))
            LA = 2
            pend = []

            def stA(i):
                n, kvh, j, ji, nj = items[i]
                psn, psk = PS(i % 3)
                k.op("pe", lambda e: e.matmul(
                    psn[:, 0:256].rearrange("p (a b) -> p a b", a=2), lhsT=kT[0:64, kvh, j * 128:(j + 1) * 128],
                    rhs=qT[0:64, 2 * kvh:2 * kvh + 2, n * 128:(n + 1) * 128], start=True, stop=True),
                    reads=[("kT", j), ("qT", n)], writes=[psk])

            def stBC(i):
                n, kvh, j, ji, nj = items[i]
                psn, psk = PS(i % 3)
                ebi = i % 3
                blk = n * 2 + kvh
                pa, pak = PS(4 + (blk % 2))
                k.op("act", lambda e: e.activation(out=eb[ebi], in_=psn[:, 0:256], func=AF.Exp, scale=scale),
                     reads=[], writes=[psk, ("eb", ebi)])
                if j != n:
                    mk = cstb[:, 128:384] if j < n else cstb[:, 384:640]
                    k.op("dve", lambda e: e.tensor_tensor(out=eb[ebi], in0=eb[ebi], in1=mk, op=ALU.mult),
                         reads=["cstb"], writes=[("eb", ebi)])
                k.op("pe", lambda e: e.matmul(
                    pa[0:65, 0:256], lhsT=vx[:, j, kvh, :], rhs=eb[ebi], start=(ji == 0), stop=(ji == nj - 1)),
                    reads=[("vx", j), "vx1", ("eb", ebi)], writes=[pak])
                if ji == 0 and pend:
                    pend.pop(0)()
                if ji == nj - 1:
                    pend.append(lambda n=n, kvh=kvh, blk=blk: attn_finish(
                        4 + (blk % 2), 256, 2, tmp,
                        lambda jj: (oall[:, n, (2 * kvh + jj) * 64:(2 * kvh + jj + 1) * 64], ("oall", n, 2 * kvh + jj)),
                        lambda jj: sinkexp[:, 2 * kvh + jj:2 * kvh + jj + 1], "s"))

            for i in range(min(LA, len(items))):
                stA(i)
            for i in range(len(items)):
                if i + LA < len(items):
                    stA(i + LA)
                stBC(i)
            while pend:
                pend.pop(0)()
            for tt in range(NT):
                okeys = [("oall", tt, h) for h in range(4)]
                out_norm_store(oall[:, tt, :], okeys, 256, vrow(vbc, "swa_out_norm"), 512, tmp, tt, tt, "so")
            A.reset(fbase, bbase)

        def ssd_phase(l, vbc):
            fbase, bbase = A.p, A.pb
            convw = A.f32(48).rearrange("p (c k) -> p c k", c=8)
            BT = A.bf16(2 * S).rearrange("p (g t) -> p g t", g=2)
            CT = A.bf16(2 * S).rearrange("p (g t) -> p g t", g=2)
            dt = A.f32(NT * 16).rearrange("p (t d) -> p t d", t=NT)
            av = A.f32(NT * 16).rearrange("p (t d) -> p t d", t=NT)
            cum = A.f32(NT * 16).rearrange("p (t d) -> p t d", t=NT)
            ncf = A.f32(NT * 8).rearrange("p (t d) -> p t d", t=NT)
            sc1 = A.f32(NT * 16).rearrange("p (t d) -> p t d", t=NT)
            sc2 = A.f32(NT * 16).rearrange("p (t d) -> p t d", t=NT)
            dw = A.f32(NT * 16).rearrange("p (t d) -> p t d", t=NT)
            tot = A.f32(NCH * 16).rearrange("p (c d) -> p c d", c=NCH)
            etot = A.f32(NCH * 16).rearrange("p (c d) -> p c d", c=NCH)
            Abc = A.f32(16)
            t16 = [A.f32(NT * 16).rearrange("p (t d) -> p t d", t=NT) for _ in range(3)]
            cumT = [A.f32(512).rearrange("p (d l) -> p d l", d=2) for _ in range(2)]
            Rst = [A.f32(256) for _ in range(2)]
            Rtmp = A.f32(256)
            Eb = [A.f32(256) for _ in range(4)]
            ya = [A.f32(256) for _ in range(2)]
            yg = [A.f32(256) for _ in range(2)]
            zb = [A.f32(256) for _ in range(2)]
            szb = [A.f32(256) for _ in range(2)]
            tmp = {"sq": A.f32(256), "ss": A.f32(8), "rs": A.f32(8), "yb": [A.bf16(256) for _ in range(2)]}
            xin = [A.bf16(S + 4)] * 2
            dg = A.bf16(5 * 128).rearrange("p (k c) -> p k c", k=5)
            xTf = A.bf16(2 * S).rearrange("p (c t) -> p c t", c=2)
            xs = A.bf16(NT * 256).rearrange("p (t d) -> p t d", t=NT)
            Btok = A.bf16(NT * 128).rearrange("p (t d) -> p t d", t=NT)
            Sinb = [A.bf16((NCH + 1) * 256).rearrange("p (c d) -> p c d", c=NCH + 1) for _ in range(2)]
            Mb = [A.bf16(256) for _ in range(4)]
            GT = [A.bf16(512) for _ in range(2)]
            xdt = [A.bf16(2 * 2 * 256).rearrange("p (r t d) -> p r t d", r=2, t=2) for _ in range(2)]
            xdw = [A.bf16(2 * 256).rearrange("p (t d) -> p t d", t=2) for _ in range(2)]
            k.dma(convw, convp_d[l].rearrange("p (c k) -> p c k", c=8), writes=["convw"])
            ci = [0]

            def conv_chunk(cc, dst_fn):
                xi = 0
                k.op("dve", lambda e, xi=xi: e.memset(xin[xi][:, 0:2], 0.0), writes=[("xin", xi)])
                k.op("dve", lambda e, xi=xi: e.memset(xin[xi][:, S + 2:S + 4], 0.0), writes=[("xin", xi)])
                k.dma(xin[xi][:, 2:S + 2], xbc_d[cc * 128:(cc + 1) * 128, :], reads=[("xbc_d",)], writes=[("xin", xi)])
                for kk in range(5):
                    k.op("dve", lambda e, kk=kk, cc=cc: e.tensor_scalar_mul(out=dg[:, kk, :], in0=identf, scalar1=convw[:, cc, kk:kk + 1]),
                         reads=["cst", "convw"], writes=["dg"])
                for t4 in range(S // 512):
                    pb, pk = PS(ci[0] % 2)
                    ci[0] += 1
                    for kk in range(5):
                        k.op("pe", lambda e, pb=pb, kk=kk, xi=xi, t4=t4: e.matmul(
                            pb[:], lhsT=dg[:, kk, :], rhs=xin[xi][:, t4 * 512 + kk:t4 * 512 + kk + 512],
                            start=(kk == 0), stop=(kk == 4)), reads=["dg", ("xin", xi)], writes=[pk])
                    dst, dkey = dst_fn(t4)
                    k.op("act", lambda e, pb=pb, dst=dst, cc=cc: e.activation(out=dst, in_=pb[:], func=AF.Silu, bias=convw[:, cc, 5:6]),
                         reads=["convw"], writes=[pk, dkey])
                ci[0] += 1

            for g in range(2):
                conv_chunk(4 + g, lambda t4, g=g: (BT[:, g, t4 * 512:(t4 + 1) * 512], ("BT", g, t4)))
                conv_chunk(6 + g, lambda t4, g=g: (CT[:, g, t4 * 512:(t4 + 1) * 512], ("CT", g, t4)))
            for tt in range(NT):
                k.dma(dt[:, tt, :], ptok_d[tt * 128:(tt + 1) * 128, 512:528], reads=[("ptok_d",)], writes=["dt"])
            bias_bc = vrow(vbc, "ssd_dt_bias").unsqueeze(1).to_broadcast([128, NT, 16])
            k.op("dve", lambda e: e.tensor_tensor(out=dt, in0=dt, in1=bias_bc, op=ALU.add), reads=["vbc"], writes=["dt"])
            k.op("dve", lambda e: e.tensor_scalar_mul(out=t16[0], in0=dt, scalar1=-1.0), reads=["dt"], writes=["t16a"])
            k.op("dve", lambda e: e.tensor_tensor(out=t16[0], in0=t16[0], in1=dt, op=ALU.max), reads=["dt"], writes=["t16a"])
            k.op("act", lambda e: e.activation(out=t16[1], in_=t16[0], func=AF.Exp, scale=-1.0), reads=["t16a"], writes=["t16b"])
            k.op("act", lambda e: e.activation(out=t16[0], in_=t16[1], func=AF.Ln, bias=1.0), reads=["t16b"], writes=["t16a"])
            k.op("dve", lambda e: e.tensor_scalar_max(out=t16[1], in0=dt, scalar1=0.0), reads=["dt"], writes=["t16b"])
            k.op("dve", lambda e: e.tensor_tensor(out=dt, in0=t16[0], in1=t16[1], op=ALU.add), reads=["t16a", "t16b"], writes=["dt"])
            k.op("act", lambda e: e.activation(out=Abc, in_=vrow(vbc, "ssd_a_log"), func=AF.Exp), reads=["vbc"], writes=["Abc"])
            k.op("dve", lambda e: e.scalar_tensor_tensor(out=av, in0=dt, scalar=-1.0, in1=Abc.unsqueeze(1).to_broadcast([128, NT, 16]),
                                                         op0=ALU.mult, op1=ALU.mult), reads=["dt", "Abc"], writes=["av"])
            dbg_dump("ssd_dt", dt, ["dt"])
            TRI_LE = cst[:, C_R0F:C_R0F + 128]
            TRI_LT = cst[:, C_R0B:C_R0B + 128]
            for c in range(NCH):
                t0_, t1_ = 2 * c, 2 * c + 1
                pb, pk = PS(c % 2)
                pv = pb[:, 0:48].rearrange("p (t d) -> p t d", t=3)
                for d, tri in ((0, TRI_LE), (1, TRI_LT)):
                    cs_ = slice(d * 8, d * 8 + 8)
                    k.op("pe", lambda e, pv=pv, tri=tri, cs_=cs_, t0_=t0_: e.matmul(pv[:, 0, cs_], lhsT=tri, rhs=av[:, t0_, cs_], start=True, stop=True),
                         reads=["av", "cst"], writes=[pk])
                    k.op("pe", lambda e, pv=pv, cs_=cs_, t0_=t0_: e.matmul(pv[:, 1, cs_], lhsT=ones_f, rhs=av[:, t0_, cs_], start=False, stop=False,
                                                                          skip_group_check=True), reads=["av", "cst"], writes=[pk])
                    k.op("pe", lambda e, pv=pv, tri=tri, cs_=cs_, t1_=t1_: e.matmul(pv[:, 1, cs_], lhsT=tri, rhs=av[:, t1_, cs_], start=False, stop=True,
                                                                                  skip_group_check=True), reads=["av", "cst"], writes=[pk])
                k.op("pe", lambda e, pv=pv, t0_=t0_: e.matmul(pv[:, 2, :], lhsT=ones_f, rhs=av[:, t0_, :], start=False, stop=False, skip_group_check=True),
                     reads=["av", "cst"], writes=[pk])
                k.op("pe", lambda e, pv=pv, t1_=t1_: e.matmul(pv[:, 2, :], lhsT=ones_f, rhs=av[:, t1_, :], start=False, stop=True, skip_group_check=True),
                     reads=["av", "cst"], writes=[pk])
                copy_op("dve", cum[:, t0_:t0_ + 2, :], pv[:, 0:2, :], reads=[], writes=[pk, "cum"])
                copy_op("dve", tot[:, c, :], pv[:, 2, :], reads=[], writes=[pk, "tot"])
            totb = lambda d: tot[:, :, d * 8:d * 8 + 8].unsqueeze(2).to_broadcast([128, NCH, 2, 8])
            v4 = lambda a, d: a[:, :, d * 8:d * 8 + 8].rearrange("p (c t) h -> p c t h", t=2)
            k.op("dve", lambda e: e.tensor_scalar_mul(out=ncf, in0=cum[:, :, 0:8], scalar1=-1.0), reads=["cum"], writes=["ncf"])
            k.op("act", lambda e: e.activation(out=sc1[:, :, 0:8], in_=cum[:, :, 0:8], func=AF.Exp), reads=["cum"], writes=["sc1a"])
            k.op("act", lambda e: e.activation(out=sc2[:, :, 8:16], in_=cum[:, :, 8:16], func=AF.Exp), reads=["cum"], writes=["sc2b"])
            k.op("dve", lambda e: e.tensor_tensor(out=v4(t16[2], 0), in0=totb(0), in1=v4(cum, 0), op=ALU.subtract), reads=["cum", "tot"], writes=["t16c0"])
            k.op("dve", lambda e: e.tensor_tensor(out=v4(t16[2], 1), in0=totb(1), in1=v4(cum, 1), op=ALU.subtract), reads=["cum", "tot"], writes=["t16c1"])
            k.op("act", lambda e: e.activation(out=sc2[:, :, 0:8], in_=t16[2][:, :, 0:8], func=AF.Exp), reads=["t16c0"], writes=["sc2a"])
            k.op("act", lambda e: e.activation(out=sc1[:, :, 8:16], in_=t16[2][:, :, 8:16], func=AF.Exp), reads=["t16c1"], writes=["sc1b"])
            k.op("act", lambda e: e.activation(out=etot, in_=tot, func=AF.Exp), reads=["tot"], writes=["etot"])
            k.op("dve", lambda e: e.tensor_tensor(out=dw, in0=dt, in1=sc2, op=ALU.mult), reads=["dt", "sc2a", "sc2b"], writes=["dw"])
            dbg_dump("ssd_cum", cum, ["cum"])
            Dbc = vrow(vbc, "ssd_d")
            v3 = lambda a: a.rearrange("p (h d) -> p h d", h=4)
            for g in range(2):
                for j in range(2):
                    conv_chunk(2 * g + j, lambda t4, j=j: (xTf[:, j, t4 * 512:(t4 + 1) * 512], ("xTf", t4)))
                for tt in range(NT):
                    pb, pk = PS(6)
                    pv = pb[:, 0:256].rearrange("p (c t) -> p c t", c=2)
                    for j in range(2):
                        k.op("pe", lambda e, pv=pv, j=j, tt=tt: e.transpose(pv[:, j, :], xTf[:, j, tt * 128:(tt + 1) * 128], identb),
                             reads=[("xTf", tt // 4), "cstb"], writes=[pk])
                    k.op("pe", lambda e, pb=pb, tt=tt: e.transpose(pb[:, 256:384], BT[:, g, tt * 128:(tt + 1) * 128], identb),
                         reads=[("BT", g, tt // 4), "cstb"], writes=[pk])
                    copy_op("act", xs[:, tt, :], pb[:, 0:256], reads=[], writes=[pk, ("xs", tt)])
                    copy_op("dve", Btok[:, tt, :], pb[:, 256:384], reads=[], writes=[pk, ("Btok", tt)])
                if g == 0:
                    dbg_dump("ssd_xs", xs, [("xs", tt) for tt in range(NT)], BF16)
                xs4 = xs.rearrange("p t (h d) -> p t h d", h=4)
                wi = 0
                for d in range(2):
                    k.op("dve", lambda e, d=d: e.memset(Rst[d], 0.0), writes=[("Rst", d)])
                    order = list(range(NCH)) if d == 0 else list(range(NCH - 1, -1, -1))
                    first_slot = 0 if d == 0 else NCH
                    k.op("dve", lambda e, d=d, first_slot=first_slot: e.memset(Sinb[d][:, first_slot, :], 0.0), writes=[("Sinb", d, first_slot)])
                    for c in order:
                        wr = wi % 2
                        wi += 1
                        dwb = dw[:, 2 * c:2 * c + 2, d * 8 + 4 * g:d * 8 + 4 * g + 4].unsqueeze(3).to_broadcast([128, 2, 4, 64])
                        k.op("dve", lambda e, wr=wr, dwb=dwb, c=c: e.tensor_tensor(out=xdw[wr].rearrange("p t (h d) -> p t h d", h=4),
                                                                                in0=xs4[:, 2 * c:2 * c + 2, :, :], in1=dwb, op=ALU.mult),
                             reads=[("xs", 2 * c), ("xs", 2 * c + 1), "dw"], writes=[("xdw", wr)])
                        pb, pk = PS(c % 2)
                        for ti in range(2):
                            tt = 2 * c + ti
                            k.op("pe", lambda e, pb=pb, tt=tt, ti=ti, wr=wr: e.matmul(
                                pb[:, 0:256], lhsT=Btok[:, tt, :], rhs=xdw[wr][:, ti, :], start=(ti == 0), stop=(ti == 1)),
                                reads=[("Btok", tt), ("xdw", wr)], writes=[pk])
                        etb = etot[:, c, d * 8 + 4 * g:d * 8 + 4 * g + 4].unsqueeze(2).to_broadcast([128, 4, 64])
                        k.op("dve", lambda e, d=d, etb=etb: e.tensor_tensor(out=v3(Rtmp), in0=v3(Rst[d]), in1=etb, op=ALU.mult),
                             reads=[("Rst", d), "etot"], writes=["Rtmp"])
                        k.op("dve", lambda e, pb=pb, d=d: e.tensor_tensor(out=Rst[d], in0=pb[:, 0:256], in1=Rtmp, op=ALU.add),
                             reads=["Rtmp"], writes=[pk, ("Rst", d)])
                        slot = c + 1 if d == 0 else c
                        k.op("act", lambda e, d=d, slot=slot: e.copy(Sinb[d][:, slot, :], Rst[d]), reads=[("Rst", d)], writes=[("Sinb", d, slot)])
                def prologue(c):
                    t0_, t1_ = 2 * c, 2 * c + 1
                    cr = c % 2
                    pb2, pk2 = PS(7)
                    for d, (r0, r1) in ((0, (C_R0F, C_R1F)), (1, (C_R0B, C_R1B))):
                        cs_ = slice(d * 8, d * 8 + 8)
                        o_ = pb2[0:8, d * 256:(d + 1) * 256]
                        k.op("pe", lambda e: e.matmul(o_, lhsT=av[:, t0_, cs_], rhs=cst[:, r0:r0 + 256],
                                                      start=(d == 0), stop=False, skip_group_check=True),
                             reads=["av", "cst"], writes=[pk2])
                        k.op("pe", lambda e: e.matmul(o_, lhsT=av[:, t1_, cs_], rhs=cst[:, r1:r1 + 256],
                                                      start=False, stop=True, skip_group_check=True),
                             reads=["av", "cst"], writes=[pk2])
                    copy_op("act", cumT[cr][0:8, :, :], pb2[0:8, 0:512].rearrange("p (d l) -> p d l", d=2), reads=[], writes=[pk2, ("cumT", cr)])
                    for d in range(2):
                        dtb = dt[:, 2 * c:2 * c + 2, d * 8 + 4 * g:d * 8 + 4 * g + 4].unsqueeze(3).to_broadcast([128, 2, 4, 64])
                        k.op("dve", lambda e: e.tensor_tensor(out=xdt[d][:, cr].rearrange("p t (h d) -> p t h d", h=4),
                                                              in0=xs4[:, 2 * c:2 * c + 2, :, :], in1=dtb, op=ALU.mult),
                             reads=[("xs", 2 * c), ("xs", 2 * c + 1), "dt"], writes=[("xdt", d, cr)])
                    pb, pk = PS(7)
                    for si in range(2):
                        k.op("pe", lambda e: e.matmul(pb[:, si * 256:(si + 1) * 256], lhsT=BT[:, g, (2 * c + si) * 128:(2 * c + si + 1) * 128],
                                                      rhs=CT[:, g, c * 256:(c + 1) * 256], start=(si == 0), stop=True, skip_group_check=True),
                             reads=[("BT", g, (2 * c + si) // 4), ("CT", g, c // 2)], writes=[pk])
                    copy_op("act", GT[cr], pb[:, 0:512], reads=[], writes=[pk, ("GT", cr)])

                def item_front(c, it):
                    h, d = it // 2, it % 2
                    hh = 4 * g + h
                    cr = c % 2
                    sel = cst[0:8, C_SEL + hh * 128:C_SEL + (hh + 1) * 128]
                    for si in range(2):
                        pb, pk = PS(2 * (it % 2) + si)
                        if d == 0:
                            lo, hi = (0, 256) if si == 0 else (128, 256)
                            mask = cstb[:, 640:896] if si == 0 else cstb[:, 640:768]
                        else:
                            lo, hi = (0, 128) if si == 0 else (0, 256)
                            mask = cstb[:, 896 + 128:896 + 256] if si == 0 else cstb[:, 896:896 + 256]
                        w = hi - lo
                        k.op("pe", lambda e: e.matmul(pb[:, 0:w], lhsT=sel, rhs=cumT[cr][0:8, d, lo:hi], start=True, stop=False),
                             reads=[("cumT", cr), "cst"], writes=[pk])
                        k.op("pe", lambda e: e.matmul(pb[:, 0:w], lhsT=identb, rhs=mask, start=False, stop=True),
                             reads=["cstb"], writes=[pk])

                def item_back(c, it, started, ybank, ykey):
                    h, d = it // 2, it % 2
                    hh = 4 * g + h
                    cr = c % 2
                    mms = []
                    for si in range(2):
                        pb, pk = PS(2 * (it % 2) + si)
                        if d == 0:
                            lo, hi = (0, 256) if si == 0 else (128, 256)
                        else:
                            lo, hi = (0, 128) if si == 0 else (0, 256)
                        w = hi - lo
                        ebi = 2 * d + si
                        tt_s = 2 * c + si
                        if d == 0:
                            k.op("act", lambda e: e.activation(out=Eb[ebi][:, 0:w], in_=pb[:, 0:w], func=AF.Exp, bias=ncf[:, tt_s, hh:hh + 1], scale=1.0),
                                 reads=["ncf"], writes=[pk, ("Eb", ebi)])
                        else:
                            k.op("act", lambda e: e.activation(out=Eb[ebi][:, 0:w], in_=pb[:, 0:w], func=AF.Exp, bias=cum[:, tt_s, 8 + hh:9 + hh], scale=-1.0),
                                 reads=["cum"], writes=[pk, ("Eb", ebi)])
                        k.op("dve", lambda e: e.tensor_tensor(out=Mb[ebi][:, 0:w], in0=Eb[ebi][:, 0:w], in1=GT[cr][:, si * 256 + lo:si * 256 + hi], op=ALU.mult),
                             reads=[("Eb", ebi), ("GT", cr)], writes=[("Mb", ebi)])
                        for li in range(2):
                            if lo <= li * 128 < hi:
                                mms.append((ebi, li * 128 - lo, li, si))
                    for (ebi, off, li, si) in mms:
                        st_ = not started[li]
                        started[li] = True
                        k.op("pe", lambda e: e.matmul(
                            ybank[li][:, h * 64:(h + 1) * 64], lhsT=Mb[ebi][:, off:off + 128], rhs=xdt[d][:, cr, si, h * 64:(h + 1) * 64],
                            start=st_, stop=False, skip_group_check=True),
                            reads=[("Mb", ebi), ("xdt", d, cr)], writes=[ykey[li]])

                def epilogue(c, ybank, ykey):
                    for li in range(2):
                        tt = 2 * c + li
                        pf7, pfk = PS(6 if False else 7)
                        pf = pf7[:, 0:256]
                        pbw = pf7[:, 256:512]
                        k.op("pe", lambda e: e.matmul(pf, lhsT=CT[:, g, tt * 128:(tt + 1) * 128], rhs=Sinb[0][:, c, :],
                                                      start=True, stop=True), reads=[("CT", g, tt // 4), ("Sinb", 0, c)], writes=[pfk])
                        k.op("pe", lambda e: e.matmul(pbw, lhsT=CT[:, g, tt * 128:(tt + 1) * 128], rhs=Sinb[1][:, c + 1, :],
                                                      start=False, stop=True, skip_group_check=True),
                             reads=[("CT", g, tt // 4), ("Sinb", 1, c + 1)], writes=[pfk])
                        ecfb = sc1[:, tt, 4 * g:4 * g + 4].unsqueeze(2).to_broadcast([128, 4, 64])
                        erbb = sc1[:, tt, 8 + 4 * g:12 + 4 * g].unsqueeze(2).to_broadcast([128, 4, 64])
                        dbb = Dbc[:, 4 * g:4 * g + 4].unsqueeze(2).to_broadcast([128, 4, 64])
                        yi = li
                        k.op("dve", lambda e: e.tensor_tensor(out=v3(ya[yi]), in0=v3(pf), in1=ecfb, op=ALU.mult),
                             reads=["sc1a"], writes=[pfk, ("ya", yi)])
                        k.op("dve", lambda e: e.tensor_tensor(out=v3(Rtmp), in0=v3(pbw), in1=erbb, op=ALU.mult),
                             reads=["sc1b"], writes=[pfk, "Rtmp"])
                        k.op("dve", lambda e: e.tensor_tensor(out=ya[yi], in0=ya[yi], in1=Rtmp, op=ALU.add), reads=["Rtmp"], writes=[("ya", yi)])
                        k.op("dve", lambda e: e.tensor_tensor(out=v3(Rtmp), in0=xs4[:, tt, :, :], in1=dbb, op=ALU.mult),
                             reads=[("xs", tt), "vbc"], writes=["Rtmp"])
                        k.op("dve", lambda e: e.tensor_tensor(out=ya[yi], in0=ya[yi], in1=Rtmp, op=ALU.add), reads=["Rtmp"], writes=[("ya", yi)])
                        k.op("dve", lambda e: e.tensor_tensor(out=ya[yi], in0=ybank[li][:, 0:256], in1=ya[yi], op=ALU.add),
                             reads=[], writes=[ykey[li], ("ya", yi)])
                        zi = li
                        k.dma(zb[zi], ptok_d[tt * 128:(tt + 1) * 128, g * 256:(g + 1) * 256], reads=[("ptok_d",)], writes=[("zb", zi)])
                        k.op("act", lambda e: e.activation(out=szb[zi], in_=zb[zi], func=AF.Silu), reads=[("zb", zi)], writes=[("szb", zi)])
                        k.op("dve", lambda e: e.tensor_tensor(out=yg[yi], in0=ya[yi], in1=szb[zi], op=ALU.mult),
                             reads=[("szb", zi), ("ya", yi)], writes=[("yg", yi)])
                        out_norm_store(yg[yi], [("yg", yi)], 256, vrow(vbc, "ssd_norm", g * 256, 256), g * 256, tmp, tt, tt, "do")

                prologue(0)
                for c in range(NCH):
                    yb0, yk0 = PS(4)
                    yb1, yk1 = PS(5)
                    ybank = (yb0, yb1)
                    ykey = (yk0, yk1)
                    started = [False, False]
                    item_front(c, 0)
                    for it in range(8):
                        if it + 1 < 8:
                            item_front(c, it + 1)
                        item_back(c, it, started, ybank, ykey)
                    if c + 1 < NCH:
                        prologue(c + 1)
                    epilogue(c, ybank, ykey)
            A.reset(fbase, bbase)

        def mixer_phase(l):
            A.reset()
            vbc = A.f32(NV)
            k.dma(vbc, vec_d[l:l + 1, :].partition_broadcast(128), writes=["vbc"])
            ssd_phase(l, vbc)
            k.barrier(barsc[:, 1:2])
            swa_phase(l, vbc)
            k.barrier(barsc[:, 2:3])
            mla_phase(l, vbc)
            if l == 0:
                k.barrier(barsc[:, 3:4])
                dbg_dump("ptok", ptok_d, [("ptok_d",)], big=True)
                dbg_dump("xbc", xbc_d, [("xbc_d",)], BF16, big=True)
                dbg_dump("y", y_d, [("y_d",)], BF16, big=True)

        finals = []
        for p in range(L + 1):
            l_prev = p - 1 if p > 0 else None
            l_next = p if p < L else None
            if stop_after is not None and p > stop_after[0]:
                break
            finals = block_phase(p, l_prev, l_next)
            k.barrier(barsc[:, 0:1])
            if l_next is not None:
                if stop_after is not None and stop_after == (p, "block"):
                    break
                mixer_phase(l_next)
                k.barrier(barsc[:, 0:1])
                if stop_after is not None and stop_after == (p, "mix"):
                    break
        k.emit(list(finals) + dbg_outs)
        print("ops recorded:", k.nops, "arena hi (KiB):", A.hi * 4 / 1024, A.hib * 2 / 1024)
    return nc


def prep_common(inp, S):
    L = inp["ffn1_norm"].shape[0]
    vec = np.zeros((L, NV), np.float32)
    for n, (o, s) in VOFF.items():
        vec[:, o:o + s] = np.asarray(inp[n], np.float32).reshape(L, s)
    cw = np.asarray(inp["ssd_conv_w"], np.float32)
    cb = np.asarray(inp["ssd_conv_b"], np.float32)
    convp = np.zeros((L, 128, 8, 6), np.float32)
    for c in range(8):
        convp[:, :, c, 0:5] = cw[:, :, c * 128:(c + 1) * 128].transpose(0, 2, 1)
        convp[:, :, c, 5] = cb[:, c * 128:(c + 1) * 128]
    com = {"vecs": vec, "convp": np.ascontiguousarray(convp.reshape(L, 128, 48)), "cst": make_consts(),
           "cs_swa": rope_tables(S, 64), "cs_mla": rope_tables(S, 32)}
    for n in ("ffn1_gate", "ffn1_up", "ffn1_down", "ffn2_gate", "ffn2_up", "ffn2_down", "w_in", "w_out",
              "mla_w_uq", "mla_w_ukv"):
        com[n] = np.ascontiguousarray(np.asarray(inp[n], np.float32))
    return com, L


_NC_CACHE = {}


def kernel(**inputs):
    x = np.asarray(inputs["x"], np.float32)
    B, S, _ = x.shape
    com, L = prep_common(inputs, S)
    key = (S, L)
    if key not in _NC_CACHE:
        _NC_CACHE[key] = build(S, L)
    nc = _NC_CACHE[key]
    ncores = 8
    in_maps = []
    for c in range(ncores):
        m = dict(com)
        m["x"] = np.ascontiguousarray(x[c % B])
        in_maps.append(m)
    res = run_bass_kernel_spmd(nc, in_maps, core_ids=list(range(ncores)))
    out = np.stack([np.asarray(res.results[b]["out"], np.float32) for b in range(B)], axis=0)
    return out
```

```python
import contextlib
import numpy as np
import concourse.bass as bass
import concourse.mybir as mybir
from concourse.bass_utils import run_bass_kernel_spmd

F32 = mybir.dt.float32
BF16 = mybir.dt.bfloat16
AF = mybir.ActivationFunctionType
ALU = mybir.AluOpType
AX = mybir.AxisListType

D = 1024
DFF = 2816
NPROJ = 2480
NTM = 1456
EPS = 1e-6
BIG = 1.0e4
GF = 4
DEBUG = False
ALT_DVE_ONLY = False
N_DMA_SEMS = 40

VOFF = {}
_o = 0
for _n, _s in (("ffn1_norm", 1024), ("mix_norm", 1024), ("ffn2_norm", 1024), ("ssd_norm", 512),
               ("swa_q_norm", 64), ("swa_k_norm", 64), ("swa_out_norm", 256), ("mla_q_lat_norm", 256),
               ("mla_kv_norm", 128), ("mla_q_norm", 96), ("mla_k_norm", 96), ("mla_out_norm", 256),
               ("ssd_dt_bias", 16), ("ssd_a_log", 16), ("ssd_d", 8), ("swa_sink", 4)):
    VOFF[_n] = (_o, _s)
    _o += _s
NV = _o

C_ID = 0
C_R0F = 128
C_R1F = 384
C_R0B = 640
C_R1B = 896
C_NEGF = 1152
C_POSB = 1408
C_GE = 1664
C_SEL = 1792
NCST = C_SEL + 1024


def make_consts():
    c = np.zeros((128, NCST), np.float32)
    i = np.arange(128)
    le = (i[:, None] <= i[None, :]).astype(np.float32)
    lt = (i[:, None] < i[None, :]).astype(np.float32)
    ge = (i[:, None] >= i[None, :]).astype(np.float32)
    gt = (i[:, None] > i[None, :]).astype(np.float32)
    c[:, C_ID:C_ID + 128] = np.eye(128)
    c[:, C_R0F:C_R0F + 128] = le
    c[:, C_R0F + 128:C_R0F + 256] = 1.0
    c[:, C_R1F + 128:C_R1F + 256] = le
    c[:, C_R0B:C_R0B + 128] = lt
    c[:, C_R0B + 128:C_R0B + 256] = 1.0
    c[:, C_R1B + 128:C_R1B + 256] = lt
    c[:, C_NEGF:C_NEGF + 128] = -BIG * gt
    c[:, C_POSB + 128:C_POSB + 256] = BIG * lt
    c[:, C_GE:C_GE + 128] = ge
    for h in range(8):
        c[h, C_SEL + h * 128:C_SEL + (h + 1) * 128] = 1.0
    return c


def rope_tables(n, dim):
    inv = 1.0 / np.power(np.float32(10000.0), np.arange(0, dim, 2, dtype=np.float32) / np.float32(dim))
    ang = np.arange(n, dtype=np.float32)[:, None] * inv[None, :].astype(np.float32)
    return np.concatenate([np.cos(ang), np.sin(ang)], axis=1).astype(np.float32)


def _freeze(fn):
    import types
    if fn.__closure__ is None:
        return fn
    cells = []
    for c in fn.__closure__:
        try:
            cells.append(types.CellType(c.cell_contents))
        except ValueError:
            cells.append(c)
    return types.FunctionType(fn.__code__, fn.__globals__, fn.__name__, fn.__defaults__, tuple(cells))


class Op:
    __slots__ = ("eng", "fn", "waits", "signal", "sem", "val", "is_dma")

    def __init__(self, eng, fn, is_dma=False):
        self.eng = eng
        self.fn = fn
        self.waits = []
        self.signal = False
        self.sem = None
        self.val = None
        self.is_dma = is_dma


class KB:
    ENGS = ("pe", "act", "dve", "pool", "sp")

    def __init__(self, nc):
        self.nc = nc
        self.prog = {e: [] for e in self.ENGS}
        self.last_w = {}
        self.readers = {}
        self.dma_rr = 0
        self.dma_last = [None] * N_DMA_SEMS
        self.dma_uses = [0] * N_DMA_SEMS
        self.bar = None
        self.nops = 0
        self.carrier = None

    def _deps(self, op, reads, writes):
        deps = []
        if self.bar is not None:
            deps.append(self.bar)
        for k in reads:
            w = self.last_w.get(k)
            if w is not None:
                deps.append(w)
        for k in writes:
            w = self.last_w.get(k)
            if w is not None:
                deps.append(w)
            deps.extend(self.readers.get(k, ()))
        for k in reads:
            self.readers.setdefault(k, []).append(op)
        for k in writes:
            self.last_w[k] = op
            self.readers[k] = []
        seen = set()
        for d in deps:
            if d is op or id(d) in seen:
                continue
            seen.add(id(d))
            if op.eng == "pe" and d.eng == "pe" and not d.is_dma and not op.is_dma:
                continue
            op.waits.append(d)

    def op(self, eng, fn, reads=(), writes=()):
        o = Op(eng, _freeze(fn))
        self._deps(o, reads, writes)
        self.prog[eng].append(o)
        self.nops += 1
        return o

    def dma(self, out, in_, reads=(), writes=(), q="sp", **kw):
        o = Op(q, None, is_dma=True)
        o.fn = lambda e, out=out, in_=in_, kw=kw: e.dma_start(out=out, in_=in_, **kw)
        self._deps(o, reads, writes)
        i = self.dma_rr
        self.dma_rr = (self.dma_rr + 1) % N_DMA_SEMS
        prev = self.dma_last[i]
        if prev is not None:
            o.waits.append(prev)
        self.dma_last[i] = o
        self.dma_uses[i] += 1
        o.sem = i
        o.val = 16 * self.dma_uses[i]
        o.signal = True
        self.prog[q].append(o)
        self.nops += 1
        return o

    def barrier(self, scratch_ap):
        o = Op("dve", lambda e: e.memset(scratch_ap, 0.0))
        if self.bar is not None:
            o.waits.append(self.bar)
        for e in self.ENGS:
            for p in reversed(self.prog[e]):
                if not p.is_dma:
                    o.waits.append(p)
                    break
        for d in self.dma_last:
            if d is not None:
                o.waits.append(d)
        self.prog["dve"].append(o)
        self.bar = o
        self.last_w = {}
        self.readers = {}
        return o

    def emit(self, final_wait_ops=()):
        nc = self.nc
        for e in self.ENGS:
            for o in self.prog[e]:
                for w in o.waits:
                    w.signal = True
        for o in final_wait_ops:
            o.signal = True
        for e in self.ENGS:
            c = 0
            for o in self.prog[e]:
                if o.is_dma:
                    continue
                if o.signal:
                    c += 1
                    o.sem = e
                    o.val = c
        with contextlib.ExitStack() as st:
            esem = {e: st.enter_context(nc.semaphore("s_" + e)) for e in self.ENGS}
            dsem = [st.enter_context(nc.semaphore("d_%d" % i)) for i in range(N_DMA_SEMS)]
            block = st.enter_context(nc.Block())

            def sem_of(o):
                return dsem[o.sem] if o.is_dma else esem[o.sem]

            def run(e, h):
                waited = {}
                for o in self.prog[e]:
                    need = {}
                    for w in o.waits:
                        key = ("d", w.sem) if w.is_dma else ("e", w.sem)
                        if waited.get(key, 0) >= w.val:
                            continue
                        waited[key] = w.val
                        need[key] = w
                    need = list(need.values())
                    if e == "pe" and not o.is_dma:
                        for w in need[:-1]:
                            h.ldweights(self.carrier)._wait_ge(sem_of(w), w.val)
                        ins = o.fn(h)
                        if need:
                            ins._wait_ge(sem_of(need[-1]), need[-1].val)
                    else:
                        for w in need:
                            h.wait_ge(sem_of(w), w.val)
                        ins = o.fn(h)
                    if o.is_dma:
                        ins.then_inc(dsem[o.sem], 16)
                    elif o.signal:
                        ins.then_inc(esem[e], 1)
                if e == "sp":
                    for o in final_wait_ops:
                        h.wait_ge(sem_of(o), o.val)

            @block.tensor
            def _(t):
                run("pe", t)

            @block.scalar
            def _(s):
                run("act", s)

            @block.vector
            def _(v):
                run("dve", v)

            @block.gpsimd
            def _(g):
                run("pool", g)

            @block.sync
            def _(s):
                run("sp", s)


class Arena:
    def __init__(self, apf, nf, apb, nb):
        self.apf, self.nf, self.apb, self.nb = apf, nf, apb, nb
        self.p = 0
        self.pb = 0
        self.hi = 0
        self.hib = 0

    def reset(self, to=0, tob=0):
        self.p = to
        self.pb = tob

    def f32(self, cols):
        a = self.apf[:, self.p:self.p + cols]
        self.p += (cols + 7) // 8 * 8
        self.hi = max(self.hi, self.p)
        assert self.p <= self.nf, ("f32 arena overflow", self.p, self.nf)
        return a

    def bf16(self, cols):
        a = self.apb[:, self.pb:self.pb + cols]
        self.pb += (cols + 15) // 16 * 16
        self.hib = max(self.hib, self.pb)
        assert self.pb <= self.nb, ("bf16 arena overflow", self.pb, self.nb)
        return a


def build(S, L, stop_after=None):
    assert S % 512 == 0
    NT = S // 128
    TB = min(1024, S)
    NBLK = S // TB
    NTB = TB // 128
    NCH = S // 256

    nc = bass.Bass("TRN2", target_bir_lowering=False)
    dr = lambda name, shape, dt=F32, kind="ExternalInput": nc.dram_tensor(name, list(shape), dt, kind=kind).ap()
    x_d = dr("x", [S, D])
    wg_d = [dr("ffn1_gate", [L, D, DFF]), dr("ffn2_gate", [L, D, DFF])]
    wu_d = [dr("ffn1_up", [L, D, DFF]), dr("ffn2_up", [L, D, DFF])]
    wd_d = [dr("ffn1_down", [L, DFF, D]), dr("ffn2_down", [L, DFF, D])]
    win_d = dr("w_in", [L, D, NPROJ])
    wout_d = dr("w_out", [L, D, D])
    wuq_d = dr("mla_w_uq", [L, 256, 384])
    wukv_d = dr("mla_w_ukv", [L, 128, 512])
    vec_d = dr("vecs", [L, NV])
    convp_d = dr("convp", [L, 128, 48])
    cst_d = dr("cst", [128, NCST])
    cs_swa_d = dr("cs_swa", [S, 64])
    cs_mla_d = dr("cs_mla", [S, 32])
    out_d = dr("out", [S, D], kind="ExternalOutput")
    xbc_d = dr("xbc_scr", [1024, S], BF16, kind="Internal")
    ptok_d = dr("ptok_scr", [S, NTM], F32, kind="Internal")
    y_d = dr("y_scr", [S, D], BF16, kind="Internal")

    ARENA_F = 17 * 1024
    ARENA_B = 56 * 1024
    st = contextlib.ExitStack()
    with st:
        arena_f = st.enter_context(nc.sbuf_tensor("arena_f", [128, ARENA_F], F32))
        arena_b = st.enter_context(nc.sbuf_tensor("arena_b", [128, ARENA_B], BF16))
        cst = st.enter_context(nc.sbuf_tensor("cst_sb", [128, NCST], F32))
        cstb = st.enter_context(nc.sbuf_tensor("cstb_sb", [128, 128 * 5 + 512], BF16))
        barsc = st.enter_context(nc.sbuf_tensor("barsc", [128, 8], F32))
        banks = [st.enter_context(nc.psum_tensor("bank%d" % i, [128, 1024] if i == 6 else [128, 512], BF16 if i == 6 else F32))
                 for i in range(8)]
        k = KB(nc)
        A = Arena(arena_f[:], ARENA_F, arena_b[:], ARENA_B)

        k.carrier = cstb[:, 0:1]
        identf = cst[:, C_ID:C_ID + 128]
        identb = cstb[:, 0:128]
        ones_f = cst[:, C_R0F + 128:C_R0F + 256]

        k.dma(cst[:], cst_d, writes=["cst"])
        k.dma(cstb[:, 0:128], cst_d[:, C_ID:C_ID + 128], writes=["cstb"], q="pool")
        for j in range(2):
            k.dma(cstb[:, 128 + j * 128:256 + j * 128], cst_d[:, C_GE:C_GE + 128], writes=["cstb"], q="pool")
            k.dma(cstb[:, 384 + j * 128:512 + j * 128], cst_d[:, C_R0F:C_R0F + 128], writes=["cstb"], q="pool")

        k.dma(cstb[:, 640:896], cst_d[:, C_NEGF:C_NEGF + 256], writes=["cstb"], q="pool")
        k.dma(cstb[:, 896:1152], cst_d[:, C_POSB:C_POSB + 256], writes=["cstb"], q="pool")

        def PS(i):
            return banks[i], "bank%d" % i

        rr = {"n": 0}
        dbg_outs = []
        dbg_names = set()

        def dbg_dump(name, ap, reads, dt=F32, big=False):
            if not DEBUG or ("dbg_" + name) in dbg_names:
                return
            dbg_names.add("dbg_" + name)
            t = nc.dram_tensor("dbg_" + name, list(ap.shape), dt, kind="ExternalOutput").ap()
            if big:
                for r0 in range(0, ap.shape[0], 256):
                    dbg_outs.append(k.dma(t[r0:r0 + 256], ap[r0:r0 + 256], reads=reads))
            else:
                dbg_outs.append(k.dma(t, ap, reads=reads))

        def alt():
            rr["n"] += 1
            return "dve" if (rr["n"] % 2 or ALT_DVE_ONLY) else "act"

        def copy_op(eng, out, in_, reads, writes):
            if eng == "act":
                return k.op("act", lambda e: e.copy(out, in_), reads, writes)
            return k.op(eng, lambda e: e.tensor_copy(out, in_), reads, writes)

        def vrow(vbc, name, lo=0, n=None):
            o, s = VOFF[name]
            n = s - lo if n is None else n
            return vbc[:, o + lo:o + lo + n]

        def block_phase(pidx, l_prev, l_next):
            A.reset()
            vbcA = A.f32(3072) if l_prev is not None else None
            vbcB = A.f32(3072) if l_next is not None else None
            xb = A.f32(NTB * D).rearrange("p (t d) -> p t d", t=NTB)
            hT = A.bf16(8 * TB).rearrange("p (c t) -> p c t", c=8)
            hb = [A.bf16(D) for _ in range(2)]
            junk = A.f32(D)
            ss = A.f32(NTB)
            sq = A.f32(NTB)
            rstd = A.f32(NTB)
            wgu = [A.bf16(2 * 8 * GF * 128).rearrange("p (w c f) -> p w c f", w=2, c=8) for _ in range(2)]
            wdn = [A.bf16(GF * D).rearrange("p (c d) -> p c d", c=GF) for _ in range(2)]
            sg = [A.bf16(512) for _ in range(2)]
            aT = [A.bf16(GF * 512).rearrange("p (c t) -> p c t", c=GF) for _ in range(2)]
            wpj = [A.bf16(8 * 512).rearrange("p (c f) -> p c f", c=8) for _ in range(2)]
            evf = [A.bf16(512) for _ in range(2)]
            evt = [A.f32(512) for _ in range(2)]
            cnt = {"w": 0, "p": 0, "a": 0, "d": 0, "t": 0, "e": 0, "g": 0}
            if l_prev is not None:
                k.dma(vbcA, vec_d[l_prev:l_prev + 1, 0:3072].partition_broadcast(128), writes=["vbcA"])
            if l_next is not None:
                k.dma(vbcB, vec_d[l_next:l_next + 1, 0:3072].partition_broadcast(128), writes=["vbcB"])

            def norm_to_hT(vbc, vkey, nname, tag):
                k.op("dve", lambda e: e.memset(ss, 0.0), writes=["ss"])
                for tt in range(NTB):
                    k.op("act", lambda e, tt=tt: e.activation(out=junk, in_=xb[:, tt, :], func=AF.Square,
                                                              accum_out=ss[:, tt:tt + 1]),
                         reads=[("xb", tt)], writes=["junk", "ss"])
                k.op("act", lambda e: e.activation(out=sq, in_=ss, func=AF.Sqrt, scale=1.0 / D, bias=EPS),
                     reads=["ss"], writes=["sq"])
                k.op("dve", lambda e: e.reciprocal(out=rstd, in_=sq), reads=["sq"], writes=["rstd"])
                wv = vrow(vbc, nname)
                if tag == "f" and not cnt.get("dbg1"):
                    cnt["dbg1"] = 1
                    dbg_dump("ss", ss, ["ss"]); dbg_dump("rstd", rstd, ["rstd"]); dbg_dump("wv", wv, [vkey])
                    dbg_dump("xb0", xb[:, 0, :], [("xb", 0)])
                for tt in range(NTB):
                    hbi = tt % 2
                    k.op("dve", lambda e, tt=tt, hbi=hbi: e.scalar_tensor_tensor(
                        out=hb[hbi], in0=xb[:, tt, :], scalar=rstd[:, tt:tt + 1], in1=wv,
                        op0=ALU.mult, op1=ALU.mult), reads=[("xb", tt), "rstd", vkey], writes=[("hb", hbi)])
                    transpose_rows(hb[hbi], ("hb", hbi), 8, hT, tt, "hT")

            def transpose_rows(src, skey, nchunks, dstT, tt, dkey):
                for h0 in range(0, nchunks, 4):
                    nn = min(4, nchunks - h0)
                    pb, pk = PS(6)
                    pv = pb[:, (cnt["t"] % 2) * 512:(cnt["t"] % 2) * 512 + 512].rearrange("p (c t) -> p c t", c=4)
                    cnt["t"] += 1
                    for j in range(nn):
                        k.op("pe", lambda e, j=j, h0=h0: e.transpose(pv[:, j, :], src[:, (h0 + j) * 128:(h0 + j + 1) * 128], identb),
                             reads=[skey, "cstb"], writes=[pk])
                    copy_op(alt(), dstT[:, h0:h0 + nn, tt * 128:(tt + 1) * 128], pv[:, 0:nn, :],
                            reads=[], writes=[pk, (dkey, tt, h0)])

            def dump_hT(nm):
                dbg_dump(nm, hT[:, :, 0:128], [("hT", 0, 0), ("hT", 0, 4)], BF16)

            def hT_keys(t4, key="hT"):
                return [(key, tt, h0) for tt in range(t4 * 4, t4 * 4 + 4) for h0 in (0, 4)]

            def ffn(l, which, vbc, vkey):
                norm_to_hT(vbc, vkey, "ffn1_norm" if which == 0 else "ffn2_norm", "f")
                if not cnt.get("dbg2"):
                    cnt["dbg2"] = 1
                    dump_hT("hT")
                wg_l = wg_d[which][l].rearrange("(c p) f -> p c f", p=128)
                wu_l = wu_d[which][l].rearrange("(c p) f -> p c f", p=128)
                wd_l = wd_d[which][l]
                groups = []
                f0 = 0
                while f0 < DFF:
                    nf = min(GF, (DFF - f0) // 128)
                    groups.append((f0, nf))
                    f0 += nf * 128
                NG = len(groups)
                NT4 = TB // 512
                units = [(g, t4) for g in range(NG) for t4 in range(NT4)]

                def load_w(g):
                    if g >= NG:
                        return
                    wi = g % 2
                    f0, nf = groups[g]
                    k.dma(wgu[wi][:, 0, :, 0:nf * 128], wg_l[:, :, f0:f0 + nf * 128], writes=[("wgu", wi, 0)], q="pool")
                    k.dma(wgu[wi][:, 1, :, 0:nf * 128], wu_l[:, :, f0:f0 + nf * 128], writes=[("wgu", wi, 1)], q="pool")
                    k.dma(wdn[wi][:, 0:nf, :], wd_l[f0:f0 + nf * 128, :].rearrange("(c p) d -> p c d", p=128), writes=[("wdn", wi)], q="pool")

                ais = {}

                def front_pieces(u):
                    g, t4 = units[u]
                    wi = g % 2
                    nf = groups[g][1]
                    ai = cnt["a"] % 2
                    cnt["a"] += 1
                    ais[u] = ai
                    hk = hT_keys(t4)
                    pieces = []
                    for fc in range(nf):
                        def piece(fc=fc):
                            gi = cnt["g"] % 2
                            cnt["g"] += 1
                            pg, pgk = PS(gi)
                            pu, puk = PS(2 + gi)
                            for w, (pp, ppk) in enumerate(((pg, pgk), (pu, puk))):
                                for dc in range(8):
                                    k.op("pe", lambda e: e.matmul(
                                        pp[:], lhsT=wgu[wi][:, w, dc, fc * 128:(fc + 1) * 128],
                                        rhs=hT[:, dc, t4 * 512:(t4 + 1) * 512], start=(dc == 0), stop=(dc == 7)),
                                        reads=[("wgu", wi, w)] + hk, writes=[ppk])
                            k.op("act", lambda e: e.activation(out=sg[gi], in_=pg[:], func=AF.Silu),
                                 reads=[], writes=[pgk, ("sg", gi)])
                            k.op("dve", lambda e: e.tensor_tensor(
                                out=aT[ai][:, fc, :], in0=pu[:], in1=sg[gi], op=ALU.mult),
                                reads=[("sg", gi)], writes=[puk, ("aT", ai, fc)])
                        pieces.append(piece)
                    return pieces

                def back_pieces(u):
                    g, t4 = units[u]
                    wi = g % 2
                    nf = groups[g][1]
                    ai = ais[u]
                    pieces = []
                    for ts in range(4):
                        def piece(ts=ts):
                            tt = t4 * 4 + ts
                            for dh in range(2):
                                di = (4, 5, 7)[cnt["d"] % 3]
                                cnt["d"] += 1
                                pd, pdk = PS(di)
                                for fc in range(nf):
                                    k.op("pe", lambda e: e.matmul(
                                        pd[:], lhsT=aT[ai][:, fc, ts * 128:(ts + 1) * 128],
                                        rhs=wdn[wi][:, fc, dh * 512:(dh + 1) * 512], start=(fc == 0), stop=(fc == nf - 1)),
                                        reads=[("aT", ai, fc), ("wdn", wi)], writes=[pdk])
                                k.op("dve", lambda e: e.scalar_tensor_tensor(
                                    out=xb[:, tt, dh * 512:(dh + 1) * 512], in0=pd[:], scalar=0.5,
                                    in1=xb[:, tt, dh * 512:(dh + 1) * 512], op0=ALU.mult, op1=ALU.add),
                                    reads=[], writes=[pdk, ("xb", tt)])
                        pieces.append(piece)
                    return pieces

                load_w(0)
                load_w(1)
                for p_ in front_pieces(0):
                    p_()
                for u in range(len(units)):
                    fp = front_pieces(u + 1) if u + 1 < len(units) else []
                    bp = back_pieces(u)
                    n = max(len(fp), len(bp))
                    for i in range(n):
                        if i < len(fp):
                            fp[i]()
                        if i < len(bp):
                            bp[i]()
                    if units[u][1] == NT4 - 1:
                        load_w(units[u][0] + 2)

            def inproj(l, blk, vbc, vkey, after_norm=None):
                norm_to_hT(vbc, vkey, "mix_norm", "m")
                if after_norm is not None:
                    after_norm()
                win_l = win_d[l].rearrange("(c p) f -> p c f", p=128)
                t0 = blk * TB
                jobs = [("f", 512 + cg * 128, 128, cg) for cg in range(8)] + \
                       [("t", c0, ncol, j0) for (c0, ncol, j0) in ((0, 512, 0), (1536, 512, 512), (2048, 432, 1024))]

                def load_p(i):
                    if i >= len(jobs):
                        return
                    kind, c0, ncol, aux = jobs[i]
                    wi = i % 2
                    k.dma(wpj[wi][:, :, 0:ncol], win_l[:, :, c0:c0 + ncol], writes=[("wpj", wi)], q="pool")

                load_p(0)
                for i, (kind, c0, ncol, aux) in enumerate(jobs):
                    load_p(i + 1)
                    wi = i % 2
                    if kind == "f":
                        cg = aux
                        for t4 in range(TB // 512):
                            gi = cnt["g"] % 2
                            cnt["g"] += 1
                            pg, pgk = PS(gi)
                            hk = hT_keys(t4)
                            for dc in range(8):
                                k.op("pe", lambda e: e.matmul(
                                    pg[:], lhsT=wpj[wi][:, dc, 0:128], rhs=hT[:, dc, t4 * 512:(t4 + 1) * 512],
                                    start=(dc == 0), stop=(dc == 7)), reads=[("wpj", wi)] + hk, writes=[pgk])
                            ei = cnt["e"] % 2
                            cnt["e"] += 1
                            copy_op(alt(), evf[ei], pg[:], reads=[], writes=[pgk, ("evf", ei)])
                            k.dma(xbc_d[cg * 128:(cg + 1) * 128, t0 + t4 * 512:t0 + (t4 + 1) * 512], evf[ei],
                                  reads=[("evf", ei)], writes=[("xbc_d", cg)])
                    else:
                        j0 = aux
                        for tt in range(NTB):
                            gi = cnt["g"] % 2
                            cnt["g"] += 1
                            pu, puk = PS(2 + gi)
                            for dc in range(8):
                                k.op("pe", lambda e: e.matmul(
                                    pu[:, 0:ncol], lhsT=hT[:, dc, tt * 128:(tt + 1) * 128], rhs=wpj[wi][:, dc, 0:ncol],
                                    start=(dc == 0), stop=(dc == 7)),
                                    reads=[("wpj", wi), ("hT", tt, 0), ("hT", tt, 4)], writes=[puk])
                            ei = cnt["e"] % 2
                            cnt["e"] += 1
                            copy_op(alt(), evt[ei][:, 0:ncol], pu[:, 0:ncol], reads=[], writes=[puk, ("evt", ei)])
                            k.dma(ptok_d[t0 + tt * 128:t0 + (tt + 1) * 128, j0:j0 + ncol], evt[ei][:, 0:ncol],
                                  reads=[("evt", ei)], writes=[("ptok_d", blk)])

            def outproj(l, blk):
                t0 = blk * TB
                wout_l = wout_d[l].rearrange("(c p) f -> p c f", p=128)
                for tt in range(NTB):
                    hbi = tt % 2
                    k.dma(hb[hbi], y_d[t0 + tt * 128:t0 + (tt + 1) * 128, :], reads=[("y_d",)], writes=[("hb", hbi)])
                    transpose_rows(hb[hbi], ("hb", hbi), 8, hT, tt, "hT")
                for dh in range(2):
                    k.dma(wpj[dh], wout_l[:, :, dh * 512:(dh + 1) * 512], writes=[("wpj", dh)], q="pool")
                for dh in range(2):
                    wi = dh
                    for tt in range(NTB):
                        di = 4 + cnt["d"] % 2
                        cnt["d"] += 1
                        pd, pdk = PS(di)
                        for dc in range(8):
                            k.op("pe", lambda e, pd=pd, dc=dc, wi=wi, tt=tt: e.matmul(
                                pd[:], lhsT=hT[:, dc, tt * 128:(tt + 1) * 128], rhs=wpj[wi][:, dc, :],
                                start=(dc == 0), stop=(dc == 7)),
                                reads=[("wpj", wi), ("hT", tt, 0), ("hT", tt, 4)], writes=[pdk])
                        k.op("dve", lambda e, pd=pd, tt=tt, dh=dh: e.tensor_tensor(
                            out=xb[:, tt, dh * 512:(dh + 1) * 512], in0=pd[:], in1=xb[:, tt, dh * 512:(dh + 1) * 512],
                            op=ALU.add), reads=[], writes=[pdk, ("xb", tt)])

            stores = []
            src = x_d if pidx == 0 else out_d

            def load_x(blk):
                t0 = blk * TB
                for tt in range(NTB):
                    k.dma(xb[:, tt, :], src[t0 + tt * 128:t0 + (tt + 1) * 128, :],
                          reads=[("out_d", blk)], writes=[("xb", tt)])

            def store_x(blk):
                t0 = blk * TB
                for tt in range(NTB):
                    stores.append(k.dma(out_d[t0 + tt * 128:t0 + (tt + 1) * 128, :], xb[:, tt, :],
                                        reads=[("xb", tt)], writes=[("out_d", blk)]))

            load_x(0)
            for blk in range(NBLK):
                if l_prev is not None:
                    outproj(l_prev, blk)
                    ffn(l_prev, 1, vbcA, "vbcA")
                if l_next is not None:
                    ffn(l_next, 0, vbcB, "vbcB")
                    store_x(blk)
                    inproj(l_next, blk, vbcB, "vbcB",
                           after_norm=(lambda blk=blk: load_x(blk + 1)) if blk + 1 < NBLK else None)
                else:
                    store_x(blk)
                    if blk + 1 < NBLK:
                        load_x(blk + 1)
            return stores

        def headnorm_rope(src, H, Dh, wrow, rlo, rhalf, cs, dst, tmp, keys_r, keys_w, cskey):
            sqv = tmp["sq"][:, 0:H * Dh].rearrange("p (h d) -> p h d", h=H)
            ssv = tmp["ss"][:, 0:H]
            rs = tmp["rs"][:, 0:H]
            qn = tmp["qn"][:, 0:H * Dh].rearrange("p (h d) -> p h d", h=H)
            t1 = tmp["t1"][:, 0:H * rhalf].rearrange("p (h d) -> p h d", h=H)
            t2 = tmp["t2"][:, 0:H * rhalf].rearrange("p (h d) -> p h d", h=H)
            T = ["T_sq", "T_ss", "T_rs", "T_qn", "T_t1", "T_t2"]
            k.op("dve", lambda e: e.tensor_tensor(out=sqv, in0=src, in1=src, op=ALU.mult), reads=keys_r, writes=[T[0]])
            k.op("dve", lambda e: e.tensor_reduce(out=ssv, in_=sqv, axis=AX.X, op=ALU.add), reads=[T[0]], writes=[T[1]])
            k.op("act", lambda e: e.activation(out=rs, in_=ssv, func=AF.Sqrt, scale=1.0 / Dh, bias=EPS), reads=[T[1]], writes=[T[2]])
            k.op("dve", lambda e: e.reciprocal(out=ssv, in_=rs), reads=[T[2]], writes=[T[1]])
            k.op("dve", lambda e: e.tensor_tensor(out=qn, in0=src, in1=ssv.unsqueeze(2).to_broadcast([128, H, Dh]), op=ALU.mult),
                 reads=keys_r + [T[1]], writes=[T[3]])
            k.op("dve", lambda e: e.tensor_tensor(out=qn, in0=qn, in1=wrow.unsqueeze(1).to_broadcast([128, H, Dh]), op=ALU.mult),
                 reads=["vbc"], writes=[T[3]])
            if rlo > 0:
                k.op("dve", lambda e: e.tensor_copy(dst[:, :, 0:rlo], qn[:, :, 0:rlo]), reads=[T[3]], writes=keys_w)
            x1 = qn[:, :, rlo:rlo + rhalf]
            x2 = qn[:, :, rlo + rhalf:rlo + 2 * rhalf]
            cb = cs[:, 0:rhalf].unsqueeze(1).to_broadcast([128, H, rhalf])
            sb = cs[:, rhalf:2 * rhalf].unsqueeze(1).to_broadcast([128, H, rhalf])
            k.op("dve", lambda e: e.tensor_tensor(out=t1, in0=x1, in1=cb, op=ALU.mult), reads=[T[3], cskey], writes=[T[4]])
            k.op("dve", lambda e: e.tensor_tensor(out=t2, in0=x2, in1=sb, op=ALU.mult), reads=[T[3], cskey], writes=[T[5]])
            k.op("dve", lambda e: e.tensor_tensor(out=dst[:, :, rlo:rlo + rhalf], in0=t1, in1=t2, op=ALU.subtract),
                 reads=[T[4], T[5]], writes=keys_w)
            k.op("dve", lambda e: e.tensor_tensor(out=t1, in0=x1, in1=sb, op=ALU.mult), reads=[T[3], cskey], writes=[T[4]])
            k.op("dve", lambda e: e.tensor_tensor(out=t2, in0=x2, in1=cb, op=ALU.mult), reads=[T[3], cskey], writes=[T[5]])
            k.op("dve", lambda e: e.tensor_tensor(out=dst[:, :, rlo + rhalf:rlo + 2 * rhalf], in0=t1, in1=t2, op=ALU.add),
                 reads=[T[4], T[5]], writes=keys_w)

        def heads_to_T(src, skey, H, Dh, dstT, tt, dkey, bank):
            pb, pk = PS(6)
            pv = pb[:, 0:512].rearrange("p (c t) -> p c t", c=4)
            for h in range(H):
                k.op("pe", lambda e, h=h: e.transpose(pv[0:Dh, h, :], src[:, h, :], identb), reads=[skey, "cstb"], writes=[pk])
            copy_op(alt(), dstT[0:Dh, 0:H, tt * 128:(tt + 1) * 128], pv[0:Dh, 0:H, :], reads=[], writes=[pk, (dkey, tt)])

        def out_norm_store(osrc, okeys, width, wrow, col0, tmp, tt, ring, tag):
            k.op("dve", lambda e: e.memset(tmp["ss"][:, 0:1], 0.0), writes=["T_ss"])
            k.op("act", lambda e: e.activation(out=tmp["sq"][:, 0:width], in_=osrc, func=AF.Square, accum_out=tmp["ss"][:, 0:1]),
                 reads=okeys, writes=["T_sq", "T_ss"])
            k.op("act", lambda e: e.activation(out=tmp["rs"][:, 0:1], in_=tmp["ss"][:, 0:1], func=AF.Sqrt, scale=1.0 / width, bias=EPS),
                 reads=["T_ss"], writes=["T_rs"])
            k.op("dve", lambda e: e.reciprocal(out=tmp["ss"][:, 1:2], in_=tmp["rs"][:, 0:1]), reads=["T_rs"], writes=["T_ss"])
            yi = ring % 2
            yb = tmp["yb"][yi][:, 0:width]
            k.op("dve", lambda e: e.scalar_tensor_tensor(out=yb, in0=osrc, scalar=tmp["ss"][:, 1:2], in1=wrow,
                                                         op0=ALU.mult, op1=ALU.mult),
                 reads=list(okeys) + ["T_ss", "vbc"], writes=[(tag + "yb", yi)])
            k.dma(y_d[tt * 128:(tt + 1) * 128, col0:col0 + width], yb, reads=[(tag + "yb", yi)], writes=[("y_d",)])

        def attn_finish(acc_bank, ncols_q, nheads, tmp, odst_fn, extra_den, tag):
            pa, pak = PS(acc_bank)
            oT = tmp["oT"]
            k.op("act", lambda e: e.copy(oT[0:65, 0:ncols_q], pa[0:65, 0:ncols_q]), reads=[], writes=[pak, "T_oT"])
            for j in range(ncols_q // 128):
                pb, pk = PS(7)
                k.op("pe", lambda e, j=j: e.transpose(pb[:, 0:65], oT[0:65, j * 128:(j + 1) * 128], identf[0:65, 0:65]),
                     reads=["T_oT", "cst"], writes=[pk])
                den = tmp["den"]
                if extra_den is not None:
                    ed = extra_den(j)
                    k.op("dve", lambda e, ed=ed: e.tensor_tensor(out=den[:, 0:1], in0=pb[:, 64:65], in1=ed, op=ALU.add),
                         reads=["sinkexp"], writes=[pk, "T_den"])
                else:
                    k.op("dve", lambda e: e.tensor_copy(den[:, 0:1], pb[:, 64:65]), reads=[], writes=[pk, "T_den"])
                k.op("dve", lambda e: e.reciprocal(out=den[:, 1:2], in_=den[:, 0:1]), reads=["T_den"], writes=["T_rden"])
                dst, dkey = odst_fn(j)
                k.op("dve", lambda e, dst=dst: e.tensor_scalar_mul(out=dst, in0=pb[:, 0:64], scalar1=den[:, 1:2]), reads=["T_rden"], writes=[pk, dkey])

        def mla_phase(l, vbc):
            fbase, bbase = A.p, A.pb
            qT = A.bf16(4 * S).rearrange("p (h t) -> p h t", h=4)
            kT = A.bf16(4 * S).rearrange("p (h t) -> p h t", h=4)
            vx = A.bf16(NT * 4 * 65).rearrange("p (t h d) -> p t h d", t=NT, h=4)
            oall = A.f32(NT * 256).rearrange("p (t d) -> p t d", t=NT)
            wuq = A.bf16(2 * 384).rearrange("p (c f) -> p c f", c=2)
            wukv = A.bf16(512)
            pin = [A.f32(416) for _ in range(2)]
            csb = [A.f32(32) for _ in range(2)]
            nb = [A.bf16(384) for _ in range(2)]
            nT = A.bf16(3 * 128).rearrange("p (c t) -> p c t", c=3)
            qf = A.f32(384).rearrange("p (h d) -> p h d", h=4)
            kf = A.f32(384).rearrange("p (h d) -> p h d", h=4)
            qb = A.bf16(384).rearrange("p (h d) -> p h d", h=4)
            kb_ = A.bf16(384).rearrange("p (h d) -> p h d", h=4)
            tmp = {"sq": A.f32(384), "ss": A.f32(8), "rs": A.f32(8), "qn": A.f32(384), "t1": A.f32(64), "t2": A.f32(64),
                   "oT": A.f32(512), "den": A.f32(2), "yb": [A.bf16(256) for _ in range(2)]}
            eb = [A.bf16(512) for _ in range(3)]
            k.dma(wuq, wuq_d[l].rearrange("(c p) f -> p c f", p=128), writes=["wuq"], q="pool")
            k.dma(wukv, wukv_d[l], writes=["wukv"], q="pool")
            k.op("dve", lambda e: e.memset(vx[:, :, :, 64:65], 1.0), writes=["vx1"])
            for tt in range(NT):
                pi = tt % 2
                k.dma(pin[pi], ptok_d[tt * 128:(tt + 1) * 128, 1040:1456], reads=[("ptok_d",)], writes=[("pin", pi)])
                k.dma(csb[pi], cs_mla_d[tt * 128:(tt + 1) * 128, :], writes=[("cs", pi)])
                for (c0, n, wn, tagn) in ((0, 256, "mla_q_lat_norm", "a"), (256, 128, "mla_kv_norm", "b")):
                    k.op("dve", lambda e: e.memset(tmp["ss"][:, 0:1], 0.0), writes=["T_ss"])
                    k.op("act", lambda e, c0=c0, n=n, pi=pi: e.activation(out=tmp["sq"][:, 0:n], in_=pin[pi][:, c0:c0 + n],
                                                                        func=AF.Square, accum_out=tmp["ss"][:, 0:1]),
                         reads=[("pin", pi)], writes=["T_sq", "T_ss"])
                    k.op("act", lambda e, n=n: e.activation(out=tmp["rs"][:, 0:1], in_=tmp["ss"][:, 0:1], func=AF.Sqrt,
                                                            scale=1.0 / n, bias=EPS), reads=["T_ss"], writes=["T_rs"])
                    k.op("dve", lambda e: e.reciprocal(out=tmp["ss"][:, 1:2], in_=tmp["rs"][:, 0:1]), reads=["T_rs"], writes=["T_ss"])
                    k.op("dve", lambda e, c0=c0, n=n, pi=pi, wn=wn: e.scalar_tensor_tensor(
                        out=nb[pi][:, c0:c0 + n], in0=pin[pi][:, c0:c0 + n], scalar=tmp["ss"][:, 1:2], in1=vrow(vbc, wn),
                        op0=ALU.mult, op1=ALU.mult), reads=[("pin", pi), "T_ss", "vbc"], writes=[("nb", pi, tagn)])
                pb, pk = PS(6)
                pv = pb[:, 512:1024].rearrange("p (c t) -> p c t", c=4)
                for c in range(3):
                    k.op("pe", lambda e, c=c, pi=pi: e.transpose(pv[:, c, :], nb[pi][:, c * 128:(c + 1) * 128], identb),
                         reads=[("nb", pi, "a"), ("nb", pi, "b"), "cstb"], writes=[pk])
                copy_op("act", nT, pv[:, 0:3, :], reads=[], writes=[pk, "nT"])
                pq, pqk = PS(0)
                for c in range(2):
                    k.op("pe", lambda e, c=c: e.matmul(pq[:, 0:384], lhsT=nT[:, c, :], rhs=wuq[:, c, :], start=(c == 0), stop=(c == 1)),
                         reads=["nT", "wuq"], writes=[pqk])
                pkv, pkvk = PS(1)
                k.op("pe", lambda e: e.matmul(pkv[:], lhsT=nT[:, 2, :], rhs=wukv, start=True, stop=True), reads=["nT", "wukv"], writes=[pkvk])
                copy_op("act", qf, pq[:, 0:384].rearrange("p (h d) -> p h d", h=4), reads=[], writes=[pqk, "qf"])
                kvv = pkv[:].rearrange("p (h d) -> p h d", h=4)
                copy_op("act", kf[:, :, 0:64], kvv[:, :, 0:64], reads=[], writes=[pkvk, "kf"])
                copy_op("dve", vx[:, tt, :, 0:64], kvv[:, :, 64:128], reads=[], writes=[pkvk, ("vx", tt)])
                k.op("dve", lambda e, pi=pi: e.tensor_copy(kf[:, :, 64:96], pin[pi][:, 384:416].unsqueeze(1).to_broadcast([128, 4, 32])),
                     reads=[("pin", pi)], writes=["kf"])
                headnorm_rope(qf, 4, 96, vrow(vbc, "mla_q_norm"), 64, 16, csb[pi], qb, tmp, ["qf"], ["qb"], ("cs", pi))
                headnorm_rope(kf, 4, 96, vrow(vbc, "mla_k_norm"), 64, 16, csb[pi], kb_, tmp, ["kf"], ["kb"], ("cs", pi))
                heads_to_T(qb, "qb", 4, 96, qT, tt, "qT", 6)
                heads_to_T(kb_, "kb", 4, 96, kT, tt, "kT", 6)
            scale = 96.0 ** -0.5
            qkeys = lambda qi: [("qT", tt) for tt in range(qi * 4, qi * 4 + 4)]
            items = [(h, qi, kt) for h in range(4) for qi in range(S // 512) for kt in range(NT)]
            LA = 2

            def stA(i):
                h, qi, kt = items[i]
                psn, psk = PS(i % 3)
                k.op("pe", lambda e: e.matmul(
                    psn[:], lhsT=kT[0:96, h, kt * 128:(kt + 1) * 128], rhs=qT[0:96, h, qi * 512:(qi + 1) * 512],
                    start=True, stop=True), reads=[("kT", kt)] + qkeys(qi), writes=[psk])

            def stBC(i):
                h, qi, kt = items[i]
                psn, psk = PS(i % 3)
                ebi = i % 3
                blk = h * (S // 512) + qi
                pa, pak = PS(4 + (blk % 2))
                k.op("act", lambda e: e.activation(out=eb[ebi], in_=psn[:], func=AF.Exp, scale=scale),
                     reads=[], writes=[psk, ("eb", ebi)])
                k.op("pe", lambda e: e.matmul(
                    pa[0:65, :], lhsT=vx[:, kt, h, :], rhs=eb[ebi], start=(kt == 0), stop=(kt == NT - 1)),
                    reads=[("vx", kt), "vx1", ("eb", ebi)], writes=[pak])
                if kt == NT - 1:
                    attn_finish(4 + (blk % 2), 512, 1, tmp,
                                lambda j, h=h, qi=qi: (oall[:, qi * 4 + j, h * 64:(h + 1) * 64], ("oall", qi * 4 + j, h)),
                                None, "m")

            for i in range(min(LA, len(items))):
                stA(i)
            for i in range(len(items)):
                if i + LA < len(items):
                    stA(i + LA)
                stBC(i)
            for tt in range(NT):
                okeys = [("oall", tt, h) for h in range(4)]
                out_norm_store(oall[:, tt, :], okeys, 256, vrow(vbc, "mla_out_norm"), 768, tmp, tt, tt, "mo")
            A.reset(fbase, bbase)

        def swa_phase(l, vbc):
            fbase, bbase = A.p, A.pb
            qT = A.bf16(4 * S).rearrange("p (h t) -> p h t", h=4)
            kT = A.bf16(2 * S).rearrange("p (h t) -> p h t", h=2)
            vx = A.bf16(NT * 2 * 65).rearrange("p (t h d) -> p t h d", t=NT, h=2)
            oall = A.f32(NT * 256).rearrange("p (t d) -> p t d", t=NT)
            pin = [A.f32(512) for _ in range(2)]
            csb = [A.f32(64) for _ in range(2)]
            qb = A.bf16(256).rearrange("p (h d) -> p h d", h=4)
            kb_ = A.bf16(128).rearrange("p (h d) -> p h d", h=2)
            tmp = {"sq": A.f32(256), "ss": A.f32(8), "rs": A.f32(8), "qn": A.f32(256), "t1": A.f32(128), "t2": A.f32(128),
                   "oT": A.f32(512), "den": A.f32(2), "yb": [A.bf16(256) for _ in range(2)]}
            sinkexp = A.f32(4)
            eb = [A.bf16(256) for _ in range(3)]
            k.op("act", lambda e: e.activation(out=sinkexp, in_=vrow(vbc, "swa_sink"), func=AF.Exp), reads=["vbc"], writes=["sinkexp"])
            k.op("dve", lambda e: e.memset(vx[:, :, :, 64:65], 1.0), writes=["vx1"])
            for tt in range(NT):
                pi = tt % 2
                k.dma(pin[pi], ptok_d[tt * 128:(tt + 1) * 128, 528:1040], reads=[("ptok_d",)], writes=[("pin", pi)])
                k.dma(csb[pi], cs_swa_d[tt * 128:(tt + 1) * 128, :], writes=[("cs", pi)])
                qsrc = pin[pi][:, 0:256].rearrange("p (h d) -> p h d", h=4)
                ksrc = pin[pi][:, 256:384].rearrange("p (h d) -> p h d", h=2)
                headnorm_rope(qsrc, 4, 64, vrow(vbc, "swa_q_norm"), 0, 32, csb[pi], qb, tmp, [("pin", pi)], ["qb"], ("cs", pi))
                headnorm_rope(ksrc, 2, 64, vrow(vbc, "swa_k_norm"), 0, 32, csb[pi], kb_, tmp, [("pin", pi)], ["kb"], ("cs", pi))
                k.op("dve", lambda e, pi=pi, tt=tt: e.tensor_copy(vx[:, tt, :, 0:64], pin[pi][:, 384:512].rearrange("p (h d) -> p h d", h=2)),
                     reads=[("pin", pi)], writes=[("vx", tt)])
                heads_to_T(qb, "qb", 4, 64, qT, tt, "qT", 6)
                heads_to_T(kb_, "kb", 2, 64, kT, tt, "kT", 6)
            scale = 64.0 ** -0.5
            items = []
            for n in range(NT):
                for kvh in range(2):
                    js = [j for j in (n - 1, n, n + 1) if 0 <= j < NT]
                    for ji, j in enumerate(js):
                        items.append((n, kvh, j, ji, len(js)))
            LA = 2
            pend = []

            def stA(i):
                n, kvh, j, ji, nj = items[i]
                psn, psk = PS(i % 3)
                k.op("pe", lambda e: e.matmul(
                    psn[:, 0:256].rearrange("p (a b) -> p a b", a=2), lhsT=kT[0:64, kvh, j * 128:(j + 1) * 128],
                    rhs=qT[0:64, 2 * kvh:2 * kvh + 2, n * 128:(n + 1) * 128], start=True, stop=True),
                    reads=[("kT", j), ("qT", n)], writes=[psk])

            def stBC(i):
                n, kvh, j, ji, nj = items[i]
                psn, psk = PS(i % 3)
                ebi = i % 3
                blk = n * 2 + kvh
                pa, pak = PS(4 + (blk % 2))
                k.op("act", lambda e: e.activation(out=eb[ebi], in_=psn[:, 0:256], func=AF.Exp, scale=scale),
                     reads=[], writes=[psk, ("eb", ebi)])
                if j != n:
                    mk = cstb[:, 128:384] if j < n else cstb[:, 384:640]
                    k.op("dve", lambda e: e.tensor_tensor(out=eb[ebi], in0=eb[ebi], in1=mk, op=ALU.mult),
                         reads=["cstb"], writes=[("eb", ebi)])
                k.op("pe", lambda e: e.matmul(
                    pa[0:65, 0:256], lhsT=vx[:, j, kvh, :], rhs=eb[ebi], start=(ji == 0), stop=(ji == nj - 1)),
                    reads=[("vx", j), "vx1", ("eb", ebi)], writes=[pak])
                if ji == 0 and pend:
                    pend.pop(0)()
                if ji == nj - 1:
                    pend.append(lambda n=n, kvh=kvh, blk=blk: attn_finish(
                        4 + (blk % 2), 256, 2, tmp,
                        lambda jj: (oall[:, n, (2 * kvh + jj) * 64:(2 * kvh + jj + 1) * 64], ("oall", n, 2 * kvh + jj)),
                        lambda jj: sinkexp[:, 2 * kvh + jj:2 * kvh + jj + 1], "s"))

            for i in range(min(LA, len(items))):
                stA(i)
            for i in range(len(items)):
                if i + LA < len(items):
                    stA(i + LA)
                stBC(i)
            while pend:
                pend.pop(0)()
            for tt in range(NT):
                okeys = [("oall", tt, h) for h in range(4)]
                out_norm_store(oall[:, tt, :], okeys, 256, vrow(vbc, "swa_out_norm"), 512, tmp, tt, tt, "so")
            A.reset(fbase, bbase)

        def ssd_phase(l, vbc):
            fbase, bbase = A.p, A.pb
            convw = A.f32(48).rearrange("p (c k) -> p c k", c=8)
            BT = A.bf16(2 * S).rearrange("p (g t) -> p g t", g=2)
            CT = A.bf16(2 * S).rearrange("p (g t) -> p g t", g=2)
            dt = A.f32(NT * 16).rearrange("p (t d) -> p t d", t=NT)
            av = A.f32(NT * 16).rearrange("p (t d) -> p t d", t=NT)
            cum = A.f32(NT * 16).rearrange("p (t d) -> p t d", t=NT)
            ncf = A.f32(NT * 8).rearrange("p (t d) -> p t d", t=NT)
            sc1 = A.f32(NT * 16).rearrange("p (t d) -> p t d", t=NT)
            sc2 = A.f32(NT * 16).rearrange("p (t d) -> p t d", t=NT)
            dw = A.f32(NT * 16).rearrange("p (t d) -> p t d", t=NT)
            tot = A.f32(NCH * 16).rearrange("p (c d) -> p c d", c=NCH)
            etot = A.f32(NCH * 16).rearrange("p (c d) -> p c d", c=NCH)
            Abc = A.f32(16)
            t16 = [A.f32(NT * 16).rearrange("p (t d) -> p t d", t=NT) for _ in range(3)]
            cumT = [A.f32(512).rearrange("p (d l) -> p d l", d=2) for _ in range(2)]
            Rst = [A.f32(256) for _ in range(2)]
            Rtmp = A.f32(256)
            Eb = [A.f32(256) for _ in range(4)]
            ya = [A.f32(256) for _ in range(2)]
            yg = [A.f32(256) for _ in range(2)]
            zb = [A.f32(256) for _ in range(2)]
            szb = [A.f32(256) for _ in range(2)]
            tmp = {"sq": A.f32(256), "ss": A.f32(8), "rs": A.f32(8), "yb": [A.bf16(256) for _ in range(2)]}
            xin = [A.bf16(S + 4)] * 2
            dg = A.bf16(5 * 128).rearrange("p (k c) -> p k c", k=5)
            xTf = A.bf16(2 * S).rearrange("p (c t) -> p c t", c=2)
            xs = A.bf16(NT * 256).rearrange("p (t d) -> p t d", t=NT)
            Btok = A.bf16(NT * 128).rearrange("p (t d) -> p t d", t=NT)
            Sinb = [A.bf16((NCH + 1) * 256).rearrange("p (c d) -> p c d", c=NCH + 1) for _ in range(2)]
            Mb = [A.bf16(256) for _ in range(4)]
            GT = [A.bf16(512) for _ in range(2)]
            xdt = [A.bf16(2 * 2 * 256).rearrange("p (r t d) -> p r t d", r=2, t=2) for _ in range(2)]
            xdw = [A.bf16(2 * 256).rearrange("p (t d) -> p t d", t=2) for _ in range(2)]
            k.dma(convw, convp_d[l].rearrange("p (c k) -> p c k", c=8), writes=["convw"])
            ci = [0]

            def conv_chunk(cc, dst_fn):
                xi = 0
                k.op("dve", lambda e, xi=xi: e.memset(xin[xi][:, 0:2], 0.0), writes=[("xin", xi)])
                k.op("dve", lambda e, xi=xi: e.memset(xin[xi][:, S + 2:S + 4], 0.0), writes=[("xin", xi)])
                k.dma(xin[xi][:, 2:S + 2], xbc_d[cc * 128:(cc + 1) * 128, :], reads=[("xbc_d",)], writes=[("xin", xi)])
                for kk in range(5):
                    k.op("dve", lambda e, kk=kk, cc=cc: e.tensor_scalar_mul(out=dg[:, kk, :], in0=identf, scalar1=convw[:, cc, kk:kk + 1]),
                         reads=["cst", "convw"], writes=["dg"])
                for t4 in range(S // 512):
                    pb, pk = PS(ci[0] % 2)
                    ci[0] += 1
                    for kk in range(5):
                        k.op("pe", lambda e, pb=pb, kk=kk, xi=xi, t4=t4: e.matmul(
                            pb[:], lhsT=dg[:, kk, :], rhs=xin[xi][:, t4 * 512 + kk:t4 * 512 + kk + 512],
                            start=(kk == 0), stop=(kk == 4)), reads=["dg", ("xin", xi)], writes=[pk])
                    dst, dkey = dst_fn(t4)
                    k.op("act", lambda e, pb=pb, dst=dst, cc=cc: e.activation(out=dst, in_=pb[:], func=AF.Silu, bias=convw[:, cc, 5:6]),
                         reads=["convw"], writes=[pk, dkey])
                ci[0] += 1

            for g in range(2):
                conv_chunk(4 + g, lambda t4, g=g: (BT[:, g, t4 * 512:(t4 + 1) * 512], ("BT", g, t4)))
                conv_chunk(6 + g, lambda t4, g=g: (CT[:, g, t4 * 512:(t4 + 1) * 512], ("CT", g, t4)))
            for tt in range(NT):
                k.dma(dt[:, tt, :], ptok_d[tt * 128:(tt + 1) * 128, 512:528], reads=[("ptok_d",)], writes=["dt"])
            bias_bc = vrow(vbc, "ssd_dt_bias").unsqueeze(1).to_broadcast([128, NT, 16])
            k.op("dve", lambda e: e.tensor_tensor(out=dt, in0=dt, in1=bias_bc, op=ALU.add), reads=["vbc"], writes=["dt"])
            k.op("dve", lambda e: e.tensor_scalar_mul(out=t16[0], in0=dt, scalar1=-1.0), reads=["dt"], writes=["t16a"])
            k.op("dve", lambda e: e.tensor_tensor(out=t16[0], in0=t16[0], in1=dt, op=ALU.max), reads=["dt"], writes=["t16a"])
            k.op("act", lambda e: e.activation(out=t16[1], in_=t16[0], func=AF.Exp, scale=-1.0), reads=["t16a"], writes=["t16b"])
            k.op("act", lambda e: e.activation(out=t16[0], in_=t16[1], func=AF.Ln, bias=1.0), reads=["t16b"], writes=["t16a"])
            k.op("dve", lambda e: e.tensor_scalar_max(out=t16[1], in0=dt, scalar1=0.0), reads=["dt"], writes=["t16b"])
            k.op("dve", lambda e: e.tensor_tensor(out=dt, in0=t16[0], in1=t16[1], op=ALU.add), reads=["t16a", "t16b"], writes=["dt"])
            k.op("act", lambda e: e.activation(out=Abc, in_=vrow(vbc, "ssd_a_log"), func=AF.Exp), reads=["vbc"], writes=["Abc"])
            k.op("dve", lambda e: e.scalar_tensor_tensor(out=av, in0=dt, scalar=-1.0, in1=Abc.unsqueeze(1).to_broadcast([128, NT, 16]),
                                                         op0=ALU.mult, op1=ALU.mult), reads=["dt", "Abc"], writes=["av"])
            dbg_dump("ssd_dt", dt, ["dt"])
            TRI_LE = cst[:, C_R0F:C_R0F + 128]
            TRI_LT = cst[:, C_R0B:C_R0B + 128]
            for c in range(NCH):
                t0_, t1_ = 2 * c, 2 * c + 1
                pb, pk = PS(c % 2)
                pv = pb[:, 0:48].rearrange("p (t d) -> p t d", t=3)
                for d, tri in ((0, TRI_LE), (1, TRI_LT)):
                    cs_ = slice(d * 8, d * 8 + 8)
                    k.op("pe", lambda e, pv=pv, tri=tri, cs_=cs_, t0_=t0_: e.matmul(pv[:, 0, cs_], lhsT=tri, rhs=av[:, t0_, cs_], start=True, stop=True),
                         reads=["av", "cst"], writes=[pk])
                    k.op("pe", lambda e, pv=pv, cs_=cs_, t0_=t0_: e.matmul(pv[:, 1, cs_], lhsT=ones_f, rhs=av[:, t0_, cs_], start=False, stop=False,
                                                                          skip_group_check=True), reads=["av", "cst"], writes=[pk])
                    k.op("pe", lambda e, pv=pv, tri=tri, cs_=cs_, t1_=t1_: e.matmul(pv[:, 1, cs_], lhsT=tri, rhs=av[:, t1_, cs_], start=False, stop=True,
                                                                                  skip_group_check=True), reads=["av", "cst"], writes=[pk])
                k.op("pe", lambda e, pv=pv, t0_=t0_: e.matmul(pv[:, 2, :], lhsT=ones_f, rhs=av[:, t0_, :], start=False, stop=False, skip_group_check=True),
                     reads=["av", "cst"], writes=[pk])
                k.op("pe", lambda e, pv=pv, t1_=t1_: e.matmul(pv[:, 2, :], lhsT=ones_f, rhs=av[:, t1_, :], start=False, stop=True, skip_group_check=True),
                     reads=["av", "cst"], writes=[pk])
                copy_op("dve", cum[:, t0_:t0_ + 2, :], pv[:, 0:2, :], reads=[], writes=[pk, "cum"])
                copy_op("dve", tot[:, c, :], pv[:, 2, :], reads=[], writes=[pk, "tot"])
            totb = lambda d: tot[:, :, d * 8:d * 8 + 8].unsqueeze(2).to_broadcast([128, NCH, 2, 8])
            v4 = lambda a, d: a[:, :, d * 8:d * 8 + 8].rearrange("p (c t) h -> p c t h", t=2)
            k.op("dve", lambda e: e.tensor_scalar_mul(out=ncf, in0=cum[:, :, 0:8], scalar1=-1.0), reads=["cum"], writes=["ncf"])
            k.op("act", lambda e: e.activation(out=sc1[:, :, 0:8], in_=cum[:, :, 0:8], func=AF.Exp), reads=["cum"], writes=["sc1a"])
            k.op("act", lambda e: e.activation(out=sc2[:, :, 8:16], in_=cum[:, :, 8:16], func=AF.Exp), reads=["cum"], writes=["sc2b"])
            k.op("dve", lambda e: e.tensor_tensor(out=v4(t16[2], 0), in0=totb(0), in1=v4(cum, 0), op=ALU.subtract), reads=["cum", "tot"], writes=["t16c0"])
            k.op("dve", lambda e: e.tensor_tensor(out=v4(t16[2], 1), in0=totb(1), in1=v4(cum, 1), op=ALU.subtract), reads=["cum", "tot"], writes=["t16c1"])
            k.op("act", lambda e: e.activation(out=sc2[:, :, 0:8], in_=t16[2][:, :, 0:8], func=AF.Exp), reads=["t16c0"], writes=["sc2a"])
            k.op("act", lambda e: e.activation(out=sc1[:, :, 8:16], in_=t16[2][:, :, 8:16], func=AF.Exp), reads=["t16c1"], writes=["sc1b"])
            k.op("act", lambda e: e.activation(out=etot, in_=tot, func=AF.Exp), reads=["tot"], writes=["etot"])
            k.op("dve", lambda e: e.tensor_tensor(out=dw, in0=dt, in1=sc2, op=ALU.mult), reads=["dt", "sc2a", "sc2b"], writes=["dw"])
            dbg_dump("ssd_cum", cum, ["cum"])
            Dbc = vrow(vbc, "ssd_d")
            v3 = lambda a: a.rearrange("p (h d) -> p h d", h=4)
            for g in range(2):
                for j in range(2):
                    conv_chunk(2 * g + j, lambda t4, j=j: (xTf[:, j, t4 * 512:(t4 + 1) * 512], ("xTf", t4)))
                for tt in range(NT):
                    pb, pk = PS(6)
                    pv = pb[:, 0:256].rearrange("p (c t) -> p c t", c=2)
                    for j in range(2):
                        k.op("pe", lambda e, pv=pv, j=j, tt=tt: e.transpose(pv[:, j, :], xTf[:, j, tt * 128:(tt + 1) * 128], identb),
                             reads=[("xTf", tt // 4), "cstb"], writes=[pk])
                    k.op("pe", lambda e, pb=pb, tt=tt: e.transpose(pb[:, 256:384], BT[:, g, tt * 128:(tt + 1) * 128], identb),
                         reads=[("BT", g, tt // 4), "cstb"], writes=[pk])
                    copy_op("act", xs[:, tt, :], pb[:, 0:256], reads=[], writes=[pk, ("xs", tt)])
                    copy_op("dve", Btok[:, tt, :], pb[:, 256:384], reads=[], writes=[pk, ("Btok", tt)])
                if g == 0:
                    dbg_dump("ssd_xs", xs, [("xs", tt) for tt in range(NT)], BF16)
                xs4 = xs.rearrange("p t (h d) -> p t h d", h=4)
                wi = 0
                for d in range(2):
                    k.op("dve", lambda e, d=d: e.memset(Rst[d], 0.0), writes=[("Rst", d)])
                    order = list(range(NCH)) if d == 0 else list(range(NCH - 1, -1, -1))
                    first_slot = 0 if d == 0 else NCH
                    k.op("dve", lambda e, d=d, first_slot=first_slot: e.memset(Sinb[d][:, first_slot, :], 0.0), writes=[("Sinb", d, first_slot)])
                    for c in order:
                        wr = wi % 2
                        wi += 1
                        dwb = dw[:, 2 * c:2 * c + 2, d * 8 + 4 * g:d * 8 + 4 * g + 4].unsqueeze(3).to_broadcast([128, 2, 4, 64])
                        k.op("dve", lambda e, wr=wr, dwb=dwb, c=c: e.tensor_tensor(out=xdw[wr].rearrange("p t (h d) -> p t h d", h=4),
                                                                                in0=xs4[:, 2 * c:2 * c + 2, :, :], in1=dwb, op=ALU.mult),
                             reads=[("xs", 2 * c), ("xs", 2 * c + 1), "dw"], writes=[("xdw", wr)])
                        pb, pk = PS(c % 2)
                        for ti in range(2):
                            tt = 2 * c + ti
                            k.op("pe", lambda e, pb=pb, tt=tt, ti=ti, wr=wr: e.matmul(
                                pb[:, 0:256], lhsT=Btok[:, tt, :], rhs=xdw[wr][:, ti, :], start=(ti == 0), stop=(ti == 1)),
                                reads=[("Btok", tt), ("xdw", wr)], writes=[pk])
                        etb = etot[:, c, d * 8 + 4 * g:d * 8 + 4 * g + 4].unsqueeze(2).to_broadcast([128, 4, 64])
                        k.op("dve", lambda e, d=d, etb=etb: e.tensor_tensor(out=v3(Rtmp), in0=v3(Rst[d]), in1=etb, op=ALU.mult),
                             reads=[("Rst", d), "etot"], writes=["Rtmp"])
                        k.op("dve", lambda e, pb=pb, d=d: e.tensor_tensor(out=Rst[d], in0=pb[:, 0:256], in1=Rtmp, op=ALU.add),
                             reads=["Rtmp"], writes=[pk, ("Rst", d)])
                        slot = c + 1 if d == 0 else c
                        k.op("act", lambda e, d=d, slot=slot: e.copy(Sinb[d][:, slot, :], Rst[d]), reads=[("Rst", d)], writes=[("Sinb", d, slot)])
                def prologue(c):
                    t0_, t1_ = 2 * c, 2 * c + 1
                    cr = c % 2
                    pb2, pk2 = PS(7)
                    for d, (r0, r1) in ((0, (C_R0F, C_R1F)), (1, (C_R0B, C_R1B))):
                        cs_ = slice(d * 8, d * 8 + 8)
                        o_ = pb2[0:8, d * 256:(d + 1) * 256]
                        k.op("pe", lambda e: e.matmul(o_, lhsT=av[:, t0_, cs_], rhs=cst[:, r0:r0 + 256],
                                                      start=(d == 0), stop=False, skip_group_check=True),
                             reads=["av", "cst"], writes=[pk2])
                        k.op("pe", lambda e: e.matmul(o_, lhsT=av[:, t1_, cs_], rhs=cst[:, r1:r1 + 256],
                                                      start=False, stop=True, skip_group_check=True),
                             reads=["av", "cst"], writes=[pk2])
                    copy_op("act", cumT[cr][0:8, :, :], pb2[0:8, 0:512].rearrange("p (d l) -> p d l", d=2), reads=[], writes=[pk2, ("cumT", cr)])
                    for d in range(2):
                        dtb = dt[:, 2 * c:2 * c + 2, d * 8 + 4 * g:d * 8 + 4 * g + 4].unsqueeze(3).to_broadcast([128, 2, 4, 64])
                        k.op("dve", lambda e: e.tensor_tensor(out=xdt[d][:, cr].rearrange("p t (h d) -> p t h d", h=4),
                                                              in0=xs4[:, 2 * c:2 * c + 2, :, :], in1=dtb, op=ALU.mult),
                             reads=[("xs", 2 * c), ("xs", 2 * c + 1), "dt"], writes=[("xdt", d, cr)])
                    pb, pk = PS(7)
                    for si in range(2):
                        k.op("pe", lambda e: e.matmul(pb[:, si * 256:(si + 1) * 256], lhsT=BT[:, g, (2 * c + si) * 128:(2 * c + si + 1) * 128],
                                                      rhs=CT[:, g, c * 256:(c + 1) * 256], start=(si == 0), stop=True, skip_group_check=True),
                             reads=[("BT", g, (2 * c + si) // 4), ("CT", g, c // 2)], writes=[pk])
                    copy_op("act", GT[cr], pb[:, 0:512], reads=[], writes=[pk, ("GT", cr)])

                def item_front(c, it):
                    h, d = it // 2, it % 2
                    hh = 4 * g + h
                    cr = c % 2
                    sel = cst[0:8, C_SEL + hh * 128:C_SEL + (hh + 1) * 128]
                    for si in range(2):
                        pb, pk = PS(2 * (it % 2) + si)
                        if d == 0:
                            lo, hi = (0, 256) if si == 0 else (128, 256)
                            mask = cstb[:, 640:896] if si == 0 else cstb[:, 640:768]
                        else:
                            lo, hi = (0, 128) if si == 0 else (0, 256)
                            mask = cstb[:, 896 + 128:896 + 256] if si == 0 else cstb[:, 896:896 + 256]
                        w = hi - lo
                        k.op("pe", lambda e: e.matmul(pb[:, 0:w], lhsT=sel, rhs=cumT[cr][0:8, d, lo:hi], start=True, stop=False),
                             reads=[("cumT", cr), "cst"], writes=[pk])
                        k.op("pe", lambda e: e.matmul(pb[:, 0:w], lhsT=identb, rhs=mask, start=False, stop=True),
                             reads=["cstb"], writes=[pk])

                def item_back(c, it, started, ybank, ykey):
                    h, d = it // 2, it % 2
                    hh = 4 * g + h
                    cr = c % 2
                    mms = []
                    for si in range(2):
                        pb, pk = PS(2 * (it % 2) + si)
                        if d == 0:
                            lo, hi = (0, 256) if si == 0 else (128, 256)
                        else:
                            lo, hi = (0, 128) if si == 0 else (0, 256)
                        w = hi - lo
                        ebi = 2 * d + si
                        tt_s = 2 * c + si
                        if d == 0:
                            k.op("act", lambda e: e.activation(out=Eb[ebi][:, 0:w], in_=pb[:, 0:w], func=AF.Exp, bias=ncf[:, tt_s, hh:hh + 1], scale=1.0),
                                 reads=["ncf"], writes=[pk, ("Eb", ebi)])
                        else:
                            k.op("act", lambda e: e.activation(out=Eb[ebi][:, 0:w], in_=pb[:, 0:w], func=AF.Exp, bias=cum[:, tt_s, 8 + hh:9 + hh], scale=-1.0),
                                 reads=["cum"], writes=[pk, ("Eb", ebi)])
                        k.op("dve", lambda e: e.tensor_tensor(out=Mb[ebi][:, 0:w], in0=Eb[ebi][:, 0:w], in1=GT[cr][:, si * 256 + lo:si * 256 + hi], op=ALU.mult),
                             reads=[("Eb", ebi), ("GT", cr)], writes=[("Mb", ebi)])
                        for li in range(2):
                            if lo <= li * 128 < hi:
                                mms.append((ebi, li * 128 - lo, li, si))
                    for (ebi, off, li, si) in mms:
                        st_ = not started[li]
                        started[li] = True
                        k.op("pe", lambda e: e.matmul(
                            ybank[li][:, h * 64:(h + 1) * 64], lhsT=Mb[ebi][:, off:off + 128], rhs=xdt[d][:, cr, si, h * 64:(h + 1) * 64],
                            start=st_, stop=False, skip_group_check=True),
                            reads=[("Mb", ebi), ("xdt", d, cr)], writes=[ykey[li]])

                def epilogue(c, ybank, ykey):
                    for li in range(2):
                        tt = 2 * c + li
                        pf7, pfk = PS(6 if False else 7)
                        pf = pf7[:, 0:256]
                        pbw = pf7[:, 256:512]
                        k.op("pe", lambda e: e.matmul(pf, lhsT=CT[:, g, tt * 128:(tt + 1) * 128], rhs=Sinb[0][:, c, :],
                                                      start=True, stop=True), reads=[("CT", g, tt // 4), ("Sinb", 0, c)], writes=[pfk])
                        k.op("pe", lambda e: e.matmul(pbw, lhsT=CT[:, g, tt * 128:(tt + 1) * 128], rhs=Sinb[1][:, c + 1, :],
                                                      start=False, stop=True, skip_group_check=True),
                             reads=[("CT", g, tt // 4), ("Sinb", 1, c + 1)], writes=[pfk])
                        ecfb = sc1[:, tt, 4 * g:4 * g + 4].unsqueeze(2).to_broadcast([128, 4, 64])
                        erbb = sc1[:, tt, 8 + 4 * g:12 + 4 * g].unsqueeze(2).to_broadcast([128, 4, 64])
                        dbb = Dbc[:, 4 * g:4 * g + 4].unsqueeze(2).to_broadcast([128, 4, 64])
                        yi = li
                        k.op("dve", lambda e: e.tensor_tensor(out=v3(ya[yi]), in0=v3(pf), in1=ecfb, op=ALU.mult),
                             reads=["sc1a"], writes=[pfk, ("ya", yi)])
                        k.op("dve", lambda e: e.tensor_tensor(out=v3(Rtmp), in0=v3(pbw), in1=erbb, op=ALU.mult),
                             reads=["sc1b"], writes=[pfk, "Rtmp"])
                        k.op("dve", lambda e: e.tensor_tensor(out=ya[yi], in0=ya[yi], in1=Rtmp, op=ALU.add), reads=["Rtmp"], writes=[("ya", yi)])
                        k.op("dve", lambda e: e.tensor_tensor(out=v3(Rtmp), in0=xs4[:, tt, :, :], in1=dbb, op=ALU.mult),
                             reads=[("xs", tt), "vbc"], writes=["Rtmp"])
                        k.op("dve", lambda e: e.tensor_tensor(out=ya[yi], in0=ya[yi], in1=Rtmp, op=ALU.add), reads=["Rtmp"], writes=[("ya", yi)])
                        k.op("dve", lambda e: e.tensor_tensor(out=ya[yi], in0=ybank[li][:, 0:256], in1=ya[yi], op=ALU.add),
                             reads=[], writes=[ykey[li], ("ya", yi)])
                        zi = li
                        k.dma(zb[zi], ptok_d[tt * 128:(tt + 1) * 128, g * 256:(g + 1) * 256], reads=[("ptok_d",)], writes=[("zb", zi)])
                        k.op("act", lambda e: e.activation(out=szb[zi], in_=zb[zi], func=AF.Silu), reads=[("zb", zi)], writes=[("szb", zi)])
                        k.op("dve", lambda e: e.tensor_tensor(out=yg[yi], in0=ya[yi], in1=szb[zi], op=ALU.mult),
                             reads=[("szb", zi), ("ya", yi)], writes=[("yg", yi)])
                        out_norm_store(yg[yi], [("yg", yi)], 256, vrow(vbc, "ssd_norm", g * 256, 256), g * 256, tmp, tt, tt, "do")

                prologue(0)
                for c in range(NCH):
                    yb0, yk0 = PS(4)
                    yb1, yk1 = PS(5)
                    ybank = (yb0, yb1)
                    ykey = (yk0, yk1)
                    started = [False, False]
                    item_front(c, 0)
                    for it in range(8):
                        if it + 1 < 8:
                            item_front(c, it + 1)
                        item_back(c, it, started, ybank, ykey)
                    if c + 1 < NCH:
                        prologue(c + 1)
                    epilogue(c, ybank, ykey)
            A.reset(fbase, bbase)

        def mixer_phase(l):
            A.reset()
            vbc = A.f32(NV)
            k.dma(vbc, vec_d[l:l + 1, :].partition_broadcast(128), writes=["vbc"])
            ssd_phase(l, vbc)
            k.barrier(barsc[:, 1:2])
            swa_phase(l, vbc)
            k.barrier(barsc[:, 2:3])
            mla_phase(l, vbc)
            if l == 0:
                k.barrier(barsc[:, 3:4])
                dbg_dump("ptok", ptok_d, [("ptok_d",)], big=True)
                dbg_dump("xbc", xbc_d, [("xbc_d",)], BF16, big=True)
                dbg_dump("y", y_d, [("y_d",)], BF16, big=True)

        finals = []
        for p in range(L + 1):
            l_prev = p - 1 if p > 0 else None
            l_next = p if p < L else None
            if stop_after is not None and p > stop_after[0]:
                break
            finals = block_phase(p, l_prev, l_next)
            k.barrier(barsc[:, 0:1])
            if l_next is not None:
                if stop_after is not None and stop_after == (p, "block"):
                    break
                mixer_phase(l_next)
                k.barrier(barsc[:, 0:1])
                if stop_after is not None and stop_after == (p, "mix"):
                    break
        k.emit(list(finals) + dbg_outs)
        print("ops recorded:", k.nops, "arena hi (KiB):", A.hi * 4 / 1024, A.hib * 2 / 1024)
    return nc


def prep_common(inp, S):
    L = inp["ffn1_norm"].shape[0]
    vec = np.zeros((L, NV), np.float32)
    for n, (o, s) in VOFF.items():
        vec[:, o:o + s] = np.asarray(inp[n], np.float32).reshape(L, s)
    cw = np.asarray(inp["ssd_conv_w"], np.float32)
    cb = np.asarray(inp["ssd_conv_b"], np.float32)
    convp = np.zeros((L, 128, 8, 6), np.float32)
    for c in range(8):
        convp[:, :, c, 0:5] = cw[:, :, c * 128:(c + 1) * 128].transpose(0, 2, 1)
        convp[:, :, c, 5] = cb[:, c * 128:(c + 1) * 128]
    com = {"vecs": vec, "convp": np.ascontiguousarray(convp.reshape(L, 128, 48)), "cst": make_consts(),
           "cs_swa": rope_tables(S, 64), "cs_mla": rope_tables(S, 32)}
    for n in ("ffn1_gate", "ffn1_up", "ffn1_down", "ffn2_gate", "ffn2_up", "ffn2_down", "w_in", "w_out",
              "mla_w_uq", "mla_w_ukv"):
        com[n] = np.ascontiguousarray(np.asarray(inp[n], np.float32))
    return com, L


_NC_CACHE = {}


def kernel(**inputs):
    x = np.asarray(inputs["x"], np.float32)
    B, S, _ = x.shape
    com, L = prep_common(inputs, S)
    key = (S, L)
    if key not in _NC_CACHE:
        _NC_CACHE[key] = build(S, L)
    nc = _NC_CACHE[key]
    ncores = 8
    in_maps = []
    for c in range(ncores):
        m = dict(com)
        m["x"] = np.ascontiguousarray(x[c % B])
        in_maps.append(m)
    res = run_bass_kernel_spmd(nc, in_maps, core_ids=list(range(ncores)))
    out = np.stack([np.asarray(res.results[b]["out"], np.float32) for b in range(B)], axis=0)
    return out
```

```python
import contextlib
import numpy as np
import concourse.bass as bass
import concourse.mybir as mybir
from concourse.bass_utils import run_bass_kernel_spmd

F32 = mybir.dt.float32
BF16 = mybir.dt.bfloat16
AF = mybir.ActivationFunctionType
ALU = mybir.AluOpType
AX = mybir.AxisListType

D = 1024
DFF = 2816
NPROJ = 2480
NTM = 1456
EPS = 1e-6
BIG = 1.0e4
GF = 4
DEBUG = False
ALT_DVE_ONLY = False
N_DMA_SEMS = 40

VOFF = {}
_o = 0
for _n, _s in (("ffn1_norm", 1024), ("mix_norm", 1024), ("ffn2_norm", 1024), ("ssd_norm", 512),
               ("swa_q_norm", 64), ("swa_k_norm", 64), ("swa_out_norm", 256), ("mla_q_lat_norm", 256),
               ("mla_kv_norm", 128), ("mla_q_norm", 96), ("mla_k_norm", 96), ("mla_out_norm", 256),
               ("ssd_dt_bias", 16), ("ssd_a_log", 16), ("ssd_d", 8), ("swa_sink", 4)):
    VOFF[_n] = (_o, _s)
    _o += _s
NV = _o

C_ID = 0
C_R0F = 128
C_R1F = 384
C_R0B = 640
C_R1B = 896
C_NEGF = 1152
C_POSB = 1408
C_GE = 1664
C_SEL = 1792
NCST = C_SEL + 1024


def make_consts():
    c = np.zeros((128, NCST), np.float32)
    i = np.arange(128)
    le = (i[:, None] <= i[None, :]).astype(np.float32)
    lt = (i[:, None] < i[None, :]).astype(np.float32)
    ge = (i[:, None] >= i[None, :]).astype(np.float32)
    gt = (i[:, None] > i[None, :]).astype(np.float32)
    c[:, C_ID:C_ID + 128] = np.eye(128)
    c[:, C_R0F:C_R0F + 128] = le
    c[:, C_R0F + 128:C_R0F + 256] = 1.0
    c[:, C_R1F + 128:C_R1F + 256] = le
    c[:, C_R0B:C_R0B + 128] = lt
    c[:, C_R0B + 128:C_R0B + 256] = 1.0
    c[:, C_R1B + 128:C_R1B + 256] = lt
    c[:, C_NEGF:C_NEGF + 128] = -BIG * gt
    c[:, C_POSB + 128:C_POSB + 256] = BIG * lt
    c[:, C_GE:C_GE + 128] = ge
    for h in range(8):
        c[h, C_SEL + h * 128:C_SEL + (h + 1) * 128] = 1.0
    return c


def rope_tables(n, dim):
    inv = 1.0 / np.power(np.float32(10000.0), np.arange(0, dim, 2, dtype=np.float32) / np.float32(dim))
    ang = np.arange(n, dtype=np.float32)[:, None] * inv[None, :].astype(np.float32)
    return np.concatenate([np.cos(ang), np.sin(ang)], axis=1).astype(np.float32)


def _freeze(fn):
    import types
    if fn.__closure__ is None:
        return fn
    cells = []
    for c in fn.__closure__:
        try:
            cells.append(types.CellType(c.cell_contents))
        except ValueError:
            cells.append(c)
    return types.FunctionType(fn.__code__, fn.__globals__, fn.__name__, fn.__defaults__, tuple(cells))


class Op:
    __slots__ = ("eng", "fn", "waits", "signal", "sem", "val", "is_dma")

    def __init__(self, eng, fn, is_dma=False):
        self.eng = eng
        self.fn = fn
        self.waits = []
        self.signal = False
        self.sem = None
        self.val = None
        self.is_dma = is_dma


class KB:
    ENGS = ("pe", "act", "dve", "pool", "sp")

    def __init__(self, nc):
        self.nc = nc
        self.prog = {e: [] for e in self.ENGS}
        self.last_w = {}
        self.readers = {}
        self.dma_rr = 0
        self.dma_last = [None] * N_DMA_SEMS
        self.dma_uses = [0] * N_DMA_SEMS
        self.bar = None
        self.nops = 0
        self.carrier = None

    def _deps(self, op, reads, writes):
        deps = []
        if self.bar is not None:
            deps.append(self.bar)
        for k in reads:
            w = self.last_w.get(k)
            if w is not None:
                deps.append(w)
        for k in writes:
            w = self.last_w.get(k)
            if w is not None:
                deps.append(w)
            deps.extend(self.readers.get(k, ()))
        for k in reads:
            self.readers.setdefault(k, []).append(op)
        for k in writes:
            self.last_w[k] = op
            self.readers[k] = []
        seen = set()
        for d in deps:
            if d is op or id(d) in seen:
                continue
            seen.add(id(d))
            if op.eng == "pe" and d.eng == "pe" and not d.is_dma and not op.is_dma:
                continue
            op.waits.append(d)

    def op(self, eng, fn, reads=(), writes=()):
        o = Op(eng, _freeze(fn))
        self._deps(o, reads, writes)
        self.prog[eng].append(o)
        self.nops += 1
        return o

    def dma(self, out, in_, reads=(), writes=(), q="sp", **kw):
        o = Op(q, None, is_dma=True)
        o.fn = lambda e, out=out, in_=in_, kw=kw: e.dma_start(out=out, in_=in_, **kw)
        self._deps(o, reads, writes)
        i = self.dma_rr
        self.dma_rr = (self.dma_rr + 1) % N_DMA_SEMS
        prev = self.dma_last[i]
        if prev is not None:
            o.waits.append(prev)
        self.dma_last[i] = o
        self.dma_uses[i] += 1
        o.sem = i
        o.val = 16 * self.dma_uses[i]
        o.signal = True
        self.prog[q].append(o)
        self.nops += 1
        return o

    def barrier(self, scratch_ap):
        o = Op("dve", lambda e: e.memset(scratch_ap, 0.0))
        if self.bar is not None:
            o.waits.append(self.bar)
        for e in self.ENGS:
            for p in reversed(self.prog[e]):
                if not p.is_dma:
                    o.waits.append(p)
                    break
        for d in self.dma_last:
            if d is not None:
                o.waits.append(d)
        self.prog["dve"].append(o)
        self.bar = o
        self.last_w = {}
        self.readers = {}
        return o

    def emit(self, final_wait_ops=()):
        nc = self.nc
        for e in self.ENGS:
            for o in self.prog[e]:
                for w in o.waits:
                    w.signal = True
        for o in final_wait_ops:
            o.signal = True
        for e in self.ENGS:
            c = 0
            for o in self.prog[e]:
                if o.is_dma:
                    continue
                if o.signal:
                    c += 1
                    o.sem = e
                    o.val = c
        with contextlib.ExitStack() as st:
            esem = {e: st.enter_context(nc.semaphore("s_" + e)) for e in self.ENGS}
            dsem = [st.enter_context(nc.semaphore("d_%d" % i)) for i in range(N_DMA_SEMS)]
            block = st.enter_context(nc.Block())

            def sem_of(o):
                return dsem[o.sem] if o.is_dma else esem[o.sem]

            def run(e, h):
                waited = {}
                for o in self.prog[e]:
                    need = {}
                    for w in o.waits:
                        key = ("d", w.sem) if w.is_dma else ("e", w.sem)
                        if waited.get(key, 0) >= w.val:
                            continue
                        waited[key] = w.val
                        need[key] = w
                    need = list(need.values())
                    if e == "pe" and not o.is_dma:
                        for w in need[:-1]:
                            h.ldweights(self.carrier)._wait_ge(sem_of(w), w.val)
                        ins = o.fn(h)
                        if need:
                            ins._wait_ge(sem_of(need[-1]), need[-1].val)
                    else:
                        for w in need:
                            h.wait_ge(sem_of(w), w.val)
                        ins = o.fn(h)
                    if o.is_dma:
                        ins.then_inc(dsem[o.sem], 16)
                    elif o.signal:
                        ins.then_inc(esem[e], 1)
                if e == "sp":
                    for o in final_wait_ops:
                        h.wait_ge(sem_of(o), o.val)

            @block.tensor
            def _(t):
                run("pe", t)

            @block.scalar
            def _(s):
                run("act", s)

            @block.vector
            def _(v):
                run("dve", v)

            @block.gpsimd
            def _(g):
                run("pool", g)

            @block.sync
            def _(s):
                run("sp", s)


class Arena:
    def __init__(self, apf, nf, apb, nb):
        self.apf, self.nf, self.apb, self.nb = apf, nf, apb, nb
        self.p = 0
        self.pb = 0
        self.hi = 0
        self.hib = 0

    def reset(self, to=0, tob=0):
        self.p = to
        self.pb = tob

    def f32(self, cols):
        a = self.apf[:, self.p:self.p + cols]
        self.p += (cols + 7) // 8 * 8
        self.hi = max(self.hi, self.p)
        assert self.p <= self.nf, ("f32 arena overflow", self.p, self.nf)
        return a

    def bf16(self, cols):
        a = self.apb[:, self.pb:self.pb + cols]
        self.pb += (cols + 15) // 16 * 16
        self.hib = max(self.hib, self.pb)
        assert self.pb <= self.nb, ("bf16 arena overflow", self.pb, self.nb)
        return a


def build(S, L, stop_after=None):
    assert S % 512 == 0
    NT = S // 128
    TB = min(1024, S)
    NBLK = S // TB
    NTB = TB // 128
    NCH = S // 256

    nc = bass.Bass("TRN2", target_bir_lowering=False)
    dr = lambda name, shape, dt=F32, kind="ExternalInput": nc.dram_tensor(name, list(shape), dt, kind=kind).ap()
    x_d = dr("x", [S, D])
    wg_d = [dr("ffn1_gate", [L, D, DFF]), dr("ffn2_gate", [L, D, DFF])]
    wu_d = [dr("ffn1_up", [L, D, DFF]), dr("ffn2_up", [L, D, DFF])]
    wd_d = [dr("ffn1_down", [L, DFF, D]), dr("ffn2_down", [L, DFF, D])]
    win_d = dr("w_in", [L, D, NPROJ])
    wout_d = dr("w_out", [L, D, D])
    wuq_d = dr("mla_w_uq", [L, 256, 384])
    wukv_d = dr("mla_w_ukv", [L, 128, 512])
    vec_d = dr("vecs", [L, NV])
    convp_d = dr("convp", [L, 128, 48])
    cst_d = dr("cst", [128, NCST])
    cs_swa_d = dr("cs_swa", [S, 64])
    cs_mla_d = dr("cs_mla", [S, 32])
    out_d = dr("out", [S, D], kind="ExternalOutput")
    xbc_d = dr("xbc_scr", [1024, S], BF16, kind="Internal")
    ptok_d = dr("ptok_scr", [S, NTM], F32, kind="Internal")
    y_d = dr("y_scr", [S, D], BF16, kind="Internal")

    ARENA_F = 17 * 1024
    ARENA_B = 56 * 1024
    st = contextlib.ExitStack()
    with st:
        arena_f = st.enter_context(nc.sbuf_tensor("arena_f", [128, ARENA_F], F32))
        arena_b = st.enter_context(nc.sbuf_tensor("arena_b", [128, ARENA_B], BF16))
        cst = st.enter_context(nc.sbuf_tensor("cst_sb", [128, NCST], F32))
        cstb = st.enter_context(nc.sbuf_tensor("cstb_sb", [128, 128 * 5 + 512], BF16))
        barsc = st.enter_context(nc.sbuf_tensor("barsc", [128, 8], F32))
        banks = [st.enter_context(nc.psum_tensor("bank%d" % i, [128, 1024] if i == 6 else [128, 512], BF16 if i == 6 else F32))
                 for i in range(8)]
        k = KB(nc)
        A = Arena(arena_f[:], ARENA_F, arena_b[:], ARENA_B)

        k.carrier = cstb[:, 0:1]
        identf = cst[:, C_ID:C_ID + 128]
        identb = cstb[:, 0:128]
        ones_f = cst[:, C_R0F + 128:C_R0F + 256]

        k.dma(cst[:], cst_d, writes=["cst"])
        k.dma(cstb[:, 0:128], cst_d[:, C_ID:C_ID + 128], writes=["cstb"], q="pool")
        for j in range(2):
            k.dma(cstb[:, 128 + j * 128:256 + j * 128], cst_d[:, C_GE:C_GE + 128], writes=["cstb"], q="pool")
            k.dma(cstb[:, 384 + j * 128:512 + j * 128], cst_d[:, C_R0F:C_R0F + 128], writes=["cstb"], q="pool")

        k.dma(cstb[:, 640:896], cst_d[:, C_NEGF:C_NEGF + 256], writes=["cstb"], q="pool")
        k.dma(cstb[:, 896:1152], cst_d[:, C_POSB:C_POSB + 256], writes=["cstb"], q="pool")

        def PS(i):
            return banks[i], "bank%d" % i

        rr = {"n": 0}
        dbg_outs = []
        dbg_names = set()

        def dbg_dump(name, ap, reads, dt=F32, big=False):
            if not DEBUG or ("dbg_" + name) in dbg_names:
                return
            dbg_names.add("dbg_" + name)
            t = nc.dram_tensor("dbg_" + name, list(ap.shape), dt, kind="ExternalOutput").ap()
            if big:
                for r0 in range(0, ap.shape[0], 256):
                    dbg_outs.append(k.dma(t[r0:r0 + 256], ap[r0:r0 + 256], reads=reads))
            else:
                dbg_outs.append(k.dma(t, ap, reads=reads))

        def alt():
            rr["n"] += 1
            return "dve" if (rr["n"] % 2 or ALT_DVE_ONLY) else "act"

        def copy_op(eng, out, in_, reads, writes):
            if eng == "act":
                return k.op("act", lambda e: e.copy(out, in_), reads, writes)
            return k.op(eng, lambda e: e.tensor_copy(out, in_), reads, writes)

        def vrow(vbc, name, lo=0, n=None):
            o, s = VOFF[name]
            n = s - lo if n is None else n
            return vbc[:, o + lo:o + lo + n]

        def block_phase(pidx, l_prev, l_next):
            A.reset()
            vbcA = A.f32(3072) if l_prev is not None else None
            vbcB = A.f32(3072) if l_next is not None else None
            xb = A.f32(NTB * D).rearrange("p (t d) -> p t d", t=NTB)
            hT = A.bf16(8 * TB).rearrange("p (c t) -> p c t", c=8)
            hb = [A.bf16(D) for _ in range(2)]
            junk = A.f32(D)
            ss = A.f32(NTB)
            sq = A.f32(NTB)
            rstd = A.f32(NTB)
            wgu = [A.bf16(2 * 8 * GF * 128).rearrange("p (w c f) -> p w c f", w=2, c=8) for _ in range(2)]
            wdn = [A.bf16(GF * D).rearrange("p (c d) -> p c d", c=GF) for _ in range(2)]
            sg = [A.bf16(512) for _ in range(2)]
            aT = [A.bf16(GF * 512).rearrange("p (c t) -> p c t", c=GF) for _ in range(2)]
            wpj = [A.bf16(8 * 512).rearrange("p (c f) -> p c f", c=8) for _ in range(2)]
            evf = [A.bf16(512) for _ in range(2)]
            evt = [A.f32(512) for _ in range(2)]
            cnt = {"w": 0, "p": 0, "a": 0, "d": 0, "t": 0, "e": 0, "g": 0}
            if l_prev is not None:
                k.dma(vbcA, vec_d[l_prev:l_prev + 1, 0:3072].partition_broadcast(128), writes=["vbcA"])
            if l_next is not None:
                k.dma(vbcB, vec_d[l_next:l_next + 1, 0:3072].partition_broadcast(128), writes=["vbcB"])

            def norm_to_hT(vbc, vkey, nname, tag):
                k.op("dve", lambda e: e.memset(ss, 0.0), writes=["ss"])
                for tt in range(NTB):
                    k.op("act", lambda e, tt=tt: e.activation(out=junk, in_=xb[:, tt, :], func=AF.Square,
                                                              accum_out=ss[:, tt:tt + 1]),
                         reads=[("xb", tt)], writes=["junk", "ss"])
                k.op("act", lambda e: e.activation(out=sq, in_=ss, func=AF.Sqrt, scale=1.0 / D, bias=EPS),
                     reads=["ss"], writes=["sq"])
                k.op("dve", lambda e: e.reciprocal(out=rstd, in_=sq), reads=["sq"], writes=["rstd"])
                wv = vrow(vbc, nname)
                if tag == "f" and not cnt.get("dbg1"):
                    cnt["dbg1"] = 1
                    dbg_dump("ss", ss, ["ss"]); dbg_dump("rstd", rstd, ["rstd"]); dbg_dump("wv", wv, [vkey])
                    dbg_dump("xb0", xb[:, 0, :], [("xb", 0)])
                for tt in range(NTB):
                    hbi = tt % 2
                    k.op("dve", lambda e, tt=tt, hbi=hbi: e.scalar_tensor_tensor(
                        out=hb[hbi], in0=xb[:, tt, :], scalar=rstd[:, tt:tt + 1], in1=wv,
                        op0=ALU.mult, op1=ALU.mult), reads=[("xb", tt), "rstd", vkey], writes=[("hb", hbi)])
                    transpose_rows(hb[hbi], ("hb", hbi), 8, hT, tt, "hT")

            def transpose_rows(src, skey, nchunks, dstT, tt, dkey):
                for h0 in range(0, nchunks, 4):
                    nn = min(4, nchunks - h0)
                    pb, pk = PS(6)
                    pv = pb[:, (cnt["t"] % 2) * 512:(cnt["t"] % 2) * 512 + 512].rearrange("p (c t) -> p c t", c=4)
                    cnt["t"] += 1
                    for j in range(nn):
                        k.op("pe", lambda e, j=j, h0=h0: e.transpose(pv[:, j, :], src[:, (h0 + j) * 128:(h0 + j + 1) * 128], identb),
                             reads=[skey, "cstb"], writes=[pk])
                    copy_op(alt(), dstT[:, h0:h0 + nn, tt * 128:(tt + 1) * 128], pv[:, 0:nn, :],
                            reads=[], writes=[pk, (dkey, tt, h0)])

            def dump_hT(nm):
                dbg_dump(nm, hT[:, :, 0:128], [("hT", 0, 0), ("hT", 0, 4)], BF16)

            def hT_keys(t4, key="hT"):
                return [(key, tt, h0) for tt in range(t4 * 4, t4 * 4 + 4) for h0 in (0, 4)]

            def ffn(l, which, vbc, vkey):
                norm_to_hT(vbc, vkey, "ffn1_norm" if which == 0 else "ffn2_norm", "f")
                if not cnt.get("dbg2"):
                    cnt["dbg2"] = 1
                    dump_hT("hT")
                wg_l = wg_d[which][l].rearrange("(c p) f -> p c f", p=128)
                wu_l = wu_d[which][l].rearrange("(c p) f -> p c f", p=128)
                wd_l = wd_d[which][l]
                groups = []
                f0 = 0
                while f0 < DFF:
                    nf = min(GF, (DFF - f0) // 128)
                    groups.append((f0, nf))
                    f0 += nf * 128
                NG = len(groups)
                NT4 = TB // 512
                units = [(g, t4) for g in range(NG) for t4 in range(NT4)]

                def load_w(g):
                    if g >= NG:
                        return
                    wi = g % 2
                    f0, nf = groups[g]
                    k.dma(wgu[wi][:, 0, :, 0:nf * 128], wg_l[:, :, f0:f0 + nf * 128], writes=[("wgu", wi, 0)], q="pool")
                    k.dma(wgu[wi][:, 1, :, 0:nf * 128], wu_l[:, :, f0:f0 + nf * 128], writes=[("wgu", wi, 1)], q="pool")
                    k.dma(wdn[wi][:, 0:nf, :], wd_l[f0:f0 + nf * 128, :].rearrange("(c p) d -> p c d", p=128), writes=[("wdn", wi)], q="pool")

                ais = {}

                def front_pieces(u):
                    g, t4 = units[u]
                    wi = g % 2
                    nf = groups[g][1]
                    ai = cnt["a"] % 2
                    cnt["a"] += 1
                    ais[u] = ai
                    hk = hT_keys(t4)
                    pieces = []
                    for fc in range(nf):
                        def piece(fc=fc):
                            gi = cnt["g"] % 2
                            cnt["g"] += 1
                            pg, pgk = PS(gi)
                            pu, puk = PS(2 + gi)
                            for w, (pp, ppk) in enumerate(((pg, pgk), (pu, puk))):
                                for dc in range(8):
                                    k.op("pe", lambda e: e.matmul(
                                        pp[:], lhsT=wgu[wi][:, w, dc, fc * 128:(fc + 1) * 128],
                                        rhs=hT[:, dc, t4 * 512:(t4 + 1) * 512], start=(dc == 0), stop=(dc == 7)),
                                        reads=[("wgu", wi, w)] + hk, writes=[ppk])
                            k.op("act", lambda e: e.activation(out=sg[gi], in_=pg[:], func=AF.Silu),
                                 reads=[], writes=[pgk, ("sg", gi)])
                            k.op("dve", lambda e: e.tensor_tensor(
                                out=aT[ai][:, fc, :], in0=pu[:], in1=sg[gi], op=ALU.mult),
                                reads=[("sg", gi)], writes=[puk, ("aT", ai, fc)])
                        pieces.append(piece)
                    return pieces

                def back_pieces(u):
                    g, t4 = units[u]
                    wi = g % 2
                    nf = groups[g][1]
                    ai = ais[u]
                    pieces = []
                    for ts in range(4):
                        def piece(ts=ts):
                            tt = t4 * 4 + ts
                            for dh in range(2):
                                di = (4, 5, 7)[cnt["d"] % 3]
                                cnt["d"] += 1
                                pd, pdk = PS(di)
                                for fc in range(nf):
                                    k.op("pe", lambda e: e.matmul(
                                        pd[:], lhsT=aT[ai][:, fc, ts * 128:(ts + 1) * 128],
                                        rhs=wdn[wi][:, fc, dh * 512:(dh + 1) * 512], start=(fc == 0), stop=(fc == nf - 1)),
                                        reads=[("aT", ai, fc), ("wdn", wi)], writes=[pdk])
                                k.op("dve", lambda e: e.scalar_tensor_tensor(
                                    out=xb[:, tt, dh * 512:(dh + 1) * 512], in0=pd[:], scalar=0.5,
                                    in1=xb[:, tt, dh * 512:(dh + 1) * 512], op0=ALU.mult, op1=ALU.add),
                                    reads=[], writes=[pdk, ("xb", tt)])
                        pieces.append(piece)
                    return pieces

                load_w(0)
                load_w(1)
                for p_ in front_pieces(0):
                    p_()
                for u in range(len(units)):
                    fp = front_pieces(u + 1) if u + 1 < len(units) else []
                    bp = back_pieces(u)
                    n = max(len(fp), len(bp))
                    for i in range(n):
                        if i < len(fp):
                            fp[i]()
                        if i < len(bp):
                            bp[i]()
                    if units[u][1] == NT4 - 1:
                        load_w(units[u][0] + 2)

            def inproj(l, blk, vbc, vkey, after_norm=None):
                norm_to_hT(vbc, vkey, "mix_norm", "m")
                if after_norm is not None:
                    after_norm()
                win_l = win_d[l].rearrange("(c p) f -> p c f", p=128)
                t0 = blk * TB
                jobs = [("f", 512 + cg * 128, 128, cg) for cg in range(8)] + \
                       [("t", c0, ncol, j0) for (c0, ncol, j0) in ((0, 512, 0), (1536, 512, 512), (2048, 432, 1024))]

                def load_p(i):
                    if i >= len(jobs):
                        return
                    kind, c0, ncol, aux = jobs[i]
                    wi = i % 2
                    k.dma(wpj[wi][:, :, 0:ncol], win_l[:, :, c0:c0 + ncol], writes=[("wpj", wi)], q="pool")

                load_p(0)
                for i, (kind, c0, ncol, aux) in enumerate(jobs):
                    load_p(i + 1)
                    wi = i % 2
                    if kind == "f":
                        cg = aux
                        for t4 in range(TB // 512):
                            gi = cnt["g"] % 2
                            cnt["g"] += 1
                            pg, pgk = PS(gi)
                            hk = hT_keys(t4)
                            for dc in range(8):
                                k.op("pe", lambda e: e.matmul(
                                    pg[:], lhsT=wpj[wi][:, dc, 0:128], rhs=hT[:, dc, t4 * 512:(t4 + 1) * 512],
                                    start=(dc == 0), stop=(dc == 7)), reads=[("wpj", wi)] + hk, writes=[pgk])
                            ei = cnt["e"] % 2
                            cnt["e"] += 1
                            copy_op(alt(), evf[ei], pg[:], reads=[], writes=[pgk, ("evf", ei)])
                            k.dma(xbc_d[cg * 128:(cg + 1) * 128, t0 + t4 * 512:t0 + (t4 + 1) * 512], evf[ei],
                                  reads=[("evf", ei)], writes=[("xbc_d", cg)])
                    else:
                        j0 = aux
                        for tt in range(NTB):
                            gi = cnt["g"] % 2
                            cnt["g"] += 1
                            pu, puk = PS(2 + gi)
                            for dc in range(8):
                                k.op("pe", lambda e: e.matmul(
                                    pu[:, 0:ncol], lhsT=hT[:, dc, tt * 128:(tt + 1) * 128], rhs=wpj[wi][:, dc, 0:ncol],
                                    start=(dc == 0), stop=(dc == 7)),
                                    reads=[("wpj", wi), ("hT", tt, 0), ("hT", tt, 4)], writes=[puk])
                            ei = cnt["e"] % 2
                            cnt["e"] += 1
                            copy_op(alt(), evt[ei][:, 0:ncol], pu[:, 0:ncol], reads=[], writes=[puk, ("evt", ei)])
                            k.dma(ptok_d[t0 + tt * 128:t0 + (tt + 1) * 128, j0:j0 + ncol], evt[ei][:, 0:ncol],
                                  reads=[("evt", ei)], writes=[("ptok_d", blk)])

            def outproj(l, blk):
                t0 = blk * TB
                wout_l = wout_d[l].rearrange("(c p) f -> p c f", p=128)
                for tt in range(NTB):
                    hbi = tt % 2
                    k.dma(hb[hbi], y_d[t0 + tt * 128:t0 + (tt + 1) * 128, :], reads=[("y_d",)], writes=[("hb", hbi)])
                    transpose_rows(hb[hbi], ("hb", hbi), 8, hT, tt, "hT")
                for dh in range(2):
                    k.dma(wpj[dh], wout_l[:, :, dh * 512:(dh + 1) * 512], writes=[("wpj", dh)], q="pool")
                for dh in range(2):
                    wi = dh
                    for tt in range(NTB):
                        di = 4 + cnt["d"] % 2
                        cnt["d"] += 1
                        pd, pdk = PS(di)
                        for dc in range(8):
                            k.op("pe", lambda e, pd=pd, dc=dc, wi=wi, tt=tt: e.matmul(
                                pd[:], lhsT=hT[:, dc, tt * 128:(tt + 1) * 128], rhs=wpj[wi][:, dc, :],
                                start=(dc == 0), stop=(dc == 7)),
                                reads=[("wpj", wi), ("hT", tt, 0), ("hT", tt, 4)], writes=[pdk])
                        k.op("dve", lambda e, pd=pd, tt=tt, dh=dh: e.tensor_tensor(
                            out=xb[:, tt, dh * 512:(dh + 1) * 512], in0=pd[:], in1=xb[:, tt, dh * 512:(dh + 1) * 512],
                            op=ALU.add), reads=[], writes=[pdk, ("xb", tt)])

            stores = []
            src = x_d if pidx == 0 else out_d

            def load_x(blk):
                t0 = blk * TB
                for tt in range(NTB):
                    k.dma(xb[:, tt, :], src[t0 + tt * 128:t0 + (tt + 1) * 128, :],
                          reads=[("out_d", blk)], writes=[("xb", tt)])

            def store_x(blk):
                t0 = blk * TB
                for tt in range(NTB):
                    stores.append(k.dma(out_d[t0 + tt * 128:t0 + (tt + 1) * 128, :], xb[:, tt, :],
                                        reads=[("xb", tt)], writes=[("out_d", blk)]))

            load_x(0)
            for blk in range(NBLK):
                if l_prev is not None:
                    outproj(l_prev, blk)
                    ffn(l_prev, 1, vbcA, "vbcA")
                if l_next is not None:
                    ffn(l_next, 0, vbcB, "vbcB")
                    store_x(blk)
                    inproj(l_next, blk, vbcB, "vbcB",
                           after_norm=(lambda blk=blk: load_x(blk + 1)) if blk + 1 < NBLK else None)
                else:
                    store_x(blk)
                    if blk + 1 < NBLK:
                        load_x(blk + 1)
            return stores

        def mk_tmpset(pfx, n, nr):
            return {"pfx": pfx, "sq": A.f32(n), "ss": A.f32(8), "rs": A.f32(8), "t1": A.f32(nr), "t2": A.f32(nr)}

        def run_interleaved(gens):
            gens = list(gens)
            while gens:
                for g_ in list(gens):
                    try:
                        next(g_)
                    except StopIteration:
                        gens.remove(g_)

        def headnorm_rope(src, H, Dh, wrow, rlo, rhalf, cs, dst, tmp, keys_r, keys_w, cskey):
            sqv = tmp["sq"][:, 0:H * Dh].rearrange("p (h d) -> p h d", h=H)
            ssv = tmp["ss"][:, 0:H]
            rs = tmp["rs"][:, 0:H]
            qn = sqv
            t1 = tmp["t1"][:, 0:H * rhalf].rearrange("p (h d) -> p h d", h=H)
            t2 = tmp["t2"][:, 0:H * rhalf].rearrange("p (h d) -> p h d", h=H)
            p_ = tmp["pfx"]
            T = [p_ + "sq", p_ + "ss", p_ + "rs", p_ + "sq", p_ + "t1", p_ + "t2"]
            k.op("dve", lambda e: e.tensor_tensor(out=sqv, in0=src, in1=src, op=ALU.mult), reads=keys_r, writes=[T[0]])
            yield
            k.op("dve", lambda e: e.tensor_reduce(out=ssv, in_=sqv, axis=AX.X, op=ALU.add), reads=[T[0]], writes=[T[1]])
            yield
            k.op("act", lambda e: e.activation(out=rs, in_=ssv, func=AF.Sqrt, scale=1.0 / Dh, bias=EPS), reads=[T[1]], writes=[T[2]])
            yield
            k.op("dve", lambda e: e.reciprocal(out=ssv, in_=rs), reads=[T[2]], writes=[T[1]])
            yield
            k.op("dve", lambda e: e.tensor_tensor(out=qn, in0=src, in1=ssv.unsqueeze(2).to_broadcast([128, H, Dh]), op=ALU.mult),
                 reads=keys_r + [T[1]], writes=[T[3]])
            yield
            k.op("dve", lambda e: e.tensor_tensor(out=qn, in0=qn, in1=wrow.unsqueeze(1).to_broadcast([128, H, Dh]), op=ALU.mult),
                 reads=["vbc"], writes=[T[3]])
            yield
            if rlo > 0:
                k.op("dve", lambda e: e.tensor_copy(dst[:, :, 0:rlo], qn[:, :, 0:rlo]), reads=[T[3]], writes=keys_w)
                yield
            x1 = qn[:, :, rlo:rlo + rhalf]
            x2 = qn[:, :, rlo + rhalf:rlo + 2 * rhalf]
            cb = cs[:, 0:rhalf].unsqueeze(1).to_broadcast([128, H, rhalf])
            sb = cs[:, rhalf:2 * rhalf].unsqueeze(1).to_broadcast([128, H, rhalf])
            k.op("dve", lambda e: e.tensor_tensor(out=t1, in0=x1, in1=cb, op=ALU.mult), reads=[T[3], cskey], writes=[T[4]])
            yield
            k.op("dve", lambda e: e.tensor_tensor(out=t2, in0=x2, in1=sb, op=ALU.mult), reads=[T[3], cskey], writes=[T[5]])
            yield
            k.op("dve", lambda e: e.tensor_tensor(out=dst[:, :, rlo:rlo + rhalf], in0=t1, in1=t2, op=ALU.subtract),
                 reads=[T[4], T[5]], writes=keys_w)
            yield
            k.op("dve", lambda e: e.tensor_tensor(out=t1, in0=x1, in1=sb, op=ALU.mult), reads=[T[3], cskey], writes=[T[4]])
            yield
            k.op("dve", lambda e: e.tensor_tensor(out=t2, in0=x2, in1=cb, op=ALU.mult), reads=[T[3], cskey], writes=[T[5]])
            yield
            k.op("dve", lambda e: e.tensor_tensor(out=dst[:, :, rlo + rhalf:rlo + 2 * rhalf], in0=t1, in1=t2, op=ALU.add),
                 reads=[T[4], T[5]], writes=keys_w)
            yield

        def heads_to_T(src, skey, H, Dh, dstT, tt, dkey, bank):
            pb, pk = PS(6)
            pv = pb[:, 0:512].rearrange("p (c t) -> p c t", c=4)
            for h in range(H):
                k.op("pe", lambda e, h=h: e.transpose(pv[0:Dh, h, :], src[:, h, :], identb), reads=[skey, "cstb"], writes=[pk])
            copy_op(alt(), dstT[0:Dh, 0:H, tt * 128:(tt + 1) * 128], pv[0:Dh, 0:H, :], reads=[], writes=[pk, (dkey, tt)])

        def out_norm_store(osrc, okeys, width, wrow, col0, tmp, tt, ring, tag):
            k.op("dve", lambda e: e.memset(tmp["ss"][:, 0:1], 0.0), writes=["T_ss"])
            k.op("act", lambda e: e.activation(out=tmp["sq"][:, 0:width], in_=osrc, func=AF.Square, accum_out=tmp["ss"][:, 0:1]),
                 reads=okeys, writes=["T_sq", "T_ss"])
            k.op("act", lambda e: e.activation(out=tmp["rs"][:, 0:1], in_=tmp["ss"][:, 0:1], func=AF.Sqrt, scale=1.0 / width, bias=EPS),
                 reads=["T_ss"], writes=["T_rs"])
            k.op("dve", lambda e: e.reciprocal(out=tmp["ss"][:, 1:2], in_=tmp["rs"][:, 0:1]), reads=["T_rs"], writes=["T_ss"])
            yi = ring % 2
            yb = tmp["yb"][yi][:, 0:width]
            k.op("dve", lambda e: e.scalar_tensor_tensor(out=yb, in0=osrc, scalar=tmp["ss"][:, 1:2], in1=wrow,
                                                         op0=ALU.mult, op1=ALU.mult),
                 reads=list(okeys) + ["T_ss", "vbc"], writes=[(tag + "yb", yi)])
            k.dma(y_d[tt * 128:(tt + 1) * 128, col0:col0 + width], yb, reads=[(tag + "yb", yi)], writes=[("y_d",)])

        def attn_finish(acc_bank, ncols_q, nheads, tmp, odst_fn, extra_den, tag):
            pa, pak = PS(acc_bank)
            oT = tmp["oT"]
            k.op("act", lambda e: e.copy(oT[0:65, 0:ncols_q], pa[0:65, 0:ncols_q]), reads=[], writes=[pak, "T_oT"])
            for j in range(ncols_q // 128):
                pb, pk = PS(7)
                k.op("pe", lambda e, j=j: e.transpose(pb[:, 0:65], oT[0:65, j * 128:(j + 1) * 128], identf[0:65, 0:65]),
                     reads=["T_oT", "cst"], writes=[pk])
                den = tmp["den"]
                if extra_den is not None:
                    ed = extra_den(j)
                    k.op("dve", lambda e, ed=ed: e.tensor_tensor(out=den[:, 0:1], in0=pb[:, 64:65], in1=ed, op=ALU.add),
                         reads=["sinkexp"], writes=[pk, "T_den"])
                else:
                    k.op("dve", lambda e: e.tensor_copy(den[:, 0:1], pb[:, 64:65]), reads=[], writes=[pk, "T_den"])
                k.op("dve", lambda e: e.reciprocal(out=den[:, 1:2], in_=den[:, 0:1]), reads=["T_den"], writes=["T_rden"])
                dst, dkey = odst_fn(j)
                k.op("dve", lambda e, dst=dst: e.tensor_scalar_mul(out=dst, in0=pb[:, 0:64], scalar1=den[:, 1:2]), reads=["T_rden"], writes=[pk, dkey])

        def mla_phase(l, vbc):
            fbase, bbase = A.p, A.pb
            qT = A.bf16(4 * S).rearrange("p (h t) -> p h t", h=4)
            kT = A.bf16(4 * S).rearrange("p (h t) -> p h t", h=4)
            vx = A.bf16(NT * 4 * 65).rearrange("p (t h d) -> p t h d", t=NT, h=4)
            oall = A.f32(NT * 256).rearrange("p (t d) -> p t d", t=NT)
            wuq = A.bf16(2 * 384).rearrange("p (c f) -> p c f", c=2)
            wukv = A.bf16(512)
            pin = [A.f32(416) for _ in range(2)]
            csb = [A.f32(32) for _ in range(2)]
            nb = [A.bf16(384) for _ in range(2)]
            nT = A.bf16(3 * 128).rearrange("p (c t) -> p c t", c=3)
            qf = A.f32(384).rearrange("p (h d) -> p h d", h=4)
            kf = A.f32(384).rearrange("p (h d) -> p h d", h=4)
            qb = A.bf16(384).rearrange("p (h d) -> p h d", h=4)
            kb_ = A.bf16(384).rearrange("p (h d) -> p h d", h=4)
            tmp = {"sq": A.f32(384), "ss": A.f32(8), "rs": A.f32(8),
                   "oT": A.f32(512), "den": A.f32(2), "yb": [A.bf16(256) for _ in range(2)]}
            tsets = [mk_tmpset("Ta_", 384, 64), mk_tmpset("Tb_", 384, 64)]
            eb = [A.bf16(512) for _ in range(3)]
            k.dma(wuq, wuq_d[l].rearrange("(c p) f -> p c f", p=128), writes=["wuq"], q="pool")
            k.dma(wukv, wukv_d[l], writes=["wukv"], q="pool")
            k.op("dve", lambda e: e.memset(vx[:, :, :, 64:65], 1.0), writes=["vx1"])
            for tt in range(NT):
                pi = tt % 2
                k.dma(pin[pi], ptok_d[tt * 128:(tt + 1) * 128, 1040:1456], reads=[("ptok_d",)], writes=[("pin", pi)])
                k.dma(csb[pi], cs_mla_d[tt * 128:(tt + 1) * 128, :], writes=[("cs", pi)])
                for (c0, n, wn, tagn) in ((0, 256, "mla_q_lat_norm", "a"), (256, 128, "mla_kv_norm", "b")):
                    k.op("dve", lambda e: e.memset(tmp["ss"][:, 0:1], 0.0), writes=["T_ss"])
                    k.op("act", lambda e, c0=c0, n=n, pi=pi: e.activation(out=tmp["sq"][:, 0:n], in_=pin[pi][:, c0:c0 + n],
                                                                        func=AF.Square, accum_out=tmp["ss"][:, 0:1]),
                         reads=[("pin", pi)], writes=["T_sq", "T_ss"])
                    k.op("act", lambda e, n=n: e.activation(out=tmp["rs"][:, 0:1], in_=tmp["ss"][:, 0:1], func=AF.Sqrt,
                                                            scale=1.0 / n, bias=EPS), reads=["T_ss"], writes=["T_rs"])
                    k.op("dve", lambda e: e.reciprocal(out=tmp["ss"][:, 1:2], in_=tmp["rs"][:, 0:1]), reads=["T_rs"], writes=["T_ss"])
                    k.op("dve", lambda e, c0=c0, n=n, pi=pi, wn=wn: e.scalar_tensor_tensor(
                        out=nb[pi][:, c0:c0 + n], in0=pin[pi][:, c0:c0 + n], scalar=tmp["ss"][:, 1:2], in1=vrow(vbc, wn),
                        op0=ALU.mult, op1=ALU.mult), reads=[("pin", pi), "T_ss", "vbc"], writes=[("nb", pi, tagn)])
                pb, pk = PS(6)
                pv = pb[:, 512:1024].rearrange("p (c t) -> p c t", c=4)
                for c in range(3):
                    k.op("pe", lambda e, c=c, pi=pi: e.transpose(pv[:, c, :], nb[pi][:, c * 128:(c + 1) * 128], identb),
                         reads=[("nb", pi, "a"), ("nb", pi, "b"), "cstb"], writes=[pk])
                copy_op("act", nT, pv[:, 0:3, :], reads=[], writes=[pk, "nT"])
                pq, pqk = PS(0)
                for c in range(2):
                    k.op("pe", lambda e, c=c: e.matmul(pq[:, 0:384], lhsT=nT[:, c, :], rhs=wuq[:, c, :], start=(c == 0), stop=(c == 1)),
                         reads=["nT", "wuq"], writes=[pqk])
                pkv, pkvk = PS(1)
                k.op("pe", lambda e: e.matmul(pkv[:], lhsT=nT[:, 2, :], rhs=wukv, start=True, stop=True), reads=["nT", "wukv"], writes=[pkvk])
                copy_op("act", qf, pq[:, 0:384].rearrange("p (h d) -> p h d", h=4), reads=[], writes=[pqk, "qf"])
                kvv = pkv[:].rearrange("p (h d) -> p h d", h=4)
                copy_op("act", kf[:, :, 0:64], kvv[:, :, 0:64], reads=[], writes=[pkvk, "kf"])
                copy_op("dve", vx[:, tt, :, 0:64], kvv[:, :, 64:128], reads=[], writes=[pkvk, ("vx", tt)])
                k.op("dve", lambda e, pi=pi: e.tensor_copy(kf[:, :, 64:96], pin[pi][:, 384:416].unsqueeze(1).to_broadcast([128, 4, 32])),
                     reads=[("pin", pi)], writes=["kf"])
                run_interleaved([
                    headnorm_rope(qf, 4, 96, vrow(vbc, "mla_q_norm"), 64, 16, csb[pi], qb, tsets[0], ["qf"], ["qb"], ("cs", pi)),
                    headnorm_rope(kf, 4, 96, vrow(vbc, "mla_k_norm"), 64, 16, csb[pi], kb_, tsets[1], ["kf"], ["kb"], ("cs", pi))])
                heads_to_T(qb, "qb", 4, 96, qT, tt, "qT", 6)
                heads_to_T(kb_, "kb", 4, 96, kT, tt, "kT", 6)
            scale = 96.0 ** -0.5
            qkeys = lambda qi: [("qT", tt) for tt in range(qi * 4, qi * 4 + 4)]
            items = [(h, qi, kt) for h in range(4) for qi in range(S // 512) for kt in range(NT)]
            LA = 2

            def stA(i):
                h, qi, kt = items[i]
                psn, psk = PS(i % 3)
                k.op("pe", lambda e: e.matmul(
                    psn[:], lhsT=kT[0:96, h, kt * 128:(kt + 1) * 128], rhs=qT[0:96, h, qi * 512:(qi + 1) * 512],
                    start=True, stop=True), reads=[("kT", kt)] + qkeys(qi), writes=[psk])

            def stBC(i):
                h, qi, kt = items[i]
                psn, psk = PS(i % 3)
                ebi = i % 3
                blk = h * (S // 512) + qi
                pa, pak = PS(4 + (blk % 2))
                k.op("act", lambda e: e.activation(out=eb[ebi], in_=psn[:], func=AF.Exp, scale=scale),
                     reads=[], writes=[psk, ("eb", ebi)])
                k.op("pe", lambda e: e.matmul(
                    pa[0:65, :], lhsT=vx[:, kt, h, :], rhs=eb[ebi], start=(kt == 0), stop=(kt == NT - 1)),
                    reads=[("vx", kt), "vx1", ("eb", ebi)], writes=[pak])
                if kt == NT - 1:
                    attn_finish(4 + (blk % 2), 512, 1, tmp,
                                lambda j, h=h, qi=qi: (oall[:, qi * 4 + j, h * 64:(h + 1) * 64], ("oall", qi * 4 + j, h)),
                                None, "m")

            for i in range(min(LA, len(items))):
                stA(i)
            for i in range(len(items)):
                if i + LA < len(items):
                    stA(i + LA)
                stBC(i)
            for tt in range(NT):
                okeys = [("oall", tt, h) for h in range(4)]
                out_norm_store(oall[:, tt, :], okeys, 256, vrow(vbc, "mla_out_norm"), 768, tmp, tt, tt, "mo")
            A.reset(fbase, bbase)

        def swa_phase(l, vbc):
            fbase, bbase = A.p, A.pb
            qT = A.bf16(4 * S).rearrange("p (h t) -> p h t", h=4)
            kT = A.bf16(2 * S).rearrange("p (h t) -> p h t", h=2)
            vx = A.bf16(NT * 2 * 65).rearrange("p (t h d) -> p t h d", t=NT, h=2)
            oall = A.f32(NT * 256).rearrange("p (t d) -> p t d", t=NT)
            pin = [A.f32(512) for _ in range(2)]
            csb = [A.f32(64) for _ in range(2)]
            qb = [A.bf16(256).rearrange("p (h d) -> p h d", h=4) for _ in range(2)]
            kb_ = [A.bf16(128).rearrange("p (h d) -> p h d", h=2) for _ in range(2)]
            tmp = {"sq": A.f32(256), "ss": A.f32(8), "rs": A.f32(8),
                   "oT": A.f32(512), "den": A.f32(2), "yb": [A.bf16(256) for _ in range(2)]}
            tsets = [mk_tmpset("T%d_" % i, 256, 128) for i in range(4)]
            sinkexp = A.f32(4)
            eb = [A.bf16(256) for _ in range(3)]
            k.op("act", lambda e: e.activation(out=sinkexp, in_=vrow(vbc, "swa_sink"), func=AF.Exp), reads=["vbc"], writes=["sinkexp"])
            k.op("dve", lambda e: e.memset(vx[:, :, :, 64:65], 1.0), writes=["vx1"])
            for tt0 in range(0, NT, 2):
                gens = []
                for u in range(2):
                    tt = tt0 + u
                    pi = tt % 2
                    k.dma(pin[pi], ptok_d[tt * 128:(tt + 1) * 128, 528:1040], reads=[("ptok_d",)], writes=[("pin", pi)])
                    k.dma(csb[pi], cs_swa_d[tt * 128:(tt + 1) * 128, :], writes=[("cs", pi)])
                    qsrc = pin[pi][:, 0:256].rearrange("p (h d) -> p h d", h=4)
                    ksrc = pin[pi][:, 256:384].rearrange("p (h d) -> p h d", h=2)
                    gens.append(headnorm_rope(qsrc, 4, 64, vrow(vbc, "swa_q_norm"), 0, 32, csb[pi], qb[pi], tsets[2 * u],
                                              [("pin", pi)], [("qb", pi)], ("cs", pi)))
                    gens.append(headnorm_rope(ksrc, 2, 64, vrow(vbc, "swa_k_norm"), 0, 32, csb[pi], kb_[pi], tsets[2 * u + 1],
                                              [("pin", pi)], [("kb", pi)], ("cs", pi)))
                run_interleaved(gens)
                for u in range(2):
                    tt = tt0 + u
                    pi = tt % 2
                    k.op("dve", lambda e: e.tensor_copy(vx[:, tt, :, 0:64], pin[pi][:, 384:512].rearrange("p (h d) -> p h d", h=2)),
                         reads=[("pin", pi)], writes=[("vx", tt)])
                    heads_to_T(qb[pi], ("qb", pi), 4, 64, qT, tt, "qT", 6)
                    heads_to_T(kb_[pi], ("kb", pi), 2, 64, kT, tt, "kT", 6)
            scale = 64.0 ** -0.5
            items = []
            for n in range(NT):
                for kvh in range(2):
                    js = [j for j in (n - 1, n, n + 1) if 0 <= j < NT]
                    for ji, j in enumerate(js):
                        items.append((n, kvh, j, ji, len(js)))
            LA = 2
            pend = []

            def stA(i):
                n, kvh, j, ji, nj = items[i]
                psn, psk = PS(i % 3)
                k.op("pe", lambda e: e.matmul(
                    psn[:, 0:256].rearrange("p (a b) -> p a b", a=2), lhsT=kT[0:64, kvh, j * 128:(j + 1) * 128],
                    rhs=qT[0:64, 2 * kvh:2 * kvh + 2, n * 128:(n + 1) * 128], start=True, stop=True),
                    reads=[("kT", j), ("qT", n)], writes=[psk])

            def stBC(i):
                n, kvh, j, ji, nj = items[i]
                psn, psk = PS(i % 3)
                ebi = i % 3
                blk = n * 2 + kvh
                pa, pak = PS(4 + (blk % 2))
                k.op("act", lambda e: e.activation(out=eb[ebi], in_=psn[:, 0:256], func=AF.Exp, scale=scale),
                     reads=[], writes=[psk, ("eb", ebi)])
                if j != n:
                    mk = cstb[:, 128:384] if j < n else cstb[:, 384:640]
                    k.op("dve", lambda e: e.tensor_tensor(out=eb[ebi], in0=eb[ebi], in1=mk, op=ALU.mult),
                         reads=["cstb"], writes=[("eb", ebi)])
                k.op("pe", lambda e: e.matmul(
                    pa[0:65, 0:256], lhsT=vx[:, j, kvh, :], rhs=eb[ebi], start=(ji == 0), stop=(ji == nj - 1)),
                    reads=[("vx", j), "vx1", ("eb", ebi)], writes=[pak])
                if ji == 0 and pend:
                    pend.pop(0)()
                if ji == nj - 1:
                    pend.append(lambda n=n, kvh=kvh, blk=blk: attn_finish(
                        4 + (blk % 2), 256, 2, tmp,
                        lambda jj: (oall[:, n, (2 * kvh + jj) * 64:(2 * kvh + jj + 1) * 64], ("oall", n, 2 * kvh + jj)),
                        lambda jj: sinkexp[:, 2 * kvh + jj:2 * kvh + jj + 1], "s"))

            for i in range(min(LA, len(items))):
                stA(i)
            for i in range(len(items)):
                if i + LA < len(items):
                    stA(i + LA)
                stBC(i)
            while pend:
                pend.pop(0)()
            for tt in range(NT):
                okeys = [("oall", tt, h) for h in range(4)]
                out_norm_store(oall[:, tt, :], okeys, 256, vrow(vbc, "swa_out_norm"), 512, tmp, tt, tt, "so")
            A.reset(fbase, bbase)

        def ssd_phase(l, vbc):
            fbase, bbase = A.p, A.pb
            convw = A.f32(48).rearrange("p (c k) -> p c k", c=8)
            BT = A.bf16(2 * S).rearrange("p (g t) -> p g t", g=2)
            CT = A.bf16(2 * S).rearrange("p (g t) -> p g t", g=2)
            dt = A.f32(NT * 16).rearrange("p (t d) -> p t d", t=NT)
            av = A.f32(NT * 16).rearrange("p (t d) -> p t d", t=NT)
            cum = A.f32(NT * 16).rearrange("p (t d) -> p t d", t=NT)
            ncf = A.f32(NT * 8).rearrange("p (t d) -> p t d", t=NT)
            sc1 = A.f32(NT * 16).rearrange("p (t d) -> p t d", t=NT)
            sc2 = A.f32(NT * 16).rearrange("p (t d) -> p t d", t=NT)
            dw = A.f32(NT * 16).rearrange("p (t d) -> p t d", t=NT)
            tot = A.f32(NCH * 16).rearrange("p (c d) -> p c d", c=NCH)
            etot = A.f32(NCH * 16).rearrange("p (c d) -> p c d", c=NCH)
            Abc = A.f32(16)
            t16 = [A.f32(NT * 16).rearrange("p (t d) -> p t d", t=NT) for _ in range(3)]
            cumT = [A.f32(512).rearrange("p (d l) -> p d l", d=2) for _ in range(2)]
            Rst = [A.f32(256) for _ in range(2)]
            Rtmp = A.f32(256)
            Eb = [A.f32(256) for _ in range(4)]
            ya = [A.f32(256) for _ in range(2)]
            yg = [A.f32(256) for _ in range(2)]
            zb = [A.f32(256) for _ in range(2)]
            szb = [A.f32(256) for _ in range(2)]
            tmp = {"sq": A.f32(256), "ss": A.f32(8), "rs": A.f32(8), "yb": [A.bf16(256) for _ in range(2)]}
            xin = [A.bf16(S + 4)] * 2
            dg = A.bf16(5 * 128).rearrange("p (k c) -> p k c", k=5)
            xTf = A.bf16(2 * S).rearrange("p (c t) -> p c t", c=2)
            xs = A.bf16(NT * 256).rearrange("p (t d) -> p t d", t=NT)
            Btok = A.bf16(NT * 128).rearrange("p (t d) -> p t d", t=NT)
            Sinb = [A.bf16((NCH + 1) * 256).rearrange("p (c d) -> p c d", c=NCH + 1) for _ in range(2)]
            Mb = [A.bf16(256) for _ in range(4)]
            GT = [A.bf16(512) for _ in range(2)]
            xdt = [A.bf16(2 * 2 * 256).rearrange("p (r t d) -> p r t d", r=2, t=2) for _ in range(2)]
            xdw = [A.bf16(2 * 256).rearrange("p (t d) -> p t d", t=2) for _ in range(2)]
            k.dma(convw, convp_d[l].rearrange("p (c k) -> p c k", c=8), writes=["convw"])
            ci = [0]

            def conv_chunk(cc, dst_fn):
                xi = 0
                k.op("dve", lambda e, xi=xi: e.memset(xin[xi][:, 0:2], 0.0), writes=[("xin", xi)])
                k.op("dve", lambda e, xi=xi: e.memset(xin[xi][:, S + 2:S + 4], 0.0), writes=[("xin", xi)])
                k.dma(xin[xi][:, 2:S + 2], xbc_d[cc * 128:(cc + 1) * 128, :], reads=[("xbc_d",)], writes=[("xin", xi)])
                for kk in range(5):
                    k.op("dve", lambda e, kk=kk, cc=cc: e.tensor_scalar_mul(out=dg[:, kk, :], in0=identf, scalar1=convw[:, cc, kk:kk + 1]),
                         reads=["cst", "convw"], writes=["dg"])
                for t4 in range(S // 512):
                    pb, pk = PS(ci[0] % 2)
                    ci[0] += 1
                    for kk in range(5):
                        k.op("pe", lambda e, pb=pb, kk=kk, xi=xi, t4=t4: e.matmul(
                            pb[:], lhsT=dg[:, kk, :], rhs=xin[xi][:, t4 * 512 + kk:t4 * 512 + kk + 512],
                            start=(kk == 0), stop=(kk == 4)), reads=["dg", ("xin", xi)], writes=[pk])
                    dst, dkey = dst_fn(t4)
                    k.op("act", lambda e, pb=pb, dst=dst, cc=cc: e.activation(out=dst, in_=pb[:], func=AF.Silu, bias=convw[:, cc, 5:6]),
                         reads=["convw"], writes=[pk, dkey])
                ci[0] += 1

            for g in range(2):
                conv_chunk(4 + g, lambda t4, g=g: (BT[:, g, t4 * 512:(t4 + 1) * 512], ("BT", g, t4)))
                conv_chunk(6 + g, lambda t4, g=g: (CT[:, g, t4 * 512:(t4 + 1) * 512], ("CT", g, t4)))
            for tt in range(NT):
                k.dma(dt[:, tt, :], ptok_d[tt * 128:(tt + 1) * 128, 512:528], reads=[("ptok_d",)], writes=["dt"])
            bias_bc = vrow(vbc, "ssd_dt_bias").unsqueeze(1).to_broadcast([128, NT, 16])
            k.op("dve", lambda e: e.tensor_tensor(out=dt, in0=dt, in1=bias_bc, op=ALU.add), reads=["vbc"], writes=["dt"])
            k.op("dve", lambda e: e.tensor_scalar_mul(out=t16[0], in0=dt, scalar1=-1.0), reads=["dt"], writes=["t16a"])
            k.op("dve", lambda e: e.tensor_tensor(out=t16[0], in0=t16[0], in1=dt, op=ALU.max), reads=["dt"], writes=["t16a"])
            k.op("act", lambda e: e.activation(out=t16[1], in_=t16[0], func=AF.Exp, scale=-1.0), reads=["t16a"], writes=["t16b"])
            k.op("act", lambda e: e.activation(out=t16[0], in_=t16[1], func=AF.Ln, bias=1.0), reads=["t16b"], writes=["t16a"])
            k.op("dve", lambda e: e.tensor_scalar_max(out=t16[1], in0=dt, scalar1=0.0), reads=["dt"], writes=["t16b"])
            k.op("dve", lambda e: e.tensor_tensor(out=dt, in0=t16[0], in1=t16[1], op=ALU.add), reads=["t16a", "t16b"], writes=["dt"])
            k.op("act", lambda e: e.activation(out=Abc, in_=vrow(vbc, "ssd_a_log"), func=AF.Exp), reads=["vbc"], writes=["Abc"])
            k.op("dve", lambda e: e.scalar_tensor_tensor(out=av, in0=dt, scalar=-1.0, in1=Abc.unsqueeze(1).to_broadcast([128, NT, 16]),
                                                         op0=ALU.mult, op1=ALU.mult), reads=["dt", "Abc"], writes=["av"])
            dbg_dump("ssd_dt", dt, ["dt"])
            TRI_LE = cst[:, C_R0F:C_R0F + 128]
            TRI_LT = cst[:, C_R0B:C_R0B + 128]
            for c in range(NCH):
                t0_, t1_ = 2 * c, 2 * c + 1
                pb, pk = PS(c % 2)
                pv = pb[:, 0:48].rearrange("p (t d) -> p t d", t=3)
                for d, tri in ((0, TRI_LE), (1, TRI_LT)):
                    cs_ = slice(d * 8, d * 8 + 8)
                    k.op("pe", lambda e, pv=pv, tri=tri, cs_=cs_, t0_=t0_: e.matmul(pv[:, 0, cs_], lhsT=tri, rhs=av[:, t0_, cs_], start=True, stop=True),
                         reads=["av", "cst"], writes=[pk])
                    k.op("pe", lambda e, pv=pv, cs_=cs_, t0_=t0_: e.matmul(pv[:, 1, cs_], lhsT=ones_f, rhs=av[:, t0_, cs_], start=False, stop=False,
                                                                          skip_group_check=True), reads=["av", "cst"], writes=[pk])
                    k.op("pe", lambda e, pv=pv, tri=tri, cs_=cs_, t1_=t1_: e.matmul(pv[:, 1, cs_], lhsT=tri, rhs=av[:, t1_, cs_], start=False, stop=True,
                                                                                  skip_group_check=True), reads=["av", "cst"], writes=[pk])
                k.op("pe", lambda e, pv=pv, t0_=t0_: e.matmul(pv[:, 2, :], lhsT=ones_f, rhs=av[:, t0_, :], start=False, stop=False, skip_group_check=True),
                     reads=["av", "cst"], writes=[pk])
                k.op("pe", lambda e, pv=pv, t1_=t1_: e.matmul(pv[:, 2, :], lhsT=ones_f, rhs=av[:, t1_, :], start=False, stop=True, skip_group_check=True),
                     reads=["av", "cst"], writes=[pk])
                copy_op("dve", cum[:, t0_:t0_ + 2, :], pv[:, 0:2, :], reads=[], writes=[pk, "cum"])
                copy_op("dve", tot[:, c, :], pv[:, 2, :], reads=[], writes=[pk, "tot"])
            totb = lambda d: tot[:, :, d * 8:d * 8 + 8].unsqueeze(2).to_broadcast([128, NCH, 2, 8])
            v4 = lambda a, d: a[:, :, d * 8:d * 8 + 8].rearrange("p (c t) h -> p c t h", t=2)
            k.op("dve", lambda e: e.tensor_scalar_mul(out=ncf, in0=cum[:, :, 0:8], scalar1=-1.0), reads=["cum"], writes=["ncf"])
            k.op("act", lambda e: e.activation(out=sc1[:, :, 0:8], in_=cum[:, :, 0:8], func=AF.Exp), reads=["cum"], writes=["sc1a"])
            k.op("act", lambda e: e.activation(out=sc2[:, :, 8:16], in_=cum[:, :, 8:16], func=AF.Exp), reads=["cum"], writes=["sc2b"])
            k.op("dve", lambda e: e.tensor_tensor(out=v4(t16[2], 0), in0=totb(0), in1=v4(cum, 0), op=ALU.subtract), reads=["cum", "tot"], writes=["t16c0"])
            k.op("dve", lambda e: e.tensor_tensor(out=v4(t16[2], 1), in0=totb(1), in1=v4(cum, 1), op=ALU.subtract), reads=["cum", "tot"], writes=["t16c1"])
            k.op("act", lambda e: e.activation(out=sc2[:, :, 0:8], in_=t16[2][:, :, 0:8], func=AF.Exp), reads=["t16c0"], writes=["sc2a"])
            k.op("act", lambda e: e.activation(out=sc1[:, :, 8:16], in_=t16[2][:, :, 8:16], func=AF.Exp), reads=["t16c1"], writes=["sc1b"])
            k.op("act", lambda e: e.activation(out=etot, in_=tot, func=AF.Exp), reads=["tot"], writes=["etot"])
            k.op("dve", lambda e: e.tensor_tensor(out=dw, in0=dt, in1=sc2, op=ALU.mult), reads=["dt", "sc2a", "sc2b"], writes=["dw"])
            dbg_dump("ssd_cum", cum, ["cum"])
            Dbc = vrow(vbc, "ssd_d")
            v3 = lambda a: a.rearrange("p (h d) -> p h d", h=4)
            for g in range(2):
                for j in range(2):
                    conv_chunk(2 * g + j, lambda t4, j=j: (xTf[:, j, t4 * 512:(t4 + 1) * 512], ("xTf", t4)))
                for tt in range(NT):
                    pb, pk = PS(6)
                    pv = pb[:, 0:256].rearrange("p (c t) -> p c t", c=2)
                    for j in range(2):
                        k.op("pe", lambda e, pv=pv, j=j, tt=tt: e.transpose(pv[:, j, :], xTf[:, j, tt * 128:(tt + 1) * 128], identb),
                             reads=[("xTf", tt // 4), "cstb"], writes=[pk])
                    k.op("pe", lambda e, pb=pb, tt=tt: e.transpose(pb[:, 256:384], BT[:, g, tt * 128:(tt + 1) * 128], identb),
                         reads=[("BT", g, tt // 4), "cstb"], writes=[pk])
                    copy_op("act", xs[:, tt, :], pb[:, 0:256], reads=[], writes=[pk, ("xs", tt)])
                    copy_op("dve", Btok[:, tt, :], pb[:, 256:384], reads=[], writes=[pk, ("Btok", tt)])
                if g == 0:
                    dbg_dump("ssd_xs", xs, [("xs", tt) for tt in range(NT)], BF16)
                xs4 = xs.rearrange("p t (h d) -> p t h d", h=4)
                wi = 0
                for d in range(2):
                    k.op("dve", lambda e, d=d: e.memset(Rst[d], 0.0), writes=[("Rst", d)])
                    order = list(range(NCH)) if d == 0 else list(range(NCH - 1, -1, -1))
                    first_slot = 0 if d == 0 else NCH
                    k.op("dve", lambda e, d=d, first_slot=first_slot: e.memset(Sinb[d][:, first_slot, :], 0.0), writes=[("Sinb", d, first_slot)])
                    for c in order:
                        wr = wi % 2
                        wi += 1
                        dwb = dw[:, 2 * c:2 * c + 2, d * 8 + 4 * g:d * 8 + 4 * g + 4].unsqueeze(3).to_broadcast([128, 2, 4, 64])
                        k.op("dve", lambda e, wr=wr, dwb=dwb, c=c: e.tensor_tensor(out=xdw[wr].rearrange("p t (h d) -> p t h d", h=4),
                                                                                in0=xs4[:, 2 * c:2 * c + 2, :, :], in1=dwb, op=ALU.mult),
                             reads=[("xs", 2 * c), ("xs", 2 * c + 1), "dw"], writes=[("xdw", wr)])
                        pb, pk = PS(c % 2)
                        for ti in range(2):
                            tt = 2 * c + ti
                            k.op("pe", lambda e, pb=pb, tt=tt, ti=ti, wr=wr: e.matmul(
                                pb[:, 0:256], lhsT=Btok[:, tt, :], rhs=xdw[wr][:, ti, :], start=(ti == 0), stop=(ti == 1)),
                                reads=[("Btok", tt), ("xdw", wr)], writes=[pk])
                        etb = etot[:, c, d * 8 + 4 * g:d * 8 + 4 * g + 4].unsqueeze(2).to_broadcast([128, 4, 64])
                        k.op("dve", lambda e, d=d, etb=etb: e.tensor_tensor(out=v3(Rtmp), in0=v3(Rst[d]), in1=etb, op=ALU.mult),
                             reads=[("Rst", d), "etot"], writes=["Rtmp"])
                        k.op("dve", lambda e, pb=pb, d=d: e.tensor_tensor(out=Rst[d], in0=pb[:, 0:256], in1=Rtmp, op=ALU.add),
                             reads=["Rtmp"], writes=[pk, ("Rst", d)])
                        slot = c + 1 if d == 0 else c
                        k.op("act", lambda e, d=d, slot=slot: e.copy(Sinb[d][:, slot, :], Rst[d]), reads=[("Rst", d)], writes=[("Sinb", d, slot)])
                def prologue(c):
                    t0_, t1_ = 2 * c, 2 * c + 1
                    cr = c % 2
                    pb2, pk2 = PS(7)
                    for d, (r0, r1) in ((0, (C_R0F, C_R1F)), (1, (C_R0B, C_R1B))):
                        cs_ = slice(d * 8, d * 8 + 8)
                        o_ = pb2[0:8, d * 256:(d + 1) * 256]
                        k.op("pe", lambda e: e.matmul(o_, lhsT=av[:, t0_, cs_], rhs=cst[:, r0:r0 + 256],
                                                      start=(d == 0), stop=False, skip_group_check=True),
                             reads=["av", "cst"], writes=[pk2])
                        k.op("pe", lambda e: e.matmul(o_, lhsT=av[:, t1_, cs_], rhs=cst[:, r1:r1 + 256],
                                                      start=False, stop=True, skip_group_check=True),
                             reads=["av", "cst"], writes=[pk2])
                    copy_op("act", cumT[cr][0:8, :, :], pb2[0:8, 0:512].rearrange("p (d l) -> p d l", d=2), reads=[], writes=[pk2, ("cumT", cr)])
                    for d in range(2):
                        dtb = dt[:, 2 * c:2 * c + 2, d * 8 + 4 * g:d * 8 + 4 * g + 4].unsqueeze(3).to_broadcast([128, 2, 4, 64])
                        k.op("dve", lambda e: e.tensor_tensor(out=xdt[d][:, cr].rearrange("p t (h d) -> p t h d", h=4),
                                                              in0=xs4[:, 2 * c:2 * c + 2, :, :], in1=dtb, op=ALU.mult),
                             reads=[("xs", 2 * c), ("xs", 2 * c + 1), "dt"], writes=[("xdt", d, cr)])
                    pb, pk = PS(7)
                    for si in range(2):
                        k.op("pe", lambda e: e.matmul(pb[:, si * 256:(si + 1) * 256], lhsT=BT[:, g, (2 * c + si) * 128:(2 * c + si + 1) * 128],
                                                      rhs=CT[:, g, c * 256:(c + 1) * 256], start=(si == 0), stop=True, skip_group_check=True),
                             reads=[("BT", g, (2 * c + si) // 4), ("CT", g, c // 2)], writes=[pk])
                    copy_op("act", GT[cr], pb[:, 0:512], reads=[], writes=[pk, ("GT", cr)])

                def item_front(c, it):
                    h, d = it // 2, it % 2
                    hh = 4 * g + h
                    cr = c % 2
                    sel = cst[0:8, C_SEL + hh * 128:C_SEL + (hh + 1) * 128]
                    for si in range(2):
                        pb, pk = PS(2 * (it % 2) + si)
                        if d == 0:
                            lo, hi = (0, 256) if si == 0 else (128, 256)
                            mask = cstb[:, 640:896] if si == 0 else cstb[:, 640:768]
                        else:
                            lo, hi = (0, 128) if si == 0 else (0, 256)
                            mask = cstb[:, 896 + 128:896 + 256] if si == 0 else cstb[:, 896:896 + 256]
                        w = hi - lo
                        k.op("pe", lambda e: e.matmul(pb[:, 0:w], lhsT=sel, rhs=cumT[cr][0:8, d, lo:hi], start=True, stop=False),
                             reads=[("cumT", cr), "cst"], writes=[pk])
                        k.op("pe", lambda e: e.matmul(pb[:, 0:w], lhsT=identb, rhs=mask, start=False, stop=True),
                             reads=["cstb"], writes=[pk])

                def item_back(c, it, started, ybank, ykey):
                    h, d = it // 2, it % 2
                    hh = 4 * g + h
                    cr = c % 2
                    mms = []
                    for si in range(2):
                        pb, pk = PS(2 * (it % 2) + si)
                        if d == 0:
                            lo, hi = (0, 256) if si == 0 else (128, 256)
                        else:
                            lo, hi = (0, 128) if si == 0 else (0, 256)
                        w = hi - lo
                        ebi = 2 * d + si
                        tt_s = 2 * c + si
                        if d == 0:
                            k.op("act", lambda e: e.activation(out=Eb[ebi][:, 0:w], in_=pb[:, 0:w], func=AF.Exp, bias=ncf[:, tt_s, hh:hh + 1], scale=1.0),
                                 reads=["ncf"], writes=[pk, ("Eb", ebi)])
                        else:
                            k.op("act", lambda e: e.activation(out=Eb[ebi][:, 0:w], in_=pb[:, 0:w], func=AF.Exp, bias=cum[:, tt_s, 8 + hh:9 + hh], scale=-1.0),
                                 reads=["cum"], writes=[pk, ("Eb", ebi)])
                        k.op("dve", lambda e: e.tensor_tensor(out=Mb[ebi][:, 0:w], in0=Eb[ebi][:, 0:w], in1=GT[cr][:, si * 256 + lo:si * 256 + hi], op=ALU.mult),
                             reads=[("Eb", ebi), ("GT", cr)], writes=[("Mb", ebi)])
                        for li in range(2):
                            if lo <= li * 128 < hi:
                                mms.append((ebi, li * 128 - lo, li, si))
                    for (ebi, off, li, si) in mms:
                        st_ = not started[li]
                        started[li] = True
                        k.op("pe", lambda e: e.matmul(
                            ybank[li][:, h * 64:(h + 1) * 64], lhsT=Mb[ebi][:, off:off + 128], rhs=xdt[d][:, cr, si, h * 64:(h + 1) * 64],
                            start=st_, stop=False, skip_group_check=True),
                            reads=[("Mb", ebi), ("xdt", d, cr)], writes=[ykey[li]])

                def epilogue(c, ybank, ykey):
                    for li in range(2):
                        tt = 2 * c + li
                        pf7, pfk = PS(6 if False else 7)
                        pf = pf7[:, 0:256]
                        pbw = pf7[:, 256:512]
                        k.op("pe", lambda e: e.matmul(pf, lhsT=CT[:, g, tt * 128:(tt + 1) * 128], rhs=Sinb[0][:, c, :],
                                                      start=True, stop=True), reads=[("CT", g, tt // 4), ("Sinb", 0, c)], writes=[pfk])
                        k.op("pe", lambda e: e.matmul(pbw, lhsT=CT[:, g, tt * 128:(tt + 1) * 128], rhs=Sinb[1][:, c + 1, :],
                                                      start=False, stop=True, skip_group_check=True),
                             reads=[("CT", g, tt // 4), ("Sinb", 1, c + 1)], writes=[pfk])
                        ecfb = sc1[:, tt, 4 * g:4 * g + 4].unsqueeze(2).to_broadcast([128, 4, 64])
                        erbb = sc1[:, tt, 8 + 4 * g:12 + 4 * g].unsqueeze(2).to_broadcast([128, 4, 64])
                        dbb = Dbc[:, 4 * g:4 * g + 4].unsqueeze(2).to_broadcast([128, 4, 64])
                        yi = li
                        k.op("dve", lambda e: e.tensor_tensor(out=v3(ya[yi]), in0=v3(pf), in1=ecfb, op=ALU.mult),
                             reads=["sc1a"], writes=[pfk, ("ya", yi)])
                        k.op("dve", lambda e: e.tensor_tensor(out=v3(Rtmp), in0=v3(pbw), in1=erbb, op=ALU.mult),
                             reads=["sc1b"], writes=[pfk, "Rtmp"])
                        k.op("dve", lambda e: e.tensor_tensor(out=ya[yi], in0=ya[yi], in1=Rtmp, op=ALU.add), reads=["Rtmp"], writes=[("ya", yi)])
                        k.op("dve", lambda e: e.tensor_tensor(out=v3(Rtmp), in0=xs4[:, tt, :, :], in1=dbb, op=ALU.mult),
                             reads=[("xs", tt), "vbc"], writes=["Rtmp"])
                        k.op("dve", lambda e: e.tensor_tensor(out=ya[yi], in0=ya[yi], in1=Rtmp, op=ALU.add), reads=["Rtmp"], writes=[("ya", yi)])
                        k.op("dve", lambda e: e.tensor_tensor(out=ya[yi], in0=ybank[li][:, 0:256], in1=ya[yi], op=ALU.add),
                             reads=[], writes=[ykey[li], ("ya", yi)])
                        zi = li
                        k.dma(zb[zi], ptok_d[tt * 128:(tt + 1) * 128, g * 256:(g + 1) * 256], reads=[("ptok_d",)], writes=[("zb", zi)])
                        k.op("act", lambda e: e.activation(out=szb[zi], in_=zb[zi], func=AF.Silu), reads=[("zb", zi)], writes=[("szb", zi)])
                        k.op("dve", lambda e: e.tensor_tensor(out=yg[yi], in0=ya[yi], in1=szb[zi], op=ALU.mult),
                             reads=[("szb", zi), ("ya", yi)], writes=[("yg", yi)])
                        out_norm_store(yg[yi], [("yg", yi)], 256, vrow(vbc, "ssd_norm", g * 256, 256), g * 256, tmp, tt, tt, "do")

                prologue(0)
                for c in range(NCH):
                    yb0, yk0 = PS(4)
                    yb1, yk1 = PS(5)
                    ybank = (yb0, yb1)
                    ykey = (yk0, yk1)
                    started = [False, False]
                    item_front(c, 0)
                    for it in range(8):
                        if it + 1 < 8:
                            item_front(c, it + 1)
                        item_back(c, it, started, ybank, ykey)
                    if c + 1 < NCH:
                        prologue(c + 1)
                    epilogue(c, ybank, ykey)
            A.reset(fbase, bbase)

        def mixer_phase(l):
            A.reset()
            vbc = A.f32(NV)
            k.dma(vbc, vec_d[l:l + 1, :].partition_broadcast(128), writes=["vbc"])
            ssd_phase(l, vbc)
            k.barrier(barsc[:, 1:2])
            swa_phase(l, vbc)
            k.barrier(barsc[:, 2:3])
            mla_phase(l, vbc)
            if l == 0:
                k.barrier(barsc[:, 3:4])
                dbg_dump("ptok", ptok_d, [("ptok_d",)], big=True)
                dbg_dump("xbc", xbc_d, [("xbc_d",)], BF16, big=True)
                dbg_dump("y", y_d, [("y_d",)], BF16, big=True)

        finals = []
        for p in range(L + 1):
            l_prev = p - 1 if p > 0 else None
            l_next = p if p < L else None
            if stop_after is not None and p > stop_after[0]:
                break
            finals = block_phase(p, l_prev, l_next)
            k.barrier(barsc[:, 0:1])
            if l_next is not None:
                if stop_after is not None and stop_after == (p, "block"):
                    break
                mixer_phase(l_next)
                k.barrier(barsc[:, 0:1])
                if stop_after is not None and stop_after == (p, "mix"):
                    break
        k.emit(list(finals) + dbg_outs)
        print("ops recorded:", k.nops, "arena hi (KiB):", A.hi * 4 / 1024, A.hib * 2 / 1024)
    return nc


def prep_common(inp, S):
    L = inp["ffn1_norm"].shape[0]
    vec = np.zeros((L, NV), np.float32)
    for n, (o, s) in VOFF.items():
        vec[:, o:o + s] = np.asarray(inp[n], np.float32).reshape(L, s)
    cw = np.asarray(inp["ssd_conv_w"], np.float32)
    cb = np.asarray(inp["ssd_conv_b"], np.float32)
    convp = np.zeros((L, 128, 8, 6), np.float32)
    for c in range(8):
        convp[:, :, c, 0:5] = cw[:, :, c * 128:(c + 1) * 128].transpose(0, 2, 1)
        convp[:, :, c, 5] = cb[:, c * 128:(c + 1) * 128]
    com = {"vecs": vec, "convp": np.ascontiguousarray(convp.reshape(L, 128, 48)), "cst": make_consts(),
           "cs_swa": rope_tables(S, 64), "cs_mla": rope_tables(S, 32)}
    for n in ("ffn1_gate", "ffn1_up", "ffn1_down", "ffn2_gate", "ffn2_up", "ffn2_down", "w_in", "w_out",
              "mla_w_uq", "mla_w_ukv"):
        com[n] = np.ascontiguousarray(np.asarray(inp[n], np.float32))
    return com, L


_NC_CACHE = {}


def kernel(**inputs):
    x = np.asarray(inputs["x"], np.float32)
    B, S, _ = x.shape
    com, L = prep_common(inputs, S)
    key = (S, L)
    if key not in _NC_CACHE:
        _NC_CACHE[key] = build(S, L)
    nc = _NC_CACHE[key]
    ncores = 8
    in_maps = []
    for c in range(ncores):
        m = dict(com)
        m["x"] = np.ascontiguousarray(x[c % B])
        in_maps.append(m)
    res = run_bass_kernel_spmd(nc, in_maps, core_ids=list(range(ncores)))
    out = np.stack([np.asarray(res.results[b]["out"], np.float32) for b in range(B)], axis=0)
    return out
```

```python
import contextlib
import numpy as np
import concourse.bass as bass
import concourse.mybir as mybir
from concourse.bass_utils import run_bass_kernel_spmd

F32 = mybir.dt.float32
BF16 = mybir.dt.bfloat16
AF = mybir.ActivationFunctionType
ALU = mybir.AluOpType
AX = mybir.AxisListType

D = 1024
DFF = 2816
NPROJ = 2480
NTM = 1456
EPS = 1e-6
BIG = 1.0e4
GF = 4
DEBUG = False
ALT_DVE_ONLY = False
N_DMA_SEMS = 40

VOFF = {}
_o = 0
for _n, _s in (("ffn1_norm", 1024), ("mix_norm", 1024), ("ffn2_norm", 1024), ("ssd_norm", 512),
               ("swa_q_norm", 64), ("swa_k_norm", 64), ("swa_out_norm", 256), ("mla_q_lat_norm", 256),
               ("mla_kv_norm", 128), ("mla_q_norm", 96), ("mla_k_norm", 96), ("mla_out_norm", 256),
               ("ssd_dt_bias", 16), ("ssd_a_log", 16), ("ssd_d", 8), ("swa_sink", 4)):
    VOFF[_n] = (_o, _s)
    _o += _s
NV = _o

C_ID = 0
C_R0F = 128
C_R1F = 384
C_R0B = 640
C_R1B = 896
C_NEGF = 1152
C_POSB = 1408
C_GE = 1664
C_SEL = 1792
NCST = C_SEL + 1024


def make_consts():
    c = np.zeros((128, NCST), np.float32)
    i = np.arange(128)
    le = (i[:, None] <= i[None, :]).astype(np.float32)
    lt = (i[:, None] < i[None, :]).astype(np.float32)
    ge = (i[:, None] >= i[None, :]).astype(np.float32)
    gt = (i[:, None] > i[None, :]).astype(np.float32)
    c[:, C_ID:C_ID + 128] = np.eye(128)
    c[:, C_R0F:C_R0F + 128] = le
    c[:, C_R0F + 128:C_R0F + 256] = 1.0
    c[:, C_R1F + 128:C_R1F + 256] = le
    c[:, C_R0B:C_R0B + 128] = lt
    c[:, C_R0B + 128:C_R0B + 256] = 1.0
    c[:, C_R1B + 128:C_R1B + 256] = lt
    c[:, C_NEGF:C_NEGF + 128] = -BIG * gt
    c[:, C_POSB + 128:C_POSB + 256] = BIG * lt
    c[:, C_GE:C_GE + 128] = ge
    for h in range(8):
        c[h, C_SEL + h * 128:C_SEL + (h + 1) * 128] = 1.0
    return c


def rope_tables(n, dim):
    inv = 1.0 / np.power(np.float32(10000.0), np.arange(0, dim, 2, dtype=np.float32) / np.float32(dim))
    ang = np.arange(n, dtype=np.float32)[:, None] * inv[None, :].astype(np.float32)
    return np.concatenate([np.cos(ang), np.sin(ang)], axis=1).astype(np.float32)


def _freeze(fn):
    import types
    if fn.__closure__ is None:
        return fn
    cells = []
    for c in fn.__closure__:
        try:
            cells.append(types.CellType(c.cell_contents))
        except ValueError:
            cells.append(c)
    return types.FunctionType(fn.__code__, fn.__globals__, fn.__name__, fn.__defaults__, tuple(cells))


class Op:
    __slots__ = ("eng", "fn", "waits", "signal", "sem", "val", "is_dma")

    def __init__(self, eng, fn, is_dma=False):
        self.eng = eng
        self.fn = fn
        self.waits = []
        self.signal = False
        self.sem = None
        self.val = None
        self.is_dma = is_dma


class KB:
    ENGS = ("pe", "act", "dve", "pool", "sp")

    def __init__(self, nc):
        self.nc = nc
        self.prog = {e: [] for e in self.ENGS}
        self.last_w = {}
        self.readers = {}
        self.dma_rr = 0
        self.dma_last = [None] * N_DMA_SEMS
        self.dma_uses = [0] * N_DMA_SEMS
        self.bar = None
        self.nops = 0
        self.carrier = None

    def _deps(self, op, reads, writes):
        deps = []
        if self.bar is not None:
            deps.append(self.bar)
        for k in reads:
            w = self.last_w.get(k)
            if w is not None:
                deps.append(w)
        for k in writes:
            w = self.last_w.get(k)
            if w is not None:
                deps.append(w)
            deps.extend(self.readers.get(k, ()))
        for k in reads:
            self.readers.setdefault(k, []).append(op)
        for k in writes:
            self.last_w[k] = op
            self.readers[k] = []
        seen = set()
        for d in deps:
            if d is op or id(d) in seen:
                continue
            seen.add(id(d))
            if op.eng == "pe" and d.eng == "pe" and not d.is_dma and not op.is_dma:
                continue
            op.waits.append(d)

    def op(self, eng, fn, reads=(), writes=()):
        o = Op(eng, _freeze(fn))
        self._deps(o, reads, writes)
        self.prog[eng].append(o)
        self.nops += 1
        return o

    def dma(self, out, in_, reads=(), writes=(), q="sp", **kw):
        o = Op(q, None, is_dma=True)
        o.fn = lambda e, out=out, in_=in_, kw=kw: e.dma_start(out=out, in_=in_, **kw)
        self._deps(o, reads, writes)
        i = self.dma_rr
        self.dma_rr = (self.dma_rr + 1) % N_DMA_SEMS
        prev = self.dma_last[i]
        if prev is not None:
            o.waits.append(prev)
        self.dma_last[i] = o
        self.dma_uses[i] += 1
        o.sem = i
        o.val = 16 * self.dma_uses[i]
        o.signal = True
        self.prog[q].append(o)
        self.nops += 1
        return o

    def barrier(self, scratch_ap):
        o = Op("dve", lambda e: e.memset(scratch_ap, 0.0))
        if self.bar is not None:
            o.waits.append(self.bar)
        for e in self.ENGS:
            for p in reversed(self.prog[e]):
                if not p.is_dma:
                    o.waits.append(p)
                    break
        for d in self.dma_last:
            if d is not None:
                o.waits.append(d)
        self.prog["dve"].append(o)
        self.bar = o
        self.last_w = {}
        self.readers = {}
        return o

    def emit(self, final_wait_ops=()):
        nc = self.nc
        for e in self.ENGS:
            for o in self.prog[e]:
                for w in o.waits:
                    w.signal = True
        for o in final_wait_ops:
            o.signal = True
        for e in self.ENGS:
            c = 0
            for o in self.prog[e]:
                if o.is_dma:
                    continue
                if o.signal:
                    c += 1
                    o.sem = e
                    o.val = c
        with contextlib.ExitStack() as st:
            esem = {e: st.enter_context(nc.semaphore("s_" + e)) for e in self.ENGS}
            dsem = [st.enter_context(nc.semaphore("d_%d" % i)) for i in range(N_DMA_SEMS)]
            block = st.enter_context(nc.Block())

            def sem_of(o):
                return dsem[o.sem] if o.is_dma else esem[o.sem]

            def run(e, h):
                waited = {}
                for o in self.prog[e]:
                    need = {}
                    for w in o.waits:
                        key = ("d", w.sem) if w.is_dma else ("e", w.sem)
                        if waited.get(key, 0) >= w.val:
                            continue
                        waited[key] = w.val
                        need[key] = w
                    need = list(need.values())
                    if e == "pe" and not o.is_dma:
                        for w in need[:-1]:
                            h.ldweights(self.carrier)._wait_ge(sem_of(w), w.val)
                        ins = o.fn(h)
                        if need:
                            ins._wait_ge(sem_of(need[-1]), need[-1].val)
                    else:
                        for w in need:
                            h.wait_ge(sem_of(w), w.val)
                        ins = o.fn(h)
                    if o.is_dma:
                        ins.then_inc(dsem[o.sem], 16)
                    elif o.signal:
                        ins.then_inc(esem[e], 1)
                if e == "sp":
                    for o in final_wait_ops:
                        h.wait_ge(sem_of(o), o.val)

            @block.tensor
            def _(t):
                run("pe", t)

            @block.scalar
            def _(s):
                run("act", s)

            @block.vector
            def _(v):
                run("dve", v)

            @block.gpsimd
            def _(g):
                run("pool", g)

            @block.sync
            def _(s):
                run("sp", s)


class Arena:
    def __init__(self, apf, nf, apb, nb):
        self.apf, self.nf, self.apb, self.nb = apf, nf, apb, nb
        self.p = 0
        self.pb = 0
        self.hi = 0
        self.hib = 0

    def reset(self, to=0, tob=0):
        self.p = to
        self.pb = tob

    def f32(self, cols):
        a = self.apf[:, self.p:self.p + cols]
        self.p += (cols + 7) // 8 * 8
        self.hi = max(self.hi, self.p)
        assert self.p <= self.nf, ("f32 arena overflow", self.p, self.nf)
        return a

    def bf16(self, cols):
        a = self.apb[:, self.pb:self.pb + cols]
        self.pb += (cols + 15) // 16 * 16
        self.hib = max(self.hib, self.pb)
        assert self.pb <= self.nb, ("bf16 arena overflow", self.pb, self.nb)
        return a


def build(S, L, stop_after=None):
    assert S % 512 == 0
    NT = S // 128
    TB = min(1024, S)
    NBLK = S // TB
    NTB = TB // 128
    NCH = S // 256

    nc = bass.Bass("TRN2", target_bir_lowering=False)
    dr = lambda name, shape, dt=F32, kind="ExternalInput": nc.dram_tensor(name, list(shape), dt, kind=kind).ap()
    x_d = dr("x", [S, D])
    wg_d = [dr("ffn1_gate", [L, D, DFF]), dr("ffn2_gate", [L, D, DFF])]
    wu_d = [dr("ffn1_up", [L, D, DFF]), dr("ffn2_up", [L, D, DFF])]
    wd_d = [dr("ffn1_down", [L, DFF, D]), dr("ffn2_down", [L, DFF, D])]
    win_d = dr("w_in", [L, D, NPROJ])
    wout_d = dr("w_out", [L, D, D])
    wuq_d = dr("mla_w_uq", [L, 256, 384])
    wukv_d = dr("mla_w_ukv", [L, 128, 512])
    vec_d = dr("vecs", [L, NV])
    convp_d = dr("convp", [L, 128, 48])
    cst_d = dr("cst", [128, NCST])
    cs_swa_d = dr("cs_swa", [S, 64])
    cs_mla_d = dr("cs_mla", [S, 32])
    out_d = dr("out", [S, D], kind="ExternalOutput")
    xbc_d = dr("xbc_scr", [1024, S], BF16, kind="Internal")
    ptok_d = dr("ptok_scr", [S, NTM], F32, kind="Internal")
    y_d = dr("y_scr", [S, D], BF16, kind="Internal")

    ARENA_F = 17 * 1024
    ARENA_B = 56 * 1024
    st = contextlib.ExitStack()
    with st:
        arena_f = st.enter_context(nc.sbuf_tensor("arena_f", [128, ARENA_F], F32))
        arena_b = st.enter_context(nc.sbuf_tensor("arena_b", [128, ARENA_B], BF16))
        cst = st.enter_context(nc.sbuf_tensor("cst_sb", [128, NCST], F32))
        cstb = st.enter_context(nc.sbuf_tensor("cstb_sb", [128, 128 * 5 + 512], BF16))
        barsc = st.enter_context(nc.sbuf_tensor("barsc", [128, 8], F32))
        banks = [st.enter_context(nc.psum_tensor("bank%d" % i, [128, 1024] if i == 6 else [128, 512], BF16 if i == 6 else F32))
                 for i in range(8)]
        k = KB(nc)
        A = Arena(arena_f[:], ARENA_F, arena_b[:], ARENA_B)

        k.carrier = cstb[:, 0:1]
        identf = cst[:, C_ID:C_ID + 128]
        identb = cstb[:, 0:128]
        ones_f = cst[:, C_R0F + 128:C_R0F + 256]

        k.dma(cst[:], cst_d, writes=["cst"])
        k.dma(cstb[:, 0:128], cst_d[:, C_ID:C_ID + 128], writes=["cstb"], q="pool")
        for j in range(2):
            k.dma(cstb[:, 128 + j * 128:256 + j * 128], cst_d[:, C_GE:C_GE + 128], writes=["cstb"], q="pool")
            k.dma(cstb[:, 384 + j * 128:512 + j * 128], cst_d[:, C_R0F:C_R0F + 128], writes=["cstb"], q="pool")

        k.dma(cstb[:, 640:896], cst_d[:, C_NEGF:C_NEGF + 256], writes=["cstb"], q="pool")
        k.dma(cstb[:, 896:1152], cst_d[:, C_POSB:C_POSB + 256], writes=["cstb"], q="pool")

        def PS(i):
            return banks[i], "bank%d" % i

        rr = {"n": 0}
        dbg_outs = []
        dbg_names = set()

        def dbg_dump(name, ap, reads, dt=F32, big=False):
            if not DEBUG or ("dbg_" + name) in dbg_names:
                return
            dbg_names.add("dbg_" + name)
            t = nc.dram_tensor("dbg_" + name, list(ap.shape), dt, kind="ExternalOutput").ap()
            if big:
                for r0 in range(0, ap.shape[0], 256):
                    dbg_outs.append(k.dma(t[r0:r0 + 256], ap[r0:r0 + 256], reads=reads))
            else:
                dbg_outs.append(k.dma(t, ap, reads=reads))

        def alt():
            rr["n"] += 1
            return "dve" if (rr["n"] % 2 or ALT_DVE_ONLY) else "act"

        def copy_op(eng, out, in_, reads, writes):
            if eng == "act":
                return k.op("act", lambda e: e.copy(out, in_), reads, writes)
            return k.op(eng, lambda e: e.tensor_copy(out, in_), reads, writes)

        def vrow(vbc, name, lo=0, n=None):
            o, s = VOFF[name]
            n = s - lo if n is None else n
            return vbc[:, o + lo:o + lo + n]

        def block_phase(pidx, l_prev, l_next):
            A.reset()
            vbcA = A.f32(3072) if l_prev is not None else None
            vbcB = A.f32(3072) if l_next is not None else None
            xb = A.f32(NTB * D).rearrange("p (t d) -> p t d", t=NTB)
            hT = A.bf16(8 * TB).rearrange("p (c t) -> p c t", c=8)
            hb = [A.bf16(D) for _ in range(2)]
            junk = A.f32(D)
            ss = A.f32(NTB)
            sq = A.f32(NTB)
            rstd = A.f32(NTB)
            wgu = [A.bf16(2 * 8 * GF * 128).rearrange("p (w c f) -> p w c f", w=2, c=8) for _ in range(2)]
            wdn = [A.bf16(GF * D).rearrange("p (c d) -> p c d", c=GF) for _ in range(2)]
            sg = [A.bf16(512) for _ in range(2)]
            aT = [A.bf16(GF * 512).rearrange("p (c t) -> p c t", c=GF) for _ in range(2)]
            wpj = [A.bf16(8 * 512).rearrange("p (c f) -> p c f", c=8) for _ in range(2)]
            evf = [A.bf16(512) for _ in range(2)]
            evt = [A.f32(512) for _ in range(2)]
            cnt = {"w": 0, "p": 0, "a": 0, "d": 0, "t": 0, "e": 0, "g": 0}
            if l_prev is not None:
                k.dma(vbcA, vec_d[l_prev:l_prev + 1, 0:3072].partition_broadcast(128), writes=["vbcA"])
            if l_next is not None:
                k.dma(vbcB, vec_d[l_next:l_next + 1, 0:3072].partition_broadcast(128), writes=["vbcB"])

            def norm_to_hT(vbc, vkey, nname, tag):
                k.op("dve", lambda e: e.memset(ss, 0.0), writes=["ss"])
                for tt in range(NTB):
                    k.op("act", lambda e, tt=tt: e.activation(out=junk, in_=xb[:, tt, :], func=AF.Square,
                                                              accum_out=ss[:, tt:tt + 1]),
                         reads=[("xb", tt)], writes=["junk", "ss"])
                k.op("act", lambda e: e.activation(out=sq, in_=ss, func=AF.Sqrt, scale=1.0 / D, bias=EPS),
                     reads=["ss"], writes=["sq"])
                k.op("dve", lambda e: e.reciprocal(out=rstd, in_=sq), reads=["sq"], writes=["rstd"])
                wv = vrow(vbc, nname)
                if tag == "f" and not cnt.get("dbg1"):
                    cnt["dbg1"] = 1
                    dbg_dump("ss", ss, ["ss"]); dbg_dump("rstd", rstd, ["rstd"]); dbg_dump("wv", wv, [vkey])
                    dbg_dump("xb0", xb[:, 0, :], [("xb", 0)])
                for tt in range(NTB):
                    hbi = tt % 2
                    k.op("dve", lambda e, tt=tt, hbi=hbi: e.scalar_tensor_tensor(
                        out=hb[hbi], in0=xb[:, tt, :], scalar=rstd[:, tt:tt + 1], in1=wv,
                        op0=ALU.mult, op1=ALU.mult), reads=[("xb", tt), "rstd", vkey], writes=[("hb", hbi)])
                    transpose_rows(hb[hbi], ("hb", hbi), 8, hT, tt, "hT")

            def transpose_rows(src, skey, nchunks, dstT, tt, dkey):
                for h0 in range(0, nchunks, 4):
                    nn = min(4, nchunks - h0)
                    pb, pk = PS(6)
                    pv = pb[:, (cnt["t"] % 2) * 512:(cnt["t"] % 2) * 512 + 512].rearrange("p (c t) -> p c t", c=4)
                    cnt["t"] += 1
                    for j in range(nn):
                        k.op("pe", lambda e, j=j, h0=h0: e.transpose(pv[:, j, :], src[:, (h0 + j) * 128:(h0 + j + 1) * 128], identb),
                             reads=[skey, "cstb"], writes=[pk])
                    copy_op(alt(), dstT[:, h0:h0 + nn, tt * 128:(tt + 1) * 128], pv[:, 0:nn, :],
                            reads=[], writes=[pk, (dkey, tt, h0)])

            def dump_hT(nm):
                dbg_dump(nm, hT[:, :, 0:128], [("hT", 0, 0), ("hT", 0, 4)], BF16)

            def hT_keys(t4, key="hT"):
                return [(key, tt, h0) for tt in range(t4 * 4, t4 * 4 + 4) for h0 in (0, 4)]

            def ffn(l, which, vbc, vkey):
                norm_to_hT(vbc, vkey, "ffn1_norm" if which == 0 else "ffn2_norm", "f")
                if not cnt.get("dbg2"):
                    cnt["dbg2"] = 1
                    dump_hT("hT")
                wg_l = wg_d[which][l].rearrange("(c p) f -> p c f", p=128)
                wu_l = wu_d[which][l].rearrange("(c p) f -> p c f", p=128)
                wd_l = wd_d[which][l]
                groups = []
                f0 = 0
                while f0 < DFF:
                    nf = min(GF, (DFF - f0) // 128)
                    groups.append((f0, nf))
                    f0 += nf * 128
                NG = len(groups)
                NT4 = TB // 512
                units = [(g, t4) for g in range(NG) for t4 in range(NT4)]

                def load_w(g):
                    if g >= NG:
                        return
                    wi = g % 2
                    f0, nf = groups[g]
                    k.dma(wgu[wi][:, 0, :, 0:nf * 128], wg_l[:, :, f0:f0 + nf * 128], writes=[("wgu", wi, 0)], q="pool")
                    k.dma(wgu[wi][:, 1, :, 0:nf * 128], wu_l[:, :, f0:f0 + nf * 128], writes=[("wgu", wi, 1)], q="pool")
                    k.dma(wdn[wi][:, 0:nf, :], wd_l[f0:f0 + nf * 128, :].rearrange("(c p) d -> p c d", p=128), writes=[("wdn", wi)], q="pool")

                ais = {}

                def front_pieces(u):
                    g, t4 = units[u]
                    wi = g % 2
                    nf = groups[g][1]
                    ai = cnt["a"] % 2
                    cnt["a"] += 1
                    ais[u] = ai
                    hk = hT_keys(t4)
                    pieces = []
                    for fc in range(nf):
                        def piece(fc=fc):
                            gi = cnt["g"] % 2
                            cnt["g"] += 1
                            pg, pgk = PS(gi)
                            pu, puk = PS(2 + gi)
                            for w, (pp, ppk) in enumerate(((pg, pgk), (pu, puk))):
                                for dc in range(8):
                                    k.op("pe", lambda e: e.matmul(
                                        pp[:], lhsT=wgu[wi][:, w, dc, fc * 128:(fc + 1) * 128],
                                        rhs=hT[:, dc, t4 * 512:(t4 + 1) * 512], start=(dc == 0), stop=(dc == 7)),
                                        reads=[("wgu", wi, w)] + hk, writes=[ppk])
                            k.op("act", lambda e: e.activation(out=sg[gi], in_=pg[:], func=AF.Silu),
                                 reads=[], writes=[pgk, ("sg", gi)])
                            k.op("dve", lambda e: e.tensor_tensor(
                                out=aT[ai][:, fc, :], in0=pu[:], in1=sg[gi], op=ALU.mult),
                                reads=[("sg", gi)], writes=[puk, ("aT", ai, fc)])
                        pieces.append(piece)
                    return pieces

                def back_pieces(u):
                    g, t4 = units[u]
                    wi = g % 2
                    nf = groups[g][1]
                    ai = ais[u]
                    pieces = []
                    for ts in range(4):
                        def piece(ts=ts):
                            tt = t4 * 4 + ts
                            for dh in range(2):
                                di = (4, 5, 7)[cnt["d"] % 3]
                                cnt["d"] += 1
                                pd, pdk = PS(di)
                                for fc in range(nf):
                                    k.op("pe", lambda e: e.matmul(
                                        pd[:], lhsT=aT[ai][:, fc, ts * 128:(ts + 1) * 128],
                                        rhs=wdn[wi][:, fc, dh * 512:(dh + 1) * 512], start=(fc == 0), stop=(fc == nf - 1)),
                                        reads=[("aT", ai, fc), ("wdn", wi)], writes=[pdk])
                                k.op("dve", lambda e: e.scalar_tensor_tensor(
                                    out=xb[:, tt, dh * 512:(dh + 1) * 512], in0=pd[:], scalar=0.5,
                                    in1=xb[:, tt, dh * 512:(dh + 1) * 512], op0=ALU.mult, op1=ALU.add),
                                    reads=[], writes=[pdk, ("xb", tt)])
                        pieces.append(piece)
                    return pieces

                load_w(0)
                load_w(1)
                for p_ in front_pieces(0):
                    p_()
                for u in range(len(units)):
                    fp = front_pieces(u + 1) if u + 1 < len(units) else []
                    bp = back_pieces(u)
                    n = max(len(fp), len(bp))
                    for i in range(n):
                        if i < len(fp):
                            fp[i]()
                        if i < len(bp):
                            bp[i]()
                    if units[u][1] == NT4 - 1:
                        load_w(units[u][0] + 2)

            def inproj(l, blk, vbc, vkey, after_norm=None):
                norm_to_hT(vbc, vkey, "mix_norm", "m")
                if after_norm is not None:
                    after_norm()
                win_l = win_d[l].rearrange("(c p) f -> p c f", p=128)
                t0 = blk * TB
                jobs = [("f", 512 + cg * 128, 128, cg) for cg in range(8)] + \
                       [("t", c0, ncol, j0) for (c0, ncol, j0) in ((0, 512, 0), (1536, 512, 512), (2048, 432, 1024))]

                def load_p(i):
                    if i >= len(jobs):
                        return
                    kind, c0, ncol, aux = jobs[i]
                    wi = i % 2
                    k.dma(wpj[wi][:, :, 0:ncol], win_l[:, :, c0:c0 + ncol], writes=[("wpj", wi)], q="pool")

                load_p(0)
                for i, (kind, c0, ncol, aux) in enumerate(jobs):
                    load_p(i + 1)
                    wi = i % 2
                    if kind == "f":
                        cg = aux
                        for t4 in range(TB // 512):
                            gi = cnt["g"] % 2
                            cnt["g"] += 1
                            pg, pgk = PS(gi)
                            hk = hT_keys(t4)
                            for dc in range(8):
                                k.op("pe", lambda e: e.matmul(
                                    pg[:], lhsT=wpj[wi][:, dc, 0:128], rhs=hT[:, dc, t4 * 512:(t4 + 1) * 512],
                                    start=(dc == 0), stop=(dc == 7)), reads=[("wpj", wi)] + hk, writes=[pgk])
                            ei = cnt["e"] % 2
                            cnt["e"] += 1
                            copy_op(alt(), evf[ei], pg[:], reads=[], writes=[pgk, ("evf", ei)])
                            k.dma(xbc_d[cg * 128:(cg + 1) * 128, t0 + t4 * 512:t0 + (t4 + 1) * 512], evf[ei],
                                  reads=[("evf", ei)], writes=[("xbc_d", cg)])
                    else:
                        j0 = aux
                        for tt in range(NTB):
                            gi = cnt["g"] % 2
                            cnt["g"] += 1
                            pu, puk = PS(2 + gi)
                            for dc in range(8):
                                k.op("pe", lambda e: e.matmul(
                                    pu[:, 0:ncol], lhsT=hT[:, dc, tt * 128:(tt + 1) * 128], rhs=wpj[wi][:, dc, 0:ncol],
                                    start=(dc == 0), stop=(dc == 7)),
                                    reads=[("wpj", wi), ("hT", tt, 0), ("hT", tt, 4)], writes=[puk])
                            ei = cnt["e"] % 2
                            cnt["e"] += 1
                            copy_op(alt(), evt[ei][:, 0:ncol], pu[:, 0:ncol], reads=[], writes=[puk, ("evt", ei)])
                            k.dma(ptok_d[t0 + tt * 128:t0 + (tt + 1) * 128, j0:j0 + ncol], evt[ei][:, 0:ncol],
                                  reads=[("evt", ei)], writes=[("ptok_d", blk)])

            def outproj(l, blk):
                t0 = blk * TB
                wout_l = wout_d[l].rearrange("(c p) f -> p c f", p=128)
                for tt in range(NTB):
                    hbi = tt % 2
                    k.dma(hb[hbi], y_d[t0 + tt * 128:t0 + (tt + 1) * 128, :], reads=[("y_d",)], writes=[("hb", hbi)])
                    transpose_rows(hb[hbi], ("hb", hbi), 8, hT, tt, "hT")
                for dh in range(2):
                    k.dma(wpj[dh], wout_l[:, :, dh * 512:(dh + 1) * 512], writes=[("wpj", dh)], q="pool")
                for dh in range(2):
                    wi = dh
                    for tt in range(NTB):
                        di = 4 + cnt["d"] % 2
                        cnt["d"] += 1
                        pd, pdk = PS(di)
                        for dc in range(8):
                            k.op("pe", lambda e, pd=pd, dc=dc, wi=wi, tt=tt: e.matmul(
                                pd[:], lhsT=hT[:, dc, tt * 128:(tt + 1) * 128], rhs=wpj[wi][:, dc, :],
                                start=(dc == 0), stop=(dc == 7)),
                                reads=[("wpj", wi), ("hT", tt, 0), ("hT", tt, 4)], writes=[pdk])
                        k.op("dve", lambda e, pd=pd, tt=tt, dh=dh: e.tensor_tensor(
                            out=xb[:, tt, dh * 512:(dh + 1) * 512], in0=pd[:], in1=xb[:, tt, dh * 512:(dh + 1) * 512],
                            op=ALU.add), reads=[], writes=[pdk, ("xb", tt)])

            stores = []
            src = x_d if pidx == 0 else out_d

            def load_x(blk):
                t0 = blk * TB
                for tt in range(NTB):
                    k.dma(xb[:, tt, :], src[t0 + tt * 128:t0 + (tt + 1) * 128, :],
                          reads=[("out_d", blk)], writes=[("xb", tt)])

            def store_x(blk):
                t0 = blk * TB
                for tt in range(NTB):
                    stores.append(k.dma(out_d[t0 + tt * 128:t0 + (tt + 1) * 128, :], xb[:, tt, :],
                                        reads=[("xb", tt)], writes=[("out_d", blk)]))

            load_x(0)
            for blk in range(NBLK):
                if l_prev is not None:
                    outproj(l_prev, blk)
                    ffn(l_prev, 1, vbcA, "vbcA")
                if l_next is not None:
                    ffn(l_next, 0, vbcB, "vbcB")
                    store_x(blk)
                    inproj(l_next, blk, vbcB, "vbcB",
                           after_norm=(lambda blk=blk: load_x(blk + 1)) if blk + 1 < NBLK else None)
                else:
                    store_x(blk)
                    if blk + 1 < NBLK:
                        load_x(blk + 1)
            return stores

        def mk_tmpset(pfx, n, nr):
            return {"pfx": pfx, "sq": A.f32(n), "ss": A.f32(8), "rs": A.f32(8), "t1": A.f32(nr), "t2": A.f32(nr)}

        def run_interleaved(gens):
            gens = list(gens)
            while gens:
                for g_ in list(gens):
                    try:
                        next(g_)
                    except StopIteration:
                        gens.remove(g_)

        def headnorm_rope(src, H, Dh, wrow, rlo, rhalf, cs, dst, tmp, keys_r, keys_w, cskey):
            sqv = tmp["sq"][:, 0:H * Dh].rearrange("p (h d) -> p h d", h=H)
            ssv = tmp["ss"][:, 0:H]
            rs = tmp["rs"][:, 0:H]
            qn = sqv
            t1 = tmp["t1"][:, 0:H * rhalf].rearrange("p (h d) -> p h d", h=H)
            t2 = tmp["t2"][:, 0:H * rhalf].rearrange("p (h d) -> p h d", h=H)
            p_ = tmp["pfx"]
            T = [p_ + "sq", p_ + "ss", p_ + "rs", p_ + "sq", p_ + "t1", p_ + "t2"]
            k.op("dve", lambda e: e.tensor_tensor(out=sqv, in0=src, in1=src, op=ALU.mult), reads=keys_r, writes=[T[0]])
            yield
            k.op("dve", lambda e: e.tensor_reduce(out=ssv, in_=sqv, axis=AX.X, op=ALU.add), reads=[T[0]], writes=[T[1]])
            yield
            k.op("act", lambda e: e.activation(out=rs, in_=ssv, func=AF.Sqrt, scale=1.0 / Dh, bias=EPS), reads=[T[1]], writes=[T[2]])
            yield
            k.op("dve", lambda e: e.reciprocal(out=ssv, in_=rs), reads=[T[2]], writes=[T[1]])
            yield
            k.op("dve", lambda e: e.tensor_tensor(out=qn, in0=src, in1=ssv.unsqueeze(2).to_broadcast([128, H, Dh]), op=ALU.mult),
                 reads=keys_r + [T[1]], writes=[T[3]])
            yield
            k.op("dve", lambda e: e.tensor_tensor(out=qn, in0=qn, in1=wrow.unsqueeze(1).to_broadcast([128, H, Dh]), op=ALU.mult),
                 reads=["vbc"], writes=[T[3]])
            yield
            if rlo > 0:
                k.op("dve", lambda e: e.tensor_copy(dst[:, :, 0:rlo], qn[:, :, 0:rlo]), reads=[T[3]], writes=keys_w)
                yield
            x1 = qn[:, :, rlo:rlo + rhalf]
            x2 = qn[:, :, rlo + rhalf:rlo + 2 * rhalf]
            cb = cs[:, 0:rhalf].unsqueeze(1).to_broadcast([128, H, rhalf])
            sb = cs[:, rhalf:2 * rhalf].unsqueeze(1).to_broadcast([128, H, rhalf])
            k.op("dve", lambda e: e.tensor_tensor(out=t1, in0=x1, in1=cb, op=ALU.mult), reads=[T[3], cskey], writes=[T[4]])
            yield
            k.op("dve", lambda e: e.tensor_tensor(out=t2, in0=x2, in1=sb, op=ALU.mult), reads=[T[3], cskey], writes=[T[5]])
            yield
            k.op("dve", lambda e: e.tensor_tensor(out=dst[:, :, rlo:rlo + rhalf], in0=t1, in1=t2, op=ALU.subtract),
                 reads=[T[4], T[5]], writes=keys_w)
            yield
            k.op("dve", lambda e: e.tensor_tensor(out=t1, in0=x1, in1=sb, op=ALU.mult), reads=[T[3], cskey], writes=[T[4]])
            yield
            k.op("dve", lambda e: e.tensor_tensor(out=t2, in0=x2, in1=cb, op=ALU.mult), reads=[T[3], cskey], writes=[T[5]])
            yield
            k.op("dve", lambda e: e.tensor_tensor(out=dst[:, :, rlo + rhalf:rlo + 2 * rhalf], in0=t1, in1=t2, op=ALU.add),
                 reads=[T[4], T[5]], writes=keys_w)
            yield

        def heads_to_T(src, skey, H, Dh, dstT, tt, dkey, bank):
            pb, pk = PS(6)
            pv = pb[:, 0:512].rearrange("p (c t) -> p c t", c=4)
            for h in range(H):
                k.op("pe", lambda e, h=h: e.transpose(pv[0:Dh, h, :], src[:, h, :], identb), reads=[skey, "cstb"], writes=[pk])
            copy_op(alt(), dstT[0:Dh, 0:H, tt * 128:(tt + 1) * 128], pv[0:Dh, 0:H, :], reads=[], writes=[pk, (dkey, tt)])

        def out_norm_store(osrc, okeys, width, wrow, col0, tmp, tt, ring, tag):
            k.op("dve", lambda e: e.memset(tmp["ss"][:, 0:1], 0.0), writes=["T_ss"])
            k.op("act", lambda e: e.activation(out=tmp["sq"][:, 0:width], in_=osrc, func=AF.Square, accum_out=tmp["ss"][:, 0:1]),
                 reads=okeys, writes=["T_sq", "T_ss"])
            k.op("act", lambda e: e.activation(out=tmp["rs"][:, 0:1], in_=tmp["ss"][:, 0:1], func=AF.Sqrt, scale=1.0 / width, bias=EPS),
                 reads=["T_ss"], writes=["T_rs"])
            k.op("dve", lambda e: e.reciprocal(out=tmp["ss"][:, 1:2], in_=tmp["rs"][:, 0:1]), reads=["T_rs"], writes=["T_ss"])
            yi = ring % 2
            yb = tmp["yb"][yi][:, 0:width]
            k.op("dve", lambda e: e.scalar_tensor_tensor(out=yb, in0=osrc, scalar=tmp["ss"][:, 1:2], in1=wrow,
                                                         op0=ALU.mult, op1=ALU.mult),
                 reads=list(okeys) + ["T_ss", "vbc"], writes=[(tag + "yb", yi)])
            k.dma(y_d[tt * 128:(tt + 1) * 128, col0:col0 + width], yb, reads=[(tag + "yb", yi)], writes=[("y_d",)])

        def attn_finish(acc_bank, ncols_q, nheads, tmp, odst_fn, extra_den, tag):
            pa, pak = PS(acc_bank)
            oT = tmp["oT"]
            k.op("act", lambda e: e.copy(oT[0:65, 0:ncols_q], pa[0:65, 0:ncols_q]), reads=[], writes=[pak, "T_oT"])
            for j in range(ncols_q // 128):
                pb, pk = PS(7)
                k.op("pe", lambda e, j=j: e.transpose(pb[:, 0:65], oT[0:65, j * 128:(j + 1) * 128], identf[0:65, 0:65]),
                     reads=["T_oT", "cst"], writes=[pk])
                den = tmp["den"]
                if extra_den is not None:
                    ed = extra_den(j)
                    k.op("dve", lambda e, ed=ed: e.tensor_tensor(out=den[:, 0:1], in0=pb[:, 64:65], in1=ed, op=ALU.add),
                         reads=["sinkexp"], writes=[pk, "T_den"])
                else:
                    k.op("dve", lambda e: e.tensor_copy(den[:, 0:1], pb[:, 64:65]), reads=[], writes=[pk, "T_den"])
                k.op("dve", lambda e: e.reciprocal(out=den[:, 1:2], in_=den[:, 0:1]), reads=["T_den"], writes=["T_rden"])
                dst, dkey = odst_fn(j)
                k.op("dve", lambda e, dst=dst: e.tensor_scalar_mul(out=dst, in0=pb[:, 0:64], scalar1=den[:, 1:2]), reads=["T_rden"], writes=[pk, dkey])

        def mla_phase(l, vbc):
            fbase, bbase = A.p, A.pb
            qT = A.bf16(4 * S).rearrange("p (h t) -> p h t", h=4)
            kT = A.bf16(4 * S).rearrange("p (h t) -> p h t", h=4)
            vx = A.bf16(NT * 4 * 65).rearrange("p (t h d) -> p t h d", t=NT, h=4)
            oall = A.f32(NT * 256).rearrange("p (t d) -> p t d", t=NT)
            wuq = A.bf16(2 * 384).rearrange("p (c f) -> p c f", c=2)
            wukv = A.bf16(512)
            pin = [A.f32(416) for _ in range(2)]
            csb = [A.f32(32) for _ in range(2)]
            nb = [A.bf16(384) for _ in range(2)]
            nT = A.bf16(3 * 128).rearrange("p (c t) -> p c t", c=3)
            qf = A.f32(384).rearrange("p (h d) -> p h d", h=4)
            kf = A.f32(384).rearrange("p (h d) -> p h d", h=4)
            qb = A.bf16(384).rearrange("p (h d) -> p h d", h=4)
            kb_ = A.bf16(384).rearrange("p (h d) -> p h d", h=4)
            tmp = {"sq": A.f32(384), "ss": A.f32(8), "rs": A.f32(8),
                   "oT": A.f32(512), "den": A.f32(2), "yb": [A.bf16(256) for _ in range(2)]}
            tsets = [mk_tmpset("Ta_", 384, 64), mk_tmpset("Tb_", 384, 64)]
            eb = [A.bf16(512) for _ in range(3)]
            k.dma(wuq, wuq_d[l].rearrange("(c p) f -> p c f", p=128), writes=["wuq"], q="pool")
            k.dma(wukv, wukv_d[l], writes=["wukv"], q="pool")
            k.op("dve", lambda e: e.memset(vx[:, :, :, 64:65], 1.0), writes=["vx1"])
            for tt in range(NT):
                pi = tt % 2
                k.dma(pin[pi], ptok_d[tt * 128:(tt + 1) * 128, 1040:1456], reads=[("ptok_d",)], writes=[("pin", pi)])
                k.dma(csb[pi], cs_mla_d[tt * 128:(tt + 1) * 128, :], writes=[("cs", pi)])
                for (c0, n, wn, tagn) in ((0, 256, "mla_q_lat_norm", "a"), (256, 128, "mla_kv_norm", "b")):
                    k.op("dve", lambda e: e.memset(tmp["ss"][:, 0:1], 0.0), writes=["T_ss"])
                    k.op("act", lambda e, c0=c0, n=n, pi=pi: e.activation(out=tmp["sq"][:, 0:n], in_=pin[pi][:, c0:c0 + n],
                                                                        func=AF.Square, accum_out=tmp["ss"][:, 0:1]),
                         reads=[("pin", pi)], writes=["T_sq", "T_ss"])
                    k.op("act", lambda e, n=n: e.activation(out=tmp["rs"][:, 0:1], in_=tmp["ss"][:, 0:1], func=AF.Sqrt,
                                                            scale=1.0 / n, bias=EPS), reads=["T_ss"], writes=["T_rs"])
                    k.op("dve", lambda e: e.reciprocal(out=tmp["ss"][:, 1:2], in_=tmp["rs"][:, 0:1]), reads=["T_rs"], writes=["T_ss"])
                    k.op("dve", lambda e, c0=c0, n=n, pi=pi, wn=wn: e.scalar_tensor_tensor(
                        out=nb[pi][:, c0:c0 + n], in0=pin[pi][:, c0:c0 + n], scalar=tmp["ss"][:, 1:2], in1=vrow(vbc, wn),
                        op0=ALU.mult, op1=ALU.mult), reads=[("pin", pi), "T_ss", "vbc"], writes=[("nb", pi, tagn)])
                pb, pk = PS(6)
                pv = pb[:, 512:1024].rearrange("p (c t) -> p c t", c=4)
                for c in range(3):
                    k.op("pe", lambda e, c=c, pi=pi: e.transpose(pv[:, c, :], nb[pi][:, c * 128:(c + 1) * 128], identb),
                         reads=[("nb", pi, "a"), ("nb", pi, "b"), "cstb"], writes=[pk])
                copy_op("act", nT, pv[:, 0:3, :], reads=[], writes=[pk, "nT"])
                pq, pqk = PS(0)
                for c in range(2):
                    k.op("pe", lambda e, c=c: e.matmul(pq[:, 0:384], lhsT=nT[:, c, :], rhs=wuq[:, c, :], start=(c == 0), stop=(c == 1)),
                         reads=["nT", "wuq"], writes=[pqk])
                pkv, pkvk = PS(1)
                k.op("pe", lambda e: e.matmul(pkv[:], lhsT=nT[:, 2, :], rhs=wukv, start=True, stop=True), reads=["nT", "wukv"], writes=[pkvk])
                copy_op("act", qf, pq[:, 0:384].rearrange("p (h d) -> p h d", h=4), reads=[], writes=[pqk, "qf"])
                kvv = pkv[:].rearrange("p (h d) -> p h d", h=4)
                copy_op("act", kf[:, :, 0:64], kvv[:, :, 0:64], reads=[], writes=[pkvk, "kf"])
                copy_op("dve", vx[:, tt, :, 0:64], kvv[:, :, 64:128], reads=[], writes=[pkvk, ("vx", tt)])
                k.op("dve", lambda e, pi=pi: e.tensor_copy(kf[:, :, 64:96], pin[pi][:, 384:416].unsqueeze(1).to_broadcast([128, 4, 32])),
                     reads=[("pin", pi)], writes=["kf"])
                run_interleaved([
                    headnorm_rope(qf, 4, 96, vrow(vbc, "mla_q_norm"), 64, 16, csb[pi], qb, tsets[0], ["qf"], ["qb"], ("cs", pi)),
                    headnorm_rope(kf, 4, 96, vrow(vbc, "mla_k_norm"), 64, 16, csb[pi], kb_, tsets[1], ["kf"], ["kb"], ("cs", pi))])
                heads_to_T(qb, "qb", 4, 96, qT, tt, "qT", 6)
                heads_to_T(kb_, "kb", 4, 96, kT, tt, "kT", 6)
            scale = 96.0 ** -0.5
            qkeys = lambda qi: [("qT", tt) for tt in range(qi * 4, qi * 4 + 4)]
            items = [(h, qi, kt) for h in range(4) for qi in range(S // 512) for kt in range(NT)]
            LA = 2

            def stA(i):
                h, qi, kt = items[i]
                psn, psk = PS(i % 3)
                k.op("pe", lambda e: e.matmul(
                    psn[:], lhsT=kT[0:96, h, kt * 128:(kt + 1) * 128], rhs=qT[0:96, h, qi * 512:(qi + 1) * 512],
                    start=True, stop=True), reads=[("kT", kt)] + qkeys(qi), writes=[psk])

            def stBC(i):
                h, qi, kt = items[i]
                psn, psk = PS(i % 3)
                ebi = i % 3
                blk = h * (S // 512) + qi
                pa, pak = PS(4 + (blk % 2))
                k.op("act", lambda e: e.activation(out=eb[ebi], in_=psn[:], func=AF.Exp, scale=scale),
                     reads=[], writes=[psk, ("eb", ebi)])
                k.op("pe", lambda e: e.matmul(
                    pa[0:65, :], lhsT=vx[:, kt, h, :], rhs=eb[ebi], start=(kt == 0), stop=(kt == NT - 1)),
                    reads=[("vx", kt), "vx1", ("eb", ebi)], writes=[pak])
                if kt == NT - 1:
                    attn_finish(4 + (blk % 2), 512, 1, tmp,
                                lambda j, h=h, qi=qi: (oall[:, qi * 4 + j, h * 64:(h + 1) * 64], ("oall", qi * 4 + j, h)),
                                None, "m")

            for i in range(min(LA, len(items))):
                stA(i)
            for i in range(len(items)):
                if i + LA < len(items):
                    stA(i + LA)
                stBC(i)
            for tt in range(NT):
                okeys = [("oall", tt, h) for h in range(4)]
                out_norm_store(oall[:, tt, :], okeys, 256, vrow(vbc, "mla_out_norm"), 768, tmp, tt, tt, "mo")
            A.reset(fbase, bbase)

        def swa_phase(l, vbc):
            fbase, bbase = A.p, A.pb
            qT = A.bf16(4 * S).rearrange("p (h t) -> p h t", h=4)
            kT = A.bf16(2 * S).rearrange("p (h t) -> p h t", h=2)
            vx = A.bf16(NT * 2 * 65).rearrange("p (t h d) -> p t h d", t=NT, h=2)
            oall = A.f32(NT * 256).rearrange("p (t d) -> p t d", t=NT)
            pin = [A.f32(512) for _ in range(2)]
            csb = [A.f32(64) for _ in range(2)]
            qb = [A.bf16(256).rearrange("p (h d) -> p h d", h=4) for _ in range(2)]
            kb_ = [A.bf16(128).rearrange("p (h d) -> p h d", h=2) for _ in range(2)]
            tmp = {"sq": A.f32(256), "ss": A.f32(8), "rs": A.f32(8),
                   "oT": A.f32(512), "den": A.f32(2), "yb": [A.bf16(256) for _ in range(2)]}
            tsets = [mk_tmpset("T%d_" % i, 256, 128) for i in range(4)]
            sinkexp = A.f32(4)
            eb = [A.bf16(256) for _ in range(3)]
            k.op("act", lambda e: e.activation(out=sinkexp, in_=vrow(vbc, "swa_sink"), func=AF.Exp), reads=["vbc"], writes=["sinkexp"])
            k.op("dve", lambda e: e.memset(vx[:, :, :, 64:65], 1.0), writes=["vx1"])
            for tt0 in range(0, NT, 2):
                gens = []
                for u in range(2):
                    tt = tt0 + u
                    pi = tt % 2
                    k.dma(pin[pi], ptok_d[tt * 128:(tt + 1) * 128, 528:1040], reads=[("ptok_d",)], writes=[("pin", pi)])
                    k.dma(csb[pi], cs_swa_d[tt * 128:(tt + 1) * 128, :], writes=[("cs", pi)])
                    qsrc = pin[pi][:, 0:256].rearrange("p (h d) -> p h d", h=4)
                    ksrc = pin[pi][:, 256:384].rearrange("p (h d) -> p h d", h=2)
                    gens.append(headnorm_rope(qsrc, 4, 64, vrow(vbc, "swa_q_norm"), 0, 32, csb[pi], qb[pi], tsets[2 * u],
                                              [("pin", pi)], [("qb", pi)], ("cs", pi)))
                    gens.append(headnorm_rope(ksrc, 2, 64, vrow(vbc, "swa_k_norm"), 0, 32, csb[pi], kb_[pi], tsets[2 * u + 1],
                                              [("pin", pi)], [("kb", pi)], ("cs", pi)))
                run_interleaved(gens)
                for u in range(2):
                    tt = tt0 + u
                    pi = tt % 2
                    k.op("dve", lambda e: e.tensor_copy(vx[:, tt, :, 0:64], pin[pi][:, 384:512].rearrange("p (h d) -> p h d", h=2)),
                         reads=[("pin", pi)], writes=[("vx", tt)])
                    heads_to_T(qb[pi], ("qb", pi), 4, 64, qT, tt, "qT", 6)
                    heads_to_T(kb_[pi], ("kb", pi), 2, 64, kT, tt, "kT", 6)
            scale = 64.0 ** -0.5
            items = []
            for n in range(NT):
                for kvh in range(2):
                    js = [j for j in (n - 1, n, n + 1) if 0 <= j < NT]
                    for ji, j in enumerate(js):
                        items.append((n, kvh, j, ji, len(js)))
            LA = 2
            pend = []

            def stA(i):
                n, kvh, j, ji, nj = items[i]
                psn, psk = PS(i % 3)
                k.op("pe", lambda e: e.matmul(
                    psn[:, 0:256].rearrange("p (a b) -> p a b", a=2), lhsT=kT[0:64, kvh, j * 128:(j + 1) * 128],
                    rhs=qT[0:64, 2 * kvh:2 * kvh + 2, n * 128:(n + 1) * 128], start=True, stop=True),
                    reads=[("kT", j), ("qT", n)], writes=[psk])

            def stBC(i):
                n, kvh, j, ji, nj = items[i]
                psn, psk = PS(i % 3)
                ebi = i % 3
                blk = n * 2 + kvh
                pa, pak = PS(4 + (blk % 2))
                k.op("act", lambda e: e.activation(out=eb[ebi], in_=psn[:, 0:256], func=AF.Exp, scale=scale),
                     reads=[], writes=[psk, ("eb", ebi)])
                if j != n:
                    mk = cstb[:, 128:384] if j < n else cstb[:, 384:640]
                    k.op("dve", lambda e: e.tensor_tensor(out=eb[ebi], in0=eb[ebi], in1=mk, op=ALU.mult),
                         reads=["cstb"], writes=[("eb", ebi)])
                k.op("pe", lambda e: e.matmul(
                    pa[0:65, 0:256], lhsT=vx[:, j, kvh, :], rhs=eb[ebi], start=(ji == 0), stop=(ji == nj - 1)),
                    reads=[("vx", j), "vx1", ("eb", ebi)], writes=[pak])
                if ji == 0 and pend:
                    pend.pop(0)()
                if ji == nj - 1:
                    pend.append(lambda n=n, kvh=kvh, blk=blk: attn_finish(
                        4 + (blk % 2), 256, 2, tmp,
                        lambda jj: (oall[:, n, (2 * kvh + jj) * 64:(2 * kvh + jj + 1) * 64], ("oall", n, 2 * kvh + jj)),
                        lambda jj: sinkexp[:, 2 * kvh + jj:2 * kvh + jj + 1], "s"))

            for i in range(min(LA, len(items))):
                stA(i)
            for i in range(len(items)):
                if i + LA < len(items):
                    stA(i + LA)
                stBC(i)
            while pend:
                pend.pop(0)()
            for tt in range(NT):
                okeys = [("oall", tt, h) for h in range(4)]
                out_norm_store(oall[:, tt, :], okeys, 256, vrow(vbc, "swa_out_norm"), 512, tmp, tt, tt, "so")
            A.reset(fbase, bbase)

        def ssd_phase(l, vbc):
            fbase, bbase = A.p, A.pb
            convw = A.f32(48).rearrange("p (c k) -> p c k", c=8)
            BT = A.bf16(2 * S).rearrange("p (g t) -> p g t", g=2)
            CT = A.bf16(2 * S).rearrange("p (g t) -> p g t", g=2)
            dt = A.f32(NT * 16).rearrange("p (t d) -> p t d", t=NT)
            av = A.f32(NT * 16).rearrange("p (t d) -> p t d", t=NT)
            cum = A.f32(NT * 16).rearrange("p (t d) -> p t d", t=NT)
            ncf = A.f32(NT * 8).rearrange("p (t d) -> p t d", t=NT)
            sc1 = A.f32(NT * 16).rearrange("p (t d) -> p t d", t=NT)
            sc2 = A.f32(NT * 16).rearrange("p (t d) -> p t d", t=NT)
            dw = A.f32(NT * 16).rearrange("p (t d) -> p t d", t=NT)
            tot = A.f32(NCH * 16).rearrange("p (c d) -> p c d", c=NCH)
            etot = A.f32(NCH * 16).rearrange("p (c d) -> p c d", c=NCH)
            Abc = A.f32(16)
            t16 = [A.f32(NT * 16).rearrange("p (t d) -> p t d", t=NT) for _ in range(3)]
            cumT = [A.f32(512).rearrange("p (d l) -> p d l", d=2) for _ in range(2)]
            Rst = [A.f32(256) for _ in range(2)]
            Rtmp = A.f32(256)
            Eb = [A.f32(256) for _ in range(4)]
            ya = [A.f32(256) for _ in range(2)]
            yg = [A.f32(256) for _ in range(2)]
            zb = [A.f32(256) for _ in range(2)]
            szb = [A.f32(256) for _ in range(2)]
            tmp = {"sq": A.f32(256), "ss": A.f32(8), "rs": A.f32(8), "yb": [A.bf16(256) for _ in range(2)]}
            xin = [A.bf16(S + 4)] * 2
            dg = A.bf16(5 * 128).rearrange("p (k c) -> p k c", k=5)
            xTf = A.bf16(2 * S).rearrange("p (c t) -> p c t", c=2)
            xs = A.bf16(NT * 256).rearrange("p (t d) -> p t d", t=NT)
            Btok = A.bf16(NT * 128).rearrange("p (t d) -> p t d", t=NT)
            Sinb = [A.bf16((NCH + 1) * 256).rearrange("p (c d) -> p c d", c=NCH + 1) for _ in range(2)]
            Mb = [A.bf16(256) for _ in range(4)]
            GT = [A.bf16(512) for _ in range(2)]
            xdt = [A.bf16(2 * 2 * 256).rearrange("p (r t d) -> p r t d", r=2, t=2) for _ in range(2)]
            xdw = [A.bf16(2 * 256).rearrange("p (t d) -> p t d", t=2) for _ in range(2)]
            k.dma(convw, convp_d[l].rearrange("p (c k) -> p c k", c=8), writes=["convw"])
            ci = [0]

            def conv_chunk(cc, dst_fn):
                xi = 0
                k.op("dve", lambda e, xi=xi: e.memset(xin[xi][:, 0:2], 0.0), writes=[("xin", xi)])
                k.op("dve", lambda e, xi=xi: e.memset(xin[xi][:, S + 2:S + 4], 0.0), writes=[("xin", xi)])
                k.dma(xin[xi][:, 2:S + 2], xbc_d[cc * 128:(cc + 1) * 128, :], reads=[("xbc_d",)], writes=[("xin", xi)])
                for kk in range(5):
                    k.op("dve", lambda e, kk=kk, cc=cc: e.tensor_scalar_mul(out=dg[:, kk, :], in0=identf, scalar1=convw[:, cc, kk:kk + 1]),
                         reads=["cst", "convw"], writes=["dg"])
                for t4 in range(S // 512):
                    pb, pk = PS(ci[0] % 2)
                    ci[0] += 1
                    for kk in range(5):
                        k.op("pe", lambda e, pb=pb, kk=kk, xi=xi, t4=t4: e.matmul(
                            pb[:], lhsT=dg[:, kk, :], rhs=xin[xi][:, t4 * 512 + kk:t4 * 512 + kk + 512],
                            start=(kk == 0), stop=(kk == 4)), reads=["dg", ("xin", xi)], writes=[pk])
                    dst, dkey = dst_fn(t4)
                    k.op("act", lambda e, pb=pb, dst=dst, cc=cc: e.activation(out=dst, in_=pb[:], func=AF.Silu, bias=convw[:, cc, 5:6]),
                         reads=["convw"], writes=[pk, dkey])
                ci[0] += 1

            for g in range(2):
                conv_chunk(4 + g, lambda t4, g=g: (BT[:, g, t4 * 512:(t4 + 1) * 512], ("BT", g, t4)))
                conv_chunk(6 + g, lambda t4, g=g: (CT[:, g, t4 * 512:(t4 + 1) * 512], ("CT", g, t4)))
            for tt in range(NT):
                k.dma(dt[:, tt, :], ptok_d[tt * 128:(tt + 1) * 128, 512:528], reads=[("ptok_d",)], writes=["dt"])
            bias_bc = vrow(vbc, "ssd_dt_bias").unsqueeze(1).to_broadcast([128, NT, 16])
            k.op("dve", lambda e: e.tensor_tensor(out=dt, in0=dt, in1=bias_bc, op=ALU.add), reads=["vbc"], writes=["dt"])
            k.op("dve", lambda e: e.tensor_scalar_mul(out=t16[0], in0=dt, scalar1=-1.0), reads=["dt"], writes=["t16a"])
            k.op("dve", lambda e: e.tensor_tensor(out=t16[0], in0=t16[0], in1=dt, op=ALU.max), reads=["dt"], writes=["t16a"])
            k.op("act", lambda e: e.activation(out=t16[1], in_=t16[0], func=AF.Exp, scale=-1.0), reads=["t16a"], writes=["t16b"])
            k.op("act", lambda e: e.activation(out=t16[0], in_=t16[1], func=AF.Ln, bias=1.0), reads=["t16b"], writes=["t16a"])
            k.op("dve", lambda e: e.tensor_scalar_max(out=t16[1], in0=dt, scalar1=0.0), reads=["dt"], writes=["t16b"])
            k.op("dve", lambda e: e.tensor_tensor(out=dt, in0=t16[0], in1=t16[1], op=ALU.add), reads=["t16a", "t16b"], writes=["dt"])
            k.op("act", lambda e: e.activation(out=Abc, in_=vrow(vbc, "ssd_a_log"), func=AF.Exp), reads=["vbc"], writes=["Abc"])
            k.op("dve", lambda e: e.scalar_tensor_tensor(out=av, in0=dt, scalar=-1.0, in1=Abc.unsqueeze(1).to_broadcast([128, NT, 16]),
                                                         op0=ALU.mult, op1=ALU.mult), reads=["dt", "Abc"], writes=["av"])
            dbg_dump("ssd_dt", dt, ["dt"])
            TRI_LE = cst[:, C_R0F:C_R0F + 128]
            TRI_LT = cst[:, C_R0B:C_R0B + 128]
            for c in range(NCH):
                t0_, t1_ = 2 * c, 2 * c + 1
                pb, pk = PS(c % 2)
                pv = pb[:, 0:48].rearrange("p (t d) -> p t d", t=3)
                for d, tri in ((0, TRI_LE), (1, TRI_LT)):
                    cs_ = slice(d * 8, d * 8 + 8)
                    k.op("pe", lambda e, pv=pv, tri=tri, cs_=cs_, t0_=t0_: e.matmul(pv[:, 0, cs_], lhsT=tri, rhs=av[:, t0_, cs_], start=True, stop=True),
                         reads=["av", "cst"], writes=[pk])
                    k.op("pe", lambda e, pv=pv, cs_=cs_, t0_=t0_: e.matmul(pv[:, 1, cs_], lhsT=ones_f, rhs=av[:, t0_, cs_], start=False, stop=False,
                                                                          skip_group_check=True), reads=["av", "cst"], writes=[pk])
                    k.op("pe", lambda e, pv=pv, tri=tri, cs_=cs_, t1_=t1_: e.matmul(pv[:, 1, cs_], lhsT=tri, rhs=av[:, t1_, cs_], start=False, stop=True,
                                                                                  skip_group_check=True), reads=["av", "cst"], writes=[pk])
                k.op("pe", lambda e, pv=pv, t0_=t0_: e.matmul(pv[:, 2, :], lhsT=ones_f, rhs=av[:, t0_, :], start=False, stop=False, skip_group_check=True),
                     reads=["av", "cst"], writes=[pk])
                k.op("pe", lambda e, pv=pv, t1_=t1_: e.matmul(pv[:, 2, :], lhsT=ones_f, rhs=av[:, t1_, :], start=False, stop=True, skip_group_check=True),
                     reads=["av", "cst"], writes=[pk])
                copy_op("dve", cum[:, t0_:t0_ + 2, :], pv[:, 0:2, :], reads=[], writes=[pk, "cum"])
                copy_op("dve", tot[:, c, :], pv[:, 2, :], reads=[], writes=[pk, "tot"])
            totb = lambda d: tot[:, :, d * 8:d * 8 + 8].unsqueeze(2).to_broadcast([128, NCH, 2, 8])
            v4 = lambda a, d: a[:, :, d * 8:d * 8 + 8].rearrange("p (c t) h -> p c t h", t=2)
            k.op("dve", lambda e: e.tensor_scalar_mul(out=ncf, in0=cum[:, :, 0:8], scalar1=-1.0), reads=["cum"], writes=["ncf"])
            k.op("act", lambda e: e.activation(out=sc1[:, :, 0:8], in_=cum[:, :, 0:8], func=AF.Exp), reads=["cum"], writes=["sc1a"])
            k.op("act", lambda e: e.activation(out=sc2[:, :, 8:16], in_=cum[:, :, 8:16], func=AF.Exp), reads=["cum"], writes=["sc2b"])
            k.op("dve", lambda e: e.tensor_tensor(out=v4(t16[2], 0), in0=totb(0), in1=v4(cum, 0), op=ALU.subtract), reads=["cum", "tot"], writes=["t16c0"])
            k.op("dve", lambda e: e.tensor_tensor(out=v4(t16[2], 1), in0=totb(1), in1=v4(cum, 1), op=ALU.subtract), reads=["cum", "tot"], writes=["t16c1"])
            k.op("act", lambda e: e.activation(out=sc2[:, :, 0:8], in_=t16[2][:, :, 0:8], func=AF.Exp), reads=["t16c0"], writes=["sc2a"])
            k.op("act", lambda e: e.activation(out=sc1[:, :, 8:16], in_=t16[2][:, :, 8:16], func=AF.Exp), reads=["t16c1"], writes=["sc1b"])
            k.op("act", lambda e: e.activation(out=etot, in_=tot, func=AF.Exp), reads=["tot"], writes=["etot"])
            k.op("dve", lambda e: e.tensor_tensor(out=dw, in0=dt, in1=sc2, op=ALU.mult), reads=["dt", "sc2a", "sc2b"], writes=["dw"])
            dbg_dump("ssd_cum", cum, ["cum"])
            Dbc = vrow(vbc, "ssd_d")
            v3 = lambda a: a.rearrange("p (h d) -> p h d", h=4)
            for g in range(2):
                for j in range(2):
                    conv_chunk(2 * g + j, lambda t4, j=j: (xTf[:, j, t4 * 512:(t4 + 1) * 512], ("xTf", t4)))
                for tt in range(NT):
                    pb, pk = PS(6)
                    pv = pb[:, 0:256].rearrange("p (c t) -> p c t", c=2)
                    for j in range(2):
                        k.op("pe", lambda e, pv=pv, j=j, tt=tt: e.transpose(pv[:, j, :], xTf[:, j, tt * 128:(tt + 1) * 128], identb),
                             reads=[("xTf", tt // 4), "cstb"], writes=[pk])
                    k.op("pe", lambda e, pb=pb, tt=tt: e.transpose(pb[:, 256:384], BT[:, g, tt * 128:(tt + 1) * 128], identb),
                         reads=[("BT", g, tt // 4), "cstb"], writes=[pk])
                    copy_op("act", xs[:, tt, :], pb[:, 0:256], reads=[], writes=[pk, ("xs", tt)])
                    copy_op("dve", Btok[:, tt, :], pb[:, 256:384], reads=[], writes=[pk, ("Btok", tt)])
                if g == 0:
                    dbg_dump("ssd_xs", xs, [("xs", tt) for tt in range(NT)], BF16)
                xs4 = xs.rearrange("p t (h d) -> p t h d", h=4)
                wi = 0
                for d in range(2):
                    k.op("dve", lambda e, d=d: e.memset(Rst[d], 0.0), writes=[("Rst", d)])
                    order = list(range(NCH)) if d == 0 else list(range(NCH - 1, -1, -1))
                    first_slot = 0 if d == 0 else NCH
                    k.op("dve", lambda e, d=d, first_slot=first_slot: e.memset(Sinb[d][:, first_slot, :], 0.0), writes=[("Sinb", d, first_slot)])
                    for c in order:
                        wr = wi % 2
                        wi += 1
                        dwb = dw[:, 2 * c:2 * c + 2, d * 8 + 4 * g:d * 8 + 4 * g + 4].unsqueeze(3).to_broadcast([128, 2, 4, 64])
                        k.op("dve", lambda e, wr=wr, dwb=dwb, c=c: e.tensor_tensor(out=xdw[wr].rearrange("p t (h d) -> p t h d", h=4),
                                                                                in0=xs4[:, 2 * c:2 * c + 2, :, :], in1=dwb, op=ALU.mult),
                             reads=[("xs", 2 * c), ("xs", 2 * c + 1), "dw"], writes=[("xdw", wr)])
                        pb, pk = PS(c % 2)
                        for ti in range(2):
                            tt = 2 * c + ti
                            k.op("pe", lambda e, pb=pb, tt=tt, ti=ti, wr=wr: e.matmul(
                                pb[:, 0:256], lhsT=Btok[:, tt, :], rhs=xdw[wr][:, ti, :], start=(ti == 0), stop=(ti == 1)),
                                reads=[("Btok", tt), ("xdw", wr)], writes=[pk])
                        etb = etot[:, c, d * 8 + 4 * g:d * 8 + 4 * g + 4].unsqueeze(2).to_broadcast([128, 4, 64])
                        k.op("dve", lambda e, d=d, etb=etb: e.tensor_tensor(out=v3(Rtmp), in0=v3(Rst[d]), in1=etb, op=ALU.mult),
                             reads=[("Rst", d), "etot"], writes=["Rtmp"])
                        k.op("dve", lambda e, pb=pb, d=d: e.tensor_tensor(out=Rst[d], in0=pb[:, 0:256], in1=Rtmp, op=ALU.add),
                             reads=["Rtmp"], writes=[pk, ("Rst", d)])
                        slot = c + 1 if d == 0 else c
                        k.op("act", lambda e, d=d, slot=slot: e.copy(Sinb[d][:, slot, :], Rst[d]), reads=[("Rst", d)], writes=[("Sinb", d, slot)])
                def prologue(c):
                    t0_, t1_ = 2 * c, 2 * c + 1
                    cr = c % 2
                    pb2, pk2 = PS(7)
                    for d, (r0, r1) in ((0, (C_R0F, C_R1F)), (1, (C_R0B, C_R1B))):
                        cs_ = slice(d * 8, d * 8 + 8)
                        o_ = pb2[0:8, d * 256:(d + 1) * 256]
                        k.op("pe", lambda e: e.matmul(o_, lhsT=av[:, t0_, cs_], rhs=cst[:, r0:r0 + 256],
                                                      start=(d == 0), stop=False, skip_group_check=True),
                             reads=["av", "cst"], writes=[pk2])
                        k.op("pe", lambda e: e.matmul(o_, lhsT=av[:, t1_, cs_], rhs=cst[:, r1:r1 + 256],
                                                      start=False, stop=True, skip_group_check=True),
                             reads=["av", "cst"], writes=[pk2])
                    copy_op("act", cumT[cr][0:8, :, :], pb2[0:8, 0:512].rearrange("p (d l) -> p d l", d=2), reads=[], writes=[pk2, ("cumT", cr)])
                    for d in range(2):
                        dtb = dt[:, 2 * c:2 * c + 2, d * 8 + 4 * g:d * 8 + 4 * g + 4].unsqueeze(3).to_broadcast([128, 2, 4, 64])
                        k.op("dve", lambda e: e.tensor_tensor(out=xdt[d][:, cr].rearrange("p t (h d) -> p t h d", h=4),
                                                              in0=xs4[:, 2 * c:2 * c + 2, :, :], in1=dtb, op=ALU.mult),
                             reads=[("xs", 2 * c), ("xs", 2 * c + 1), "dt"], writes=[("xdt", d, cr)])
                    pb, pk = PS(7)
                    for si in range(2):
                        k.op("pe", lambda e: e.matmul(pb[:, si * 256:(si + 1) * 256], lhsT=BT[:, g, (2 * c + si) * 128:(2 * c + si + 1) * 128],
                                                      rhs=CT[:, g, c * 256:(c + 1) * 256], start=(si == 0), stop=True, skip_group_check=True),
                             reads=[("BT", g, (2 * c + si) // 4), ("CT", g, c // 2)], writes=[pk])
                    copy_op("act", GT[cr], pb[:, 0:512], reads=[], writes=[pk, ("GT", cr)])

                def item_front(c, it):
                    h, d = it // 2, it % 2
                    hh = 4 * g + h
                    cr = c % 2
                    sel = cst[0:8, C_SEL + hh * 128:C_SEL + (hh + 1) * 128]
                    for si in range(2):
                        pb, pk = PS(2 * (it % 2) + si)
                        if d == 0:
                            lo, hi = (0, 256) if si == 0 else (128, 256)
                            mask = cstb[:, 640:896] if si == 0 else cstb[:, 640:768]
                        else:
                            lo, hi = (0, 128) if si == 0 else (0, 256)
                            mask = cstb[:, 896 + 128:896 + 256] if si == 0 else cstb[:, 896:896 + 256]
                        w = hi - lo
                        k.op("pe", lambda e: e.matmul(pb[:, 0:w], lhsT=sel, rhs=cumT[cr][0:8, d, lo:hi], start=True, stop=False),
                             reads=[("cumT", cr), "cst"], writes=[pk])
                        k.op("pe", lambda e: e.matmul(pb[:, 0:w], lhsT=identb, rhs=mask, start=False, stop=True),
                             reads=["cstb"], writes=[pk])

                def item_back(c, it, started, ybank, ykey):
                    h, d = it // 2, it % 2
                    hh = 4 * g + h
                    cr = c % 2
                    mms = []
                    for si in range(2):
                        pb, pk = PS(2 * (it % 2) + si)
                        if d == 0:
                            lo, hi = (0, 256) if si == 0 else (128, 256)
                        else:
                            lo, hi = (0, 128) if si == 0 else (0, 256)
                        w = hi - lo
                        ebi = 2 * d + si
                        tt_s = 2 * c + si
                        if d == 0:
                            k.op("act", lambda e: e.activation(out=Eb[ebi][:, 0:w], in_=pb[:, 0:w], func=AF.Exp, bias=ncf[:, tt_s, hh:hh + 1], scale=1.0),
                                 reads=["ncf"], writes=[pk, ("Eb", ebi)])
                        else:
                            k.op("act", lambda e: e.activation(out=Eb[ebi][:, 0:w], in_=pb[:, 0:w], func=AF.Exp, bias=cum[:, tt_s, 8 + hh:9 + hh], scale=-1.0),
                                 reads=["cum"], writes=[pk, ("Eb", ebi)])
                        k.op("dve", lambda e: e.tensor_tensor(out=Mb[ebi][:, 0:w], in0=Eb[ebi][:, 0:w], in1=GT[cr][:, si * 256 + lo:si * 256 + hi], op=ALU.mult),
                             reads=[("Eb", ebi), ("GT", cr)], writes=[("Mb", ebi)])
                        for li in range(2):
                            if lo <= li * 128 < hi:
                                mms.append((ebi, li * 128 - lo, li, si))
                    for (ebi, off, li, si) in mms:
                        st_ = not started[li]
                        started[li] = True
                        k.op("pe", lambda e: e.matmul(
                            ybank[li][:, h * 64:(h + 1) * 64], lhsT=Mb[ebi][:, off:off + 128], rhs=xdt[d][:, cr, si, h * 64:(h + 1) * 64],
                            start=st_, stop=False, skip_group_check=True),
                            reads=[("Mb", ebi), ("xdt", d, cr)], writes=[ykey[li]])

                def epilogue(c, ybank, ykey):
                    for li in range(2):
                        tt = 2 * c + li
                        pf7, pfk = PS(6 if False else 7)
                        pf = pf7[:, 0:256]
                        pbw = pf7[:, 256:512]
                        k.op("pe", lambda e: e.matmul(pf, lhsT=CT[:, g, tt * 128:(tt + 1) * 128], rhs=Sinb[0][:, c, :],
                                                      start=True, stop=True), reads=[("CT", g, tt // 4), ("Sinb", 0, c)], writes=[pfk])
                        k.op("pe", lambda e: e.matmul(pbw, lhsT=CT[:, g, tt * 128:(tt + 1) * 128], rhs=Sinb[1][:, c + 1, :],
                                                      start=False, stop=True, skip_group_check=True),
                             reads=[("CT", g, tt // 4), ("Sinb", 1, c + 1)], writes=[pfk])
                        ecfb = sc1[:, tt, 4 * g:4 * g + 4].unsqueeze(2).to_broadcast([128, 4, 64])
                        erbb = sc1[:, tt, 8 + 4 * g:12 + 4 * g].unsqueeze(2).to_broadcast([128, 4, 64])
                        dbb = Dbc[:, 4 * g:4 * g + 4].unsqueeze(2).to_broadcast([128, 4, 64])
                        yi = li
                        k.op("dve", lambda e: e.tensor_tensor(out=v3(ya[yi]), in0=v3(pf), in1=ecfb, op=ALU.mult),
                             reads=["sc1a"], writes=[pfk, ("ya", yi)])
                        k.op("dve", lambda e: e.tensor_tensor(out=v3(Rtmp), in0=v3(pbw), in1=erbb, op=ALU.mult),
                             reads=["sc1b"], writes=[pfk, "Rtmp"])
                        k.op("dve", lambda e: e.tensor_tensor(out=ya[yi], in0=ya[yi], in1=Rtmp, op=ALU.add), reads=["Rtmp"], writes=[("ya", yi)])
                        k.op("dve", lambda e: e.tensor_tensor(out=v3(Rtmp), in0=xs4[:, tt, :, :], in1=dbb, op=ALU.mult),
                             reads=[("xs", tt), "vbc"], writes=["Rtmp"])
                        k.op("dve", lambda e: e.tensor_tensor(out=ya[yi], in0=ya[yi], in1=Rtmp, op=ALU.add), reads=["Rtmp"], writes=[("ya", yi)])
                        k.op("dve", lambda e: e.tensor_tensor(out=ya[yi], in0=ybank[li][:, 0:256], in1=ya[yi], op=ALU.add),
                             reads=[], writes=[ykey[li], ("ya", yi)])
                        zi = li
                        k.dma(zb[zi], ptok_d[tt * 128:(tt + 1) * 128, g * 256:(g + 1) * 256], reads=[("ptok_d",)], writes=[("zb", zi)])
                        k.op("act", lambda e: e.activation(out=szb[zi], in_=zb[zi], func=AF.Silu), reads=[("zb", zi)], writes=[("szb", zi)])
                        k.op("dve", lambda e: e.tensor_tensor(out=yg[yi], in0=ya[yi], in1=szb[zi], op=ALU.mult),
                             reads=[("szb", zi), ("ya", yi)], writes=[("yg", yi)])
                        out_norm_store(yg[yi], [("yg", yi)], 256, vrow(vbc, "ssd_norm", g * 256, 256), g * 256, tmp, tt, tt, "do")

                prologue(0)
                for c in range(NCH):
                    yb0, yk0 = PS(4)
                    yb1, yk1 = PS(5)
                    ybank = (yb0, yb1)
                    ykey = (yk0, yk1)
                    started = [False, False]
                    item_front(c, 0)
                    for it in range(8):
                        if it + 1 < 8:
                            item_front(c, it + 1)
                        item_back(c, it, started, ybank, ykey)
                    if c + 1 < NCH:
                        prologue(c + 1)
                    epilogue(c, ybank, ykey)
            A.reset(fbase, bbase)

        def mixer_phase(l):
            A.reset()
            vbc = A.f32(NV)
            k.dma(vbc, vec_d[l:l + 1, :].partition_broadcast(128), writes=["vbc"])
            ssd_phase(l, vbc)
            k.barrier(barsc[:, 1:2])
            swa_phase(l, vbc)
            k.barrier(barsc[:, 2:3])
            mla_phase(l, vbc)
            if l == 0:
                k.barrier(barsc[:, 3:4])
                dbg_dump("ptok", ptok_d, [("ptok_d",)], big=True)
                dbg_dump("xbc", xbc_d, [("xbc_d",)], BF16, big=True)
                dbg_dump("y", y_d, [("y_d",)], BF16, big=True)

        finals = []
        for p in range(L + 1):
            l_prev = p - 1 if p > 0 else None
            l_next = p if p < L else None
            if stop_after is not None and p > stop_after[0]:
                break
            finals = block_phase(p, l_prev, l_next)
            k.barrier(barsc[:, 0:1])
            if l_next is not None:
                if stop_after is not None and stop_after == (p, "block"):
                    break
                mixer_phase(l_next)
                k.barrier(barsc[:, 0:1])
                if stop_after is not None and stop_after == (p, "mix"):
                    break
        k.emit(list(finals) + dbg_outs)
        print("ops recorded:", k.nops, "arena hi (KiB):", A.hi * 4 / 1024, A.hib * 2 / 1024)
    return nc


def prep_common(inp, S):
    L = inp["ffn1_norm"].shape[0]
    vec = np.zeros((L, NV), np.float32)
    for n, (o, s) in VOFF.items():
        vec[:, o:o + s] = np.asarray(inp[n], np.float32).reshape(L, s)
    cw = np.asarray(inp["ssd_conv_w"], np.float32)
    cb = np.asarray(inp["ssd_conv_b"], np.float32)
    convp = np.zeros((L, 128, 8, 6), np.float32)
    for c in range(8):
        convp[:, :, c, 0:5] = cw[:, :, c * 128:(c + 1) * 128].transpose(0, 2, 1)
        convp[:, :, c, 5] = cb[:, c * 128:(c + 1) * 128]
    com = {"vecs": vec, "convp": np.ascontiguousarray(convp.reshape(L, 128, 48)), "cst": make_consts(),
           "cs_swa": rope_tables(S, 64), "cs_mla": rope_tables(S, 32)}
    for n in ("ffn1_gate", "ffn1_up", "ffn1_down", "ffn2_gate", "ffn2_up", "ffn2_down", "w_in", "w_out",
              "mla_w_uq", "mla_w_ukv"):
        com[n] = np.ascontiguousarray(np.asarray(inp[n], np.float32))
    return com, L


_NC_CACHE = {}


def kernel(**inputs):
    x = np.asarray(inputs["x"], np.float32)
    B, S, _ = x.shape
    com, L = prep_common(inputs, S)
    key = (S, L)
    if key not in _NC_CACHE:
        _NC_CACHE[key] = build(S, L)
    nc = _NC_CACHE[key]
    ncores = 8
    place = [0, 1, 4, 5][:B] if B <= 4 else list(range(B))
    zero_map = None
    in_maps = []
    for c in range(ncores):
        if c in place:
            m = dict(com)
            m["x"] = np.ascontiguousarray(x[place.index(c)])
        else:
            if zero_map is None:
                zero_map = {kk: np.zeros_like(v) for kk, v in com.items()}
                zero_map["x"] = np.zeros((S, D), np.float32)
            m = zero_map
        in_maps.append(m)
    res = run_bass_kernel_spmd(nc, in_maps, core_ids=list(range(ncores)))
    out = np.stack([np.asarray(res.results[place[b]]["out"], np.float32) for b in range(B)], axis=0)
    return out
```

```python
import contextlib
import numpy as np
import concourse.bass as bass
import concourse.mybir as mybir
from concourse.bass_utils import run_bass_kernel_spmd

F32 = mybir.dt.float32
BF16 = mybir.dt.bfloat16
AF = mybir.ActivationFunctionType
ALU = mybir.AluOpType
AX = mybir.AxisListType

D = 1024
DFF = 2816
NPROJ = 2480
NTM = 1456
EPS = 1e-6
BIG = 1.0e4
GF = 4
DEBUG = False
ALT_DVE_ONLY = False
N_DMA_SEMS = 40

VOFF = {}
_o = 0
for _n, _s in (("ffn1_norm", 1024), ("mix_norm", 1024), ("ffn2_norm", 1024), ("ssd_norm", 512),
               ("swa_q_norm", 64), ("swa_k_norm", 64), ("swa_out_norm", 256), ("mla_q_lat_norm", 256),
               ("mla_kv_norm", 128), ("mla_q_norm", 96), ("mla_k_norm", 96), ("mla_out_norm", 256),
               ("ssd_dt_bias", 16), ("ssd_a_log", 16), ("ssd_d", 8), ("swa_sink", 4)):
    VOFF[_n] = (_o, _s)
    _o += _s
NV = _o

C_ID = 0
C_R0F = 128
C_R1F = 384
C_R0B = 640
C_R1B = 896
C_NEGF = 1152
C_POSB = 1408
C_GE = 1664
C_SEL = 1792
NCST = C_SEL + 1024


def make_consts():
    c = np.zeros((128, NCST), np.float32)
    i = np.arange(128)
    le = (i[:, None] <= i[None, :]).astype(np.float32)
    lt = (i[:, None] < i[None, :]).astype(np.float32)
    ge = (i[:, None] >= i[None, :]).astype(np.float32)
    gt = (i[:, None] > i[None, :]).astype(np.float32)
    c[:, C_ID:C_ID + 128] = np.eye(128)
    c[:, C_R0F:C_R0F + 128] = le
    c[:, C_R0F + 128:C_R0F + 256] = 1.0
    c[:, C_R1F + 128:C_R1F + 256] = le
    c[:, C_R0B:C_R0B + 128] = lt
    c[:, C_R0B + 128:C_R0B + 256] = 1.0
    c[:, C_R1B + 128:C_R1B + 256] = lt
    c[:, C_NEGF:C_NEGF + 128] = -BIG * gt
    c[:, C_POSB + 128:C_POSB + 256] = BIG * lt
    c[:, C_GE:C_GE + 128] = ge
    for h in range(8):
        c[h, C_SEL + h * 128:C_SEL + (h + 1) * 128] = 1.0
    return c


def rope_tables(n, dim):
    inv = 1.0 / np.power(np.float32(10000.0), np.arange(0, dim, 2, dtype=np.float32) / np.float32(dim))
    ang = np.arange(n, dtype=np.float32)[:, None] * inv[None, :].astype(np.float32)
    return np.concatenate([np.cos(ang), np.sin(ang)], axis=1).astype(np.float32)


def _freeze(fn):
    import types
    if fn.__closure__ is None:
        return fn
    cells = []
    for c in fn.__closure__:
        try:
            cells.append(types.CellType(c.cell_contents))
        except ValueError:
            cells.append(c)
    return types.FunctionType(fn.__code__, fn.__globals__, fn.__name__, fn.__defaults__, tuple(cells))


class Op:
    __slots__ = ("eng", "fn", "waits", "signal", "sem", "val", "is_dma")

    def __init__(self, eng, fn, is_dma=False):
        self.eng = eng
        self.fn = fn
        self.waits = []
        self.signal = False
        self.sem = None
        self.val = None
        self.is_dma = is_dma


class KB:
    ENGS = ("pe", "act", "dve", "pool", "sp")

    def __init__(self, nc):
        self.nc = nc
        self.prog = {e: [] for e in self.ENGS}
        self.last_w = {}
        self.readers = {}
        self.dma_rr = 0
        self.dma_last = [None] * N_DMA_SEMS
        self.dma_uses = [0] * N_DMA_SEMS
        self.bar = None
        self.nops = 0
        self.carrier = None

    def _deps(self, op, reads, writes):
        deps = []
        if self.bar is not None:
            deps.append(self.bar)
        for k in reads:
            w = self.last_w.get(k)
            if w is not None:
                deps.append(w)
        for k in writes:
            w = self.last_w.get(k)
            if w is not None:
                deps.append(w)
            deps.extend(self.readers.get(k, ()))
        for k in reads:
            self.readers.setdefault(k, []).append(op)
        for k in writes:
            self.last_w[k] = op
            self.readers[k] = []
        seen = set()
        for d in deps:
            if d is op or id(d) in seen:
                continue
            seen.add(id(d))
            if op.eng == "pe" and d.eng == "pe" and not d.is_dma and not op.is_dma:
                continue
            op.waits.append(d)

    def op(self, eng, fn, reads=(), writes=()):
        o = Op(eng, _freeze(fn))
        self._deps(o, reads, writes)
        self.prog[eng].append(o)
        self.nops += 1
        return o

    def dma(self, out, in_, reads=(), writes=(), q="sp", **kw):
        o = Op(q, None, is_dma=True)
        o.fn = lambda e, out=out, in_=in_, kw=kw: e.dma_start(out=out, in_=in_, **kw)
        self._deps(o, reads, writes)
        i = self.dma_rr
        self.dma_rr = (self.dma_rr + 1) % N_DMA_SEMS
        prev = self.dma_last[i]
        if prev is not None:
            o.waits.append(prev)
        self.dma_last[i] = o
        self.dma_uses[i] += 1
        o.sem = i
        o.val = 16 * self.dma_uses[i]
        o.signal = True
        self.prog[q].append(o)
        self.nops += 1
        return o

    def barrier(self, scratch_ap):
        o = Op("dve", lambda e: e.memset(scratch_ap, 0.0))
        if self.bar is not None:
            o.waits.append(self.bar)
        for e in self.ENGS:
            for p in reversed(self.prog[e]):
                if not p.is_dma:
                    o.waits.append(p)
                    break
        for d in self.dma_last:
            if d is not None:
                o.waits.append(d)
        self.prog["dve"].append(o)
        self.bar = o
        self.last_w = {}
        self.readers = {}
        return o

    def emit(self, final_wait_ops=()):
        nc = self.nc
        for e in self.ENGS:
            for o in self.prog[e]:
                for w in o.waits:
                    w.signal = True
        for o in final_wait_ops:
            o.signal = True
        for e in self.ENGS:
            c = 0
            for o in self.prog[e]:
                if o.is_dma:
                    continue
                if o.signal:
                    c += 1
                    o.sem = e
                    o.val = c
        with contextlib.ExitStack() as st:
            esem = {e: st.enter_context(nc.semaphore("s_" + e)) for e in self.ENGS}
            dsem = [st.enter_context(nc.semaphore("d_%d" % i)) for i in range(N_DMA_SEMS)]
            block = st.enter_context(nc.Block())

            def sem_of(o):
                return dsem[o.sem] if o.is_dma else esem[o.sem]

            def run(e, h):
                waited = {}
                for o in self.prog[e]:
                    need = {}
                    for w in o.waits:
                        key = ("d", w.sem) if w.is_dma else ("e", w.sem)
                        if waited.get(key, 0) >= w.val:
                            continue
                        waited[key] = w.val
                        need[key] = w
                    need = list(need.values())
                    if e == "pe" and not o.is_dma:
                        for w in need[:-1]:
                            h.ldweights(self.carrier)._wait_ge(sem_of(w), w.val)
                        ins = o.fn(h)
                        if need:
                            ins._wait_ge(sem_of(need[-1]), need[-1].val)
                    else:
                        for w in need:
                            h.wait_ge(sem_of(w), w.val)
                        ins = o.fn(h)
                    if o.is_dma:
                        ins.then_inc(dsem[o.sem], 16)
                    elif o.signal:
                        ins.then_inc(esem[e], 1)
                if e == "sp":
                    for o in final_wait_ops:
                        h.wait_ge(sem_of(o), o.val)

            @block.tensor
            def _(t):
                run("pe", t)

            @block.scalar
            def _(s):
                run("act", s)

            @block.vector
            def _(v):
                run("dve", v)

            @block.gpsimd
            def _(g):
                run("pool", g)

            @block.sync
            def _(s):
                run("sp", s)


class Arena:
    def __init__(self, apf, nf, apb, nb):
        self.apf, self.nf, self.apb, self.nb = apf, nf, apb, nb
        self.p = 0
        self.pb = 0
        self.hi = 0
        self.hib = 0

    def reset(self, to=0, tob=0):
        self.p = to
        self.pb = tob

    def f32(self, cols):
        a = self.apf[:, self.p:self.p + cols]
        self.p += (cols + 7) // 8 * 8
        self.hi = max(self.hi, self.p)
        assert self.p <= self.nf, ("f32 arena overflow", self.p, self.nf)
        return a

    def bf16(self, cols):
        a = self.apb[:, self.pb:self.pb + cols]
        self.pb += (cols + 15) // 16 * 16
        self.hib = max(self.hib, self.pb)
        assert self.pb <= self.nb, ("bf16 arena overflow", self.pb, self.nb)
        return a


def build(S, L, stop_after=None):
    assert S % 512 == 0
    NT = S // 128
    TB = min(1024, S)
    NBLK = S // TB
    NTB = TB // 128
    NCH = S // 256

    nc = bass.Bass("TRN2", target_bir_lowering=False)
    dr = lambda name, shape, dt=F32, kind="ExternalInput": nc.dram_tensor(name, list(shape), dt, kind=kind).ap()
    x_d = dr("x", [S, D])
    wg_d = [dr("ffn1_gate", [L, D, DFF]), dr("ffn2_gate", [L, D, DFF])]
    wu_d = [dr("ffn1_up", [L, D, DFF]), dr("ffn2_up", [L, D, DFF])]
    wd_d = [dr("ffn1_down", [L, DFF, D]), dr("ffn2_down", [L, DFF, D])]
    win_d = dr("w_in", [L, D, NPROJ])
    wout_d = dr("w_out", [L, D, D])
    wuq_d = dr("mla_w_uq", [L, 256, 384])
    wukv_d = dr("mla_w_ukv", [L, 128, 512])
    vec_d = dr("vecs", [L, NV])
    convp_d = dr("convp", [L, 128, 48])
    cst_d = dr("cst", [128, NCST])
    cs_swa_d = dr("cs_swa", [S, 64])
    cs_mla_d = dr("cs_mla", [S, 32])
    out_d = dr("out", [S, D], kind="ExternalOutput")
    xbc_d = dr("xbc_scr", [1024, S], BF16, kind="Internal")
    ptok_d = dr("ptok_scr", [S, NTM], F32, kind="Internal")
    y_d = dr("y_scr", [S, D], BF16, kind="Internal")

    ARENA_F = 17 * 1024
    ARENA_B = 56 * 1024
    st = contextlib.ExitStack()
    with st:
        arena_f = st.enter_context(nc.sbuf_tensor("arena_f", [128, ARENA_F], F32))
        arena_b = st.enter_context(nc.sbuf_tensor("arena_b", [128, ARENA_B], BF16))
        cst = st.enter_context(nc.sbuf_tensor("cst_sb", [128, NCST], F32))
        cstb = st.enter_context(nc.sbuf_tensor("cstb_sb", [128, 128 * 5 + 512], BF16))
        barsc = st.enter_context(nc.sbuf_tensor("barsc", [128, 8], F32))
        banks = [st.enter_context(nc.psum_tensor("bank%d" % i, [128, 1024] if i == 6 else [128, 512], BF16 if i == 6 else F32))
                 for i in range(8)]
        k = KB(nc)
        A = Arena(arena_f[:], ARENA_F, arena_b[:], ARENA_B)

        k.carrier = cstb[:, 0:1]
        identf = cst[:, C_ID:C_ID + 128]
        identb = cstb[:, 0:128]
        ones_f = cst[:, C_R0F + 128:C_R0F + 256]

        k.dma(cst[:], cst_d, writes=["cst"])
        k.dma(cstb[:, 0:128], cst_d[:, C_ID:C_ID + 128], writes=["cstb"], q="pool")
        for j in range(2):
            k.dma(cstb[:, 128 + j * 128:256 + j * 128], cst_d[:, C_GE:C_GE + 128], writes=["cstb"], q="pool")
            k.dma(cstb[:, 384 + j * 128:512 + j * 128], cst_d[:, C_R0F:C_R0F + 128], writes=["cstb"], q="pool")

        k.dma(cstb[:, 640:896], cst_d[:, C_NEGF:C_NEGF + 256], writes=["cstb"], q="pool")
        k.dma(cstb[:, 896:1152], cst_d[:, C_POSB:C_POSB + 256], writes=["cstb"], q="pool")

        def PS(i):
            return banks[i], "bank%d" % i

        rr = {"n": 0}
        dbg_outs = []
        dbg_names = set()

        def dbg_dump(name, ap, reads, dt=F32, big=False):
            if not DEBUG or ("dbg_" + name) in dbg_names:
                return
            dbg_names.add("dbg_" + name)
            t = nc.dram_tensor("dbg_" + name, list(ap.shape), dt, kind="ExternalOutput").ap()
            if big:
                for r0 in range(0, ap.shape[0], 256):
                    dbg_outs.append(k.dma(t[r0:r0 + 256], ap[r0:r0 + 256], reads=reads))
            else:
                dbg_outs.append(k.dma(t, ap, reads=reads))

        def alt():
            rr["n"] += 1
            return "dve" if (rr["n"] % 2 or ALT_DVE_ONLY) else "act"

        def copy_op(eng, out, in_, reads, writes):
            if eng == "act":
                return k.op("act", lambda e: e.copy(out, in_), reads, writes)
            return k.op(eng, lambda e: e.tensor_copy(out, in_), reads, writes)

        def vrow(vbc, name, lo=0, n=None):
            o, s = VOFF[name]
            n = s - lo if n is None else n
            return vbc[:, o + lo:o + lo + n]

        def block_phase(pidx, l_prev, l_next):
            A.reset()
            vbcA = A.f32(3072) if l_prev is not None else None
            vbcB = A.f32(3072) if l_next is not None else None
            xb = A.f32(NTB * D).rearrange("p (t d) -> p t d", t=NTB)
            hT = A.bf16(8 * TB).rearrange("p (c t) -> p c t", c=8)
            hb = [A.bf16(D) for _ in range(2)]
            junk = A.f32(D)
            ss = A.f32(NTB)
            sq = A.f32(NTB)
            rstd = A.f32(NTB)
            wgu = [A.bf16(2 * 8 * GF * 128).rearrange("p (w c f) -> p w c f", w=2, c=8) for _ in range(2)]
            wdn = [A.bf16(GF * D).rearrange("p (c d) -> p c d", c=GF) for _ in range(2)]
            sg = [A.bf16(512) for _ in range(2)]
            aT = [A.bf16(GF * 512).rearrange("p (c t) -> p c t", c=GF) for _ in range(2)]
            wpj = [A.bf16(8 * 512).rearrange("p (c f) -> p c f", c=8) for _ in range(2)]
            evf = [A.bf16(512) for _ in range(2)]
            evt = [A.f32(512) for _ in range(2)]
            cnt = {"w": 0, "p": 0, "a": 0, "d": 0, "t": 0, "e": 0, "g": 0}
            if l_prev is not None:
                k.dma(vbcA, vec_d[l_prev:l_prev + 1, 0:3072].partition_broadcast(128), writes=["vbcA"])
            if l_next is not None:
                k.dma(vbcB, vec_d[l_next:l_next + 1, 0:3072].partition_broadcast(128), writes=["vbcB"])

            def norm_to_hT(vbc, vkey, nname, tag):
                k.op("dve", lambda e: e.memset(ss, 0.0), writes=["ss"])
                for tt in range(NTB):
                    k.op("act", lambda e, tt=tt: e.activation(out=junk, in_=xb[:, tt, :], func=AF.Square,
                                                              accum_out=ss[:, tt:tt + 1]),
                         reads=[("xb", tt)], writes=["junk", "ss"])
                k.op("act", lambda e: e.activation(out=sq, in_=ss, func=AF.Sqrt, scale=1.0 / D, bias=EPS),
                     reads=["ss"], writes=["sq"])
                k.op("dve", lambda e: e.reciprocal(out=rstd, in_=sq), reads=["sq"], writes=["rstd"])
                wv = vrow(vbc, nname)
                if tag == "f" and not cnt.get("dbg1"):
                    cnt["dbg1"] = 1
                    dbg_dump("ss", ss, ["ss"]); dbg_dump("rstd", rstd, ["rstd"]); dbg_dump("wv", wv, [vkey])
                    dbg_dump("xb0", xb[:, 0, :], [("xb", 0)])
                for tt in range(NTB):
                    hbi = tt % 2
                    k.op("dve", lambda e, tt=tt, hbi=hbi: e.scalar_tensor_tensor(
                        out=hb[hbi], in0=xb[:, tt, :], scalar=rstd[:, tt:tt + 1], in1=wv,
                        op0=ALU.mult, op1=ALU.mult), reads=[("xb", tt), "rstd", vkey], writes=[("hb", hbi)])
                    transpose_rows(hb[hbi], ("hb", hbi), 8, hT, tt, "hT")

            def transpose_rows(src, skey, nchunks, dstT, tt, dkey):
                for h0 in range(0, nchunks, 4):
                    nn = min(4, nchunks - h0)
                    pb, pk = PS(6)
                    pv = pb[:, (cnt["t"] % 2) * 512:(cnt["t"] % 2) * 512 + 512].rearrange("p (c t) -> p c t", c=4)
                    cnt["t"] += 1
                    for j in range(nn):
                        k.op("pe", lambda e, j=j, h0=h0: e.transpose(pv[:, j, :], src[:, (h0 + j) * 128:(h0 + j + 1) * 128], identb),
                             reads=[skey, "cstb"], writes=[pk])
                    copy_op(alt(), dstT[:, h0:h0 + nn, tt * 128:(tt + 1) * 128], pv[:, 0:nn, :],
                            reads=[], writes=[pk, (dkey, tt, h0)])

            def dump_hT(nm):
                dbg_dump(nm, hT[:, :, 0:128], [("hT", 0, 0), ("hT", 0, 4)], BF16)

            def hT_keys(t4, key="hT"):
                return [(key, tt, h0) for tt in range(t4 * 4, t4 * 4 + 4) for h0 in (0, 4)]

            def ffn(l, which, vbc, vkey):
                norm_to_hT(vbc, vkey, "ffn1_norm" if which == 0 else "ffn2_norm", "f")
                if not cnt.get("dbg2"):
                    cnt["dbg2"] = 1
                    dump_hT("hT")
                wg_l = wg_d[which][l].rearrange("(c p) f -> p c f", p=128)
                wu_l = wu_d[which][l].rearrange("(c p) f -> p c f", p=128)
                wd_l = wd_d[which][l]
                groups = []
                f0 = 0
                while f0 < DFF:
                    nf = min(GF, (DFF - f0) // 128)
                    groups.append((f0, nf))
                    f0 += nf * 128
                NG = len(groups)
                NT4 = TB // 512
                units = [(g, t4) for g in range(NG) for t4 in range(NT4)]

                def load_w(g):
                    if g >= NG:
                        return
                    wi = g % 2
                    f0, nf = groups[g]
                    k.dma(wgu[wi][:, 0, :, 0:nf * 128], wg_l[:, :, f0:f0 + nf * 128], writes=[("wgu", wi, 0)], q="pool")
                    k.dma(wgu[wi][:, 1, :, 0:nf * 128], wu_l[:, :, f0:f0 + nf * 128], writes=[("wgu", wi, 1)], q="pool")
                    k.dma(wdn[wi][:, 0:nf, :], wd_l[f0:f0 + nf * 128, :].rearrange("(c p) d -> p c d", p=128), writes=[("wdn", wi)], q="pool")

                ais = {}

                def front_pieces(u):
                    g, t4 = units[u]
                    wi = g % 2
                    nf = groups[g][1]
                    ai = cnt["a"] % 2
                    cnt["a"] += 1
                    ais[u] = ai
                    hk = hT_keys(t4)
                    pieces = []
                    for fc in range(nf):
                        def piece(fc=fc):
                            gi = cnt["g"] % 2
                            cnt["g"] += 1
                            pg, pgk = PS(gi)
                            pu, puk = PS(2 + gi)
                            for w, (pp, ppk) in enumerate(((pg, pgk), (pu, puk))):
                                for dc in range(8):
                                    k.op("pe", lambda e: e.matmul(
                                        pp[:], lhsT=wgu[wi][:, w, dc, fc * 128:(fc + 1) * 128],
                                        rhs=hT[:, dc, t4 * 512:(t4 + 1) * 512], start=(dc == 0), stop=(dc == 7)),
                                        reads=[("wgu", wi, w)] + hk, writes=[ppk])
                            k.op("act", lambda e: e.activation(out=sg[gi], in_=pg[:], func=AF.Silu),
                                 reads=[], writes=[pgk, ("sg", gi)])
                            k.op("dve", lambda e: e.tensor_tensor(
                                out=aT[ai][:, fc, :], in0=pu[:], in1=sg[gi], op=ALU.mult),
                                reads=[("sg", gi)], writes=[puk, ("aT", ai, fc)])
                        pieces.append(piece)
                    return pieces

                def back_pieces(u):
                    g, t4 = units[u]
                    wi = g % 2
                    nf = groups[g][1]
                    ai = ais[u]
                    pieces = []
                    for ts in range(4):
                        def piece(ts=ts):
                            tt = t4 * 4 + ts
                            for dh in range(2):
                                di = (4, 5, 7)[cnt["d"] % 3]
                                cnt["d"] += 1
                                pd, pdk = PS(di)
                                for fc in range(nf):
                                    k.op("pe", lambda e: e.matmul(
                                        pd[:], lhsT=aT[ai][:, fc, ts * 128:(ts + 1) * 128],
                                        rhs=wdn[wi][:, fc, dh * 512:(dh + 1) * 512], start=(fc == 0), stop=(fc == nf - 1)),
                                        reads=[("aT", ai, fc), ("wdn", wi)], writes=[pdk])
                                k.op("dve", lambda e: e.scalar_tensor_tensor(
                                    out=xb[:, tt, dh * 512:(dh + 1) * 512], in0=pd[:], scalar=0.5,
                                    in1=xb[:, tt, dh * 512:(dh + 1) * 512], op0=ALU.mult, op1=ALU.add),
                                    reads=[], writes=[pdk, ("xb", tt)])
                        pieces.append(piece)
                    return pieces

                load_w(0)
                load_w(1)
                for p_ in front_pieces(0):
                    p_()
                for u in range(len(units)):
                    fp = front_pieces(u + 1) if u + 1 < len(units) else []
                    bp = back_pieces(u)
                    n = max(len(fp), len(bp))
                    for i in range(n):
                        if i < len(fp):
                            fp[i]()
                        if i < len(bp):
                            bp[i]()
                    if units[u][1] == NT4 - 1:
                        load_w(units[u][0] + 2)

            def inproj(l, blk, vbc, vkey, after_norm=None):
                norm_to_hT(vbc, vkey, "mix_norm", "m")
                if after_norm is not None:
                    after_norm()
                win_l = win_d[l].rearrange("(c p) f -> p c f", p=128)
                t0 = blk * TB
                jobs = [("f", 512 + cg * 128, 128, cg) for cg in range(8)] + \
                       [("t", c0, ncol, j0) for (c0, ncol, j0) in ((0, 512, 0), (1536, 512, 512), (2048, 432, 1024))]

                def load_p(i):
                    if i >= len(jobs):
                        return
                    kind, c0, ncol, aux = jobs[i]
                    wi = i % 2
                    k.dma(wpj[wi][:, :, 0:ncol], win_l[:, :, c0:c0 + ncol], writes=[("wpj", wi)], q="pool")

                load_p(0)
                for i, (kind, c0, ncol, aux) in enumerate(jobs):
                    load_p(i + 1)
                    wi = i % 2
                    if kind == "f":
                        cg = aux
                        for t4 in range(TB // 512):
                            gi = cnt["g"] % 2
                            cnt["g"] += 1
                            pg, pgk = PS(gi)
                            hk = hT_keys(t4)
                            for dc in range(8):
                                k.op("pe", lambda e: e.matmul(
                                    pg[:], lhsT=wpj[wi][:, dc, 0:128], rhs=hT[:, dc, t4 * 512:(t4 + 1) * 512],
                                    start=(dc == 0), stop=(dc == 7)), reads=[("wpj", wi)] + hk, writes=[pgk])
                            ei = cnt["e"] % 2
                            cnt["e"] += 1
                            copy_op(alt(), evf[ei], pg[:], reads=[], writes=[pgk, ("evf", ei)])
                            k.dma(xbc_d[cg * 128:(cg + 1) * 128, t0 + t4 * 512:t0 + (t4 + 1) * 512], evf[ei],
                                  reads=[("evf", ei)], writes=[("xbc_d", cg)])
                    else:
                        j0 = aux
                        for tt in range(NTB):
                            gi = cnt["g"] % 2
                            cnt["g"] += 1
                            pu, puk = PS(2 + gi)
                            for dc in range(8):
                                k.op("pe", lambda e: e.matmul(
                                    pu[:, 0:ncol], lhsT=hT[:, dc, tt * 128:(tt + 1) * 128], rhs=wpj[wi][:, dc, 0:ncol],
                                    start=(dc == 0), stop=(dc == 7)),
                                    reads=[("wpj", wi), ("hT", tt, 0), ("hT", tt, 4)], writes=[puk])
                            ei = cnt["e"] % 2
                            cnt["e"] += 1
                            copy_op(alt(), evt[ei][:, 0:ncol], pu[:, 0:ncol], reads=[], writes=[puk, ("evt", ei)])
                            k.dma(ptok_d[t0 + tt * 128:t0 + (tt + 1) * 128, j0:j0 + ncol], evt[ei][:, 0:ncol],
                                  reads=[("evt", ei)], writes=[("ptok_d", blk)])

            def outproj(l, blk):
                t0 = blk * TB
                wout_l = wout_d[l].rearrange("(c p) f -> p c f", p=128)
                for tt in range(NTB):
                    hbi = tt % 2
                    k.dma(hb[hbi], y_d[t0 + tt * 128:t0 + (tt + 1) * 128, :], reads=[("y_d",)], writes=[("hb", hbi)])
                    transpose_rows(hb[hbi], ("hb", hbi), 8, hT, tt, "hT")
                for dh in range(2):
                    k.dma(wpj[dh], wout_l[:, :, dh * 512:(dh + 1) * 512], writes=[("wpj", dh)], q="pool")
                for dh in range(2):
                    wi = dh
                    for tt in range(NTB):
                        di = 4 + cnt["d"] % 2
                        cnt["d"] += 1
                        pd, pdk = PS(di)
                        for dc in range(8):
                            k.op("pe", lambda e, pd=pd, dc=dc, wi=wi, tt=tt: e.matmul(
                                pd[:], lhsT=hT[:, dc, tt * 128:(tt + 1) * 128], rhs=wpj[wi][:, dc, :],
                                start=(dc == 0), stop=(dc == 7)),
                                reads=[("wpj", wi), ("hT", tt, 0), ("hT", tt, 4)], writes=[pdk])
                        k.op("dve", lambda e, pd=pd, tt=tt, dh=dh: e.tensor_tensor(
                            out=xb[:, tt, dh * 512:(dh + 1) * 512], in0=pd[:], in1=xb[:, tt, dh * 512:(dh + 1) * 512],
                            op=ALU.add), reads=[], writes=[pdk, ("xb", tt)])

            stores = []
            src = x_d if pidx == 0 else out_d

            def load_x(blk):
                t0 = blk * TB
                for tt in range(NTB):
                    k.dma(xb[:, tt, :], src[t0 + tt * 128:t0 + (tt + 1) * 128, :],
                          reads=[("out_d", blk)], writes=[("xb", tt)])

            def store_x(blk):
                t0 = blk * TB
                for tt in range(NTB):
                    stores.append(k.dma(out_d[t0 + tt * 128:t0 + (tt + 1) * 128, :], xb[:, tt, :],
                                        reads=[("xb", tt)], writes=[("out_d", blk)]))

            load_x(0)
            for blk in range(NBLK):
                if l_prev is not None:
                    outproj(l_prev, blk)
                    ffn(l_prev, 1, vbcA, "vbcA")
                if l_next is not None:
                    ffn(l_next, 0, vbcB, "vbcB")
                    store_x(blk)
                    inproj(l_next, blk, vbcB, "vbcB",
                           after_norm=(lambda blk=blk: load_x(blk + 1)) if blk + 1 < NBLK else None)
                else:
                    store_x(blk)
                    if blk + 1 < NBLK:
                        load_x(blk + 1)
            return stores

        def mk_tmpset(pfx, n, nr):
            return {"pfx": pfx, "sq": A.f32(n), "ss": A.f32(8), "rs": A.f32(8), "t1": A.f32(nr), "t2": A.f32(nr)}

        def run_interleaved(gens):
            gens = list(gens)
            while gens:
                for g_ in list(gens):
                    try:
                        next(g_)
                    except StopIteration:
                        gens.remove(g_)

        def headnorm_rope(src, H, Dh, wrow, rlo, rhalf, cs, dst, tmp, keys_r, keys_w, cskey):
            sqv = tmp["sq"][:, 0:H * Dh].rearrange("p (h d) -> p h d", h=H)
            ssv = tmp["ss"][:, 0:H]
            rs = tmp["rs"][:, 0:H]
            qn = sqv
            t1 = tmp["t1"][:, 0:H * rhalf].rearrange("p (h d) -> p h d", h=H)
            t2 = tmp["t2"][:, 0:H * rhalf].rearrange("p (h d) -> p h d", h=H)
            p_ = tmp["pfx"]
            T = [p_ + "sq", p_ + "ss", p_ + "rs", p_ + "sq", p_ + "t1", p_ + "t2"]
            k.op("dve", lambda e: e.tensor_tensor(out=sqv, in0=src, in1=src, op=ALU.mult), reads=keys_r, writes=[T[0]])
            yield
            k.op("dve", lambda e: e.tensor_reduce(out=ssv, in_=sqv, axis=AX.X, op=ALU.add), reads=[T[0]], writes=[T[1]])
            yield
            k.op("act", lambda e: e.activation(out=rs, in_=ssv, func=AF.Sqrt, scale=1.0 / Dh, bias=EPS), reads=[T[1]], writes=[T[2]])
            yield
            k.op("dve", lambda e: e.reciprocal(out=ssv, in_=rs), reads=[T[2]], writes=[T[1]])
            yield
            k.op("dve", lambda e: e.tensor_tensor(out=qn, in0=src, in1=ssv.unsqueeze(2).to_broadcast([128, H, Dh]), op=ALU.mult),
                 reads=keys_r + [T[1]], writes=[T[3]])
            yield
            k.op("dve", lambda e: e.tensor_tensor(out=qn, in0=qn, in1=wrow.unsqueeze(1).to_broadcast([128, H, Dh]), op=ALU.mult),
                 reads=["vbc"], writes=[T[3]])
            yield
            if rlo > 0:
                k.op("dve", lambda e: e.tensor_copy(dst[:, :, 0:rlo], qn[:, :, 0:rlo]), reads=[T[3]], writes=keys_w)
                yield
            x1 = qn[:, :, rlo:rlo + rhalf]
            x2 = qn[:, :, rlo + rhalf:rlo + 2 * rhalf]
            cb = cs[:, 0:rhalf].unsqueeze(1).to_broadcast([128, H, rhalf])
            sb = cs[:, rhalf:2 * rhalf].unsqueeze(1).to_broadcast([128, H, rhalf])
            k.op("dve", lambda e: e.tensor_tensor(out=t1, in0=x1, in1=cb, op=ALU.mult), reads=[T[3], cskey], writes=[T[4]])
            yield
            k.op("dve", lambda e: e.tensor_tensor(out=t2, in0=x2, in1=sb, op=ALU.mult), reads=[T[3], cskey], writes=[T[5]])
            yield
            k.op("dve", lambda e: e.tensor_tensor(out=dst[:, :, rlo:rlo + rhalf], in0=t1, in1=t2, op=ALU.subtract),
                 reads=[T[4], T[5]], writes=keys_w)
            yield
            k.op("dve", lambda e: e.tensor_tensor(out=t1, in0=x1, in1=sb, op=ALU.mult), reads=[T[3], cskey], writes=[T[4]])
            yield
            k.op("dve", lambda e: e.tensor_tensor(out=t2, in0=x2, in1=cb, op=ALU.mult), reads=[T[3], cskey], writes=[T[5]])
            yield
            k.op("dve", lambda e: e.tensor_tensor(out=dst[:, :, rlo + rhalf:rlo + 2 * rhalf], in0=t1, in1=t2, op=ALU.add),
                 reads=[T[4], T[5]], writes=keys_w)
            yield

        def heads_to_T(src, skey, H, Dh, dstT, tt, dkey, bank):
            pb, pk = PS(6)
            pv = pb[:, 0:512].rearrange("p (c t) -> p c t", c=4)
            for h in range(H):
                k.op("pe", lambda e, h=h: e.transpose(pv[0:Dh, h, :], src[:, h, :], identb), reads=[skey, "cstb"], writes=[pk])
            copy_op(alt(), dstT[0:Dh, 0:H, tt * 128:(tt + 1) * 128], pv[0:Dh, 0:H, :], reads=[], writes=[pk, (dkey, tt)])

        def out_norm_store(osrc, okeys, width, wrow, col0, tmp, tt, ring, tag):
            k.op("dve", lambda e: e.memset(tmp["ss"][:, 0:1], 0.0), writes=["T_ss"])
            k.op("act", lambda e: e.activation(out=tmp["sq"][:, 0:width], in_=osrc, func=AF.Square, accum_out=tmp["ss"][:, 0:1]),
                 reads=okeys, writes=["T_sq", "T_ss"])
            k.op("act", lambda e: e.activation(out=tmp["rs"][:, 0:1], in_=tmp["ss"][:, 0:1], func=AF.Sqrt, scale=1.0 / width, bias=EPS),
                 reads=["T_ss"], writes=["T_rs"])
            k.op("dve", lambda e: e.reciprocal(out=tmp["ss"][:, 1:2], in_=tmp["rs"][:, 0:1]), reads=["T_rs"], writes=["T_ss"])
            yi = ring % 2
            yb = tmp["yb"][yi][:, 0:width]
            k.op("dve", lambda e: e.scalar_tensor_tensor(out=yb, in0=osrc, scalar=tmp["ss"][:, 1:2], in1=wrow,
                                                         op0=ALU.mult, op1=ALU.mult),
                 reads=list(okeys) + ["T_ss", "vbc"], writes=[(tag + "yb", yi)])
            k.dma(y_d[tt * 128:(tt + 1) * 128, col0:col0 + width], yb, reads=[(tag + "yb", yi)], writes=[("y_d",)])

        def attn_finish(acc_bank, ncols_q, nheads, tmp, odst_fn, extra_den, tag):
            pa, pak = PS(acc_bank)
            oT = tmp["oT"]
            k.op("act", lambda e: e.copy(oT[0:65, 0:ncols_q], pa[0:65, 0:ncols_q]), reads=[], writes=[pak, "T_oT"])
            for j in range(ncols_q // 128):
                pb, pk = PS(7)
                k.op("pe", lambda e, j=j: e.transpose(pb[:, 0:65], oT[0:65, j * 128:(j + 1) * 128], identf[0:65, 0:65]),
                     reads=["T_oT", "cst"], writes=[pk])
                den = tmp["den"]
                if extra_den is not None:
                    ed = extra_den(j)
                    k.op("dve", lambda e, ed=ed: e.tensor_tensor(out=den[:, 0:1], in0=pb[:, 64:65], in1=ed, op=ALU.add),
                         reads=["sinkexp"], writes=[pk, "T_den"])
                else:
                    k.op("dve", lambda e: e.tensor_copy(den[:, 0:1], pb[:, 64:65]), reads=[], writes=[pk, "T_den"])
                k.op("dve", lambda e: e.reciprocal(out=den[:, 1:2], in_=den[:, 0:1]), reads=["T_den"], writes=["T_rden"])
                dst, dkey = odst_fn(j)
                k.op("dve", lambda e, dst=dst: e.tensor_scalar_mul(out=dst, in0=pb[:, 0:64], scalar1=den[:, 1:2]), reads=["T_rden"], writes=[pk, dkey])

        def mla_phase(l, vbc):
            fbase, bbase = A.p, A.pb
            qT = A.bf16(4 * S).rearrange("p (h t) -> p h t", h=4)
            kT = A.bf16(4 * S).rearrange("p (h t) -> p h t", h=4)
            vx = A.bf16(NT * 4 * 65).rearrange("p (t h d) -> p t h d", t=NT, h=4)
            oall = A.f32(NT * 256).rearrange("p (t d) -> p t d", t=NT)
            wuq = A.bf16(2 * 384).rearrange("p (c f) -> p c f", c=2)
            wukv = A.bf16(512)
            pin = [A.f32(416) for _ in range(2)]
            csb = [A.f32(32) for _ in range(2)]
            nb = [A.bf16(384) for _ in range(2)]
            nT = A.bf16(3 * 128).rearrange("p (c t) -> p c t", c=3)
            qf = A.f32(384).rearrange("p (h d) -> p h d", h=4)
            kf = A.f32(384).rearrange("p (h d) -> p h d", h=4)
            qb = A.bf16(384).rearrange("p (h d) -> p h d", h=4)
            kb_ = A.bf16(384).rearrange("p (h d) -> p h d", h=4)
            tmp = {"sq": A.f32(384), "ss": A.f32(8), "rs": A.f32(8),
                   "oT": A.f32(512), "den": A.f32(2), "yb": [A.bf16(256) for _ in range(2)]}
            tsets = [mk_tmpset("Ta_", 384, 64), mk_tmpset("Tb_", 384, 64)]
            eb = [A.bf16(512) for _ in range(4)]
            k.dma(wuq, wuq_d[l].rearrange("(c p) f -> p c f", p=128), writes=["wuq"], q="pool")
            k.dma(wukv, wukv_d[l], writes=["wukv"], q="pool")
            k.op("dve", lambda e: e.memset(vx[:, :, :, 64:65], 1.0), writes=["vx1"])
            for tt in range(NT):
                pi = tt % 2
                k.dma(pin[pi], ptok_d[tt * 128:(tt + 1) * 128, 1040:1456], reads=[("ptok_d",)], writes=[("pin", pi)])
                k.dma(csb[pi], cs_mla_d[tt * 128:(tt + 1) * 128, :], writes=[("cs", pi)])
                for (c0, n, wn, tagn) in ((0, 256, "mla_q_lat_norm", "a"), (256, 128, "mla_kv_norm", "b")):
                    k.op("dve", lambda e: e.memset(tmp["ss"][:, 0:1], 0.0), writes=["T_ss"])
                    k.op("act", lambda e, c0=c0, n=n, pi=pi: e.activation(out=tmp["sq"][:, 0:n], in_=pin[pi][:, c0:c0 + n],
                                                                        func=AF.Square, accum_out=tmp["ss"][:, 0:1]),
                         reads=[("pin", pi)], writes=["T_sq", "T_ss"])
                    k.op("act", lambda e, n=n: e.activation(out=tmp["rs"][:, 0:1], in_=tmp["ss"][:, 0:1], func=AF.Sqrt,
                                                            scale=1.0 / n, bias=EPS), reads=["T_ss"], writes=["T_rs"])
                    k.op("dve", lambda e: e.reciprocal(out=tmp["ss"][:, 1:2], in_=tmp["rs"][:, 0:1]), reads=["T_rs"], writes=["T_ss"])
                    k.op("dve", lambda e, c0=c0, n=n, pi=pi, wn=wn: e.scalar_tensor_tensor(
                        out=nb[pi][:, c0:c0 + n], in0=pin[pi][:, c0:c0 + n], scalar=tmp["ss"][:, 1:2], in1=vrow(vbc, wn),
                        op0=ALU.mult, op1=ALU.mult), reads=[("pin", pi), "T_ss", "vbc"], writes=[("nb", pi, tagn)])
                pb, pk = PS(6)
                pv = pb[:, 512:1024].rearrange("p (c t) -> p c t", c=4)
                for c in range(3):
                    k.op("pe", lambda e, c=c, pi=pi: e.transpose(pv[:, c, :], nb[pi][:, c * 128:(c + 1) * 128], identb),
                         reads=[("nb", pi, "a"), ("nb", pi, "b"), "cstb"], writes=[pk])
                copy_op("act", nT, pv[:, 0:3, :], reads=[], writes=[pk, "nT"])
                pq, pqk = PS(0)
                for c in range(2):
                    k.op("pe", lambda e, c=c: e.matmul(pq[:, 0:384], lhsT=nT[:, c, :], rhs=wuq[:, c, :], start=(c == 0), stop=(c == 1)),
                         reads=["nT", "wuq"], writes=[pqk])
                pkv, pkvk = PS(1)
                k.op("pe", lambda e: e.matmul(pkv[:], lhsT=nT[:, 2, :], rhs=wukv, start=True, stop=True), reads=["nT", "wukv"], writes=[pkvk])
                copy_op("act", qf, pq[:, 0:384].rearrange("p (h d) -> p h d", h=4), reads=[], writes=[pqk, "qf"])
                kvv = pkv[:].rearrange("p (h d) -> p h d", h=4)
                copy_op("act", kf[:, :, 0:64], kvv[:, :, 0:64], reads=[], writes=[pkvk, "kf"])
                copy_op("dve", vx[:, tt, :, 0:64], kvv[:, :, 64:128], reads=[], writes=[pkvk, ("vx", tt)])
                k.op("dve", lambda e, pi=pi: e.tensor_copy(kf[:, :, 64:96], pin[pi][:, 384:416].unsqueeze(1).to_broadcast([128, 4, 32])),
                     reads=[("pin", pi)], writes=["kf"])
                run_interleaved([
                    headnorm_rope(qf, 4, 96, vrow(vbc, "mla_q_norm"), 64, 16, csb[pi], qb, tsets[0], ["qf"], ["qb"], ("cs", pi)),
                    headnorm_rope(kf, 4, 96, vrow(vbc, "mla_k_norm"), 64, 16, csb[pi], kb_, tsets[1], ["kf"], ["kb"], ("cs", pi))])
                heads_to_T(qb, "qb", 4, 96, qT, tt, "qT", 6)
                heads_to_T(kb_, "kb", 4, 96, kT, tt, "kT", 6)
            scale = 96.0 ** -0.5
            qkeys = lambda qi: [("qT", tt) for tt in range(qi * 4, qi * 4 + 4)]
            items = [(h, qi, kt) for h in range(4) for qi in range(S // 512) for kt in range(NT)]
            LA = 3

            def stA(i):
                h, qi, kt = items[i]
                psn, psk = PS(i % 4)
                k.op("pe", lambda e: e.matmul(
                    psn[:], lhsT=kT[0:96, h, kt * 128:(kt + 1) * 128], rhs=qT[0:96, h, qi * 512:(qi + 1) * 512],
                    start=True, stop=True), reads=[("kT", kt)] + qkeys(qi), writes=[psk])

            def stBC(i):
                h, qi, kt = items[i]
                psn, psk = PS(i % 4)
                ebi = i % 4
                blk = h * (S // 512) + qi
                pa, pak = PS(4 + (blk % 2))
                k.op("act", lambda e: e.activation(out=eb[ebi], in_=psn[:], func=AF.Exp, scale=scale),
                     reads=[], writes=[psk, ("eb", ebi)])
                k.op("pe", lambda e: e.matmul(
                    pa[0:65, :], lhsT=vx[:, kt, h, :], rhs=eb[ebi], start=(kt == 0), stop=(kt == NT - 1)),
                    reads=[("vx", kt), "vx1", ("eb", ebi)], writes=[pak])
                if kt == NT - 1:
                    attn_finish(4 + (blk % 2), 512, 1, tmp,
                                lambda j, h=h, qi=qi: (oall[:, qi * 4 + j, h * 64:(h + 1) * 64], ("oall", qi * 4 + j, h)),
                                None, "m")

            for i in range(min(LA, len(items))):
                stA(i)
            for i in range(len(items)):
                if i + LA < len(items):
                    stA(i + LA)
                stBC(i)
            for tt in range(NT):
                okeys = [("oall", tt, h) for h in range(4)]
                out_norm_store(oall[:, tt, :], okeys, 256, vrow(vbc, "mla_out_norm"), 768, tmp, tt, tt, "mo")
            A.reset(fbase, bbase)

        def swa_phase(l, vbc):
            fbase, bbase = A.p, A.pb
            qT = A.bf16(4 * S).rearrange("p (h t) -> p h t", h=4)
            kT = A.bf16(2 * S).rearrange("p (h t) -> p h t", h=2)
            vx = A.bf16(NT * 2 * 65).rearrange("p (t h d) -> p t h d", t=NT, h=2)
            oall = A.f32(NT * 256).rearrange("p (t d) -> p t d", t=NT)
            pin = [A.f32(512) for _ in range(2)]
            csb = [A.f32(64) for _ in range(2)]
            qb = [A.bf16(256).rearrange("p (h d) -> p h d", h=4) for _ in range(2)]
            kb_ = [A.bf16(128).rearrange("p (h d) -> p h d", h=2) for _ in range(2)]
            tmp = {"sq": A.f32(256), "ss": A.f32(8), "rs": A.f32(8),
                   "oT": A.f32(512), "den": A.f32(2), "yb": [A.bf16(256) for _ in range(2)]}
            tsets = [mk_tmpset("T%d_" % i, 256, 128) for i in range(4)]
            sinkexp = A.f32(4)
            eb = [A.bf16(256) for _ in range(4)]
            k.op("act", lambda e: e.activation(out=sinkexp, in_=vrow(vbc, "swa_sink"), func=AF.Exp), reads=["vbc"], writes=["sinkexp"])
            k.op("dve", lambda e: e.memset(vx[:, :, :, 64:65], 1.0), writes=["vx1"])
            for tt0 in range(0, NT, 2):
                gens = []
                for u in range(2):
                    tt = tt0 + u
                    pi = tt % 2
                    k.dma(pin[pi], ptok_d[tt * 128:(tt + 1) * 128, 528:1040], reads=[("ptok_d",)], writes=[("pin", pi)])
                    k.dma(csb[pi], cs_swa_d[tt * 128:(tt + 1) * 128, :], writes=[("cs", pi)])
                    qsrc = pin[pi][:, 0:256].rearrange("p (h d) -> p h d", h=4)
                    ksrc = pin[pi][:, 256:384].rearrange("p (h d) -> p h d", h=2)
                    gens.append(headnorm_rope(qsrc, 4, 64, vrow(vbc, "swa_q_norm"), 0, 32, csb[pi], qb[pi], tsets[2 * u],
                                              [("pin", pi)], [("qb", pi)], ("cs", pi)))
                    gens.append(headnorm_rope(ksrc, 2, 64, vrow(vbc, "swa_k_norm"), 0, 32, csb[pi], kb_[pi], tsets[2 * u + 1],
                                              [("pin", pi)], [("kb", pi)], ("cs", pi)))
                run_interleaved(gens)
                for u in range(2):
                    tt = tt0 + u
                    pi = tt % 2
                    k.op("dve", lambda e: e.tensor_copy(vx[:, tt, :, 0:64], pin[pi][:, 384:512].rearrange("p (h d) -> p h d", h=2)),
                         reads=[("pin", pi)], writes=[("vx", tt)])
                    heads_to_T(qb[pi], ("qb", pi), 4, 64, qT, tt, "qT", 6)
                    heads_to_T(kb_[pi], ("kb", pi), 2, 64, kT, tt, "kT", 6)
            scale = 64.0 ** -0.5
            items = []
            for n in range(NT):
                for kvh in range(2):
                    js = [j for j in (n - 1, n, n + 1) if 0 <= j < NT]
                    for ji, j in enumerate(js):
                        items.append((n, kvh, j, ji, len(js)))
            LA = 3
            pend = []

            def stA(i):
                n, kvh, j, ji, nj = items[i]
                psn, psk = PS(i % 4)
                k.op("pe", lambda e: e.matmul(
                    psn[:, 0:256].rearrange("p (a b) -> p a b", a=2), lhsT=kT[0:64, kvh, j * 128:(j + 1) * 128],
                    rhs=qT[0:64, 2 * kvh:2 * kvh + 2, n * 128:(n + 1) * 128], start=True, stop=True),
                    reads=[("kT", j), ("qT", n)], writes=[psk])

            def stBC(i):
                n, kvh, j, ji, nj = items[i]
                psn, psk = PS(i % 4)
                ebi = i % 4
                blk = n * 2 + kvh
                pa, pak = PS(4 + (blk % 2))
                k.op("act", lambda e: e.activation(out=eb[ebi], in_=psn[:, 0:256], func=AF.Exp, scale=scale),
                     reads=[], writes=[psk, ("eb", ebi)])
                if j != n:
                    mk = cstb[:, 128:384] if j < n else cstb[:, 384:640]
                    k.op("dve", lambda e: e.tensor_tensor(out=eb[ebi], in0=eb[ebi], in1=mk, op=ALU.mult),
                         reads=["cstb"], writes=[("eb", ebi)])
                k.op("pe", lambda e: e.matmul(
                    pa[0:65, 0:256], lhsT=vx[:, j, kvh, :], rhs=eb[ebi], start=(ji == 0), stop=(ji == nj - 1)),
                    reads=[("vx", j), "vx1", ("eb", ebi)], writes=[pak])
                if ji == 0 and pend:
                    pend.pop(0)()
                if ji == nj - 1:
                    pend.append(lambda n=n, kvh=kvh, blk=blk: attn_finish(
                        4 + (blk % 2), 256, 2, tmp,
                        lambda jj: (oall[:, n, (2 * kvh + jj) * 64:(2 * kvh + jj + 1) * 64], ("oall", n, 2 * kvh + jj)),
                        lambda jj: sinkexp[:, 2 * kvh + jj:2 * kvh + jj + 1], "s"))

            for i in range(min(LA, len(items))):
                stA(i)
            for i in range(len(items)):
                if i + LA < len(items):
                    stA(i + LA)
                stBC(i)
            while pend:
                pend.pop(0)()
            for tt in range(NT):
                okeys = [("oall", tt, h) for h in range(4)]
                out_norm_store(oall[:, tt, :], okeys, 256, vrow(vbc, "swa_out_norm"), 512, tmp, tt, tt, "so")
            A.reset(fbase, bbase)

        def ssd_phase(l, vbc):
            fbase, bbase = A.p, A.pb
            convw = A.f32(48).rearrange("p (c k) -> p c k", c=8)
            BT = A.bf16(2 * S).rearrange("p (g t) -> p g t", g=2)
            CT = A.bf16(2 * S).rearrange("p (g t) -> p g t", g=2)
            dt = A.f32(NT * 16).rearrange("p (t d) -> p t d", t=NT)
            av = A.f32(NT * 16).rearrange("p (t d) -> p t d", t=NT)
            cum = A.f32(NT * 16).rearrange("p (t d) -> p t d", t=NT)
            ncf = A.f32(NT * 8).rearrange("p (t d) -> p t d", t=NT)
            sc1 = A.f32(NT * 16).rearrange("p (t d) -> p t d", t=NT)
            sc2 = A.f32(NT * 16).rearrange("p (t d) -> p t d", t=NT)
            dw = A.f32(NT * 16).rearrange("p (t d) -> p t d", t=NT)
            tot = A.f32(NCH * 16).rearrange("p (c d) -> p c d", c=NCH)
            etot = A.f32(NCH * 16).rearrange("p (c d) -> p c d", c=NCH)
            Abc = A.f32(16)
            t16 = [A.f32(NT * 16).rearrange("p (t d) -> p t d", t=NT) for _ in range(3)]
            cumT = [A.f32(512).rearrange("p (d l) -> p d l", d=2) for _ in range(2)]
            Rst = [A.f32(256) for _ in range(2)]
            Rtmp = A.f32(256)
            Eb = [A.f32(256) for _ in range(4)]
            ya = [A.f32(256) for _ in range(2)]
            yg = [A.f32(256) for _ in range(2)]
            zb = [A.f32(256) for _ in range(2)]
            szb = [A.f32(256) for _ in range(2)]
            tmp = {"sq": A.f32(256), "ss": A.f32(8), "rs": A.f32(8), "yb": [A.bf16(256) for _ in range(2)]}
            xin = [A.bf16(S + 4)] * 2
            dg = A.bf16(5 * 128).rearrange("p (k c) -> p k c", k=5)
            xTf = A.bf16(2 * S).rearrange("p (c t) -> p c t", c=2)
            xs = A.bf16(NT * 256).rearrange("p (t d) -> p t d", t=NT)
            Btok = A.bf16(NT * 128).rearrange("p (t d) -> p t d", t=NT)
            Sinb = [A.bf16((NCH + 1) * 256).rearrange("p (c d) -> p c d", c=NCH + 1) for _ in range(2)]
            Mb = [A.bf16(256) for _ in range(4)]
            GT = [A.bf16(512) for _ in range(2)]
            xdt = [A.bf16(2 * 2 * 256).rearrange("p (r t d) -> p r t d", r=2, t=2) for _ in range(2)]
            xdw = [A.bf16(2 * 256).rearrange("p (t d) -> p t d", t=2) for _ in range(2)]
            k.dma(convw, convp_d[l].rearrange("p (c k) -> p c k", c=8), writes=["convw"])
            ci = [0]

            def conv_chunk(cc, dst_fn):
                xi = 0
                k.op("dve", lambda e, xi=xi: e.memset(xin[xi][:, 0:2], 0.0), writes=[("xin", xi)])
                k.op("dve", lambda e, xi=xi: e.memset(xin[xi][:, S + 2:S + 4], 0.0), writes=[("xin", xi)])
                k.dma(xin[xi][:, 2:S + 2], xbc_d[cc * 128:(cc + 1) * 128, :], reads=[("xbc_d",)], writes=[("xin", xi)])
                for kk in range(5):
                    k.op("dve", lambda e, kk=kk, cc=cc: e.tensor_scalar_mul(out=dg[:, kk, :], in0=identf, scalar1=convw[:, cc, kk:kk + 1]),
                         reads=["cst", "convw"], writes=["dg"])
                for t4 in range(S // 512):
                    pb, pk = PS(ci[0] % 2)
                    ci[0] += 1
                    for kk in range(5):
                        k.op("pe", lambda e, pb=pb, kk=kk, xi=xi, t4=t4: e.matmul(
                            pb[:], lhsT=dg[:, kk, :], rhs=xin[xi][:, t4 * 512 + kk:t4 * 512 + kk + 512],
                            start=(kk == 0), stop=(kk == 4)), reads=["dg", ("xin", xi)], writes=[pk])
                    dst, dkey = dst_fn(t4)
                    k.op("act", lambda e, pb=pb, dst=dst, cc=cc: e.activation(out=dst, in_=pb[:], func=AF.Silu, bias=convw[:, cc, 5:6]),
                         reads=["convw"], writes=[pk, dkey])
                ci[0] += 1

            for g in range(2):
                conv_chunk(4 + g, lambda t4, g=g: (BT[:, g, t4 * 512:(t4 + 1) * 512], ("BT", g, t4)))
                conv_chunk(6 + g, lambda t4, g=g: (CT[:, g, t4 * 512:(t4 + 1) * 512], ("CT", g, t4)))
            for tt in range(NT):
                k.dma(dt[:, tt, :], ptok_d[tt * 128:(tt + 1) * 128, 512:528], reads=[("ptok_d",)], writes=["dt"])
            bias_bc = vrow(vbc, "ssd_dt_bias").unsqueeze(1).to_broadcast([128, NT, 16])
            k.op("dve", lambda e: e.tensor_tensor(out=dt, in0=dt, in1=bias_bc, op=ALU.add), reads=["vbc"], writes=["dt"])
            k.op("dve", lambda e: e.tensor_scalar_mul(out=t16[0], in0=dt, scalar1=-1.0), reads=["dt"], writes=["t16a"])
            k.op("dve", lambda e: e.tensor_tensor(out=t16[0], in0=t16[0], in1=dt, op=ALU.max), reads=["dt"], writes=["t16a"])
            k.op("act", lambda e: e.activation(out=t16[1], in_=t16[0], func=AF.Exp, scale=-1.0), reads=["t16a"], writes=["t16b"])
            k.op("act", lambda e: e.activation(out=t16[0], in_=t16[1], func=AF.Ln, bias=1.0), reads=["t16b"], writes=["t16a"])
            k.op("dve", lambda e: e.tensor_scalar_max(out=t16[1], in0=dt, scalar1=0.0), reads=["dt"], writes=["t16b"])
            k.op("dve", lambda e: e.tensor_tensor(out=dt, in0=t16[0], in1=t16[1], op=ALU.add), reads=["t16a", "t16b"], writes=["dt"])
            k.op("act", lambda e: e.activation(out=Abc, in_=vrow(vbc, "ssd_a_log"), func=AF.Exp), reads=["vbc"], writes=["Abc"])
            k.op("dve", lambda e: e.scalar_tensor_tensor(out=av, in0=dt, scalar=-1.0, in1=Abc.unsqueeze(1).to_broadcast([128, NT, 16]),
                                                         op0=ALU.mult, op1=ALU.mult), reads=["dt", "Abc"], writes=["av"])
            dbg_dump("ssd_dt", dt, ["dt"])
            TRI_LE = cst[:, C_R0F:C_R0F + 128]
            TRI_LT = cst[:, C_R0B:C_R0B + 128]
            for c in range(NCH):
                t0_, t1_ = 2 * c, 2 * c + 1
                pb, pk = PS(c % 2)
                pv = pb[:, 0:48].rearrange("p (t d) -> p t d", t=3)
                for d, tri in ((0, TRI_LE), (1, TRI_LT)):
                    cs_ = slice(d * 8, d * 8 + 8)
                    k.op("pe", lambda e, pv=pv, tri=tri, cs_=cs_, t0_=t0_: e.matmul(pv[:, 0, cs_], lhsT=tri, rhs=av[:, t0_, cs_], start=True, stop=True),
                         reads=["av", "cst"], writes=[pk])
                    k.op("pe", lambda e, pv=pv, cs_=cs_, t0_=t0_: e.matmul(pv[:, 1, cs_], lhsT=ones_f, rhs=av[:, t0_, cs_], start=False, stop=False,
                                                                          skip_group_check=True), reads=["av", "cst"], writes=[pk])
                    k.op("pe", lambda e, pv=pv, tri=tri, cs_=cs_, t1_=t1_: e.matmul(pv[:, 1, cs_], lhsT=tri, rhs=av[:, t1_, cs_], start=False, stop=True,
                                                                                  skip_group_check=True), reads=["av", "cst"], writes=[pk])
                k.op("pe", lambda e, pv=pv, t0_=t0_: e.matmul(pv[:, 2, :], lhsT=ones_f, rhs=av[:, t0_, :], start=False, stop=False, skip_group_check=True),
                     reads=["av", "cst"], writes=[pk])
                k.op("pe", lambda e, pv=pv, t1_=t1_: e.matmul(pv[:, 2, :], lhsT=ones_f, rhs=av[:, t1_, :], start=False, stop=True, skip_group_check=True),
                     reads=["av", "cst"], writes=[pk])
                copy_op("dve", cum[:, t0_:t0_ + 2, :], pv[:, 0:2, :], reads=[], writes=[pk, "cum"])
                copy_op("dve", tot[:, c, :], pv[:, 2, :], reads=[], writes=[pk, "tot"])
            totb = lambda d: tot[:, :, d * 8:d * 8 + 8].unsqueeze(2).to_broadcast([128, NCH, 2, 8])
            v4 = lambda a, d: a[:, :, d * 8:d * 8 + 8].rearrange("p (c t) h -> p c t h", t=2)
            k.op("dve", lambda e: e.tensor_scalar_mul(out=ncf, in0=cum[:, :, 0:8], scalar1=-1.0), reads=["cum"], writes=["ncf"])
            k.op("act", lambda e: e.activation(out=sc1[:, :, 0:8], in_=cum[:, :, 0:8], func=AF.Exp), reads=["cum"], writes=["sc1a"])
            k.op("act", lambda e: e.activation(out=sc2[:, :, 8:16], in_=cum[:, :, 8:16], func=AF.Exp), reads=["cum"], writes=["sc2b"])
            k.op("dve", lambda e: e.tensor_tensor(out=v4(t16[2], 0), in0=totb(0), in1=v4(cum, 0), op=ALU.subtract), reads=["cum", "tot"], writes=["t16c0"])
            k.op("dve", lambda e: e.tensor_tensor(out=v4(t16[2], 1), in0=totb(1), in1=v4(cum, 1), op=ALU.subtract), reads=["cum", "tot"], writes=["t16c1"])
            k.op("act", lambda e: e.activation(out=sc2[:, :, 0:8], in_=t16[2][:, :, 0:8], func=AF.Exp), reads=["t16c0"], writes=["sc2a"])
            k.op("act", lambda e: e.activation(out=sc1[:, :, 8:16], in_=t16[2][:, :, 8:16], func=AF.Exp), reads=["t16c1"], writes=["sc1b"])
            k.op("act", lambda e: e.activation(out=etot, in_=tot, func=AF.Exp), reads=["tot"], writes=["etot"])
            k.op("dve", lambda e: e.tensor_tensor(out=dw, in0=dt, in1=sc2, op=ALU.mult), reads=["dt", "sc2a", "sc2b"], writes=["dw"])
            dbg_dump("ssd_cum", cum, ["cum"])
            Dbc = vrow(vbc, "ssd_d")
            v3 = lambda a: a.rearrange("p (h d) -> p h d", h=4)
            for g in range(2):
                for j in range(2):
                    conv_chunk(2 * g + j, lambda t4, j=j: (xTf[:, j, t4 * 512:(t4 + 1) * 512], ("xTf", t4)))
                for tt in range(NT):
                    pb, pk = PS(6)
                    pv = pb[:, 0:256].rearrange("p (c t) -> p c t", c=2)
                    for j in range(2):
                        k.op("pe", lambda e, pv=pv, j=j, tt=tt: e.transpose(pv[:, j, :], xTf[:, j, tt * 128:(tt + 1) * 128], identb),
                             reads=[("xTf", tt // 4), "cstb"], writes=[pk])
                    k.op("pe", lambda e, pb=pb, tt=tt: e.transpose(pb[:, 256:384], BT[:, g, tt * 128:(tt + 1) * 128], identb),
                         reads=[("BT", g, tt // 4), "cstb"], writes=[pk])
                    copy_op("act", xs[:, tt, :], pb[:, 0:256], reads=[], writes=[pk, ("xs", tt)])
                    copy_op("dve", Btok[:, tt, :], pb[:, 256:384], reads=[], writes=[pk, ("Btok", tt)])
                if g == 0:
                    dbg_dump("ssd_xs", xs, [("xs", tt) for tt in range(NT)], BF16)
                xs4 = xs.rearrange("p t (h d) -> p t h d", h=4)
                wi = 0
                for d in range(2):
                    k.op("dve", lambda e, d=d: e.memset(Rst[d], 0.0), writes=[("Rst", d)])
                    order = list(range(NCH)) if d == 0 else list(range(NCH - 1, -1, -1))
                    first_slot = 0 if d == 0 else NCH
                    k.op("dve", lambda e, d=d, first_slot=first_slot: e.memset(Sinb[d][:, first_slot, :], 0.0), writes=[("Sinb", d, first_slot)])
                    for c in order:
                        wr = wi % 2
                        wi += 1
                        dwb = dw[:, 2 * c:2 * c + 2, d * 8 + 4 * g:d * 8 + 4 * g + 4].unsqueeze(3).to_broadcast([128, 2, 4, 64])
                        k.op("dve", lambda e, wr=wr, dwb=dwb, c=c: e.tensor_tensor(out=xdw[wr].rearrange("p t (h d) -> p t h d", h=4),
                                                                                in0=xs4[:, 2 * c:2 * c + 2, :, :], in1=dwb, op=ALU.mult),
                             reads=[("xs", 2 * c), ("xs", 2 * c + 1), "dw"], writes=[("xdw", wr)])
                        pb, pk = PS(c % 2)
                        for ti in range(2):
                            tt = 2 * c + ti
                            k.op("pe", lambda e, pb=pb, tt=tt, ti=ti, wr=wr: e.matmul(
                                pb[:, 0:256], lhsT=Btok[:, tt, :], rhs=xdw[wr][:, ti, :], start=(ti == 0), stop=(ti == 1)),
                                reads=[("Btok", tt), ("xdw", wr)], writes=[pk])
                        etb = etot[:, c, d * 8 + 4 * g:d * 8 + 4 * g + 4].unsqueeze(2).to_broadcast([128, 4, 64])
                        k.op("dve", lambda e, d=d, etb=etb: e.tensor_tensor(out=v3(Rtmp), in0=v3(Rst[d]), in1=etb, op=ALU.mult),
                             reads=[("Rst", d), "etot"], writes=["Rtmp"])
                        k.op("dve", lambda e, pb=pb, d=d: e.tensor_tensor(out=Rst[d], in0=pb[:, 0:256], in1=Rtmp, op=ALU.add),
                             reads=["Rtmp"], writes=[pk, ("Rst", d)])
                        slot = c + 1 if d == 0 else c
                        k.op("act", lambda e, d=d, slot=slot: e.copy(Sinb[d][:, slot, :], Rst[d]), reads=[("Rst", d)], writes=[("Sinb", d, slot)])
                def prologue(c):
                    t0_, t1_ = 2 * c, 2 * c + 1
                    cr = c % 2
                    pb2, pk2 = PS(7)
                    for d, (r0, r1) in ((0, (C_R0F, C_R1F)), (1, (C_R0B, C_R1B))):
                        cs_ = slice(d * 8, d * 8 + 8)
                        o_ = pb2[0:8, d * 256:(d + 1) * 256]
                        k.op("pe", lambda e: e.matmul(o_, lhsT=av[:, t0_, cs_], rhs=cst[:, r0:r0 + 256],
                                                      start=(d == 0), stop=False, skip_group_check=True),
                             reads=["av", "cst"], writes=[pk2])
                        k.op("pe", lambda e: e.matmul(o_, lhsT=av[:, t1_, cs_], rhs=cst[:, r1:r1 + 256],
                                                      start=False, stop=True, skip_group_check=True),
                             reads=["av", "cst"], writes=[pk2])
                    copy_op("act", cumT[cr][0:8, :, :], pb2[0:8, 0:512].rearrange("p (d l) -> p d l", d=2), reads=[], writes=[pk2, ("cumT", cr)])
                    for d in range(2):
                        dtb = dt[:, 2 * c:2 * c + 2, d * 8 + 4 * g:d * 8 + 4 * g + 4].unsqueeze(3).to_broadcast([128, 2, 4, 64])
                        k.op("dve", lambda e: e.tensor_tensor(out=xdt[d][:, cr].rearrange("p t (h d) -> p t h d", h=4),
                                                              in0=xs4[:, 2 * c:2 * c + 2, :, :], in1=dtb, op=ALU.mult),
                             reads=[("xs", 2 * c), ("xs", 2 * c + 1), "dt"], writes=[("xdt", d, cr)])
                    pb, pk = PS(7)
                    for si in range(2):
                        k.op("pe", lambda e: e.matmul(pb[:, si * 256:(si + 1) * 256], lhsT=BT[:, g, (2 * c + si) * 128:(2 * c + si + 1) * 128],
                                                      rhs=CT[:, g, c * 256:(c + 1) * 256], start=(si == 0), stop=True, skip_group_check=True),
                             reads=[("BT", g, (2 * c + si) // 4), ("CT", g, c // 2)], writes=[pk])
                    copy_op("act", GT[cr], pb[:, 0:512], reads=[], writes=[pk, ("GT", cr)])

                def item_front(c, it):
                    h, d = it // 2, it % 2
                    hh = 4 * g + h
                    cr = c % 2
                    sel = cst[0:8, C_SEL + hh * 128:C_SEL + (hh + 1) * 128]
                    for si in range(2):
                        pb, pk = PS(2 * (it % 2) + si)
                        if d == 0:
                            lo, hi = (0, 256) if si == 0 else (128, 256)
                            mask = cstb[:, 640:896] if si == 0 else cstb[:, 640:768]
                        else:
                            lo, hi = (0, 128) if si == 0 else (0, 256)
                            mask = cstb[:, 896 + 128:896 + 256] if si == 0 else cstb[:, 896:896 + 256]
                        w = hi - lo
                        k.op("pe", lambda e: e.matmul(pb[:, 0:w], lhsT=sel, rhs=cumT[cr][0:8, d, lo:hi], start=True, stop=False),
                             reads=[("cumT", cr), "cst"], writes=[pk])
                        k.op("pe", lambda e: e.matmul(pb[:, 0:w], lhsT=identb, rhs=mask, start=False, stop=True),
                             reads=["cstb"], writes=[pk])

                def item_back(c, it, started, ybank, ykey):
                    h, d = it // 2, it % 2
                    hh = 4 * g + h
                    cr = c % 2
                    mms = []
                    for si in range(2):
                        pb, pk = PS(2 * (it % 2) + si)
                        if d == 0:
                            lo, hi = (0, 256) if si == 0 else (128, 256)
                        else:
                            lo, hi = (0, 128) if si == 0 else (0, 256)
                        w = hi - lo
                        ebi = 2 * d + si
                        tt_s = 2 * c + si
                        if d == 0:
                            k.op("act", lambda e: e.activation(out=Eb[ebi][:, 0:w], in_=pb[:, 0:w], func=AF.Exp, bias=ncf[:, tt_s, hh:hh + 1], scale=1.0),
                                 reads=["ncf"], writes=[pk, ("Eb", ebi)])
                        else:
                            k.op("act", lambda e: e.activation(out=Eb[ebi][:, 0:w], in_=pb[:, 0:w], func=AF.Exp, bias=cum[:, tt_s, 8 + hh:9 + hh], scale=-1.0),
                                 reads=["cum"], writes=[pk, ("Eb", ebi)])
                        k.op("dve", lambda e: e.tensor_tensor(out=Mb[ebi][:, 0:w], in0=Eb[ebi][:, 0:w], in1=GT[cr][:, si * 256 + lo:si * 256 + hi], op=ALU.mult),
                             reads=[("Eb", ebi), ("GT", cr)], writes=[("Mb", ebi)])
                        for li in range(2):
                            if lo <= li * 128 < hi:
                                mms.append((ebi, li * 128 - lo, li, si))
                    for (ebi, off, li, si) in mms:
                        st_ = not started[li]
                        started[li] = True
                        k.op("pe", lambda e: e.matmul(
                            ybank[li][:, h * 64:(h + 1) * 64], lhsT=Mb[ebi][:, off:off + 128], rhs=xdt[d][:, cr, si, h * 64:(h + 1) * 64],
                            start=st_, stop=False, skip_group_check=True),
                            reads=[("Mb", ebi), ("xdt", d, cr)], writes=[ykey[li]])

                def epilogue(c, ybank, ykey):
                    for li in range(2):
                        tt = 2 * c + li
                        pf7, pfk = PS(6 if False else 7)
                        pf = pf7[:, 0:256]
                        pbw = pf7[:, 256:512]
                        k.op("pe", lambda e: e.matmul(pf, lhsT=CT[:, g, tt * 128:(tt + 1) * 128], rhs=Sinb[0][:, c, :],
                                                      start=True, stop=True), reads=[("CT", g, tt // 4), ("Sinb", 0, c)], writes=[pfk])
                        k.op("pe", lambda e: e.matmul(pbw, lhsT=CT[:, g, tt * 128:(tt + 1) * 128], rhs=Sinb[1][:, c + 1, :],
                                                      start=False, stop=True, skip_group_check=True),
                             reads=[("CT", g, tt // 4), ("Sinb", 1, c + 1)], writes=[pfk])
                        ecfb = sc1[:, tt, 4 * g:4 * g + 4].unsqueeze(2).to_broadcast([128, 4, 64])
                        erbb = sc1[:, tt, 8 + 4 * g:12 + 4 * g].unsqueeze(2).to_broadcast([128, 4, 64])
                        dbb = Dbc[:, 4 * g:4 * g + 4].unsqueeze(2).to_broadcast([128, 4, 64])
                        yi = li
                        k.op("dve", lambda e: e.tensor_tensor(out=v3(ya[yi]), in0=v3(pf), in1=ecfb, op=ALU.mult),
                             reads=["sc1a"], writes=[pfk, ("ya", yi)])
                        k.op("dve", lambda e: e.tensor_tensor(out=v3(Rtmp), in0=v3(pbw), in1=erbb, op=ALU.mult),
                             reads=["sc1b"], writes=[pfk, "Rtmp"])
                        k.op("dve", lambda e: e.tensor_tensor(out=ya[yi], in0=ya[yi], in1=Rtmp, op=ALU.add), reads=["Rtmp"], writes=[("ya", yi)])
                        k.op("dve", lambda e: e.tensor_tensor(out=v3(Rtmp), in0=xs4[:, tt, :, :], in1=dbb, op=ALU.mult),
                             reads=[("xs", tt), "vbc"], writes=["Rtmp"])
                        k.op("dve", lambda e: e.tensor_tensor(out=ya[yi], in0=ya[yi], in1=Rtmp, op=ALU.add), reads=["Rtmp"], writes=[("ya", yi)])
                        k.op("dve", lambda e: e.tensor_tensor(out=ya[yi], in0=ybank[li][:, 0:256], in1=ya[yi], op=ALU.add),
                             reads=[], writes=[ykey[li], ("ya", yi)])
                        zi = li
                        k.dma(zb[zi], ptok_d[tt * 128:(tt + 1) * 128, g * 256:(g + 1) * 256], reads=[("ptok_d",)], writes=[("zb", zi)])
                        k.op("act", lambda e: e.activation(out=szb[zi], in_=zb[zi], func=AF.Silu), reads=[("zb", zi)], writes=[("szb", zi)])
                        k.op("dve", lambda e: e.tensor_tensor(out=yg[yi], in0=ya[yi], in1=szb[zi], op=ALU.mult),
                             reads=[("szb", zi), ("ya", yi)], writes=[("yg", yi)])
                        out_norm_store(yg[yi], [("yg", yi)], 256, vrow(vbc, "ssd_norm", g * 256, 256), g * 256, tmp, tt, tt, "do")

                prologue(0)
                for c in range(NCH):
                    yb0, yk0 = PS(4)
                    yb1, yk1 = PS(5)
                    ybank = (yb0, yb1)
                    ykey = (yk0, yk1)
                    started = [False, False]
                    item_front(c, 0)
                    for it in range(8):
                        if it + 1 < 8:
                            item_front(c, it + 1)
                        item_back(c, it, started, ybank, ykey)
                    if c + 1 < NCH:
                        prologue(c + 1)
                    epilogue(c, ybank, ykey)
            A.reset(fbase, bbase)

        def mixer_phase(l):
            A.reset()
            vbc = A.f32(NV)
            k.dma(vbc, vec_d[l:l + 1, :].partition_broadcast(128), writes=["vbc"])
            ssd_phase(l, vbc)
            k.barrier(barsc[:, 1:2])
            swa_phase(l, vbc)
            k.barrier(barsc[:, 2:3])
            mla_phase(l, vbc)
            if l == 0:
                k.barrier(barsc[:, 3:4])
                dbg_dump("ptok", ptok_d, [("ptok_d",)], big=True)
                dbg_dump("xbc", xbc_d, [("xbc_d",)], BF16, big=True)
                dbg_dump("y", y_d, [("y_d",)], BF16, big=True)

        finals = []
        for p in range(L + 1):
            l_prev = p - 1 if p > 0 else None
            l_next = p if p < L else None
            if stop_after is not None and p > stop_after[0]:
                break
            finals = block_phase(p, l_prev, l_next)
            k.barrier(barsc[:, 0:1])
            if l_next is not None:
                if stop_after is not None and stop_after == (p, "block"):
                    break
                mixer_phase(l_next)
                k.barrier(barsc[:, 0:1])
                if stop_after is not None and stop_after == (p, "mix"):
                    break
        k.emit(list(finals) + dbg_outs)
        print("ops recorded:", k.nops, "arena hi (KiB):", A.hi * 4 / 1024, A.hib * 2 / 1024)
    return nc


def prep_common(inp, S):
    L = inp["ffn1_norm"].shape[0]
    vec = np.zeros((L, NV), np.float32)
    for n, (o, s) in VOFF.items():
        vec[:, o:o + s] = np.asarray(inp[n], np.float32).reshape(L, s)
    cw = np.asarray(inp["ssd_conv_w"], np.float32)
    cb = np.asarray(inp["ssd_conv_b"], np.float32)
    convp = np.zeros((L, 128, 8, 6), np.float32)
    for c in range(8):
        convp[:, :, c, 0:5] = cw[:, :, c * 128:(c + 1) * 128].transpose(0, 2, 1)
        convp[:, :, c, 5] = cb[:, c * 128:(c + 1) * 128]
    com = {"vecs": vec, "convp": np.ascontiguousarray(convp.reshape(L, 128, 48)), "cst": make_consts(),
           "cs_swa": rope_tables(S, 64), "cs_mla": rope_tables(S, 32)}
    for n in ("ffn1_gate", "ffn1_up", "ffn1_down", "ffn2_gate", "ffn2_up", "ffn2_down", "w_in", "w_out",
              "mla_w_uq", "mla_w_ukv"):
        com[n] = np.ascontiguousarray(np.asarray(inp[n], np.float32))
    return com, L


_NC_CACHE = {}


def kernel(**inputs):
    x = np.asarray(inputs["x"], np.float32)
    B, S, _ = x.shape
    com, L = prep_common(inputs, S)
    key = (S, L)
    if key not in _NC_CACHE:
        _NC_CACHE[key] = build(S, L)
    nc = _NC_CACHE[key]
    ncores = 8
    place = [0, 1, 4, 5][:B] if B <= 4 else list(range(B))
    zero_map = None
    in_maps = []
    for c in range(ncores):
        if c in place:
            m = dict(com)
            m["x"] = np.ascontiguousarray(x[place.index(c)])
        else:
            if zero_map is None:
                zero_map = {kk: np.zeros_like(v) for kk, v in com.items()}
                zero_map["x"] = np.zeros((S, D), np.float32)
            m = zero_map
        in_maps.append(m)
    res = run_bass_kernel_spmd(nc, in_maps, core_ids=list(range(ncores)))
    out = np.stack([np.asarray(res.results[place[b]]["out"], np.float32) for b in range(B)], axis=0)
    return out
```

```python
import contextlib
import numpy as np
import concourse.bass as bass
import concourse.mybir as mybir
from concourse.bass_utils import run_bass_kernel_spmd

F32 = mybir.dt.float32
BF16 = mybir.dt.bfloat16
AF = mybir.ActivationFunctionType
ALU = mybir.AluOpType
AX = mybir.AxisListType

D = 1024
DFF = 2816
NPROJ = 2480
NTM = 1456
EPS = 1e-6
BIG = 1.0e4
GF = 4
DEBUG = False
ALT_DVE_ONLY = False
N_DMA_SEMS = 40

VOFF = {}
_o = 0
for _n, _s in (("ffn1_norm", 1024), ("mix_norm", 1024), ("ffn2_norm", 1024), ("ssd_norm", 512),
               ("swa_q_norm", 64), ("swa_k_norm", 64), ("swa_out_norm", 256), ("mla_q_lat_norm", 256),
               ("mla_kv_norm", 128), ("mla_q_norm", 96), ("mla_k_norm", 96), ("mla_out_norm", 256),
               ("ssd_dt_bias", 16), ("ssd_a_log", 16), ("ssd_d", 8), ("swa_sink", 4)):
    VOFF[_n] = (_o, _s)
    _o += _s
NV = _o

C_ID = 0
C_R0F = 128
C_R1F = 384
C_R0B = 640
C_R1B = 896
C_NEGF = 1152
C_POSB = 1408
C_GE = 1664
C_SEL = 1792
NCST = C_SEL + 1024


def make_consts():
    c = np.zeros((128, NCST), np.float32)
    i = np.arange(128)
    le = (i[:, None] <= i[None, :]).astype(np.float32)
    lt = (i[:, None] < i[None, :]).astype(np.float32)
    ge = (i[:, None] >= i[None, :]).astype(np.float32)
    gt = (i[:, None] > i[None, :]).astype(np.float32)
    c[:, C_ID:C_ID + 128] = np.eye(128)
    c[:, C_R0F:C_R0F + 128] = le
    c[:, C_R0F + 128:C_R0F + 256] = 1.0
    c[:, C_R1F + 128:C_R1F + 256] = le
    c[:, C_R0B:C_R0B + 128] = lt
    c[:, C_R0B + 128:C_R0B + 256] = 1.0
    c[:, C_R1B + 128:C_R1B + 256] = lt
    c[:, C_NEGF:C_NEGF + 128] = -BIG * gt
    c[:, C_POSB + 128:C_POSB + 256] = BIG * lt
    c[:, C_GE:C_GE + 128] = ge
    for h in range(8):
        c[h, C_SEL + h * 128:C_SEL + (h + 1) * 128] = 1.0
        c[32 + h, C_SEL + h * 128:C_SEL + (h + 1) * 128] = 1.0
    return c


def rope_tables(n, dim):
    inv = 1.0 / np.power(np.float32(10000.0), np.arange(0, dim, 2, dtype=np.float32) / np.float32(dim))
    ang = np.arange(n, dtype=np.float32)[:, None] * inv[None, :].astype(np.float32)
    return np.concatenate([np.cos(ang), np.sin(ang)], axis=1).astype(np.float32)


def _freeze(fn):
    import types
    if fn.__closure__ is None:
        return fn
    cells = []
    for c in fn.__closure__:
        try:
            cells.append(types.CellType(c.cell_contents))
        except ValueError:
            cells.append(c)
    return types.FunctionType(fn.__code__, fn.__globals__, fn.__name__, fn.__defaults__, tuple(cells))


class Op:
    __slots__ = ("eng", "fn", "waits", "signal", "sem", "val", "is_dma")

    def __init__(self, eng, fn, is_dma=False):
        self.eng = eng
        self.fn = fn
        self.waits = []
        self.signal = False
        self.sem = None
        self.val = None
        self.is_dma = is_dma


class KB:
    ENGS = ("pe", "act", "dve", "pool", "sp")

    def __init__(self, nc):
        self.nc = nc
        self.prog = {e: [] for e in self.ENGS}
        self.last_w = {}
        self.readers = {}
        self.dma_rr = 0
        self.dma_last = [None] * N_DMA_SEMS
        self.dma_uses = [0] * N_DMA_SEMS
        self.bar = None
        self.nops = 0
        self.carrier = None

    def _deps(self, op, reads, writes):
        deps = []
        if self.bar is not None:
            deps.append(self.bar)
        for k in reads:
            w = self.last_w.get(k)
            if w is not None:
                deps.append(w)
        for k in writes:
            w = self.last_w.get(k)
            if w is not None:
                deps.append(w)
            deps.extend(self.readers.get(k, ()))
        for k in reads:
            self.readers.setdefault(k, []).append(op)
        for k in writes:
            self.last_w[k] = op
            self.readers[k] = []
        seen = set()
        for d in deps:
            if d is op or id(d) in seen:
                continue
            seen.add(id(d))
            if op.eng == "pe" and d.eng == "pe" and not d.is_dma and not op.is_dma:
                continue
            op.waits.append(d)

    def op(self, eng, fn, reads=(), writes=()):
        o = Op(eng, _freeze(fn))
        self._deps(o, reads, writes)
        self.prog[eng].append(o)
        self.nops += 1
        return o

    def dma(self, out, in_, reads=(), writes=(), q="sp", **kw):
        o = Op(q, None, is_dma=True)
        o.fn = lambda e, out=out, in_=in_, kw=kw: e.dma_start(out=out, in_=in_, **kw)
        self._deps(o, reads, writes)
        i = self.dma_rr
        self.dma_rr = (self.dma_rr + 1) % N_DMA_SEMS
        prev = self.dma_last[i]
        if prev is not None:
            o.waits.append(prev)
        self.dma_last[i] = o
        self.dma_uses[i] += 1
        o.sem = i
        o.val = 16 * self.dma_uses[i]
        o.signal = True
        self.prog[q].append(o)
        self.nops += 1
        return o

    def barrier(self, scratch_ap):
        o = Op("dve", lambda e: e.memset(scratch_ap, 0.0))
        if self.bar is not None:
            o.waits.append(self.bar)
        for e in self.ENGS:
            for p in reversed(self.prog[e]):
                if not p.is_dma:
                    o.waits.append(p)
                    break
        for d in self.dma_last:
            if d is not None:
                o.waits.append(d)
        self.prog["dve"].append(o)
        self.bar = o
        self.last_w = {}
        self.readers = {}
        return o

    def emit(self, final_wait_ops=()):
        nc = self.nc
        for e in self.ENGS:
            for o in self.prog[e]:
                for w in o.waits:
                    w.signal = True
        for o in final_wait_ops:
            o.signal = True
        for e in self.ENGS:
            c = 0
            for o in self.prog[e]:
                if o.is_dma:
                    continue
                if o.signal:
                    c += 1
                    o.sem = e
                    o.val = c
        with contextlib.ExitStack() as st:
            esem = {e: st.enter_context(nc.semaphore("s_" + e)) for e in self.ENGS}
            dsem = [st.enter_context(nc.semaphore("d_%d" % i)) for i in range(N_DMA_SEMS)]
            block = st.enter_context(nc.Block())

            def sem_of(o):
                return dsem[o.sem] if o.is_dma else esem[o.sem]

            def run(e, h):
                waited = {}
                for o in self.prog[e]:
                    need = {}
                    for w in o.waits:
                        key = ("d", w.sem) if w.is_dma else ("e", w.sem)
                        if waited.get(key, 0) >= w.val:
                            continue
                        waited[key] = w.val
                        need[key] = w
                    need = list(need.values())
                    if e == "pe" and not o.is_dma:
                        for w in need[:-1]:
                            h.ldweights(self.carrier)._wait_ge(sem_of(w), w.val)
                        ins = o.fn(h)
                        if need:
                            ins._wait_ge(sem_of(need[-1]), need[-1].val)
                    else:
                        for w in need:
                            h.wait_ge(sem_of(w), w.val)
                        ins = o.fn(h)
                    if o.is_dma:
                        ins.then_inc(dsem[o.sem], 16)
                    elif o.signal:
                        ins.then_inc(esem[e], 1)
                if e == "sp":
                    for o in final_wait_ops:
                        h.wait_ge(sem_of(o), o.val)

            @block.tensor
            def _(t):
                run("pe", t)

            @block.scalar
            def _(s):
                run("act", s)

            @block.vector
            def _(v):
                run("dve", v)

            @block.gpsimd
            def _(g):
                run("pool", g)

            @block.sync
            def _(s):
                run("sp", s)


class Arena:
    def __init__(self, apf, nf, apb, nb):
        self.apf, self.nf, self.apb, self.nb = apf, nf, apb, nb
        self.p = 0
        self.pb = 0
        self.hi = 0
        self.hib = 0

    def reset(self, to=0, tob=0):
        self.p = to
        self.pb = tob

    def f32(self, cols):
        a = self.apf[:, self.p:self.p + cols]
        self.p += (cols + 7) // 8 * 8
        self.hi = max(self.hi, self.p)
        assert self.p <= self.nf, ("f32 arena overflow", self.p, self.nf)
        return a

    def bf16(self, cols):
        a = self.apb[:, self.pb:self.pb + cols]
        self.pb += (cols + 15) // 16 * 16
        self.hib = max(self.hib, self.pb)
        assert self.pb <= self.nb, ("bf16 arena overflow", self.pb, self.nb)
        return a


def build(S, L, stop_after=None):
    assert S % 512 == 0
    NT = S // 128
    TB = min(1024, S)
    NBLK = S // TB
    NTB = TB // 128
    NCH = S // 256

    nc = bass.Bass("TRN2", target_bir_lowering=False)
    dr = lambda name, shape, dt=F32, kind="ExternalInput": nc.dram_tensor(name, list(shape), dt, kind=kind).ap()
    x_d = dr("x", [S, D])
    wg_d = [dr("ffn1_gate", [L, D, DFF]), dr("ffn2_gate", [L, D, DFF])]
    wu_d = [dr("ffn1_up", [L, D, DFF]), dr("ffn2_up", [L, D, DFF])]
    wd_d = [dr("ffn1_down", [L, DFF, D]), dr("ffn2_down", [L, DFF, D])]
    win_d = dr("w_in", [L, D, NPROJ])
    wout_d = dr("w_out", [L, D, D])
    wuq_d = dr("mla_w_uq", [L, 256, 384])
    wukv_d = dr("mla_w_ukv", [L, 128, 512])
    vec_d = dr("vecs", [L, NV])
    convp_d = dr("convp", [L, 128, 48])
    cst_d = dr("cst", [128, NCST])
    cs_swa_d = dr("cs_swa", [S, 64])
    cs_mla_d = dr("cs_mla", [S, 32])
    out_d = dr("out", [S, D], kind="ExternalOutput")
    xbc_d = dr("xbc_scr", [1024, S], BF16, kind="Internal")
    ptok_d = dr("ptok_scr", [S, NTM], F32, kind="Internal")
    y_d = dr("y_scr", [S, D], BF16, kind="Internal")

    ARENA_F = 17 * 1024
    ARENA_B = 58 * 1024
    st = contextlib.ExitStack()
    with st:
        arena_f = st.enter_context(nc.sbuf_tensor("arena_f", [128, ARENA_F], F32))
        arena_b = st.enter_context(nc.sbuf_tensor("arena_b", [128, ARENA_B], BF16))
        cst = st.enter_context(nc.sbuf_tensor("cst_sb", [128, NCST], F32))
        cstb = st.enter_context(nc.sbuf_tensor("cstb_sb", [128, 128 * 5 + 512 + 1024], BF16))
        barsc = st.enter_context(nc.sbuf_tensor("barsc", [128, 8], F32))
        banks = [st.enter_context(nc.psum_tensor("bank%d" % i, [128, 1024] if i == 6 else [128, 512], BF16 if i == 6 else F32))
                 for i in range(8)]
        k = KB(nc)
        A = Arena(arena_f[:], ARENA_F, arena_b[:], ARENA_B)

        k.carrier = cstb[:, 0:1]
        identf = cst[:, C_ID:C_ID + 128]
        identb = cstb[:, 0:128]
        ones_f = cst[:, C_R0F + 128:C_R0F + 256]

        k.dma(cst[:], cst_d, writes=["cst"])
        k.dma(cstb[:, 0:128], cst_d[:, C_ID:C_ID + 128], writes=["cstb"], q="pool")
        for j in range(2):
            k.dma(cstb[:, 128 + j * 128:256 + j * 128], cst_d[:, C_GE:C_GE + 128], writes=["cstb"], q="pool")
            k.dma(cstb[:, 384 + j * 128:512 + j * 128], cst_d[:, C_R0F:C_R0F + 128], writes=["cstb"], q="pool")

        k.dma(cstb[:, 640:896], cst_d[:, C_NEGF:C_NEGF + 256], writes=["cstb"], q="pool")
        k.dma(cstb[:, 896:1152], cst_d[:, C_POSB:C_POSB + 256], writes=["cstb"], q="pool")
        k.dma(cstb[:, 1152:2176], cst_d[:, C_SEL:C_SEL + 1024], writes=["cstb"], q="pool")

        def PS(i):
            return banks[i], "bank%d" % i

        rr = {"n": 0}
        dbg_outs = []
        dbg_names = set()

        def dbg_dump(name, ap, reads, dt=F32, big=False):
            if not DEBUG or ("dbg_" + name) in dbg_names:
                return
            dbg_names.add("dbg_" + name)
            t = nc.dram_tensor("dbg_" + name, list(ap.shape), dt, kind="ExternalOutput").ap()
            if big:
                for r0 in range(0, ap.shape[0], 256):
                    dbg_outs.append(k.dma(t[r0:r0 + 256], ap[r0:r0 + 256], reads=reads))
            else:
                dbg_outs.append(k.dma(t, ap, reads=reads))

        def alt():
            rr["n"] += 1
            return "dve" if (rr["n"] % 2 or ALT_DVE_ONLY) else "act"

        def copy_op(eng, out, in_, reads, writes):
            if eng == "act":
                return k.op("act", lambda e: e.copy(out, in_), reads, writes)
            return k.op(eng, lambda e: e.tensor_copy(out, in_), reads, writes)

        def vrow(vbc, name, lo=0, n=None):
            o, s = VOFF[name]
            n = s - lo if n is None else n
            return vbc[:, o + lo:o + lo + n]

        def block_phase(pidx, l_prev, l_next):
            A.reset()
            vbcA = A.f32(3072) if l_prev is not None else None
            vbcB = A.f32(3072) if l_next is not None else None
            xb = A.f32(NTB * D).rearrange("p (t d) -> p t d", t=NTB)
            hT = A.bf16(8 * TB).rearrange("p (c t) -> p c t", c=8)
            hb = [A.bf16(D) for _ in range(2)]
            junk = A.f32(D)
            ss = A.f32(NTB)
            sq = A.f32(NTB)
            rstd = A.f32(NTB)
            wgu = [A.bf16(2 * 8 * GF * 128).rearrange("p (w c f) -> p w c f", w=2, c=8) for _ in range(2)]
            wdn = [A.bf16(GF * D).rearrange("p (c d) -> p c d", c=GF) for _ in range(2)]
            sg = [A.bf16(512) for _ in range(2)]
            aT = [A.bf16(GF * 512).rearrange("p (c t) -> p c t", c=GF) for _ in range(2)]
            wpj = [A.bf16(8 * 512).rearrange("p (c f) -> p c f", c=8) for _ in range(2)]
            evf = [A.bf16(512) for _ in range(2)]
            evt = [A.f32(512) for _ in range(2)]
            cnt = {"w": 0, "p": 0, "a": 0, "d": 0, "t": 0, "e": 0, "g": 0}
            if l_prev is not None:
                k.dma(vbcA, vec_d[l_prev:l_prev + 1, 0:3072].partition_broadcast(128), writes=["vbcA"])
            if l_next is not None:
                k.dma(vbcB, vec_d[l_next:l_next + 1, 0:3072].partition_broadcast(128), writes=["vbcB"])

            def norm_to_hT(vbc, vkey, nname, tag):
                k.op("dve", lambda e: e.memset(ss, 0.0), writes=["ss"])
                for tt in range(NTB):
                    k.op("act", lambda e, tt=tt: e.activation(out=junk, in_=xb[:, tt, :], func=AF.Square,
                                                              accum_out=ss[:, tt:tt + 1]),
                         reads=[("xb", tt)], writes=["junk", "ss"])
                k.op("act", lambda e: e.activation(out=sq, in_=ss, func=AF.Sqrt, scale=1.0 / D, bias=EPS),
                     reads=["ss"], writes=["sq"])
                k.op("dve", lambda e: e.reciprocal(out=rstd, in_=sq), reads=["sq"], writes=["rstd"])
                wv = vrow(vbc, nname)
                if tag == "f" and not cnt.get("dbg1"):
                    cnt["dbg1"] = 1
                    dbg_dump("ss", ss, ["ss"]); dbg_dump("rstd", rstd, ["rstd"]); dbg_dump("wv", wv, [vkey])
                    dbg_dump("xb0", xb[:, 0, :], [("xb", 0)])
                for tt in range(NTB):
                    hbi = tt % 2
                    k.op("dve", lambda e, tt=tt, hbi=hbi: e.scalar_tensor_tensor(
                        out=hb[hbi], in0=xb[:, tt, :], scalar=rstd[:, tt:tt + 1], in1=wv,
                        op0=ALU.mult, op1=ALU.mult), reads=[("xb", tt), "rstd", vkey], writes=[("hb", hbi)])
                    transpose_rows(hb[hbi], ("hb", hbi), 8, hT, tt, "hT")

            def transpose_rows(src, skey, nchunks, dstT, tt, dkey):
                for h0 in range(0, nchunks, 4):
                    nn = min(4, nchunks - h0)
                    pb, pk = PS(6)
                    pv = pb[:, (cnt["t"] % 2) * 512:(cnt["t"] % 2) * 512 + 512].rearrange("p (c t) -> p c t", c=4)
                    cnt["t"] += 1
                    for j in range(nn):
                        k.op("pe", lambda e, j=j, h0=h0: e.transpose(pv[:, j, :], src[:, (h0 + j) * 128:(h0 + j + 1) * 128], identb),
                             reads=[skey, "cstb"], writes=[pk])
                    copy_op(alt(), dstT[:, h0:h0 + nn, tt * 128:(tt + 1) * 128], pv[:, 0:nn, :],
                            reads=[], writes=[pk, (dkey, tt, h0)])

            def dump_hT(nm):
                dbg_dump(nm, hT[:, :, 0:128], [("hT", 0, 0), ("hT", 0, 4)], BF16)

            def hT_keys(t4, key="hT"):
                return [(key, tt, h0) for tt in range(t4 * 4, t4 * 4 + 4) for h0 in (0, 4)]

            def ffn(l, which, vbc, vkey):
                norm_to_hT(vbc, vkey, "ffn1_norm" if which == 0 else "ffn2_norm", "f")
                if not cnt.get("dbg2"):
                    cnt["dbg2"] = 1
                    dump_hT("hT")
                wg_l = wg_d[which][l].rearrange("(c p) f -> p c f", p=128)
                wu_l = wu_d[which][l].rearrange("(c p) f -> p c f", p=128)
                wd_l = wd_d[which][l]
                groups = []
                f0 = 0
                while f0 < DFF:
                    nf = min(GF, (DFF - f0) // 128)
                    groups.append((f0, nf))
                    f0 += nf * 128
                NG = len(groups)
                NT4 = TB // 512
                units = [(g, t4) for g in range(NG) for t4 in range(NT4)]

                def load_w(g):
                    if g >= NG:
                        return
                    wi = g % 2
                    f0, nf = groups[g]
                    k.dma(wgu[wi][:, 0, :, 0:nf * 128], wg_l[:, :, f0:f0 + nf * 128], writes=[("wgu", wi, 0)], q="pool")
                    k.dma(wgu[wi][:, 1, :, 0:nf * 128], wu_l[:, :, f0:f0 + nf * 128], writes=[("wgu", wi, 1)], q="pool")
                    k.dma(wdn[wi][:, 0:nf, :], wd_l[f0:f0 + nf * 128, :].rearrange("(c p) d -> p c d", p=128), writes=[("wdn", wi)], q="pool")

                ais = {}

                def front_pieces(u):
                    g, t4 = units[u]
                    wi = g % 2
                    nf = groups[g][1]
                    ai = cnt["a"] % 2
                    cnt["a"] += 1
                    ais[u] = ai
                    hk = hT_keys(t4)
                    pieces = []
                    for fc in range(nf):
                        def piece(fc=fc):
                            gi = cnt["g"] % 2
                            cnt["g"] += 1
                            pg, pgk = PS(gi)
                            pu, puk = PS(2 + gi)
                            for w, (pp, ppk) in enumerate(((pg, pgk), (pu, puk))):
                                for dc in range(8):
                                    k.op("pe", lambda e: e.matmul(
                                        pp[:], lhsT=wgu[wi][:, w, dc, fc * 128:(fc + 1) * 128],
                                        rhs=hT[:, dc, t4 * 512:(t4 + 1) * 512], start=(dc == 0), stop=(dc == 7)),
                                        reads=[("wgu", wi, w)] + hk, writes=[ppk])
                            k.op("act", lambda e: e.activation(out=sg[gi], in_=pg[:], func=AF.Silu),
                                 reads=[], writes=[pgk, ("sg", gi)])
                            k.op("dve", lambda e: e.tensor_tensor(
                                out=aT[ai][:, fc, :], in0=pu[:], in1=sg[gi], op=ALU.mult),
                                reads=[("sg", gi)], writes=[puk, ("aT", ai, fc)])
                        pieces.append(piece)
                    return pieces

                def back_pieces(u):
                    g, t4 = units[u]
                    wi = g % 2
                    nf = groups[g][1]
                    ai = ais[u]
                    pieces = []
                    for ts in range(4):
                        def piece(ts=ts):
                            tt = t4 * 4 + ts
                            for dh in range(2):
                                di = (4, 5, 7)[cnt["d"] % 3]
                                cnt["d"] += 1
                                pd, pdk = PS(di)
                                for fc in range(nf):
                                    k.op("pe", lambda e: e.matmul(
                                        pd[:], lhsT=aT[ai][:, fc, ts * 128:(ts + 1) * 128],
                                        rhs=wdn[wi][:, fc, dh * 512:(dh + 1) * 512], start=(fc == 0), stop=(fc == nf - 1)),
                                        reads=[("aT", ai, fc), ("wdn", wi)], writes=[pdk])
                                k.op("dve", lambda e: e.scalar_tensor_tensor(
                                    out=xb[:, tt, dh * 512:(dh + 1) * 512], in0=pd[:], scalar=0.5,
                                    in1=xb[:, tt, dh * 512:(dh + 1) * 512], op0=ALU.mult, op1=ALU.add),
                                    reads=[], writes=[pdk, ("xb", tt)])
                        pieces.append(piece)
                    return pieces

                load_w(0)
                load_w(1)
                for p_ in front_pieces(0):
                    p_()
                for u in range(len(units)):
                    fp = front_pieces(u + 1) if u + 1 < len(units) else []
                    bp = back_pieces(u)
                    n = max(len(fp), len(bp))
                    for i in range(n):
                        if i < len(fp):
                            fp[i]()
                        if i < len(bp):
                            bp[i]()
                    if units[u][1] == NT4 - 1:
                        load_w(units[u][0] + 2)

            def inproj(l, blk, vbc, vkey, after_norm=None):
                norm_to_hT(vbc, vkey, "mix_norm", "m")
                if after_norm is not None:
                    after_norm()
                win_l = win_d[l].rearrange("(c p) f -> p c f", p=128)
                t0 = blk * TB
                jobs = [("f", 512 + cg * 128, 128, cg) for cg in range(8)] + \
                       [("t", c0, ncol, j0) for (c0, ncol, j0) in ((0, 512, 0), (1536, 512, 512), (2048, 432, 1024))]

                def load_p(i):
                    if i >= len(jobs):
                        return
                    kind, c0, ncol, aux = jobs[i]
                    wi = i % 2
                    k.dma(wpj[wi][:, :, 0:ncol], win_l[:, :, c0:c0 + ncol], writes=[("wpj", wi)], q="pool")

                load_p(0)
                for i, (kind, c0, ncol, aux) in enumerate(jobs):
                    load_p(i + 1)
                    wi = i % 2
                    if kind == "f":
                        cg = aux
                        for t4 in range(TB // 512):
                            gi = cnt["g"] % 2
                            cnt["g"] += 1
                            pg, pgk = PS(gi)
                            hk = hT_keys(t4)
                            for dc in range(8):
                                k.op("pe", lambda e: e.matmul(
                                    pg[:], lhsT=wpj[wi][:, dc, 0:128], rhs=hT[:, dc, t4 * 512:(t4 + 1) * 512],
                                    start=(dc == 0), stop=(dc == 7)), reads=[("wpj", wi)] + hk, writes=[pgk])
                            ei = cnt["e"] % 2
                            cnt["e"] += 1
                            copy_op(alt(), evf[ei], pg[:], reads=[], writes=[pgk, ("evf", ei)])
                            k.dma(xbc_d[cg * 128:(cg + 1) * 128, t0 + t4 * 512:t0 + (t4 + 1) * 512], evf[ei],
                                  reads=[("evf", ei)], writes=[("xbc_d", cg)])
                    else:
                        j0 = aux
                        for tt in range(NTB):
                            gi = cnt["g"] % 2
                            cnt["g"] += 1
                            pu, puk = PS(2 + gi)
                            for dc in range(8):
                                k.op("pe", lambda e: e.matmul(
                                    pu[:, 0:ncol], lhsT=hT[:, dc, tt * 128:(tt + 1) * 128], rhs=wpj[wi][:, dc, 0:ncol],
                                    start=(dc == 0), stop=(dc == 7)),
                                    reads=[("wpj", wi), ("hT", tt, 0), ("hT", tt, 4)], writes=[puk])
                            ei = cnt["e"] % 2
                            cnt["e"] += 1
                            copy_op(alt(), evt[ei][:, 0:ncol], pu[:, 0:ncol], reads=[], writes=[puk, ("evt", ei)])
                            k.dma(ptok_d[t0 + tt * 128:t0 + (tt + 1) * 128, j0:j0 + ncol], evt[ei][:, 0:ncol],
                                  reads=[("evt", ei)], writes=[("ptok_d", blk)])

            def outproj(l, blk):
                t0 = blk * TB
                wout_l = wout_d[l].rearrange("(c p) f -> p c f", p=128)
                for tt in range(NTB):
                    hbi = tt % 2
                    k.dma(hb[hbi], y_d[t0 + tt * 128:t0 + (tt + 1) * 128, :], reads=[("y_d",)], writes=[("hb", hbi)])
                    transpose_rows(hb[hbi], ("hb", hbi), 8, hT, tt, "hT")
                for dh in range(2):
                    k.dma(wpj[dh], wout_l[:, :, dh * 512:(dh + 1) * 512], writes=[("wpj", dh)], q="pool")
                for dh in range(2):
                    wi = dh
                    for tt in range(NTB):
                        di = 4 + cnt["d"] % 2
                        cnt["d"] += 1
                        pd, pdk = PS(di)
                        for dc in range(8):
                            k.op("pe", lambda e, pd=pd, dc=dc, wi=wi, tt=tt: e.matmul(
                                pd[:], lhsT=hT[:, dc, tt * 128:(tt + 1) * 128], rhs=wpj[wi][:, dc, :],
                                start=(dc == 0), stop=(dc == 7)),
                                reads=[("wpj", wi), ("hT", tt, 0), ("hT", tt, 4)], writes=[pdk])
                        k.op("dve", lambda e, pd=pd, tt=tt, dh=dh: e.tensor_tensor(
                            out=xb[:, tt, dh * 512:(dh + 1) * 512], in0=pd[:], in1=xb[:, tt, dh * 512:(dh + 1) * 512],
                            op=ALU.add), reads=[], writes=[pdk, ("xb", tt)])

            stores = []
            src = x_d if pidx == 0 else out_d

            def load_x(blk):
                t0 = blk * TB
                for tt in range(NTB):
                    k.dma(xb[:, tt, :], src[t0 + tt * 128:t0 + (tt + 1) * 128, :],
                          reads=[("out_d", blk)], writes=[("xb", tt)])

            def store_x(blk):
                t0 = blk * TB
                for tt in range(NTB):
                    stores.append(k.dma(out_d[t0 + tt * 128:t0 + (tt + 1) * 128, :], xb[:, tt, :],
                                        reads=[("xb", tt)], writes=[("out_d", blk)]))

            load_x(0)
            for blk in range(NBLK):
                if l_prev is not None:
                    outproj(l_prev, blk)
                    ffn(l_prev, 1, vbcA, "vbcA")
                if l_next is not None:
                    ffn(l_next, 0, vbcB, "vbcB")
                    store_x(blk)
                    inproj(l_next, blk, vbcB, "vbcB",
                           after_norm=(lambda blk=blk: load_x(blk + 1)) if blk + 1 < NBLK else None)
                else:
                    store_x(blk)
                    if blk + 1 < NBLK:
                        load_x(blk + 1)
            return stores

        def mk_tmpset(pfx, n, nr):
            return {"pfx": pfx, "sq": A.f32(n), "ss": A.f32(8), "rs": A.f32(8), "t1": A.f32(nr), "t2": A.f32(nr)}

        def run_interleaved(gens):
            gens = list(gens)
            while gens:
                for g_ in list(gens):
                    try:
                        next(g_)
                    except StopIteration:
                        gens.remove(g_)

        def headnorm_rope(src, H, Dh, wrow, rlo, rhalf, cs, dst, tmp, keys_r, keys_w, cskey):
            sqv = tmp["sq"][:, 0:H * Dh].rearrange("p (h d) -> p h d", h=H)
            ssv = tmp["ss"][:, 0:H]
            rs = tmp["rs"][:, 0:H]
            qn = sqv
            t1 = tmp["t1"][:, 0:H * rhalf].rearrange("p (h d) -> p h d", h=H)
            t2 = tmp["t2"][:, 0:H * rhalf].rearrange("p (h d) -> p h d", h=H)
            p_ = tmp["pfx"]
            T = [p_ + "sq", p_ + "ss", p_ + "rs", p_ + "sq", p_ + "t1", p_ + "t2"]
            k.op("dve", lambda e: e.tensor_tensor(out=sqv, in0=src, in1=src, op=ALU.mult), reads=keys_r, writes=[T[0]])
            yield
            k.op("dve", lambda e: e.tensor_reduce(out=ssv, in_=sqv, axis=AX.X, op=ALU.add), reads=[T[0]], writes=[T[1]])
            yield
            k.op("act", lambda e: e.activation(out=rs, in_=ssv, func=AF.Sqrt, scale=1.0 / Dh, bias=EPS), reads=[T[1]], writes=[T[2]])
            yield
            k.op("dve", lambda e: e.reciprocal(out=ssv, in_=rs), reads=[T[2]], writes=[T[1]])
            yield
            k.op("dve", lambda e: e.tensor_tensor(out=qn, in0=src, in1=ssv.unsqueeze(2).to_broadcast([128, H, Dh]), op=ALU.mult),
                 reads=keys_r + [T[1]], writes=[T[3]])
            yield
            k.op("dve", lambda e: e.tensor_tensor(out=qn, in0=qn, in1=wrow.unsqueeze(1).to_broadcast([128, H, Dh]), op=ALU.mult),
                 reads=["vbc"], writes=[T[3]])
            yield
            if rlo > 0:
                k.op("dve", lambda e: e.tensor_copy(dst[:, :, 0:rlo], qn[:, :, 0:rlo]), reads=[T[3]], writes=keys_w)
                yield
            x1 = qn[:, :, rlo:rlo + rhalf]
            x2 = qn[:, :, rlo + rhalf:rlo + 2 * rhalf]
            cb = cs[:, 0:rhalf].unsqueeze(1).to_broadcast([128, H, rhalf])
            sb = cs[:, rhalf:2 * rhalf].unsqueeze(1).to_broadcast([128, H, rhalf])
            k.op("dve", lambda e: e.tensor_tensor(out=t1, in0=x1, in1=cb, op=ALU.mult), reads=[T[3], cskey], writes=[T[4]])
            yield
            k.op("dve", lambda e: e.tensor_tensor(out=t2, in0=x2, in1=sb, op=ALU.mult), reads=[T[3], cskey], writes=[T[5]])
            yield
            k.op("dve", lambda e: e.tensor_tensor(out=dst[:, :, rlo:rlo + rhalf], in0=t1, in1=t2, op=ALU.subtract),
                 reads=[T[4], T[5]], writes=keys_w)
            yield
            k.op("dve", lambda e: e.tensor_tensor(out=t1, in0=x1, in1=sb, op=ALU.mult), reads=[T[3], cskey], writes=[T[4]])
            yield
            k.op("dve", lambda e: e.tensor_tensor(out=t2, in0=x2, in1=cb, op=ALU.mult), reads=[T[3], cskey], writes=[T[5]])
            yield
            k.op("dve", lambda e: e.tensor_tensor(out=dst[:, :, rlo + rhalf:rlo + 2 * rhalf], in0=t1, in1=t2, op=ALU.add),
                 reads=[T[4], T[5]], writes=keys_w)
            yield

        def heads_to_T(src, skey, H, Dh, dstT, tt, dkey, bank):
            pb, pk = PS(6)
            pv = pb[:, 0:512].rearrange("p (c t) -> p c t", c=4)
            for h in range(H):
                k.op("pe", lambda e, h=h: e.transpose(pv[0:Dh, h, :], src[:, h, :], identb), reads=[skey, "cstb"], writes=[pk])
            copy_op(alt(), dstT[0:Dh, 0:H, tt * 128:(tt + 1) * 128], pv[0:Dh, 0:H, :], reads=[], writes=[pk, (dkey, tt)])

        def out_norm_store(osrc, okeys, width, wrow, col0, tmp, tt, ring, tag):
            k.op("dve", lambda e: e.memset(tmp["ss"][:, 0:1], 0.0), writes=["T_ss"])
            k.op("act", lambda e: e.activation(out=tmp["sq"][:, 0:width], in_=osrc, func=AF.Square, accum_out=tmp["ss"][:, 0:1]),
                 reads=okeys, writes=["T_sq", "T_ss"])
            k.op("act", lambda e: e.activation(out=tmp["rs"][:, 0:1], in_=tmp["ss"][:, 0:1], func=AF.Sqrt, scale=1.0 / width, bias=EPS),
                 reads=["T_ss"], writes=["T_rs"])
            k.op("dve", lambda e: e.reciprocal(out=tmp["ss"][:, 1:2], in_=tmp["rs"][:, 0:1]), reads=["T_rs"], writes=["T_ss"])
            yi = ring % 2
            yb = tmp["yb"][yi][:, 0:width]
            k.op("dve", lambda e: e.scalar_tensor_tensor(out=yb, in0=osrc, scalar=tmp["ss"][:, 1:2], in1=wrow,
                                                         op0=ALU.mult, op1=ALU.mult),
                 reads=list(okeys) + ["T_ss", "vbc"], writes=[(tag + "yb", yi)])
            k.dma(y_d[tt * 128:(tt + 1) * 128, col0:col0 + width], yb, reads=[(tag + "yb", yi)], writes=[("y_d",)])

        def attn_finish(acc_bank, ncols_q, nheads, tmp, odst_fn, extra_den, tag):
            pa, pak = PS(acc_bank)
            oT = tmp["oT"]
            k.op("act", lambda e: e.copy(oT[0:65, 0:ncols_q], pa[0:65, 0:ncols_q]), reads=[], writes=[pak, "T_oT"])
            for j in range(ncols_q // 128):
                pb, pk = PS(7)
                k.op("pe", lambda e, j=j: e.transpose(pb[:, 0:65], oT[0:65, j * 128:(j + 1) * 128], identf[0:65, 0:65]),
                     reads=["T_oT", "cst"], writes=[pk])
                den = tmp["den"]
                if extra_den is not None:
                    ed = extra_den(j)
                    k.op("dve", lambda e, ed=ed: e.tensor_tensor(out=den[:, 0:1], in0=pb[:, 64:65], in1=ed, op=ALU.add),
                         reads=["sinkexp"], writes=[pk, "T_den"])
                else:
                    k.op("dve", lambda e: e.tensor_copy(den[:, 0:1], pb[:, 64:65]), reads=[], writes=[pk, "T_den"])
                k.op("dve", lambda e: e.reciprocal(out=den[:, 1:2], in_=den[:, 0:1]), reads=["T_den"], writes=["T_rden"])
                dst, dkey = odst_fn(j)
                k.op("dve", lambda e, dst=dst: e.tensor_scalar_mul(out=dst, in0=pb[:, 0:64], scalar1=den[:, 1:2]), reads=["T_rden"], writes=[pk, dkey])

        def mla_phase(l, vbc):
            fbase, bbase = A.p, A.pb
            qT = A.bf16(4 * S).rearrange("p (h t) -> p h t", h=4)
            kT = A.bf16(4 * S).rearrange("p (h t) -> p h t", h=4)
            vx = A.bf16(NT * 4 * 65).rearrange("p (t h d) -> p t h d", t=NT, h=4)
            oall = A.f32(NT * 256).rearrange("p (t d) -> p t d", t=NT)
            wuq = A.bf16(2 * 384).rearrange("p (c f) -> p c f", c=2)
            wukv = A.bf16(512)
            pin = [A.f32(416) for _ in range(2)]
            csb = [A.f32(32) for _ in range(2)]
            nb = [A.bf16(384) for _ in range(2)]
            nT = A.bf16(3 * 128).rearrange("p (c t) -> p c t", c=3)
            qf = A.f32(384).rearrange("p (h d) -> p h d", h=4)
            kf = A.f32(384).rearrange("p (h d) -> p h d", h=4)
            qb = A.bf16(384).rearrange("p (h d) -> p h d", h=4)
            kb_ = A.bf16(384).rearrange("p (h d) -> p h d", h=4)
            tmp = {"sq": A.f32(384), "ss": A.f32(8), "rs": A.f32(8),
                   "oT": A.f32(512), "den": A.f32(2), "yb": [A.bf16(256) for _ in range(2)]}
            tsets = [mk_tmpset("Ta_", 384, 64), mk_tmpset("Tb_", 384, 64)]
            eb = [A.bf16(512) for _ in range(4)]
            k.dma(wuq, wuq_d[l].rearrange("(c p) f -> p c f", p=128), writes=["wuq"], q="pool")
            k.dma(wukv, wukv_d[l], writes=["wukv"], q="pool")
            k.op("dve", lambda e: e.memset(vx[:, :, :, 64:65], 1.0), writes=["vx1"])
            for tt in range(NT):
                pi = tt % 2
                k.dma(pin[pi], ptok_d[tt * 128:(tt + 1) * 128, 1040:1456], reads=[("ptok_d",)], writes=[("pin", pi)])
                k.dma(csb[pi], cs_mla_d[tt * 128:(tt + 1) * 128, :], writes=[("cs", pi)])
                for (c0, n, wn, tagn) in ((0, 256, "mla_q_lat_norm", "a"), (256, 128, "mla_kv_norm", "b")):
                    k.op("dve", lambda e: e.memset(tmp["ss"][:, 0:1], 0.0), writes=["T_ss"])
                    k.op("act", lambda e, c0=c0, n=n, pi=pi: e.activation(out=tmp["sq"][:, 0:n], in_=pin[pi][:, c0:c0 + n],
                                                                        func=AF.Square, accum_out=tmp["ss"][:, 0:1]),
                         reads=[("pin", pi)], writes=["T_sq", "T_ss"])
                    k.op("act", lambda e, n=n: e.activation(out=tmp["rs"][:, 0:1], in_=tmp["ss"][:, 0:1], func=AF.Sqrt,
                                                            scale=1.0 / n, bias=EPS), reads=["T_ss"], writes=["T_rs"])
                    k.op("dve", lambda e: e.reciprocal(out=tmp["ss"][:, 1:2], in_=tmp["rs"][:, 0:1]), reads=["T_rs"], writes=["T_ss"])
                    k.op("dve", lambda e, c0=c0, n=n, pi=pi, wn=wn: e.scalar_tensor_tensor(
                        out=nb[pi][:, c0:c0 + n], in0=pin[pi][:, c0:c0 + n], scalar=tmp["ss"][:, 1:2], in1=vrow(vbc, wn),
                        op0=ALU.mult, op1=ALU.mult), reads=[("pin", pi), "T_ss", "vbc"], writes=[("nb", pi, tagn)])
                pb, pk = PS(6)
                pv = pb[:, 512:1024].rearrange("p (c t) -> p c t", c=4)
                for c in range(3):
                    k.op("pe", lambda e, c=c, pi=pi: e.transpose(pv[:, c, :], nb[pi][:, c * 128:(c + 1) * 128], identb),
                         reads=[("nb", pi, "a"), ("nb", pi, "b"), "cstb"], writes=[pk])
                copy_op("act", nT, pv[:, 0:3, :], reads=[], writes=[pk, "nT"])
                pq, pqk = PS(0)
                for c in range(2):
                    k.op("pe", lambda e, c=c: e.matmul(pq[:, 0:384], lhsT=nT[:, c, :], rhs=wuq[:, c, :], start=(c == 0), stop=(c == 1)),
                         reads=["nT", "wuq"], writes=[pqk])
                pkv, pkvk = PS(1)
                k.op("pe", lambda e: e.matmul(pkv[:], lhsT=nT[:, 2, :], rhs=wukv, start=True, stop=True), reads=["nT", "wukv"], writes=[pkvk])
                copy_op("act", qf, pq[:, 0:384].rearrange("p (h d) -> p h d", h=4), reads=[], writes=[pqk, "qf"])
                kvv = pkv[:].rearrange("p (h d) -> p h d", h=4)
                copy_op("act", kf[:, :, 0:64], kvv[:, :, 0:64], reads=[], writes=[pkvk, "kf"])
                copy_op("dve", vx[:, tt, :, 0:64], kvv[:, :, 64:128], reads=[], writes=[pkvk, ("vx", tt)])
                k.op("dve", lambda e, pi=pi: e.tensor_copy(kf[:, :, 64:96], pin[pi][:, 384:416].unsqueeze(1).to_broadcast([128, 4, 32])),
                     reads=[("pin", pi)], writes=["kf"])
                run_interleaved([
                    headnorm_rope(qf, 4, 96, vrow(vbc, "mla_q_norm"), 64, 16, csb[pi], qb, tsets[0], ["qf"], ["qb"], ("cs", pi)),
                    headnorm_rope(kf, 4, 96, vrow(vbc, "mla_k_norm"), 64, 16, csb[pi], kb_, tsets[1], ["kf"], ["kb"], ("cs", pi))])
                heads_to_T(qb, "qb", 4, 96, qT, tt, "qT", 6)
                heads_to_T(kb_, "kb", 4, 96, kT, tt, "kT", 6)
            scale = 96.0 ** -0.5
            qkeys = lambda qi: [("qT", tt) for tt in range(qi * 4, qi * 4 + 4)]
            items = [(h, qi, kt) for h in range(4) for qi in range(S // 512) for kt in range(NT)]
            LA = 3

            def stA(i):
                h, qi, kt = items[i]
                psn, psk = PS(i % 4)
                k.op("pe", lambda e: e.matmul(
                    psn[:], lhsT=kT[0:96, h, kt * 128:(kt + 1) * 128], rhs=qT[0:96, h, qi * 512:(qi + 1) * 512],
                    start=True, stop=True), reads=[("kT", kt)] + qkeys(qi), writes=[psk])

            def stBC(i):
                h, qi, kt = items[i]
                psn, psk = PS(i % 4)
                ebi = i % 4
                blk = h * (S // 512) + qi
                pa, pak = PS(4 + (blk % 2))
                k.op("act", lambda e: e.activation(out=eb[ebi], in_=psn[:], func=AF.Exp, scale=scale),
                     reads=[], writes=[psk, ("eb", ebi)])
                k.op("pe", lambda e: e.matmul(
                    pa[0:65, :], lhsT=vx[:, kt, h, :], rhs=eb[ebi], start=(kt == 0), stop=(kt == NT - 1)),
                    reads=[("vx", kt), "vx1", ("eb", ebi)], writes=[pak])
                if kt == NT - 1:
                    attn_finish(4 + (blk % 2), 512, 1, tmp,
                                lambda j, h=h, qi=qi: (oall[:, qi * 4 + j, h * 64:(h + 1) * 64], ("oall", qi * 4 + j, h)),
                                None, "m")

            for i in range(min(LA, len(items))):
                stA(i)
            for i in range(len(items)):
                if i + LA < len(items):
                    stA(i + LA)
                stBC(i)
            for tt in range(NT):
                okeys = [("oall", tt, h) for h in range(4)]
                out_norm_store(oall[:, tt, :], okeys, 256, vrow(vbc, "mla_out_norm"), 768, tmp, tt, tt, "mo")
            A.reset(fbase, bbase)

        def swa_phase(l, vbc):
            fbase, bbase = A.p, A.pb
            qT = A.bf16(4 * S).rearrange("p (h t) -> p h t", h=4)
            kT = A.bf16(2 * S).rearrange("p (h t) -> p h t", h=2)
            vx = A.bf16(NT * 2 * 65).rearrange("p (t h d) -> p t h d", t=NT, h=2)
            oall = A.f32(NT * 256).rearrange("p (t d) -> p t d", t=NT)
            pin = [A.f32(512) for _ in range(2)]
            csb = [A.f32(64) for _ in range(2)]
            qb = [A.bf16(256).rearrange("p (h d) -> p h d", h=4) for _ in range(2)]
            kb_ = [A.bf16(128).rearrange("p (h d) -> p h d", h=2) for _ in range(2)]
            tmp = {"sq": A.f32(256), "ss": A.f32(8), "rs": A.f32(8),
                   "oT": A.f32(512), "den": A.f32(2), "yb": [A.bf16(256) for _ in range(2)]}
            tsets = [mk_tmpset("T%d_" % i, 256, 128) for i in range(4)]
            sinkexp = A.f32(4)
            eb = [A.bf16(256) for _ in range(4)]
            k.op("act", lambda e: e.activation(out=sinkexp, in_=vrow(vbc, "swa_sink"), func=AF.Exp), reads=["vbc"], writes=["sinkexp"])
            k.op("dve", lambda e: e.memset(vx[:, :, :, 64:65], 1.0), writes=["vx1"])
            for tt0 in range(0, NT, 2):
                gens = []
                for u in range(2):
                    tt = tt0 + u
                    pi = tt % 2
                    k.dma(pin[pi], ptok_d[tt * 128:(tt + 1) * 128, 528:1040], reads=[("ptok_d",)], writes=[("pin", pi)])
                    k.dma(csb[pi], cs_swa_d[tt * 128:(tt + 1) * 128, :], writes=[("cs", pi)])
                    qsrc = pin[pi][:, 0:256].rearrange("p (h d) -> p h d", h=4)
                    ksrc = pin[pi][:, 256:384].rearrange("p (h d) -> p h d", h=2)
                    gens.append(headnorm_rope(qsrc, 4, 64, vrow(vbc, "swa_q_norm"), 0, 32, csb[pi], qb[pi], tsets[2 * u],
                                              [("pin", pi)], [("qb", pi)], ("cs", pi)))
                    gens.append(headnorm_rope(ksrc, 2, 64, vrow(vbc, "swa_k_norm"), 0, 32, csb[pi], kb_[pi], tsets[2 * u + 1],
                                              [("pin", pi)], [("kb", pi)], ("cs", pi)))
                run_interleaved(gens)
                for u in range(2):
                    tt = tt0 + u
                    pi = tt % 2
                    k.op("dve", lambda e: e.tensor_copy(vx[:, tt, :, 0:64], pin[pi][:, 384:512].rearrange("p (h d) -> p h d", h=2)),
                         reads=[("pin", pi)], writes=[("vx", tt)])
                    heads_to_T(qb[pi], ("qb", pi), 4, 64, qT, tt, "qT", 6)
                    heads_to_T(kb_[pi], ("kb", pi), 2, 64, kT, tt, "kT", 6)
            scale = 64.0 ** -0.5
            items = []
            for n in range(NT):
                for kvh in range(2):
                    js = [j for j in (n - 1, n, n + 1) if 0 <= j < NT]
                    for ji, j in enumerate(js):
                        items.append((n, kvh, j, ji, len(js)))
            LA = 3
            pend = []

            def stA(i):
                n, kvh, j, ji, nj = items[i]
                psn, psk = PS(i % 4)
                k.op("pe", lambda e: e.matmul(
                    psn[:, 0:256].rearrange("p (a b) -> p a b", a=2), lhsT=kT[0:64, kvh, j * 128:(j + 1) * 128],
                    rhs=qT[0:64, 2 * kvh:2 * kvh + 2, n * 128:(n + 1) * 128], start=True, stop=True),
                    reads=[("kT", j), ("qT", n)], writes=[psk])

            def stBC(i):
                n, kvh, j, ji, nj = items[i]
                psn, psk = PS(i % 4)
                ebi = i % 4
                blk = n * 2 + kvh
                pa, pak = PS(4 + (blk % 2))
                k.op("act", lambda e: e.activation(out=eb[ebi], in_=psn[:, 0:256], func=AF.Exp, scale=scale),
                     reads=[], writes=[psk, ("eb", ebi)])
                if j != n:
                    mk = cstb[:, 128:384] if j < n else cstb[:, 384:640]
                    k.op("dve", lambda e: e.tensor_tensor(out=eb[ebi], in0=eb[ebi], in1=mk, op=ALU.mult),
                         reads=["cstb"], writes=[("eb", ebi)])
                k.op("pe", lambda e: e.matmul(
                    pa[0:65, 0:256], lhsT=vx[:, j, kvh, :], rhs=eb[ebi], start=(ji == 0), stop=(ji == nj - 1)),
                    reads=[("vx", j), "vx1", ("eb", ebi)], writes=[pak])
                if ji == 0 and pend:
                    pend.pop(0)()
                if ji == nj - 1:
                    pend.append(lambda n=n, kvh=kvh, blk=blk: attn_finish(
                        4 + (blk % 2), 256, 2, tmp,
                        lambda jj: (oall[:, n, (2 * kvh + jj) * 64:(2 * kvh + jj + 1) * 64], ("oall", n, 2 * kvh + jj)),
                        lambda jj: sinkexp[:, 2 * kvh + jj:2 * kvh + jj + 1], "s"))

            for i in range(min(LA, len(items))):
                stA(i)
            for i in range(len(items)):
                if i + LA < len(items):
                    stA(i + LA)
                stBC(i)
            while pend:
                pend.pop(0)()
            for tt in range(NT):
                okeys = [("oall", tt, h) for h in range(4)]
                out_norm_store(oall[:, tt, :], okeys, 256, vrow(vbc, "swa_out_norm"), 512, tmp, tt, tt, "so")
            A.reset(fbase, bbase)

        def ssd_phase(l, vbc):
            fbase, bbase = A.p, A.pb
            convw = A.f32(48).rearrange("p (c k) -> p c k", c=8)
            BT = A.bf16(2 * S).rearrange("p (g t) -> p g t", g=2)
            CT = A.bf16(2 * S).rearrange("p (g t) -> p g t", g=2)
            dt = A.f32(NT * 16).rearrange("p (t d) -> p t d", t=NT)
            av = A.f32(NT * 16).rearrange("p (t d) -> p t d", t=NT)
            cum = A.f32(NT * 16).rearrange("p (t d) -> p t d", t=NT)
            ncf = A.f32(NT * 8).rearrange("p (t d) -> p t d", t=NT)
            sc1 = A.f32(NT * 16).rearrange("p (t d) -> p t d", t=NT)
            sc2 = A.f32(NT * 16).rearrange("p (t d) -> p t d", t=NT)
            dw = A.f32(NT * 16).rearrange("p (t d) -> p t d", t=NT)
            tot = A.f32(NCH * 16).rearrange("p (c d) -> p c d", c=NCH)
            etot = A.f32(NCH * 16).rearrange("p (c d) -> p c d", c=NCH)
            Abc = A.f32(16)
            t16 = [A.f32(NT * 16).rearrange("p (t d) -> p t d", t=NT) for _ in range(3)]
            avd = A.f32(NT * 80).rearrange("p (t d m) -> p t d m", t=NT, d=2)
            Hb = [A.bf16(512) for _ in range(2)]
            Lb = A.bf16(512)
            Rst = [A.f32(256) for _ in range(2)]
            Rtmp = A.f32(256)
            Eb = [A.f32(256) for _ in range(4)]
            ya = [A.f32(256) for _ in range(2)]
            yg = [A.f32(256) for _ in range(2)]
            zb = [A.f32(256) for _ in range(2)]
            szb = [A.f32(256) for _ in range(2)]
            tmp = {"sq": A.f32(256), "ss": A.f32(8), "rs": A.f32(8), "yb": [A.bf16(256) for _ in range(2)]}
            xin = [A.bf16(S + 4)] * 2
            dg = A.bf16(5 * 128).rearrange("p (k c) -> p k c", k=5)
            xTf = A.bf16(2 * S).rearrange("p (c t) -> p c t", c=2)
            xs = A.bf16(NT * 256).rearrange("p (t d) -> p t d", t=NT)
            Btok = A.bf16(NT * 128).rearrange("p (t d) -> p t d", t=NT)
            Sinb = [A.bf16((NCH + 1) * 256).rearrange("p (c d) -> p c d", c=NCH + 1) for _ in range(2)]
            Mb = [A.bf16(256) for _ in range(4)]
            GT = [A.bf16(512) for _ in range(2)]
            xdt = [A.bf16(2 * 2 * 256).rearrange("p (r t d) -> p r t d", r=2, t=2) for _ in range(2)]
            xdw = [A.bf16(2 * 256).rearrange("p (t d) -> p t d", t=2) for _ in range(2)]
            k.dma(convw, convp_d[l].rearrange("p (c k) -> p c k", c=8), writes=["convw"])
            ci = [0]

            def conv_chunk(cc, dst_fn):
                xi = 0
                k.op("dve", lambda e, xi=xi: e.memset(xin[xi][:, 0:2], 0.0), writes=[("xin", xi)])
                k.op("dve", lambda e, xi=xi: e.memset(xin[xi][:, S + 2:S + 4], 0.0), writes=[("xin", xi)])
                k.dma(xin[xi][:, 2:S + 2], xbc_d[cc * 128:(cc + 1) * 128, :], reads=[("xbc_d",)], writes=[("xin", xi)])
                for kk in range(5):
                    k.op("dve", lambda e, kk=kk, cc=cc: e.tensor_scalar_mul(out=dg[:, kk, :], in0=identf, scalar1=convw[:, cc, kk:kk + 1]),
                         reads=["cst", "convw"], writes=["dg"])
                for t4 in range(S // 512):
                    pb, pk = PS(ci[0] % 2)
                    ci[0] += 1
                    for kk in range(5):
                        k.op("pe", lambda e, pb=pb, kk=kk, xi=xi, t4=t4: e.matmul(
                            pb[:], lhsT=dg[:, kk, :], rhs=xin[xi][:, t4 * 512 + kk:t4 * 512 + kk + 512],
                            start=(kk == 0), stop=(kk == 4)), reads=["dg", ("xin", xi)], writes=[pk])
                    dst, dkey = dst_fn(t4)
                    k.op("act", lambda e, pb=pb, dst=dst, cc=cc: e.activation(out=dst, in_=pb[:], func=AF.Silu, bias=convw[:, cc, 5:6]),
                         reads=["convw"], writes=[pk, dkey])
                ci[0] += 1

            for g in range(2):
                conv_chunk(4 + g, lambda t4, g=g: (BT[:, g, t4 * 512:(t4 + 1) * 512], ("BT", g, t4)))
                conv_chunk(6 + g, lambda t4, g=g: (CT[:, g, t4 * 512:(t4 + 1) * 512], ("CT", g, t4)))
            for tt in range(NT):
                k.dma(dt[:, tt, :], ptok_d[tt * 128:(tt + 1) * 128, 512:528], reads=[("ptok_d",)], writes=["dt"])
            bias_bc = vrow(vbc, "ssd_dt_bias").unsqueeze(1).to_broadcast([128, NT, 16])
            k.op("dve", lambda e: e.tensor_tensor(out=dt, in0=dt, in1=bias_bc, op=ALU.add), reads=["vbc"], writes=["dt"])
            k.op("dve", lambda e: e.tensor_scalar_mul(out=t16[0], in0=dt, scalar1=-1.0), reads=["dt"], writes=["t16a"])
            k.op("dve", lambda e: e.tensor_tensor(out=t16[0], in0=t16[0], in1=dt, op=ALU.max), reads=["dt"], writes=["t16a"])
            k.op("act", lambda e: e.activation(out=t16[1], in_=t16[0], func=AF.Exp, scale=-1.0), reads=["t16a"], writes=["t16b"])
            k.op("act", lambda e: e.activation(out=t16[0], in_=t16[1], func=AF.Ln, bias=1.0), reads=["t16b"], writes=["t16a"])
            k.op("dve", lambda e: e.tensor_scalar_max(out=t16[1], in0=dt, scalar1=0.0), reads=["dt"], writes=["t16b"])
            k.op("dve", lambda e: e.tensor_tensor(out=dt, in0=t16[0], in1=t16[1], op=ALU.add), reads=["t16a", "t16b"], writes=["dt"])
            k.op("act", lambda e: e.activation(out=Abc, in_=vrow(vbc, "ssd_a_log"), func=AF.Exp), reads=["vbc"], writes=["Abc"])
            k.op("dve", lambda e: e.scalar_tensor_tensor(out=av, in0=dt, scalar=-1.0, in1=Abc.unsqueeze(1).to_broadcast([128, NT, 16]),
                                                         op0=ALU.mult, op1=ALU.mult), reads=["dt", "Abc"], writes=["av"])
            dbg_dump("ssd_dt", dt, ["dt"])
            k.op("dve", lambda e: e.memset(avd, 0.0), writes=["avd"])
            for d in range(2):
                for c0 in (0, 32):
                    k.op("dve", lambda e: e.tensor_copy(avd[:, :, d, c0:c0 + 8], av[:, :, d * 8:d * 8 + 8]), reads=["av"], writes=["avd"])
            TRI_LE = cst[:, C_R0F:C_R0F + 128]
            TRI_LT = cst[:, C_R0B:C_R0B + 128]
            for c in range(NCH):
                t0_, t1_ = 2 * c, 2 * c + 1
                pb, pk = PS(c % 2)
                pv = pb[:, 0:48].rearrange("p (t d) -> p t d", t=3)
                for d, tri in ((0, TRI_LE), (1, TRI_LT)):
                    cs_ = slice(d * 8, d * 8 + 8)
                    k.op("pe", lambda e, pv=pv, tri=tri, cs_=cs_, t0_=t0_: e.matmul(pv[:, 0, cs_], lhsT=tri, rhs=av[:, t0_, cs_], start=True, stop=True),
                         reads=["av", "cst"], writes=[pk])
                    k.op("pe", lambda e, pv=pv, cs_=cs_, t0_=t0_: e.matmul(pv[:, 1, cs_], lhsT=ones_f, rhs=av[:, t0_, cs_], start=False, stop=False,
                                                                          skip_group_check=True), reads=["av", "cst"], writes=[pk])
                    k.op("pe", lambda e, pv=pv, tri=tri, cs_=cs_, t1_=t1_: e.matmul(pv[:, 1, cs_], lhsT=tri, rhs=av[:, t1_, cs_], start=False, stop=True,
                                                                                  skip_group_check=True), reads=["av", "cst"], writes=[pk])
                k.op("pe", lambda e, pv=pv, t0_=t0_: e.matmul(pv[:, 2, :], lhsT=ones_f, rhs=av[:, t0_, :], start=False, stop=False, skip_group_check=True),
                     reads=["av", "cst"], writes=[pk])
                k.op("pe", lambda e, pv=pv, t1_=t1_: e.matmul(pv[:, 2, :], lhsT=ones_f, rhs=av[:, t1_, :], start=False, stop=True, skip_group_check=True),
                     reads=["av", "cst"], writes=[pk])
                copy_op("dve", cum[:, t0_:t0_ + 2, :], pv[:, 0:2, :], reads=[], writes=[pk, "cum"])
                copy_op("dve", tot[:, c, :], pv[:, 2, :], reads=[], writes=[pk, "tot"])
            totb = lambda d: tot[:, :, d * 8:d * 8 + 8].unsqueeze(2).to_broadcast([128, NCH, 2, 8])
            v4 = lambda a, d: a[:, :, d * 8:d * 8 + 8].rearrange("p (c t) h -> p c t h", t=2)
            k.op("dve", lambda e: e.tensor_scalar_mul(out=ncf, in0=cum[:, :, 0:8], scalar1=-1.0), reads=["cum"], writes=["ncf"])
            k.op("act", lambda e: e.activation(out=sc1[:, :, 0:8], in_=cum[:, :, 0:8], func=AF.Exp), reads=["cum"], writes=["sc1a"])
            k.op("act", lambda e: e.activation(out=sc2[:, :, 8:16], in_=cum[:, :, 8:16], func=AF.Exp), reads=["cum"], writes=["sc2b"])
            k.op("dve", lambda e: e.tensor_tensor(out=v4(t16[2], 0), in0=totb(0), in1=v4(cum, 0), op=ALU.subtract), reads=["cum", "tot"], writes=["t16c0"])
            k.op("dve", lambda e: e.tensor_tensor(out=v4(t16[2], 1), in0=totb(1), in1=v4(cum, 1), op=ALU.subtract), reads=["cum", "tot"], writes=["t16c1"])
            k.op("act", lambda e: e.activation(out=sc2[:, :, 0:8], in_=t16[2][:, :, 0:8], func=AF.Exp), reads=["t16c0"], writes=["sc2a"])
            k.op("act", lambda e: e.activation(out=sc1[:, :, 8:16], in_=t16[2][:, :, 8:16], func=AF.Exp), reads=["t16c1"], writes=["sc1b"])
            k.op("act", lambda e: e.activation(out=etot, in_=tot, func=AF.Exp), reads=["tot"], writes=["etot"])
            k.op("dve", lambda e: e.tensor_tensor(out=dw, in0=dt, in1=sc2, op=ALU.mult), reads=["dt", "sc2a", "sc2b"], writes=["dw"])
            dbg_dump("ssd_cum", cum, ["cum"])
            Dbc = vrow(vbc, "ssd_d")
            v3 = lambda a: a.rearrange("p (h d) -> p h d", h=4)
            for g in range(2):
                for j in range(2):
                    conv_chunk(2 * g + j, lambda t4, j=j: (xTf[:, j, t4 * 512:(t4 + 1) * 512], ("xTf", t4)))
                for tt in range(NT):
                    pb, pk = PS(6)
                    pv = pb[:, 0:256].rearrange("p (c t) -> p c t", c=2)
                    for j in range(2):
                        k.op("pe", lambda e, pv=pv, j=j, tt=tt: e.transpose(pv[:, j, :], xTf[:, j, tt * 128:(tt + 1) * 128], identb),
                             reads=[("xTf", tt // 4), "cstb"], writes=[pk])
                    k.op("pe", lambda e, pb=pb, tt=tt: e.transpose(pb[:, 256:384], BT[:, g, tt * 128:(tt + 1) * 128], identb),
                         reads=[("BT", g, tt // 4), "cstb"], writes=[pk])
                    copy_op("act", xs[:, tt, :], pb[:, 0:256], reads=[], writes=[pk, ("xs", tt)])
                    copy_op("dve", Btok[:, tt, :], pb[:, 256:384], reads=[], writes=[pk, ("Btok", tt)])
                if g == 0:
                    dbg_dump("ssd_xs", xs, [("xs", tt) for tt in range(NT)], BF16)
                xs4 = xs.rearrange("p t (h d) -> p t h d", h=4)
                wi = 0
                for d in range(2):
                    k.op("dve", lambda e, d=d: e.memset(Rst[d], 0.0), writes=[("Rst", d)])
                    order = list(range(NCH)) if d == 0 else list(range(NCH - 1, -1, -1))
                    first_slot = 0 if d == 0 else NCH
                    k.op("dve", lambda e, d=d, first_slot=first_slot: e.memset(Sinb[d][:, first_slot, :], 0.0), writes=[("Sinb", d, first_slot)])
                    for c in order:
                        wr = wi % 2
                        wi += 1
                        dwb = dw[:, 2 * c:2 * c + 2, d * 8 + 4 * g:d * 8 + 4 * g + 4].unsqueeze(3).to_broadcast([128, 2, 4, 64])
                        k.op("dve", lambda e, wr=wr, dwb=dwb, c=c: e.tensor_tensor(out=xdw[wr].rearrange("p t (h d) -> p t h d", h=4),
                                                                                in0=xs4[:, 2 * c:2 * c + 2, :, :], in1=dwb, op=ALU.mult),
                             reads=[("xs", 2 * c), ("xs", 2 * c + 1), "dw"], writes=[("xdw", wr)])
                        pb, pk = PS(c % 2)
                        for ti in range(2):
                            tt = 2 * c + ti
                            k.op("pe", lambda e, pb=pb, tt=tt, ti=ti, wr=wr: e.matmul(
                                pb[:, 0:256], lhsT=Btok[:, tt, :], rhs=xdw[wr][:, ti, :], start=(ti == 0), stop=(ti == 1)),
                                reads=[("Btok", tt), ("xdw", wr)], writes=[pk])
                        etb = etot[:, c, d * 8 + 4 * g:d * 8 + 4 * g + 4].unsqueeze(2).to_broadcast([128, 4, 64])
                        k.op("dve", lambda e, d=d, etb=etb: e.tensor_tensor(out=v3(Rtmp), in0=v3(Rst[d]), in1=etb, op=ALU.mult),
                             reads=[("Rst", d), "etot"], writes=["Rtmp"])
                        k.op("dve", lambda e, pb=pb, d=d: e.tensor_tensor(out=Rst[d], in0=pb[:, 0:256], in1=Rtmp, op=ALU.add),
                             reads=["Rtmp"], writes=[pk, ("Rst", d)])
                        slot = c + 1 if d == 0 else c
                        k.op("act", lambda e, d=d, slot=slot: e.copy(Sinb[d][:, slot, :], Rst[d]), reads=[("Rst", d)], writes=[("Sinb", d, slot)])
                def prologue(c):
                    t0_, t1_ = 2 * c, 2 * c + 1
                    cr = c % 2
                    pb2, pk2 = PS(7)
                    for d, (r0, r1) in ((0, (C_R0F, C_R1F)), (1, (C_R0B, C_R1B))):
                        o_ = pb2[0:40, d * 256:(d + 1) * 256]
                        k.op("pe", lambda e: e.matmul(o_, lhsT=avd[:, t0_, d, :], rhs=cst[:, r0:r0 + 256],
                                                      start=(d == 0), stop=False, skip_group_check=True),
                             reads=["avd", "cst"], writes=[pk2])
                        k.op("pe", lambda e: e.matmul(o_, lhsT=avd[:, t1_, d, :], rhs=cst[:, r1:r1 + 256],
                                                      start=False, stop=True, skip_group_check=True),
                             reads=["avd", "cst"], writes=[pk2])
                    k.op("act", lambda e: e.copy(Hb[cr][0:40, :], pb2[0:40, 0:512]), reads=[], writes=[pk2, ("cumT", cr)])
                    k.op("dve", lambda e: e.tensor_tensor(out=Lb[0:40, :], in0=pb2[0:40, 0:512], in1=Hb[cr][0:40, :], op=ALU.subtract),
                         reads=[("cumT", cr)], writes=[pk2, "Lb"])
                    k.op("dve", lambda e: e.tensor_copy(Hb[cr][32:40, :], Lb[32:40, :]), reads=["Lb"], writes=[("cumT", cr)])
                    for d in range(2):
                        dtb = dt[:, 2 * c:2 * c + 2, d * 8 + 4 * g:d * 8 + 4 * g + 4].unsqueeze(3).to_broadcast([128, 2, 4, 64])
                        k.op("dve", lambda e: e.tensor_tensor(out=xdt[d][:, cr].rearrange("p t (h d) -> p t h d", h=4),
                                                              in0=xs4[:, 2 * c:2 * c + 2, :, :], in1=dtb, op=ALU.mult),
                             reads=[("xs", 2 * c), ("xs", 2 * c + 1), "dt"], writes=[("xdt", d, cr)])
                    pb, pk = PS(7)
                    for si in range(2):
                        k.op("pe", lambda e: e.matmul(pb[:, si * 256:(si + 1) * 256], lhsT=BT[:, g, (2 * c + si) * 128:(2 * c + si + 1) * 128],
                                                      rhs=CT[:, g, c * 256:(c + 1) * 256], start=(si == 0), stop=True, skip_group_check=True),
                             reads=[("BT", g, (2 * c + si) // 4), ("CT", g, c // 2)], writes=[pk])
                    copy_op("act", GT[cr], pb[:, 0:512], reads=[], writes=[pk, ("GT", cr)])

                def item_front(c, it):
                    h, d = it // 2, it % 2
                    hh = 4 * g + h
                    cr = c % 2
                    sel = cstb[0:40, 1152 + hh * 128:1152 + (hh + 1) * 128]
                    for si in range(2):
                        pb, pk = PS(2 * (it % 2) + si)
                        if d == 0:
                            lo, hi = (0, 256) if si == 0 else (128, 256)
                            mask = cstb[:, 640:896] if si == 0 else cstb[:, 640:768]
                        else:
                            lo, hi = (0, 128) if si == 0 else (0, 256)
                            mask = cstb[:, 896 + 128:896 + 256] if si == 0 else cstb[:, 896:896 + 256]
                        w = hi - lo
                        k.op("pe", lambda e: e.matmul(pb[:, 0:w], lhsT=sel, rhs=Hb[cr][0:40, d * 256 + lo:d * 256 + hi], start=True, stop=False),
                             reads=[("cumT", cr), "cstb"], writes=[pk])
                        k.op("pe", lambda e: e.matmul(pb[:, 0:w], lhsT=identb, rhs=mask, start=False, stop=True),
                             reads=["cstb"], writes=[pk])

                def item_back(c, it, started, ybank, ykey):
                    h, d = it // 2, it % 2
                    hh = 4 * g + h
                    cr = c % 2
                    mms = []
                    for si in range(2):
                        pb, pk = PS(2 * (it % 2) + si)
                        if d == 0:
                            lo, hi = (0, 256) if si == 0 else (128, 256)
                        else:
                            lo, hi = (0, 128) if si == 0 else (0, 256)
                        w = hi - lo
                        ebi = 2 * d + si
                        tt_s = 2 * c + si
                        if d == 0:
                            k.op("act", lambda e: e.activation(out=Eb[ebi][:, 0:w], in_=pb[:, 0:w], func=AF.Exp, bias=ncf[:, tt_s, hh:hh + 1], scale=1.0),
                                 reads=["ncf"], writes=[pk, ("Eb", ebi)])
                        else:
                            k.op("act", lambda e: e.activation(out=Eb[ebi][:, 0:w], in_=pb[:, 0:w], func=AF.Exp, bias=cum[:, tt_s, 8 + hh:9 + hh], scale=-1.0),
                                 reads=["cum"], writes=[pk, ("Eb", ebi)])
                        k.op("dve", lambda e: e.tensor_tensor(out=Mb[ebi][:, 0:w], in0=Eb[ebi][:, 0:w], in1=GT[cr][:, si * 256 + lo:si * 256 + hi], op=ALU.mult),
                             reads=[("Eb", ebi), ("GT", cr)], writes=[("Mb", ebi)])
                        for li in range(2):
                            if lo <= li * 128 < hi:
                                mms.append((ebi, li * 128 - lo, li, si))
                    for (ebi, off, li, si) in mms:
                        st_ = not started[li]
                        started[li] = True
                        k.op("pe", lambda e: e.matmul(
                            ybank[li][:, h * 64:(h + 1) * 64], lhsT=Mb[ebi][:, off:off + 128], rhs=xdt[d][:, cr, si, h * 64:(h + 1) * 64],
                            start=st_, stop=False, skip_group_check=True),
                            reads=[("Mb", ebi), ("xdt", d, cr)], writes=[ykey[li]])

                def epilogue(c, ybank, ykey):
                    for li in range(2):
                        tt = 2 * c + li
                        pf7, pfk = PS(6 if False else 7)
                        pf = pf7[:, 0:256]
                        pbw = pf7[:, 256:512]
                        k.op("pe", lambda e: e.matmul(pf, lhsT=CT[:, g, tt * 128:(tt + 1) * 128], rhs=Sinb[0][:, c, :],
                                                      start=True, stop=True), reads=[("CT", g, tt // 4), ("Sinb", 0, c)], writes=[pfk])
                        k.op("pe", lambda e: e.matmul(pbw, lhsT=CT[:, g, tt * 128:(tt + 1) * 128], rhs=Sinb[1][:, c + 1, :],
                                                      start=False, stop=True, skip_group_check=True),
                             reads=[("CT", g, tt // 4), ("Sinb", 1, c + 1)], writes=[pfk])
                        ecfb = sc1[:, tt, 4 * g:4 * g + 4].unsqueeze(2).to_broadcast([128, 4, 64])
                        erbb = sc1[:, tt, 8 + 4 * g:12 + 4 * g].unsqueeze(2).to_broadcast([128, 4, 64])
                        dbb = Dbc[:, 4 * g:4 * g + 4].unsqueeze(2).to_broadcast([128, 4, 64])
                        yi = li
                        k.op("dve", lambda e: e.tensor_tensor(out=v3(ya[yi]), in0=v3(pf), in1=ecfb, op=ALU.mult),
                             reads=["sc1a"], writes=[pfk, ("ya", yi)])
                        k.op("dve", lambda e: e.tensor_tensor(out=v3(Rtmp), in0=v3(pbw), in1=erbb, op=ALU.mult),
                             reads=["sc1b"], writes=[pfk, "Rtmp"])
                        k.op("dve", lambda e: e.tensor_tensor(out=ya[yi], in0=ya[yi], in1=Rtmp, op=ALU.add), reads=["Rtmp"], writes=[("ya", yi)])
                        k.op("dve", lambda e: e.tensor_tensor(out=v3(Rtmp), in0=xs4[:, tt, :, :], in1=dbb, op=ALU.mult),
                             reads=[("xs", tt), "vbc"], writes=["Rtmp"])
                        k.op("dve", lambda e: e.tensor_tensor(out=ya[yi], in0=ya[yi], in1=Rtmp, op=ALU.add), reads=["Rtmp"], writes=[("ya", yi)])
                        k.op("dve", lambda e: e.tensor_tensor(out=ya[yi], in0=ybank[li][:, 0:256], in1=ya[yi], op=ALU.add),
                             reads=[], writes=[ykey[li], ("ya", yi)])
                        zi = li
                        k.dma(zb[zi], ptok_d[tt * 128:(tt + 1) * 128, g * 256:(g + 1) * 256], reads=[("ptok_d",)], writes=[("zb", zi)])
                        k.op("act", lambda e: e.activation(out=szb[zi], in_=zb[zi], func=AF.Silu), reads=[("zb", zi)], writes=[("szb", zi)])
                        k.op("dve", lambda e: e.tensor_tensor(out=yg[yi], in0=ya[yi], in1=szb[zi], op=ALU.mult),
                             reads=[("szb", zi), ("ya", yi)], writes=[("yg", yi)])
                        out_norm_store(yg[yi], [("yg", yi)], 256, vrow(vbc, "ssd_norm", g * 256, 256), g * 256, tmp, tt, tt, "do")

                prologue(0)
                for c in range(NCH):
                    yb0, yk0 = PS(4)
                    yb1, yk1 = PS(5)
                    ybank = (yb0, yb1)
                    ykey = (yk0, yk1)
                    started = [False, False]
                    item_front(c, 0)
                    for it in range(8):
                        if it + 1 < 8:
                            item_front(c, it + 1)
                        item_back(c, it, started, ybank, ykey)
                    if c + 1 < NCH:
                        prologue(c + 1)
                    epilogue(c, ybank, ykey)
            A.reset(fbase, bbase)

        def mixer_phase(l):
            A.reset()
            vbc = A.f32(NV)
            k.dma(vbc, vec_d[l:l + 1, :].partition_broadcast(128), writes=["vbc"])
            ssd_phase(l, vbc)
            k.barrier(barsc[:, 1:2])
            swa_phase(l, vbc)
            k.barrier(barsc[:, 2:3])
            mla_phase(l, vbc)
            if l == 0:
                k.barrier(barsc[:, 3:4])
                dbg_dump("ptok", ptok_d, [("ptok_d",)], big=True)
                dbg_dump("xbc", xbc_d, [("xbc_d",)], BF16, big=True)
                dbg_dump("y", y_d, [("y_d",)], BF16, big=True)

        finals = []
        for p in range(L + 1):
            l_prev = p - 1 if p > 0 else None
            l_next = p if p < L else None
            if stop_after is not None and p > stop_after[0]:
                break
            finals = block_phase(p, l_prev, l_next)
            k.barrier(barsc[:, 0:1])
            if l_next is not None:
                if stop_after is not None and stop_after == (p, "block"):
                    break
                mixer_phase(l_next)
                k.barrier(barsc[:, 0:1])
                if stop_after is not None and stop_after == (p, "mix"):
                    break
        k.emit(list(finals) + dbg_outs)
        print("ops recorded:", k.nops, "arena hi (KiB):", A.hi * 4 / 1024, A.hib * 2 / 1024)
    return nc


def prep_common(inp, S):
    L = inp["ffn1_norm"].shape[0]
    vec = np.zeros((L, NV), np.float32)
    for n, (o, s) in VOFF.items():
        vec[:, o:o + s] = np.asarray(inp[n], np.float32).reshape(L, s)
    cw = np.asarray(inp["ssd_conv_w"], np.float32)
    cb = np.asarray(inp["ssd_conv_b"], np.float32)
    convp = np.zeros((L, 128, 8, 6), np.float32)
    for c in range(8):
        convp[:, :, c, 0:5] = cw[:, :, c * 128:(c + 1) * 128].transpose(0, 2, 1)
        convp[:, :, c, 5] = cb[:, c * 128:(c + 1) * 128]
    com = {"vecs": vec, "convp": np.ascontiguousarray(convp.reshape(L, 128, 48)), "cst": make_consts(),
           "cs_swa": rope_tables(S, 64), "cs_mla": rope_tables(S, 32)}
    for n in ("ffn1_gate", "ffn1_up", "ffn1_down", "ffn2_gate", "ffn2_up", "ffn2_down", "w_in", "w_out",
              "mla_w_uq", "mla_w_ukv"):
        com[n] = np.ascontiguousarray(np.asarray(inp[n], np.float32))
    return com, L


_NC_CACHE = {}


def kernel(**inputs):
    x = np.asarray(inputs["x"], np.float32)
    B, S, _ = x.shape
    com, L = prep_common(inputs, S)
    key = (S, L)
    if key not in _NC_CACHE:
        _NC_CACHE[key] = build(S, L)
    nc = _NC_CACHE[key]
    ncores = 8
    place = [0, 1, 4, 5][:B] if B <= 4 else list(range(B))
    zero_map = None
    in_maps = []
    for c in range(ncores):
        if c in place:
            m = dict(com)
            m["x"] = np.ascontiguousarray(x[place.index(c)])
        else:
            if zero_map is None:
                zero_map = {kk: np.zeros_like(v) for kk, v in com.items()}
                zero_map["x"] = np.zeros((S, D), np.float32)
            m = zero_map
        in_maps.append(m)
    res = run_bass_kernel_spmd(nc, in_maps, core_ids=list(range(ncores)))
    out = np.stack([np.asarray(res.results[place[b]]["out"], np.float32) for b in range(B)], axis=0)
    return out
```

```python
import contextlib
import numpy as np
import concourse.bass as bass
import concourse.mybir as mybir
from concourse.bass_utils import run_bass_kernel_spmd

F32 = mybir.dt.float32
BF16 = mybir.dt.bfloat16
AF = mybir.ActivationFunctionType
ALU = mybir.AluOpType
AX = mybir.AxisListType

D = 1024
DFF = 2816
NPROJ = 2480
NTM = 1456
EPS = 1e-6
BIG = 1.0e4
GF = 4
DEBUG = False
ALT_DVE_ONLY = False
N_DMA_SEMS = 40

VOFF = {}
_o = 0
for _n, _s in (("ffn1_norm", 1024), ("mix_norm", 1024), ("ffn2_norm", 1024), ("ssd_norm", 512),
               ("swa_q_norm", 64), ("swa_k_norm", 64), ("swa_out_norm", 256), ("mla_q_lat_norm", 256),
               ("mla_kv_norm", 128), ("mla_q_norm", 96), ("mla_k_norm", 96), ("mla_out_norm", 256),
               ("ssd_dt_bias", 16), ("ssd_a_log", 16), ("ssd_d", 8), ("swa_sink", 4)):
    VOFF[_n] = (_o, _s)
    _o += _s
NV = _o

C_ID = 0
C_R0F = 128
C_R1F = 384
C_R0B = 640
C_R1B = 896
C_NEGF = 1152
C_POSB = 1408
C_GE = 1664
C_SEL = 1792
NCST = C_SEL + 1024


def make_consts():
    c = np.zeros((128, NCST), np.float32)
    i = np.arange(128)
    le = (i[:, None] <= i[None, :]).astype(np.float32)
    lt = (i[:, None] < i[None, :]).astype(np.float32)
    ge = (i[:, None] >= i[None, :]).astype(np.float32)
    gt = (i[:, None] > i[None, :]).astype(np.float32)
    c[:, C_ID:C_ID + 128] = np.eye(128)
    c[:, C_R0F:C_R0F + 128] = le
    c[:, C_R0F + 128:C_R0F + 256] = 1.0
    c[:, C_R1F + 128:C_R1F + 256] = le
    c[:, C_R0B:C_R0B + 128] = lt
    c[:, C_R0B + 128:C_R0B + 256] = 1.0
    c[:, C_R1B + 128:C_R1B + 256] = lt
    c[:, C_NEGF:C_NEGF + 128] = -BIG * gt
    c[:, C_POSB + 128:C_POSB + 256] = BIG * lt
    c[:, C_GE:C_GE + 128] = ge
    for h in range(8):
        c[h, C_SEL + h * 128:C_SEL + (h + 1) * 128] = 1.0
        c[32 + h, C_SEL + h * 128:C_SEL + (h + 1) * 128] = 1.0
    return c


def rope_tables(n, dim):
    inv = 1.0 / np.power(np.float32(10000.0), np.arange(0, dim, 2, dtype=np.float32) / np.float32(dim))
    ang = np.arange(n, dtype=np.float32)[:, None] * inv[None, :].astype(np.float32)
    return np.concatenate([np.cos(ang), np.sin(ang)], axis=1).astype(np.float32)


def _freeze(fn):
    import types
    if fn.__closure__ is None:
        return fn
    cells = []
    for c in fn.__closure__:
        try:
            cells.append(types.CellType(c.cell_contents))
        except ValueError:
            cells.append(c)
    return types.FunctionType(fn.__code__, fn.__globals__, fn.__name__, fn.__defaults__, tuple(cells))


class Op:
    __slots__ = ("eng", "fn", "waits", "signal", "sem", "val", "is_dma")

    def __init__(self, eng, fn, is_dma=False):
        self.eng = eng
        self.fn = fn
        self.waits = []
        self.signal = False
        self.sem = None
        self.val = None
        self.is_dma = is_dma


class KB:
    ENGS = ("pe", "act", "dve", "pool", "sp")

    def __init__(self, nc):
        self.nc = nc
        self.prog = {e: [] for e in self.ENGS}
        self.last_w = {}
        self.readers = {}
        self.dma_rr = 0
        self.dma_last = [None] * N_DMA_SEMS
        self.dma_uses = [0] * N_DMA_SEMS
        self.bar = None
        self.nops = 0
        self.carrier = None

    def _deps(self, op, reads, writes):
        deps = []
        if self.bar is not None:
            deps.append(self.bar)
        for k in reads:
            w = self.last_w.get(k)
            if w is not None:
                deps.append(w)
        for k in writes:
            w = self.last_w.get(k)
            if w is not None:
                deps.append(w)
            deps.extend(self.readers.get(k, ()))
        for k in reads:
            self.readers.setdefault(k, []).append(op)
        for k in writes:
            self.last_w[k] = op
            self.readers[k] = []
        seen = set()
        for d in deps:
            if d is op or id(d) in seen:
                continue
            seen.add(id(d))
            if op.eng == "pe" and d.eng == "pe" and not d.is_dma and not op.is_dma:
                continue
            op.waits.append(d)

    def op(self, eng, fn, reads=(), writes=()):
        o = Op(eng, _freeze(fn))
        self._deps(o, reads, writes)
        self.prog[eng].append(o)
        self.nops += 1
        return o

    def dma(self, out, in_, reads=(), writes=(), q="sp", **kw):
        o = Op(q, None, is_dma=True)
        o.fn = lambda e, out=out, in_=in_, kw=kw: e.dma_start(out=out, in_=in_, **kw)
        self._deps(o, reads, writes)
        i = self.dma_rr
        self.dma_rr = (self.dma_rr + 1) % N_DMA_SEMS
        prev = self.dma_last[i]
        if prev is not None:
            o.waits.append(prev)
        self.dma_last[i] = o
        self.dma_uses[i] += 1
        o.sem = i
        o.val = 16 * self.dma_uses[i]
        o.signal = True
        self.prog[q].append(o)
        self.nops += 1
        return o

    def barrier(self, scratch_ap):
        o = Op("dve", lambda e: e.memset(scratch_ap, 0.0))
        if self.bar is not None:
            o.waits.append(self.bar)
        for e in self.ENGS:
            for p in reversed(self.prog[e]):
                if not p.is_dma:
                    o.waits.append(p)
                    break
        for d in self.dma_last:
            if d is not None:
                o.waits.append(d)
        self.prog["dve"].append(o)
        self.bar = o
        self.last_w = {}
        self.readers = {}
        return o

    def emit(self, final_wait_ops=()):
        nc = self.nc
        for e in self.ENGS:
            for o in self.prog[e]:
                for w in o.waits:
                    w.signal = True
        for o in final_wait_ops:
            o.signal = True
        for e in self.ENGS:
            c = 0
            for o in self.prog[e]:
                if o.is_dma:
                    continue
                if o.signal:
                    c += 1
                    o.sem = e
                    o.val = c
        with contextlib.ExitStack() as st:
            esem = {e: st.enter_context(nc.semaphore("s_" + e)) for e in self.ENGS}
            dsem = [st.enter_context(nc.semaphore("d_%d" % i)) for i in range(N_DMA_SEMS)]
            block = st.enter_context(nc.Block())

            def sem_of(o):
                return dsem[o.sem] if o.is_dma else esem[o.sem]

            def run(e, h):
                waited = {}
                for o in self.prog[e]:
                    need = {}
                    for w in o.waits:
                        key = ("d", w.sem) if w.is_dma else ("e", w.sem)
                        if waited.get(key, 0) >= w.val:
                            continue
                        waited[key] = w.val
                        need[key] = w
                    need = list(need.values())
                    if e == "pe" and not o.is_dma:
                        for w in need[:-1]:
                            h.ldweights(self.carrier)._wait_ge(sem_of(w), w.val)
                        ins = o.fn(h)
                        if need:
                            ins._wait_ge(sem_of(need[-1]), need[-1].val)
                    else:
                        for w in need:
                            h.wait_ge(sem_of(w), w.val)
                        ins = o.fn(h)
                    if o.is_dma:
                        ins.then_inc(dsem[o.sem], 16)
                    elif o.signal:
                        ins.then_inc(esem[e], 1)
                if e == "sp":
                    for o in final_wait_ops:
                        h.wait_ge(sem_of(o), o.val)

            @block.tensor
            def _(t):
                run("pe", t)

            @block.scalar
            def _(s):
                run("act", s)

            @block.vector
            def _(v):
                run("dve", v)

            @block.gpsimd
            def _(g):
                run("pool", g)

            @block.sync
            def _(s):
                run("sp", s)


class Arena:
    def __init__(self, apf, nf, apb, nb):
        self.apf, self.nf, self.apb, self.nb = apf, nf, apb, nb
        self.p = 0
        self.pb = 0
        self.hi = 0
        self.hib = 0

    def reset(self, to=0, tob=0):
        self.p = to
        self.pb = tob

    def f32(self, cols):
        a = self.apf[:, self.p:self.p + cols]
        self.p += (cols + 7) // 8 * 8
        self.hi = max(self.hi, self.p)
        assert self.p <= self.nf, ("f32 arena overflow", self.p, self.nf)
        return a

    def bf16(self, cols):
        a = self.apb[:, self.pb:self.pb + cols]
        self.pb += (cols + 15) // 16 * 16
        self.hib = max(self.hib, self.pb)
        assert self.pb <= self.nb, ("bf16 arena overflow", self.pb, self.nb)
        return a


def build(S, L, stop_after=None):
    assert S % 512 == 0
    NT = S // 128
    TB = min(1024, S)
    NBLK = S // TB
    NTB = TB // 128
    NCH = S // 256

    nc = bass.Bass("TRN2", target_bir_lowering=False)
    dr = lambda name, shape, dt=F32, kind="ExternalInput": nc.dram_tensor(name, list(shape), dt, kind=kind).ap()
    x_d = dr("x", [S, D])
    wg_d = [dr("ffn1_gate", [L, D, DFF]), dr("ffn2_gate", [L, D, DFF])]
    wu_d = [dr("ffn1_up", [L, D, DFF]), dr("ffn2_up", [L, D, DFF])]
    wd_d = [dr("ffn1_down", [L, DFF, D]), dr("ffn2_down", [L, DFF, D])]
    win_d = dr("w_in", [L, D, NPROJ])
    wout_d = dr("w_out", [L, D, D])
    wuq_d = dr("mla_w_uq", [L, 256, 384])
    wukv_d = dr("mla_w_ukv", [L, 128, 512])
    vec_d = dr("vecs", [L, NV])
    convp_d = dr("convp", [L, 128, 48])
    cst_d = dr("cst", [128, NCST])
    cs_swa_d = dr("cs_swa", [S, 64])
    cs_mla_d = dr("cs_mla", [S, 32])
    out_d = dr("out", [S, D], kind="ExternalOutput")
    xbc_d = dr("xbc_scr", [1024, S], BF16, kind="Internal")
    ptok_d = dr("ptok_scr", [S, NTM], F32, kind="Internal")
    y_d = dr("y_scr", [S, D], BF16, kind="Internal")

    ARENA_F = 17 * 1024
    ARENA_B = 58 * 1024
    st = contextlib.ExitStack()
    with st:
        arena_f = st.enter_context(nc.sbuf_tensor("arena_f", [128, ARENA_F], F32))
        arena_b = st.enter_context(nc.sbuf_tensor("arena_b", [128, ARENA_B], BF16))
        cst = st.enter_context(nc.sbuf_tensor("cst_sb", [128, NCST], F32))
        cstb = st.enter_context(nc.sbuf_tensor("cstb_sb", [128, 128 * 5 + 512 + 1024], BF16))
        barsc = st.enter_context(nc.sbuf_tensor("barsc", [128, 8], F32))
        banks = [st.enter_context(nc.psum_tensor("bank%d" % i, [128, 1024] if i == 6 else [128, 512], BF16 if i == 6 else F32))
                 for i in range(8)]
        k = KB(nc)
        A = Arena(arena_f[:], ARENA_F, arena_b[:], ARENA_B)

        k.carrier = cstb[:, 0:1]
        identf = cst[:, C_ID:C_ID + 128]
        identb = cstb[:, 0:128]
        ones_f = cst[:, C_R0F + 128:C_R0F + 256]

        k.dma(cst[:], cst_d, writes=["cst"])
        k.dma(cstb[:, 0:128], cst_d[:, C_ID:C_ID + 128], writes=["cstb"], q="pool")
        for j in range(2):
            k.dma(cstb[:, 128 + j * 128:256 + j * 128], cst_d[:, C_GE:C_GE + 128], writes=["cstb"], q="pool")
            k.dma(cstb[:, 384 + j * 128:512 + j * 128], cst_d[:, C_R0F:C_R0F + 128], writes=["cstb"], q="pool")

        k.dma(cstb[:, 640:896], cst_d[:, C_NEGF:C_NEGF + 256], writes=["cstb"], q="pool")
        k.dma(cstb[:, 896:1152], cst_d[:, C_POSB:C_POSB + 256], writes=["cstb"], q="pool")
        k.dma(cstb[:, 1152:2176], cst_d[:, C_SEL:C_SEL + 1024], writes=["cstb"], q="pool")

        def PS(i):
            return banks[i], "bank%d" % i

        rr = {"n": 0}
        dbg_outs = []
        dbg_names = set()

        def dbg_dump(name, ap, reads, dt=F32, big=False):
            if not DEBUG or ("dbg_" + name) in dbg_names:
                return
            dbg_names.add("dbg_" + name)
            t = nc.dram_tensor("dbg_" + name, list(ap.shape), dt, kind="ExternalOutput").ap()
            if big:
                for r0 in range(0, ap.shape[0], 256):
                    dbg_outs.append(k.dma(t[r0:r0 + 256], ap[r0:r0 + 256], reads=reads))
            else:
                dbg_outs.append(k.dma(t, ap, reads=reads))

        def alt():
            rr["n"] += 1
            return "dve" if (rr["n"] % 2 or ALT_DVE_ONLY) else "act"

        def copy_op(eng, out, in_, reads, writes):
            if eng == "act":
                return k.op("act", lambda e: e.copy(out, in_), reads, writes)
            return k.op(eng, lambda e: e.tensor_copy(out, in_), reads, writes)

        def vrow(vbc, name, lo=0, n=None):
            o, s = VOFF[name]
            n = s - lo if n is None else n
            return vbc[:, o + lo:o + lo + n]

        def block_phase(pidx, l_prev, l_next):
            A.reset()
            vbcA = A.f32(3072) if l_prev is not None else None
            vbcB = A.f32(3072) if l_next is not None else None
            xb = A.f32(NTB * D).rearrange("p (t d) -> p t d", t=NTB)
            hT = A.bf16(8 * TB).rearrange("p (c t) -> p c t", c=8)
            hb = [A.bf16(D) for _ in range(2)]
            junk = A.f32(D)
            ss = A.f32(NTB)
            sq = A.f32(NTB)
            rstd = A.f32(NTB)
            wgu = [A.bf16(2 * 8 * GF * 128).rearrange("p (w c f) -> p w c f", w=2, c=8) for _ in range(2)]
            wdn = [A.bf16(GF * D).rearrange("p (c d) -> p c d", c=GF) for _ in range(2)]
            sg = [A.bf16(512) for _ in range(2)]
            aT = [A.bf16(GF * 512).rearrange("p (c t) -> p c t", c=GF) for _ in range(2)]
            wpj = [A.bf16(8 * 512).rearrange("p (c f) -> p c f", c=8) for _ in range(2)]
            evf = [A.bf16(512) for _ in range(2)]
            evt = [A.f32(512) for _ in range(2)]
            cnt = {"w": 0, "p": 0, "a": 0, "d": 0, "t": 0, "e": 0, "g": 0}
            if l_prev is not None:
                k.dma(vbcA, vec_d[l_prev:l_prev + 1, 0:3072].partition_broadcast(128), writes=["vbcA"])
            if l_next is not None:
                k.dma(vbcB, vec_d[l_next:l_next + 1, 0:3072].partition_broadcast(128), writes=["vbcB"])

            def norm_to_hT(vbc, vkey, nname, tag):
                k.op("dve", lambda e: e.memset(ss, 0.0), writes=["ss"])
                for tt in range(NTB):
                    k.op("act", lambda e, tt=tt: e.activation(out=junk, in_=xb[:, tt, :], func=AF.Square,
                                                              accum_out=ss[:, tt:tt + 1]),
                         reads=[("xb", tt)], writes=["junk", "ss"])
                k.op("act", lambda e: e.activation(out=sq, in_=ss, func=AF.Sqrt, scale=1.0 / D, bias=EPS),
                     reads=["ss"], writes=["sq"])
                k.op("dve", lambda e: e.reciprocal(out=rstd, in_=sq), reads=["sq"], writes=["rstd"])
                wv = vrow(vbc, nname)
                if tag == "f" and not cnt.get("dbg1"):
                    cnt["dbg1"] = 1
                    dbg_dump("ss", ss, ["ss"]); dbg_dump("rstd", rstd, ["rstd"]); dbg_dump("wv", wv, [vkey])
                    dbg_dump("xb0", xb[:, 0, :], [("xb", 0)])
                for tt in range(NTB):
                    hbi = tt % 2
                    k.op("dve", lambda e, tt=tt, hbi=hbi: e.scalar_tensor_tensor(
                        out=hb[hbi], in0=xb[:, tt, :], scalar=rstd[:, tt:tt + 1], in1=wv,
                        op0=ALU.mult, op1=ALU.mult), reads=[("xb", tt), "rstd", vkey], writes=[("hb", hbi)])
                    transpose_rows(hb[hbi], ("hb", hbi), 8, hT, tt, "hT")

            def transpose_rows(src, skey, nchunks, dstT, tt, dkey):
                for h0 in range(0, nchunks, 4):
                    nn = min(4, nchunks - h0)
                    pb, pk = PS(6)
                    pv = pb[:, (cnt["t"] % 2) * 512:(cnt["t"] % 2) * 512 + 512].rearrange("p (c t) -> p c t", c=4)
                    cnt["t"] += 1
                    for j in range(nn):
                        k.op("pe", lambda e, j=j, h0=h0: e.transpose(pv[:, j, :], src[:, (h0 + j) * 128:(h0 + j + 1) * 128], identb),
                             reads=[skey, "cstb"], writes=[pk])
                    copy_op(alt(), dstT[:, h0:h0 + nn, tt * 128:(tt + 1) * 128], pv[:, 0:nn, :],
                            reads=[], writes=[pk, (dkey, tt, h0)])

            def dump_hT(nm):
                dbg_dump(nm, hT[:, :, 0:128], [("hT", 0, 0), ("hT", 0, 4)], BF16)

            def hT_keys(t4, key="hT"):
                return [(key, tt, h0) for tt in range(t4 * 4, t4 * 4 + 4) for h0 in (0, 4)]

            def ffn(l, which, vbc, vkey):
                norm_to_hT(vbc, vkey, "ffn1_norm" if which == 0 else "ffn2_norm", "f")
                if not cnt.get("dbg2"):
                    cnt["dbg2"] = 1
                    dump_hT("hT")
                wg_l = wg_d[which][l].rearrange("(c p) f -> p c f", p=128)
                wu_l = wu_d[which][l].rearrange("(c p) f -> p c f", p=128)
                wd_l = wd_d[which][l]
                groups = []
                f0 = 0
                while f0 < DFF:
                    nf = min(GF, (DFF - f0) // 128)
                    groups.append((f0, nf))
                    f0 += nf * 128
                NG = len(groups)
                NT4 = TB // 512
                units = [(g, t4) for g in range(NG) for t4 in range(NT4)]

                def load_w(g):
                    if g >= NG:
                        return
                    wi = g % 2
                    f0, nf = groups[g]
                    k.dma(wgu[wi][:, 0, :, 0:nf * 128], wg_l[:, :, f0:f0 + nf * 128], writes=[("wgu", wi, 0)], q="pool")
                    k.dma(wgu[wi][:, 1, :, 0:nf * 128], wu_l[:, :, f0:f0 + nf * 128], writes=[("wgu", wi, 1)], q="pool")
                    k.dma(wdn[wi][:, 0:nf, :], wd_l[f0:f0 + nf * 128, :].rearrange("(c p) d -> p c d", p=128), writes=[("wdn", wi)], q="pool")

                ais = {}

                def front_pieces(u):
                    g, t4 = units[u]
                    wi = g % 2
                    nf = groups[g][1]
                    ai = cnt["a"] % 2
                    cnt["a"] += 1
                    ais[u] = ai
                    hk = hT_keys(t4)
                    pieces = []
                    for fc in range(nf):
                        def piece(fc=fc):
                            gi = cnt["g"] % 2
                            cnt["g"] += 1
                            pg, pgk = PS(gi)
                            pu, puk = PS(2 + gi)
                            for w, (pp, ppk) in enumerate(((pg, pgk), (pu, puk))):
                                for dc in range(8):
                                    k.op("pe", lambda e: e.matmul(
                                        pp[:], lhsT=wgu[wi][:, w, dc, fc * 128:(fc + 1) * 128],
                                        rhs=hT[:, dc, t4 * 512:(t4 + 1) * 512], start=(dc == 0), stop=(dc == 7)),
                                        reads=[("wgu", wi, w)] + hk, writes=[ppk])
                            k.op("act", lambda e: e.activation(out=sg[gi], in_=pg[:], func=AF.Silu),
                                 reads=[], writes=[pgk, ("sg", gi)])
                            k.op("dve", lambda e: e.tensor_tensor(
                                out=aT[ai][:, fc, :], in0=pu[:], in1=sg[gi], op=ALU.mult),
                                reads=[("sg", gi)], writes=[puk, ("aT", ai, fc)])
                        pieces.append(piece)
                    return pieces

                def back_pieces(u):
                    g, t4 = units[u]
                    wi = g % 2
                    nf = groups[g][1]
                    ai = ais[u]
                    pieces = []
                    for ts in range(4):
                        def piece(ts=ts):
                            tt = t4 * 4 + ts
                            for dh in range(2):
                                di = (4, 5, 7)[cnt["d"] % 3]
                                cnt["d"] += 1
                                pd, pdk = PS(di)
                                for fc in range(nf):
                                    k.op("pe", lambda e: e.matmul(
                                        pd[:], lhsT=aT[ai][:, fc, ts * 128:(ts + 1) * 128],
                                        rhs=wdn[wi][:, fc, dh * 512:(dh + 1) * 512], start=(fc == 0), stop=(fc == nf - 1)),
                                        reads=[("aT", ai, fc), ("wdn", wi)], writes=[pdk])
                                k.op("dve", lambda e: e.scalar_tensor_tensor(
                                    out=xb[:, tt, dh * 512:(dh + 1) * 512], in0=pd[:], scalar=0.5,
                                    in1=xb[:, tt, dh * 512:(dh + 1) * 512], op0=ALU.mult, op1=ALU.add),
                                    reads=[], writes=[pdk, ("xb", tt)])
                        pieces.append(piece)
                    return pieces

                load_w(0)
                load_w(1)
                for p_ in front_pieces(0):
                    p_()
                for u in range(len(units)):
                    fp = front_pieces(u + 1) if u + 1 < len(units) else []
                    bp = back_pieces(u)
                    n = max(len(fp), len(bp))
                    for i in range(n):
                        if i < len(fp):
                            fp[i]()
                        if i < len(bp):
                            bp[i]()
                    if units[u][1] == NT4 - 1:
                        load_w(units[u][0] + 2)

            def inproj(l, blk, vbc, vkey, after_norm=None):
                norm_to_hT(vbc, vkey, "mix_norm", "m")
                if after_norm is not None:
                    after_norm()
                win_l = win_d[l].rearrange("(c p) f -> p c f", p=128)
                t0 = blk * TB
                jobs = [("f", 512 + cg * 128, 128, cg) for cg in range(8)] + \
                       [("t", c0, ncol, j0) for (c0, ncol, j0) in ((0, 512, 0), (1536, 512, 512), (2048, 432, 1024))]

                def load_p(i):
                    if i >= len(jobs):
                        return
                    kind, c0, ncol, aux = jobs[i]
                    wi = i % 2
                    k.dma(wpj[wi][:, :, 0:ncol], win_l[:, :, c0:c0 + ncol], writes=[("wpj", wi)], q="pool")

                load_p(0)
                for i, (kind, c0, ncol, aux) in enumerate(jobs):
                    load_p(i + 1)
                    wi = i % 2
                    if kind == "f":
                        cg = aux
                        for t4 in range(TB // 512):
                            gi = cnt["g"] % 2
                            cnt["g"] += 1
                            pg, pgk = PS(gi)
                            hk = hT_keys(t4)
                            for dc in range(8):
                                k.op("pe", lambda e: e.matmul(
                                    pg[:], lhsT=wpj[wi][:, dc, 0:128], rhs=hT[:, dc, t4 * 512:(t4 + 1) * 512],
                                    start=(dc == 0), stop=(dc == 7)), reads=[("wpj", wi)] + hk, writes=[pgk])
                            ei = cnt["e"] % 2
                            cnt["e"] += 1
                            copy_op(alt(), evf[ei], pg[:], reads=[], writes=[pgk, ("evf", ei)])
                            k.dma(xbc_d[cg * 128:(cg + 1) * 128, t0 + t4 * 512:t0 + (t4 + 1) * 512], evf[ei],
                                  reads=[("evf", ei)], writes=[("xbc_d", cg)])
                    else:
                        j0 = aux
                        for tt in range(NTB):
                            gi = cnt["g"] % 2
                            cnt["g"] += 1
                            pu, puk = PS(2 + gi)
                            for dc in range(8):
                                k.op("pe", lambda e: e.matmul(
                                    pu[:, 0:ncol], lhsT=hT[:, dc, tt * 128:(tt + 1) * 128], rhs=wpj[wi][:, dc, 0:ncol],
                                    start=(dc == 0), stop=(dc == 7)),
                                    reads=[("wpj", wi), ("hT", tt, 0), ("hT", tt, 4)], writes=[puk])
                            ei = cnt["e"] % 2
                            cnt["e"] += 1
                            copy_op(alt(), evt[ei][:, 0:ncol], pu[:, 0:ncol], reads=[], writes=[puk, ("evt", ei)])
                            k.dma(ptok_d[t0 + tt * 128:t0 + (tt + 1) * 128, j0:j0 + ncol], evt[ei][:, 0:ncol],
                                  reads=[("evt", ei)], writes=[("ptok_d", blk)])

            def outproj(l, blk):
                t0 = blk * TB
                wout_l = wout_d[l].rearrange("(c p) f -> p c f", p=128)
                for tt in range(NTB):
                    hbi = tt % 2
                    k.dma(hb[hbi], y_d[t0 + tt * 128:t0 + (tt + 1) * 128, :], reads=[("y_d",)], writes=[("hb", hbi)])
                    transpose_rows(hb[hbi], ("hb", hbi), 8, hT, tt, "hT")
                for dh in range(2):
                    k.dma(wpj[dh], wout_l[:, :, dh * 512:(dh + 1) * 512], writes=[("wpj", dh)], q="pool")
                for dh in range(2):
                    wi = dh
                    for tt in range(NTB):
                        di = 4 + cnt["d"] % 2
                        cnt["d"] += 1
                        pd, pdk = PS(di)
                        for dc in range(8):
                            k.op("pe", lambda e, pd=pd, dc=dc, wi=wi, tt=tt: e.matmul(
                                pd[:], lhsT=hT[:, dc, tt * 128:(tt + 1) * 128], rhs=wpj[wi][:, dc, :],
                                start=(dc == 0), stop=(dc == 7)),
                                reads=[("wpj", wi), ("hT", tt, 0), ("hT", tt, 4)], writes=[pdk])
                        k.op("dve", lambda e, pd=pd, tt=tt, dh=dh: e.tensor_tensor(
                            out=xb[:, tt, dh * 512:(dh + 1) * 512], in0=pd[:], in1=xb[:, tt, dh * 512:(dh + 1) * 512],
                            op=ALU.add), reads=[], writes=[pdk, ("xb", tt)])

            stores = []
            src = x_d if pidx == 0 else out_d

            def load_x(blk):
                t0 = blk * TB
                for tt in range(NTB):
                    k.dma(xb[:, tt, :], src[t0 + tt * 128:t0 + (tt + 1) * 128, :],
                          reads=[("out_d", blk)], writes=[("xb", tt)])

            def store_x(blk):
                t0 = blk * TB
                for tt in range(NTB):
                    stores.append(k.dma(out_d[t0 + tt * 128:t0 + (tt + 1) * 128, :], xb[:, tt, :],
                                        reads=[("xb", tt)], writes=[("out_d", blk)]))

            load_x(0)
            for blk in range(NBLK):
                if l_prev is not None:
                    outproj(l_prev, blk)
                    ffn(l_prev, 1, vbcA, "vbcA")
                if l_next is not None:
                    ffn(l_next, 0, vbcB, "vbcB")
                    store_x(blk)
                    inproj(l_next, blk, vbcB, "vbcB",
                           after_norm=(lambda blk=blk: load_x(blk + 1)) if blk + 1 < NBLK else None)
                else:
                    store_x(blk)
                    if blk + 1 < NBLK:
                        load_x(blk + 1)
            return stores

        def mk_tmpset(pfx, n, nr):
            return {"pfx": pfx, "sq": A.f32(n), "ss": A.f32(8), "rs": A.f32(8), "t1": A.f32(nr), "t2": A.f32(nr)}

        def run_interleaved(gens):
            gens = list(gens)
            while gens:
                for g_ in list(gens):
                    try:
                        next(g_)
                    except StopIteration:
                        gens.remove(g_)

        def headnorm_rope(src, H, Dh, wrow, rlo, rhalf, cs, dst, tmp, keys_r, keys_w, cskey):
            sqv = tmp["sq"][:, 0:H * Dh].rearrange("p (h d) -> p h d", h=H)
            ssv = tmp["ss"][:, 0:H]
            rs = tmp["rs"][:, 0:H]
            qn = sqv
            t1 = tmp["t1"][:, 0:H * rhalf].rearrange("p (h d) -> p h d", h=H)
            t2 = tmp["t2"][:, 0:H * rhalf].rearrange("p (h d) -> p h d", h=H)
            p_ = tmp["pfx"]
            T = [p_ + "sq", p_ + "ss", p_ + "rs", p_ + "sq", p_ + "t1", p_ + "t2"]
            k.op("dve", lambda e: e.tensor_tensor(out=sqv, in0=src, in1=src, op=ALU.mult), reads=keys_r, writes=[T[0]])
            yield
            k.op("dve", lambda e: e.tensor_reduce(out=ssv, in_=sqv, axis=AX.X, op=ALU.add), reads=[T[0]], writes=[T[1]])
            yield
            k.op("act", lambda e: e.activation(out=rs, in_=ssv, func=AF.Sqrt, scale=1.0 / Dh, bias=EPS), reads=[T[1]], writes=[T[2]])
            yield
            k.op("dve", lambda e: e.reciprocal(out=ssv, in_=rs), reads=[T[2]], writes=[T[1]])
            yield
            k.op("dve", lambda e: e.tensor_tensor(out=qn, in0=src, in1=ssv.unsqueeze(2).to_broadcast([128, H, Dh]), op=ALU.mult),
                 reads=keys_r + [T[1]], writes=[T[3]])
            yield
            k.op("dve", lambda e: e.tensor_tensor(out=qn, in0=qn, in1=wrow.unsqueeze(1).to_broadcast([128, H, Dh]), op=ALU.mult),
                 reads=["vbc"], writes=[T[3]])
            yield
            if rlo > 0:
                k.op("dve", lambda e: e.tensor_copy(dst[:, :, 0:rlo], qn[:, :, 0:rlo]), reads=[T[3]], writes=keys_w)
                yield
            x1 = qn[:, :, rlo:rlo + rhalf]
            x2 = qn[:, :, rlo + rhalf:rlo + 2 * rhalf]
            cb = cs[:, 0:rhalf].unsqueeze(1).to_broadcast([128, H, rhalf])
            sb = cs[:, rhalf:2 * rhalf].unsqueeze(1).to_broadcast([128, H, rhalf])
            k.op("dve", lambda e: e.tensor_tensor(out=t1, in0=x1, in1=cb, op=ALU.mult), reads=[T[3], cskey], writes=[T[4]])
            yield
            k.op("dve", lambda e: e.tensor_tensor(out=t2, in0=x2, in1=sb, op=ALU.mult), reads=[T[3], cskey], writes=[T[5]])
            yield
            k.op("dve", lambda e: e.tensor_tensor(out=dst[:, :, rlo:rlo + rhalf], in0=t1, in1=t2, op=ALU.subtract),
                 reads=[T[4], T[5]], writes=keys_w)
            yield
            k.op("dve", lambda e: e.tensor_tensor(out=t1, in0=x1, in1=sb, op=ALU.mult), reads=[T[3], cskey], writes=[T[4]])
            yield
            k.op("dve", lambda e: e.tensor_tensor(out=t2, in0=x2, in1=cb, op=ALU.mult), reads=[T[3], cskey], writes=[T[5]])
            yield
            k.op("dve", lambda e: e.tensor_tensor(out=dst[:, :, rlo + rhalf:rlo + 2 * rhalf], in0=t1, in1=t2, op=ALU.add),
                 reads=[T[4], T[5]], writes=keys_w)
            yield

        def heads_to_T(src, skey, H, Dh, dstT, tt, dkey, bank):
            pb, pk = PS(6)
            pv = pb[:, 0:512].rearrange("p (c t) -> p c t", c=4)
            for h in range(H):
                k.op("pe", lambda e, h=h: e.transpose(pv[0:Dh, h, :], src[:, h, :], identb), reads=[skey, "cstb"], writes=[pk])
            copy_op(alt(), dstT[0:Dh, 0:H, tt * 128:(tt + 1) * 128], pv[0:Dh, 0:H, :], reads=[], writes=[pk, (dkey, tt)])

        def out_norm_store(osrc, okeys, width, wrow, col0, tmp, tt, ring, tag):
            k.op("dve", lambda e: e.memset(tmp["ss"][:, 0:1], 0.0), writes=["T_ss"])
            k.op("act", lambda e: e.activation(out=tmp["sq"][:, 0:width], in_=osrc, func=AF.Square, accum_out=tmp["ss"][:, 0:1]),
                 reads=okeys, writes=["T_sq", "T_ss"])
            k.op("act", lambda e: e.activation(out=tmp["rs"][:, 0:1], in_=tmp["ss"][:, 0:1], func=AF.Sqrt, scale=1.0 / width, bias=EPS),
                 reads=["T_ss"], writes=["T_rs"])
            k.op("dve", lambda e: e.reciprocal(out=tmp["ss"][:, 1:2], in_=tmp["rs"][:, 0:1]), reads=["T_rs"], writes=["T_ss"])
            yi = ring % 2
            yb = tmp["yb"][yi][:, 0:width]
            k.op("dve", lambda e: e.scalar_tensor_tensor(out=yb, in0=osrc, scalar=tmp["ss"][:, 1:2], in1=wrow,
                                                         op0=ALU.mult, op1=ALU.mult),
                 reads=list(okeys) + ["T_ss", "vbc"], writes=[(tag + "yb", yi)])
            k.dma(y_d[tt * 128:(tt + 1) * 128, col0:col0 + width], yb, reads=[(tag + "yb", yi)], writes=[("y_d",)])

        def attn_finish(acc_bank, ncols_q, nheads, tmp, odst_fn, extra_den, tag):
            pa, pak = PS(acc_bank)
            oT = tmp["oT"]
            k.op("act", lambda e: e.copy(oT[0:65, 0:ncols_q], pa[0:65, 0:ncols_q]), reads=[], writes=[pak, "T_oT"])
            for j in range(ncols_q // 128):
                pb, pk = PS(7)
                k.op("pe", lambda e, j=j: e.transpose(pb[:, 0:65], oT[0:65, j * 128:(j + 1) * 128], identf[0:65, 0:65]),
                     reads=["T_oT", "cst"], writes=[pk])
                den = tmp["den"]
                if extra_den is not None:
                    ed = extra_den(j)
                    k.op("dve", lambda e, ed=ed: e.tensor_tensor(out=den[:, 0:1], in0=pb[:, 64:65], in1=ed, op=ALU.add),
                         reads=["sinkexp"], writes=[pk, "T_den"])
                else:
                    k.op("dve", lambda e: e.tensor_copy(den[:, 0:1], pb[:, 64:65]), reads=[], writes=[pk, "T_den"])
                k.op("dve", lambda e: e.reciprocal(out=den[:, 1:2], in_=den[:, 0:1]), reads=["T_den"], writes=["T_rden"])
                dst, dkey = odst_fn(j)
                k.op("dve", lambda e, dst=dst: e.tensor_scalar_mul(out=dst, in0=pb[:, 0:64], scalar1=den[:, 1:2]), reads=["T_rden"], writes=[pk, dkey])

        def mla_phase(l, vbc):
            fbase, bbase = A.p, A.pb
            qT = A.bf16(4 * S).rearrange("p (h t) -> p h t", h=4)
            kT = A.bf16(4 * S).rearrange("p (h t) -> p h t", h=4)
            vx = A.bf16(NT * 4 * 65).rearrange("p (t h d) -> p t h d", t=NT, h=4)
            oall = A.f32(NT * 256).rearrange("p (t d) -> p t d", t=NT)
            wuq = A.bf16(2 * 384).rearrange("p (c f) -> p c f", c=2)
            wukv = A.bf16(512)
            pin = [A.f32(416) for _ in range(2)]
            csb = [A.f32(32) for _ in range(2)]
            nb = [A.bf16(384) for _ in range(2)]
            nT = A.bf16(3 * 128).rearrange("p (c t) -> p c t", c=3)
            qf = A.f32(384).rearrange("p (h d) -> p h d", h=4)
            kf = A.f32(384).rearrange("p (h d) -> p h d", h=4)
            qb = A.bf16(384).rearrange("p (h d) -> p h d", h=4)
            kb_ = A.bf16(384).rearrange("p (h d) -> p h d", h=4)
            tmp = {"sq": A.f32(384), "ss": A.f32(8), "rs": A.f32(8),
                   "oT": A.f32(512), "den": A.f32(2), "yb": [A.bf16(256) for _ in range(2)]}
            tsets = [mk_tmpset("Ta_", 384, 64), mk_tmpset("Tb_", 384, 64)]
            eb = [A.bf16(512) for _ in range(4)]
            k.dma(wuq, wuq_d[l].rearrange("(c p) f -> p c f", p=128), writes=["wuq"], q="pool")
            k.dma(wukv, wukv_d[l], writes=["wukv"], q="pool")
            k.op("dve", lambda e: e.memset(vx[:, :, :, 64:65], 1.0), writes=["vx1"])
            for tt in range(NT):
                pi = tt % 2
                k.dma(pin[pi], ptok_d[tt * 128:(tt + 1) * 128, 1040:1456], reads=[("ptok_d",)], writes=[("pin", pi)])
                k.dma(csb[pi], cs_mla_d[tt * 128:(tt + 1) * 128, :], writes=[("cs", pi)])
                for (c0, n, wn, tagn) in ((0, 256, "mla_q_lat_norm", "a"), (256, 128, "mla_kv_norm", "b")):
                    k.op("dve", lambda e: e.memset(tmp["ss"][:, 0:1], 0.0), writes=["T_ss"])
                    k.op("act", lambda e, c0=c0, n=n, pi=pi: e.activation(out=tmp["sq"][:, 0:n], in_=pin[pi][:, c0:c0 + n],
                                                                        func=AF.Square, accum_out=tmp["ss"][:, 0:1]),
                         reads=[("pin", pi)], writes=["T_sq", "T_ss"])
                    k.op("act", lambda e, n=n: e.activation(out=tmp["rs"][:, 0:1], in_=tmp["ss"][:, 0:1], func=AF.Sqrt,
                                                            scale=1.0 / n, bias=EPS), reads=["T_ss"], writes=["T_rs"])
                    k.op("dve", lambda e: e.reciprocal(out=tmp["ss"][:, 1:2], in_=tmp["rs"][:, 0:1]), reads=["T_rs"], writes=["T_ss"])
                    k.op("dve", lambda e, c0=c0, n=n, pi=pi, wn=wn: e.scalar_tensor_tensor(
                        out=nb[pi][:, c0:c0 + n], in0=pin[pi][:, c0:c0 + n], scalar=tmp["ss"][:, 1:2], in1=vrow(vbc, wn),
                        op0=ALU.mult, op1=ALU.mult), reads=[("pin", pi), "T_ss", "vbc"], writes=[("nb", pi, tagn)])
                pb, pk = PS(6)
                pv = pb[:, 512:1024].rearrange("p (c t) -> p c t", c=4)
                for c in range(3):
                    k.op("pe", lambda e, c=c, pi=pi: e.transpose(pv[:, c, :], nb[pi][:, c * 128:(c + 1) * 128], identb),
                         reads=[("nb", pi, "a"), ("nb", pi, "b"), "cstb"], writes=[pk])
                copy_op("act", nT, pv[:, 0:3, :], reads=[], writes=[pk, "nT"])
                pq, pqk = PS(0)
                for c in range(2):
                    k.op("pe", lambda e, c=c: e.matmul(pq[:, 0:384], lhsT=nT[:, c, :], rhs=wuq[:, c, :], start=(c == 0), stop=(c == 1)),
                         reads=["nT", "wuq"], writes=[pqk])
                pkv, pkvk = PS(1)
                k.op("pe", lambda e: e.matmul(pkv[:], lhsT=nT[:, 2, :], rhs=wukv, start=True, stop=True), reads=["nT", "wukv"], writes=[pkvk])
                copy_op("act", qf, pq[:, 0:384].rearrange("p (h d) -> p h d", h=4), reads=[], writes=[pqk, "qf"])
                kvv = pkv[:].rearrange("p (h d) -> p h d", h=4)
                copy_op("act", kf[:, :, 0:64], kvv[:, :, 0:64], reads=[], writes=[pkvk, "kf"])
                copy_op("dve", vx[:, tt, :, 0:64], kvv[:, :, 64:128], reads=[], writes=[pkvk, ("vx", tt)])
                k.op("dve", lambda e, pi=pi: e.tensor_copy(kf[:, :, 64:96], pin[pi][:, 384:416].unsqueeze(1).to_broadcast([128, 4, 32])),
                     reads=[("pin", pi)], writes=["kf"])
                run_interleaved([
                    headnorm_rope(qf, 4, 96, vrow(vbc, "mla_q_norm"), 64, 16, csb[pi], qb, tsets[0], ["qf"], ["qb"], ("cs", pi)),
                    headnorm_rope(kf, 4, 96, vrow(vbc, "mla_k_norm"), 64, 16, csb[pi], kb_, tsets[1], ["kf"], ["kb"], ("cs", pi))])
                heads_to_T(qb, "qb", 4, 96, qT, tt, "qT", 6)
                heads_to_T(kb_, "kb", 4, 96, kT, tt, "kT", 6)
            scale = 96.0 ** -0.5
            qkeys = lambda qi: [("qT", tt) for tt in range(qi * 4, qi * 4 + 4)]
            items = [(h, qi, kt) for h in range(4) for qi in range(S // 512) for kt in range(NT)]
            LA = 3

            def stA(i):
                h, qi, kt = items[i]
                psn, psk = PS(i % 4)
                k.op("pe", lambda e: e.matmul(
                    psn[:], lhsT=kT[0:96, h, kt * 128:(kt + 1) * 128], rhs=qT[0:96, h, qi * 512:(qi + 1) * 512],
                    start=True, stop=True), reads=[("kT", kt)] + qkeys(qi), writes=[psk])

            def stBC(i):
                h, qi, kt = items[i]
                psn, psk = PS(i % 4)
                ebi = i % 4
                blk = h * (S // 512) + qi
                pa, pak = PS(4 + (blk % 2))
                k.op("act", lambda e: e.activation(out=eb[ebi], in_=psn[:], func=AF.Exp, scale=scale),
                     reads=[], writes=[psk, ("eb", ebi)])
                k.op("pe", lambda e: e.matmul(
                    pa[0:65, :], lhsT=vx[:, kt, h, :], rhs=eb[ebi], start=(kt == 0), stop=(kt == NT - 1)),
                    reads=[("vx", kt), "vx1", ("eb", ebi)], writes=[pak])
                if kt == NT - 1:
                    attn_finish(4 + (blk % 2), 512, 1, tmp,
                                lambda j, h=h, qi=qi: (oall[:, qi * 4 + j, h * 64:(h + 1) * 64], ("oall", qi * 4 + j, h)),
                                None, "m")

            for i in range(min(LA, len(items))):
                stA(i)
            for i in range(len(items)):
                if i + LA < len(items):
                    stA(i + LA)
                stBC(i)
            for tt in range(NT):
                okeys = [("oall", tt, h) for h in range(4)]
                out_norm_store(oall[:, tt, :], okeys, 256, vrow(vbc, "mla_out_norm"), 768, tmp, tt, tt, "mo")
            A.reset(fbase, bbase)

        def swa_phase(l, vbc):
            fbase, bbase = A.p, A.pb
            qT = A.bf16(4 * S).rearrange("p (h t) -> p h t", h=4)
            kT = A.bf16(2 * S).rearrange("p (h t) -> p h t", h=2)
            vx = A.bf16(NT * 2 * 65).rearrange("p (t h d) -> p t h d", t=NT, h=2)
            oall = A.f32(NT * 256).rearrange("p (t d) -> p t d", t=NT)
            pin = [A.f32(512) for _ in range(2)]
            csb = [A.f32(64) for _ in range(2)]
            qb = [A.bf16(256).rearrange("p (h d) -> p h d", h=4) for _ in range(2)]
            kb_ = [A.bf16(128).rearrange("p (h d) -> p h d", h=2) for _ in range(2)]
            tmp = {"sq": A.f32(256), "ss": A.f32(8), "rs": A.f32(8),
                   "oT": A.f32(512), "den": A.f32(2), "yb": [A.bf16(256) for _ in range(2)]}
            tsets = [mk_tmpset("T%d_" % i, 256, 128) for i in range(4)]
            sinkexp = A.f32(4)
            eb = [A.bf16(256) for _ in range(4)]
            k.op("act", lambda e: e.activation(out=sinkexp, in_=vrow(vbc, "swa_sink"), func=AF.Exp), reads=["vbc"], writes=["sinkexp"])
            k.op("dve", lambda e: e.memset(vx[:, :, :, 64:65], 1.0), writes=["vx1"])
            for tt0 in range(0, NT, 2):
                gens = []
                for u in range(2):
                    tt = tt0 + u
                    pi = tt % 2
                    k.dma(pin[pi], ptok_d[tt * 128:(tt + 1) * 128, 528:1040], reads=[("ptok_d",)], writes=[("pin", pi)])
                    k.dma(csb[pi], cs_swa_d[tt * 128:(tt + 1) * 128, :], writes=[("cs", pi)])
                    qsrc = pin[pi][:, 0:256].rearrange("p (h d) -> p h d", h=4)
                    ksrc = pin[pi][:, 256:384].rearrange("p (h d) -> p h d", h=2)
                    gens.append(headnorm_rope(qsrc, 4, 64, vrow(vbc, "swa_q_norm"), 0, 32, csb[pi], qb[pi], tsets[2 * u],
                                              [("pin", pi)], [("qb", pi)], ("cs", pi)))
                    gens.append(headnorm_rope(ksrc, 2, 64, vrow(vbc, "swa_k_norm"), 0, 32, csb[pi], kb_[pi], tsets[2 * u + 1],
                                              [("pin", pi)], [("kb", pi)], ("cs", pi)))
                run_interleaved(gens)
                for u in range(2):
                    tt = tt0 + u
                    pi = tt % 2
                    k.op("dve", lambda e: e.tensor_copy(vx[:, tt, :, 0:64], pin[pi][:, 384:512].rearrange("p (h d) -> p h d", h=2)),
                         reads=[("pin", pi)], writes=[("vx", tt)])
                    heads_to_T(qb[pi], ("qb", pi), 4, 64, qT, tt, "qT", 6)
                    heads_to_T(kb_[pi], ("kb", pi), 2, 64, kT, tt, "kT", 6)
            scale = 64.0 ** -0.5
            items = []
            for n in range(NT):
                for kvh in range(2):
                    js = [j for j in (n - 1, n, n + 1) if 0 <= j < NT]
                    for ji, j in enumerate(js):
                        items.append((n, kvh, j, ji, len(js)))
            LA = 3
            pend = []

            def stA(i):
                n, kvh, j, ji, nj = items[i]
                psn, psk = PS(i % 4)
                k.op("pe", lambda e: e.matmul(
                    psn[:, 0:256].rearrange("p (a b) -> p a b", a=2), lhsT=kT[0:64, kvh, j * 128:(j + 1) * 128],
                    rhs=qT[0:64, 2 * kvh:2 * kvh + 2, n * 128:(n + 1) * 128], start=True, stop=True),
                    reads=[("kT", j), ("qT", n)], writes=[psk])

            def stBC(i):
                n, kvh, j, ji, nj = items[i]
                psn, psk = PS(i % 4)
                ebi = i % 4
                blk = n * 2 + kvh
                pa, pak = PS(4 + (blk % 2))
                k.op("act", lambda e: e.activation(out=eb[ebi], in_=psn[:, 0:256], func=AF.Exp, scale=scale),
                     reads=[], writes=[psk, ("eb", ebi)])
                if j != n:
                    mk = cstb[:, 128:384] if j < n else cstb[:, 384:640]
                    k.op("dve", lambda e: e.tensor_tensor(out=eb[ebi], in0=eb[ebi], in1=mk, op=ALU.mult),
                         reads=["cstb"], writes=[("eb", ebi)])
                k.op("pe", lambda e: e.matmul(
                    pa[0:65, 0:256], lhsT=vx[:, j, kvh, :], rhs=eb[ebi], start=(ji == 0), stop=(ji == nj - 1)),
                    reads=[("vx", j), "vx1", ("eb", ebi)], writes=[pak])
                if ji == 0 and pend:
                    pend.pop(0)()
                if ji == nj - 1:
                    pend.append(lambda n=n, kvh=kvh, blk=blk: attn_finish(
                        4 + (blk % 2), 256, 2, tmp,
                        lambda jj: (oall[:, n, (2 * kvh + jj) * 64:(2 * kvh + jj + 1) * 64], ("oall", n, 2 * kvh + jj)),
                        lambda jj: sinkexp[:, 2 * kvh + jj:2 * kvh + jj + 1], "s"))

            for i in range(min(LA, len(items))):
                stA(i)
            for i in range(len(items)):
                if i + LA < len(items):
                    stA(i + LA)
                stBC(i)
            while pend:
                pend.pop(0)()
            for tt in range(NT):
                okeys = [("oall", tt, h) for h in range(4)]
                out_norm_store(oall[:, tt, :], okeys, 256, vrow(vbc, "swa_out_norm"), 512, tmp, tt, tt, "so")
            A.reset(fbase, bbase)

        def ssd_phase(l, vbc):
            fbase, bbase = A.p, A.pb
            convw = A.f32(48).rearrange("p (c k) -> p c k", c=8)
            BT = A.bf16(2 * S).rearrange("p (g t) -> p g t", g=2)
            CT = A.bf16(2 * S).rearrange("p (g t) -> p g t", g=2)
            dt = A.f32(NT * 16).rearrange("p (t d) -> p t d", t=NT)
            av = A.f32(NT * 16).rearrange("p (t d) -> p t d", t=NT)
            cum = A.f32(NT * 16).rearrange("p (t d) -> p t d", t=NT)
            ncf = A.f32(NT * 8).rearrange("p (t d) -> p t d", t=NT)
            sc1 = A.f32(NT * 16).rearrange("p (t d) -> p t d", t=NT)
            sc2 = A.f32(NT * 16).rearrange("p (t d) -> p t d", t=NT)
            dw = A.f32(NT * 16).rearrange("p (t d) -> p t d", t=NT)
            tot = A.f32(NCH * 16).rearrange("p (c d) -> p c d", c=NCH)
            etot = A.f32(NCH * 16).rearrange("p (c d) -> p c d", c=NCH)
            Abc = A.f32(16)
            t16 = [A.f32(NT * 16).rearrange("p (t d) -> p t d", t=NT) for _ in range(3)]
            avd = A.f32(NT * 80).rearrange("p (t d m) -> p t d m", t=NT, d=2)
            Hb = [A.bf16(512) for _ in range(2)]
            Lb = A.bf16(512)
            Rst = [A.f32(256) for _ in range(2)]
            Rtmp = A.f32(256)
            Eb = [A.f32(256) for _ in range(4)]
            ya = [A.f32(256) for _ in range(2)]
            yg = [A.f32(256) for _ in range(2)]
            zb = [A.f32(256) for _ in range(2)]
            szb = [A.f32(256) for _ in range(2)]
            tmp = {"sq": A.f32(256), "ss": A.f32(8), "rs": A.f32(8), "yb": [A.bf16(256) for _ in range(2)]}
            xin = [A.bf16(S + 4)] * 2
            dg = A.bf16(5 * 128).rearrange("p (k c) -> p k c", k=5)
            xTf = A.bf16(2 * S).rearrange("p (c t) -> p c t", c=2)
            xs = A.bf16(NT * 256).rearrange("p (t d) -> p t d", t=NT)
            Btok = A.bf16(NT * 128).rearrange("p (t d) -> p t d", t=NT)
            Sinb = [A.bf16((NCH + 1) * 256).rearrange("p (c d) -> p c d", c=NCH + 1) for _ in range(2)]
            Mb = [A.bf16(256) for _ in range(4)]
            GT = [A.bf16(512) for _ in range(2)]
            xdt = [A.bf16(2 * 2 * 256).rearrange("p (r t d) -> p r t d", r=2, t=2) for _ in range(2)]
            xdw = [A.bf16(2 * 256).rearrange("p (t d) -> p t d", t=2) for _ in range(2)]
            k.dma(convw, convp_d[l].rearrange("p (c k) -> p c k", c=8), writes=["convw"])
            ci = [0]

            def conv_chunk(cc, dst_fn):
                xi = 0
                k.op("dve", lambda e, xi=xi: e.memset(xin[xi][:, 0:2], 0.0), writes=[("xin", xi)])
                k.op("dve", lambda e, xi=xi: e.memset(xin[xi][:, S + 2:S + 4], 0.0), writes=[("xin", xi)])
                k.dma(xin[xi][:, 2:S + 2], xbc_d[cc * 128:(cc + 1) * 128, :], reads=[("xbc_d",)], writes=[("xin", xi)])
                for kk in range(5):
                    k.op("dve", lambda e, kk=kk, cc=cc: e.tensor_scalar_mul(out=dg[:, kk, :], in0=identf, scalar1=convw[:, cc, kk:kk + 1]),
                         reads=["cst", "convw"], writes=["dg"])
                for t4 in range(S // 512):
                    pb, pk = PS(ci[0] % 2)
                    ci[0] += 1
                    for kk in range(5):
                        k.op("pe", lambda e, pb=pb, kk=kk, xi=xi, t4=t4: e.matmul(
                            pb[:], lhsT=dg[:, kk, :], rhs=xin[xi][:, t4 * 512 + kk:t4 * 512 + kk + 512],
                            start=(kk == 0), stop=(kk == 4)), reads=["dg", ("xin", xi)], writes=[pk])
                    dst, dkey = dst_fn(t4)
                    k.op("act", lambda e, pb=pb, dst=dst, cc=cc: e.activation(out=dst, in_=pb[:], func=AF.Silu, bias=convw[:, cc, 5:6]),
                         reads=["convw"], writes=[pk, dkey])
                ci[0] += 1

            for g in range(2):
                conv_chunk(4 + g, lambda t4, g=g: (BT[:, g, t4 * 512:(t4 + 1) * 512], ("BT", g, t4)))
                conv_chunk(6 + g, lambda t4, g=g: (CT[:, g, t4 * 512:(t4 + 1) * 512], ("CT", g, t4)))
            for tt in range(NT):
                k.dma(dt[:, tt, :], ptok_d[tt * 128:(tt + 1) * 128, 512:528], reads=[("ptok_d",)], writes=["dt"])
            bias_bc = vrow(vbc, "ssd_dt_bias").unsqueeze(1).to_broadcast([128, NT, 16])
            k.op("dve", lambda e: e.tensor_tensor(out=dt, in0=dt, in1=bias_bc, op=ALU.add), reads=["vbc"], writes=["dt"])
            k.op("dve", lambda e: e.tensor_scalar_mul(out=t16[0], in0=dt, scalar1=-1.0), reads=["dt"], writes=["t16a"])
            k.op("dve", lambda e: e.tensor_tensor(out=t16[0], in0=t16[0], in1=dt, op=ALU.max), reads=["dt"], writes=["t16a"])
            k.op("act", lambda e: e.activation(out=t16[1], in_=t16[0], func=AF.Exp, scale=-1.0), reads=["t16a"], writes=["t16b"])
            k.op("act", lambda e: e.activation(out=t16[0], in_=t16[1], func=AF.Ln, bias=1.0), reads=["t16b"], writes=["t16a"])
            k.op("dve", lambda e: e.tensor_scalar_max(out=t16[1], in0=dt, scalar1=0.0), reads=["dt"], writes=["t16b"])
            k.op("dve", lambda e: e.tensor_tensor(out=dt, in0=t16[0], in1=t16[1], op=ALU.add), reads=["t16a", "t16b"], writes=["dt"])
            k.op("act", lambda e: e.activation(out=Abc, in_=vrow(vbc, "ssd_a_log"), func=AF.Exp), reads=["vbc"], writes=["Abc"])
            k.op("dve", lambda e: e.scalar_tensor_tensor(out=av, in0=dt, scalar=-1.0, in1=Abc.unsqueeze(1).to_broadcast([128, NT, 16]),
                                                         op0=ALU.mult, op1=ALU.mult), reads=["dt", "Abc"], writes=["av"])
            dbg_dump("ssd_dt", dt, ["dt"])
            k.op("dve", lambda e: e.memset(avd, 0.0), writes=["avd"])
            for d in range(2):
                for c0 in (0, 32):
                    k.op("dve", lambda e: e.tensor_copy(avd[:, :, d, c0:c0 + 8], av[:, :, d * 8:d * 8 + 8]), reads=["av"], writes=["avd"])
            TRI_LE = cst[:, C_R0F:C_R0F + 128]
            TRI_LT = cst[:, C_R0B:C_R0B + 128]
            for c in range(NCH):
                t0_, t1_ = 2 * c, 2 * c + 1
                pb, pk = PS(c % 2)
                pv = pb[:, 0:48].rearrange("p (t d) -> p t d", t=3)
                for d, tri in ((0, TRI_LE), (1, TRI_LT)):
                    cs_ = slice(d * 8, d * 8 + 8)
                    k.op("pe", lambda e, pv=pv, tri=tri, cs_=cs_, t0_=t0_: e.matmul(pv[:, 0, cs_], lhsT=tri, rhs=av[:, t0_, cs_], start=True, stop=True),
                         reads=["av", "cst"], writes=[pk])
                    k.op("pe", lambda e, pv=pv, cs_=cs_, t0_=t0_: e.matmul(pv[:, 1, cs_], lhsT=ones_f, rhs=av[:, t0_, cs_], start=False, stop=False,
                                                                          skip_group_check=True), reads=["av", "cst"], writes=[pk])
                    k.op("pe", lambda e, pv=pv, tri=tri, cs_=cs_, t1_=t1_: e.matmul(pv[:, 1, cs_], lhsT=tri, rhs=av[:, t1_, cs_], start=False, stop=True,
                                                                                  skip_group_check=True), reads=["av", "cst"], writes=[pk])
                k.op("pe", lambda e, pv=pv, t0_=t0_: e.matmul(pv[:, 2, :], lhsT=ones_f, rhs=av[:, t0_, :], start=False, stop=False, skip_group_check=True),
                     reads=["av", "cst"], writes=[pk])
                k.op("pe", lambda e, pv=pv, t1_=t1_: e.matmul(pv[:, 2, :], lhsT=ones_f, rhs=av[:, t1_, :], start=False, stop=True, skip_group_check=True),
                     reads=["av", "cst"], writes=[pk])
                copy_op("dve", cum[:, t0_:t0_ + 2, :], pv[:, 0:2, :], reads=[], writes=[pk, "cum"])
                copy_op("dve", tot[:, c, :], pv[:, 2, :], reads=[], writes=[pk, "tot"])
            totb = lambda d: tot[:, :, d * 8:d * 8 + 8].unsqueeze(2).to_broadcast([128, NCH, 2, 8])
            v4 = lambda a, d: a[:, :, d * 8:d * 8 + 8].rearrange("p (c t) h -> p c t h", t=2)
            k.op("dve", lambda e: e.tensor_scalar_mul(out=ncf, in0=cum[:, :, 0:8], scalar1=-1.0), reads=["cum"], writes=["ncf"])
            k.op("act", lambda e: e.activation(out=sc1[:, :, 0:8], in_=cum[:, :, 0:8], func=AF.Exp), reads=["cum"], writes=["sc1a"])
            k.op("act", lambda e: e.activation(out=sc2[:, :, 8:16], in_=cum[:, :, 8:16], func=AF.Exp), reads=["cum"], writes=["sc2b"])
            k.op("dve", lambda e: e.tensor_tensor(out=v4(t16[2], 0), in0=totb(0), in1=v4(cum, 0), op=ALU.subtract), reads=["cum", "tot"], writes=["t16c0"])
            k.op("dve", lambda e: e.tensor_tensor(out=v4(t16[2], 1), in0=totb(1), in1=v4(cum, 1), op=ALU.subtract), reads=["cum", "tot"], writes=["t16c1"])
            k.op("act", lambda e: e.activation(out=sc2[:, :, 0:8], in_=t16[2][:, :, 0:8], func=AF.Exp), reads=["t16c0"], writes=["sc2a"])
            k.op("act", lambda e: e.activation(out=sc1[:, :, 8:16], in_=t16[2][:, :, 8:16], func=AF.Exp), reads=["t16c1"], writes=["sc1b"])
            k.op("act", lambda e: e.activation(out=etot, in_=tot, func=AF.Exp), reads=["tot"], writes=["etot"])
            k.op("dve", lambda e: e.tensor_tensor(out=dw, in0=dt, in1=sc2, op=ALU.mult), reads=["dt", "sc2a", "sc2b"], writes=["dw"])
            dbg_dump("ssd_cum", cum, ["cum"])
            Dbc = vrow(vbc, "ssd_d")
            v3 = lambda a: a.rearrange("p (h d) -> p h d", h=4)
            for g in range(2):
                for j in range(2):
                    conv_chunk(2 * g + j, lambda t4, j=j: (xTf[:, j, t4 * 512:(t4 + 1) * 512], ("xTf", t4)))
                for tt in range(NT):
                    pb, pk = PS(6)
                    pv = pb[:, 0:256].rearrange("p (c t) -> p c t", c=2)
                    for j in range(2):
                        k.op("pe", lambda e, pv=pv, j=j, tt=tt: e.transpose(pv[:, j, :], xTf[:, j, tt * 128:(tt + 1) * 128], identb),
                             reads=[("xTf", tt // 4), "cstb"], writes=[pk])
                    k.op("pe", lambda e, pb=pb, tt=tt: e.transpose(pb[:, 256:384], BT[:, g, tt * 128:(tt + 1) * 128], identb),
                         reads=[("BT", g, tt // 4), "cstb"], writes=[pk])
                    copy_op("act", xs[:, tt, :], pb[:, 0:256], reads=[], writes=[pk, ("xs", tt)])
                    copy_op("dve", Btok[:, tt, :], pb[:, 256:384], reads=[], writes=[pk, ("Btok", tt)])
                if g == 0:
                    dbg_dump("ssd_xs", xs, [("xs", tt) for tt in range(NT)], BF16)
                xs4 = xs.rearrange("p t (h d) -> p t h d", h=4)
                wi = 0
                for d in range(2):
                    k.op("dve", lambda e, d=d: e.memset(Rst[d], 0.0), writes=[("Rst", d)])
                    order = list(range(NCH)) if d == 0 else list(range(NCH - 1, -1, -1))
                    first_slot = 0 if d == 0 else NCH
                    k.op("dve", lambda e, d=d, first_slot=first_slot: e.memset(Sinb[d][:, first_slot, :], 0.0), writes=[("Sinb", d, first_slot)])
                    for c in order:
                        wr = wi % 2
                        wi += 1
                        dwb = dw[:, 2 * c:2 * c + 2, d * 8 + 4 * g:d * 8 + 4 * g + 4].unsqueeze(3).to_broadcast([128, 2, 4, 64])
                        k.op("dve", lambda e, wr=wr, dwb=dwb, c=c: e.tensor_tensor(out=xdw[wr].rearrange("p t (h d) -> p t h d", h=4),
                                                                                in0=xs4[:, 2 * c:2 * c + 2, :, :], in1=dwb, op=ALU.mult),
                             reads=[("xs", 2 * c), ("xs", 2 * c + 1), "dw"], writes=[("xdw", wr)])
                        pb, pk = PS(c % 2)
                        for ti in range(2):
                            tt = 2 * c + ti
                            k.op("pe", lambda e, pb=pb, tt=tt, ti=ti, wr=wr: e.matmul(
                                pb[:, 0:256], lhsT=Btok[:, tt, :], rhs=xdw[wr][:, ti, :], start=(ti == 0), stop=(ti == 1)),
                                reads=[("Btok", tt), ("xdw", wr)], writes=[pk])
                        etb = etot[:, c, d * 8 + 4 * g:d * 8 + 4 * g + 4].unsqueeze(2).to_broadcast([128, 4, 64])
                        k.op("dve", lambda e, d=d, etb=etb: e.tensor_tensor(out=v3(Rtmp), in0=v3(Rst[d]), in1=etb, op=ALU.mult),
                             reads=[("Rst", d), "etot"], writes=["Rtmp"])
                        k.op("dve", lambda e, pb=pb, d=d: e.tensor_tensor(out=Rst[d], in0=pb[:, 0:256], in1=Rtmp, op=ALU.add),
                             reads=["Rtmp"], writes=[pk, ("Rst", d)])
                        slot = c + 1 if d == 0 else c
                        k.op("act", lambda e, d=d, slot=slot: e.copy(Sinb[d][:, slot, :], Rst[d]), reads=[("Rst", d)], writes=[("Sinb", d, slot)])
                def prologue(c):
                    t0_, t1_ = 2 * c, 2 * c + 1
                    cr = c % 2
                    pb2, pk2 = PS(7)
                    for d, (r0, r1) in ((0, (C_R0F, C_R1F)), (1, (C_R0B, C_R1B))):
                        o_ = pb2[0:40, d * 256:(d + 1) * 256]
                        k.op("pe", lambda e: e.matmul(o_, lhsT=avd[:, t0_, d, :], rhs=cst[:, r0:r0 + 256],
                                                      start=(d == 0), stop=False, skip_group_check=True),
                             reads=["avd", "cst"], writes=[pk2])
                        k.op("pe", lambda e: e.matmul(o_, lhsT=avd[:, t1_, d, :], rhs=cst[:, r1:r1 + 256],
                                                      start=False, stop=True, skip_group_check=True),
                             reads=["avd", "cst"], writes=[pk2])
                    k.op("act", lambda e: e.copy(Hb[cr][0:40, :], pb2[0:40, 0:512]), reads=[], writes=[pk2, ("cumT", cr)])
                    k.op("dve", lambda e: e.tensor_tensor(out=Lb[0:40, :], in0=pb2[0:40, 0:512], in1=Hb[cr][0:40, :], op=ALU.subtract),
                         reads=[("cumT", cr)], writes=[pk2, "Lb"])
                    k.op("dve", lambda e: e.tensor_copy(Hb[cr][32:40, :], Lb[32:40, :]), reads=["Lb"], writes=[("cumT", cr)])
                    for d in range(2):
                        dtb = dt[:, 2 * c:2 * c + 2, d * 8 + 4 * g:d * 8 + 4 * g + 4].unsqueeze(3).to_broadcast([128, 2, 4, 64])
                        k.op("dve", lambda e: e.tensor_tensor(out=xdt[d][:, cr].rearrange("p t (h d) -> p t h d", h=4),
                                                              in0=xs4[:, 2 * c:2 * c + 2, :, :], in1=dtb, op=ALU.mult),
                             reads=[("xs", 2 * c), ("xs", 2 * c + 1), "dt"], writes=[("xdt", d, cr)])
                    pb, pk = PS(7)
                    for si in range(2):
                        k.op("pe", lambda e: e.matmul(pb[:, si * 256:(si + 1) * 256], lhsT=BT[:, g, (2 * c + si) * 128:(2 * c + si + 1) * 128],
                                                      rhs=CT[:, g, c * 256:(c + 1) * 256], start=(si == 0), stop=True, skip_group_check=True),
                             reads=[("BT", g, (2 * c + si) // 4), ("CT", g, c // 2)], writes=[pk])
                    copy_op("act", GT[cr], pb[:, 0:512], reads=[], writes=[pk, ("GT", cr)])

                def item_front(c, it):
                    h, d = it // 2, it % 2
                    hh = 4 * g + h
                    cr = c % 2
                    sel = cstb[0:40, 1152 + hh * 128:1152 + (hh + 1) * 128]
                    for si in range(2):
                        pb, pk = PS(2 * (it % 2) + si)
                        if d == 0:
                            lo, hi = (0, 256) if si == 0 else (128, 256)
                            mask = cstb[:, 640:896] if si == 0 else cstb[:, 640:768]
                        else:
                            lo, hi = (0, 128) if si == 0 else (0, 256)
                            mask = cstb[:, 896 + 128:896 + 256] if si == 0 else cstb[:, 896:896 + 256]
                        w = hi - lo
                        k.op("pe", lambda e: e.matmul(pb[:, 0:w], lhsT=sel, rhs=Hb[cr][0:40, d * 256 + lo:d * 256 + hi], start=True, stop=False),
                             reads=[("cumT", cr), "cstb"], writes=[pk])
                        k.op("pe", lambda e: e.matmul(pb[:, 0:w], lhsT=identb, rhs=mask, start=False, stop=True),
                             reads=["cstb"], writes=[pk])

                def item_back(c, it, started, ybank, ykey):
                    h, d = it // 2, it % 2
                    hh = 4 * g + h
                    cr = c % 2
                    mms = []
                    for si in range(2):
                        pb, pk = PS(2 * (it % 2) + si)
                        if d == 0:
                            lo, hi = (0, 256) if si == 0 else (128, 256)
                        else:
                            lo, hi = (0, 128) if si == 0 else (0, 256)
                        w = hi - lo
                        ebi = 2 * d + si
                        tt_s = 2 * c + si
                        if d == 0:
                            k.op("act", lambda e: e.activation(out=Eb[ebi][:, 0:w], in_=pb[:, 0:w], func=AF.Exp, bias=ncf[:, tt_s, hh:hh + 1], scale=1.0),
                                 reads=["ncf"], writes=[pk, ("Eb", ebi)])
                        else:
                            k.op("act", lambda e: e.activation(out=Eb[ebi][:, 0:w], in_=pb[:, 0:w], func=AF.Exp, bias=cum[:, tt_s, 8 + hh:9 + hh], scale=-1.0),
                                 reads=["cum"], writes=[pk, ("Eb", ebi)])
                        k.op("dve", lambda e: e.tensor_tensor(out=Mb[ebi][:, 0:w], in0=Eb[ebi][:, 0:w], in1=GT[cr][:, si * 256 + lo:si * 256 + hi], op=ALU.mult),
                             reads=[("Eb", ebi), ("GT", cr)], writes=[("Mb", ebi)])
                        for li in range(2):
                            if lo <= li * 128 < hi:
                                mms.append((ebi, li * 128 - lo, li, si))
                    for (ebi, off, li, si) in mms:
                        st_ = not started[li]
                        started[li] = True
                        k.op("pe", lambda e: e.matmul(
                            ybank[li][:, h * 64:(h + 1) * 64], lhsT=Mb[ebi][:, off:off + 128], rhs=xdt[d][:, cr, si, h * 64:(h + 1) * 64],
                            start=st_, stop=False, skip_group_check=True),
                            reads=[("Mb", ebi), ("xdt", d, cr)], writes=[ykey[li]])

                def epilogue(c, ybank, ykey):
                    for li in range(2):
                        tt = 2 * c + li
                        pf7, pfk = PS(6 if False else 7)
                        pf = pf7[:, 0:256]
                        pbw = pf7[:, 256:512]
                        k.op("pe", lambda e: e.matmul(pf, lhsT=CT[:, g, tt * 128:(tt + 1) * 128], rhs=Sinb[0][:, c, :],
                                                      start=True, stop=True), reads=[("CT", g, tt // 4), ("Sinb", 0, c)], writes=[pfk])
                        k.op("pe", lambda e: e.matmul(pbw, lhsT=CT[:, g, tt * 128:(tt + 1) * 128], rhs=Sinb[1][:, c + 1, :],
                                                      start=False, stop=True, skip_group_check=True),
                             reads=[("CT", g, tt // 4), ("Sinb", 1, c + 1)], writes=[pfk])
                        ecfb = sc1[:, tt, 4 * g:4 * g + 4].unsqueeze(2).to_broadcast([128, 4, 64])
                        erbb = sc1[:, tt, 8 + 4 * g:12 + 4 * g].unsqueeze(2).to_broadcast([128, 4, 64])
                        dbb = Dbc[:, 4 * g:4 * g + 4].unsqueeze(2).to_broadcast([128, 4, 64])
                        yi = li
                        k.op("dve", lambda e: e.tensor_tensor(out=v3(ya[yi]), in0=v3(pf), in1=ecfb, op=ALU.mult),
                             reads=["sc1a"], writes=[pfk, ("ya", yi)])
                        k.op("dve", lambda e: e.tensor_tensor(out=v3(Rtmp), in0=v3(pbw), in1=erbb, op=ALU.mult),
                             reads=["sc1b"], writes=[pfk, "Rtmp"])
                        k.op("dve", lambda e: e.tensor_tensor(out=ya[yi], in0=ya[yi], in1=Rtmp, op=ALU.add), reads=["Rtmp"], writes=[("ya", yi)])
                        k.op("dve", lambda e: e.tensor_tensor(out=v3(Rtmp), in0=xs4[:, tt, :, :], in1=dbb, op=ALU.mult),
                             reads=[("xs", tt), "vbc"], writes=["Rtmp"])
                        k.op("dve", lambda e: e.tensor_tensor(out=ya[yi], in0=ya[yi], in1=Rtmp, op=ALU.add), reads=["Rtmp"], writes=[("ya", yi)])
                        k.op("dve", lambda e: e.tensor_tensor(out=ya[yi], in0=ybank[li][:, 0:256], in1=ya[yi], op=ALU.add),
                             reads=[], writes=[ykey[li], ("ya", yi)])
                        zi = li
                        k.dma(zb[zi], ptok_d[tt * 128:(tt + 1) * 128, g * 256:(g + 1) * 256], reads=[("ptok_d",)], writes=[("zb", zi)])
                        k.op("act", lambda e: e.activation(out=szb[zi], in_=zb[zi], func=AF.Exp, scale=-1.0), reads=[("zb", zi)], writes=[("szb", zi)])
                        k.op("dve", lambda e: e.tensor_scalar_add(out=szb[zi], in0=szb[zi], scalar1=1.0), reads=[], writes=[("szb", zi)])
                        k.op("dve", lambda e: e.reciprocal(out=szb[zi], in_=szb[zi]), reads=[], writes=[("szb", zi)])
                        k.op("dve", lambda e: e.tensor_tensor(out=yg[yi], in0=ya[yi], in1=zb[zi], op=ALU.mult),
                             reads=[("zb", zi), ("ya", yi)], writes=[("yg", yi)])
                        k.op("dve", lambda e: e.tensor_tensor(out=yg[yi], in0=yg[yi], in1=szb[zi], op=ALU.mult),
                             reads=[("szb", zi)], writes=[("yg", yi)])
                        k.op("dve", lambda e: e.tensor_tensor(out=tmp["sq"][:, 0:256], in0=yg[yi], in1=yg[yi], op=ALU.mult),
                             reads=[("yg", yi)], writes=["T_sq"])
                        k.op("dve", lambda e: e.tensor_reduce(out=tmp["ss"][:, 0:1], in_=tmp["sq"][:, 0:256], axis=AX.X, op=ALU.add),
                             reads=["T_sq"], writes=["T_ss"])
                        k.op("act", lambda e: e.activation(out=tmp["rs"][:, 0:1], in_=tmp["ss"][:, 0:1], func=AF.Ln, scale=1.0 / 256, bias=EPS),
                             reads=["T_ss"], writes=["T_rs"])
                        k.op("act", lambda e: e.activation(out=tmp["ss"][:, 1:2], in_=tmp["rs"][:, 0:1], func=AF.Exp, scale=-0.5),
                             reads=["T_rs"], writes=["T_ss"])
                        ybi = tt % 2
                        ybt = tmp["yb"][ybi][:, 0:256]
                        k.op("dve", lambda e: e.scalar_tensor_tensor(out=ybt, in0=yg[yi], scalar=tmp["ss"][:, 1:2],
                                                                     in1=vrow(vbc, "ssd_norm", g * 256, 256), op0=ALU.mult, op1=ALU.mult),
                             reads=[("yg", yi), "T_ss", "vbc"], writes=[("doyb", ybi)])
                        k.dma(y_d[tt * 128:(tt + 1) * 128, g * 256:(g + 1) * 256], ybt, reads=[("doyb", ybi)], writes=[("y_d",)])

                prologue(0)
                for c in range(NCH):
                    yb0, yk0 = PS(4)
                    yb1, yk1 = PS(5)
                    ybank = (yb0, yb1)
                    ykey = (yk0, yk1)
                    started = [False, False]
                    item_front(c, 0)
                    for it in range(8):
                        if it + 1 < 8:
                            item_front(c, it + 1)
                        item_back(c, it, started, ybank, ykey)
                        if it == 3 and c + 1 < NCH:
                            prologue(c + 1)
                    epilogue(c, ybank, ykey)
            A.reset(fbase, bbase)

        def mixer_phase(l):
            A.reset()
            vbc = A.f32(NV)
            k.dma(vbc, vec_d[l:l + 1, :].partition_broadcast(128), writes=["vbc"])
            ssd_phase(l, vbc)
            k.barrier(barsc[:, 1:2])
            swa_phase(l, vbc)
            k.barrier(barsc[:, 2:3])
            mla_phase(l, vbc)
            if l == 0:
                k.barrier(barsc[:, 3:4])
                dbg_dump("ptok", ptok_d, [("ptok_d",)], big=True)
                dbg_dump("xbc", xbc_d, [("xbc_d",)], BF16, big=True)
                dbg_dump("y", y_d, [("y_d",)], BF16, big=True)

        finals = []
        for p in range(L + 1):
            l_prev = p - 1 if p > 0 else None
            l_next = p if p < L else None
            if stop_after is not None and p > stop_after[0]:
                break
            finals = block_phase(p, l_prev, l_next)
            k.barrier(barsc[:, 0:1])
            if l_next is not None:
                if stop_after is not None and stop_after == (p, "block"):
                    break
                mixer_phase(l_next)
                k.barrier(barsc[:, 0:1])
                if stop_after is not None and stop_after == (p, "mix"):
                    break
        k.emit(list(finals) + dbg_outs)
        print("ops recorded:", k.nops, "arena hi (KiB):", A.hi * 4 / 1024, A.hib * 2 / 1024)
    return nc


def prep_common(inp, S):
    L = inp["ffn1_norm"].shape[0]
    vec = np.zeros((L, NV), np.float32)
    for n, (o, s) in VOFF.items():
        vec[:, o:o + s] = np.asarray(inp[n], np.float32).reshape(L, s)
    cw = np.asarray(inp["ssd_conv_w"], np.float32)
    cb = np.asarray(inp["ssd_conv_b"], np.float32)
    convp = np.zeros((L, 128, 8, 6), np.float32)
    for c in range(8):
        convp[:, :, c, 0:5] = cw[:, :, c * 128:(c + 1) * 128].transpose(0, 2, 1)
        convp[:, :, c, 5] = cb[:, c * 128:(c + 1) * 128]
    com = {"vecs": vec, "convp": np.ascontiguousarray(convp.reshape(L, 128, 48)), "cst": make_consts(),
           "cs_swa": rope_tables(S, 64), "cs_mla": rope_tables(S, 32)}
    for n in ("ffn1_gate", "ffn1_up", "ffn1_down", "ffn2_gate", "ffn2_up", "ffn2_down", "w_in", "w_out",
              "mla_w_uq", "mla_w_ukv"):
        com[n] = np.ascontiguousarray(np.asarray(inp[n], np.float32))
    return com, L


_NC_CACHE = {}


def kernel(**inputs):
    x = np.asarray(inputs["x"], np.float32)
    B, S, _ = x.shape
    com, L = prep_common(inputs, S)
    key = (S, L)
    if key not in _NC_CACHE:
        _NC_CACHE[key] = build(S, L)
    nc = _NC_CACHE[key]
    ncores = 8
    place = [0, 1, 4, 5][:B] if B <= 4 else list(range(B))
    zero_map = None
    in_maps = []
    for c in range(ncores):
        if c in place:
            m = dict(com)
            m["x"] = np.ascontiguousarray(x[place.index(c)])
        else:
            if zero_map is None:
                zero_map = {kk: np.zeros_like(v) for kk, v in com.items()}
                zero_map["x"] = np.zeros((S, D), np.float32)
            m = zero_map
        in_maps.append(m)
    res = run_bass_kernel_spmd(nc, in_maps, core_ids=list(range(ncores)))
    out = np.stack([np.asarray(res.results[place[b]]["out"], np.float32) for b in range(B)], axis=0)
    return out
```

```python
import contextlib
import numpy as np
import concourse.bass as bass
import concourse.mybir as mybir
from concourse.bass_utils import run_bass_kernel_spmd

F32 = mybir.dt.float32
BF16 = mybir.dt.bfloat16
AF = mybir.ActivationFunctionType
ALU = mybir.AluOpType
AX = mybir.AxisListType

D = 1024
DFF = 2816
NPROJ = 2480
NTM = 1456
EPS = 1e-6
BIG = 1.0e4
GF = 4
DEBUG = False
ALT_DVE_ONLY = False
N_DMA_SEMS = 40

VOFF = {}
_o = 0
for _n, _s in (("ffn1_norm", 1024), ("mix_norm", 1024), ("ffn2_norm", 1024), ("ssd_norm", 512),
               ("swa_q_norm", 64), ("swa_k_norm", 64), ("swa_out_norm", 256), ("mla_q_lat_norm", 256),
               ("mla_kv_norm", 128), ("mla_q_norm", 96), ("mla_k_norm", 96), ("mla_out_norm", 256),
               ("ssd_dt_bias", 16), ("ssd_a_log", 16), ("ssd_d", 8), ("swa_sink", 4)):
    VOFF[_n] = (_o, _s)
    _o += _s
NV = _o

C_ID = 0
C_R0F = 128
C_R1F = 384
C_R0B = 640
C_R1B = 896
C_NEGF = 1152
C_POSB = 1408
C_GE = 1664
C_SEL = 1792
NCST = C_SEL + 1024


def make_consts():
    c = np.zeros((128, NCST), np.float32)
    i = np.arange(128)
    le = (i[:, None] <= i[None, :]).astype(np.float32)
    lt = (i[:, None] < i[None, :]).astype(np.float32)
    ge = (i[:, None] >= i[None, :]).astype(np.float32)
    gt = (i[:, None] > i[None, :]).astype(np.float32)
    c[:, C_ID:C_ID + 128] = np.eye(128)
    c[:, C_R0F:C_R0F + 128] = le
    c[:, C_R0F + 128:C_R0F + 256] = 1.0
    c[:, C_R1F + 128:C_R1F + 256] = le
    c[:, C_R0B:C_R0B + 128] = lt
    c[:, C_R0B + 128:C_R0B + 256] = 1.0
    c[:, C_R1B + 128:C_R1B + 256] = lt
    c[:, C_NEGF:C_NEGF + 128] = -BIG * gt
    c[:, C_POSB + 128:C_POSB + 256] = BIG * lt
    c[:, C_GE:C_GE + 128] = ge
    for h in range(8):
        c[h, C_SEL + h * 128:C_SEL + (h + 1) * 128] = 1.0
        c[32 + h, C_SEL + h * 128:C_SEL + (h + 1) * 128] = 1.0
    return c


def rope_tables(n, dim):
    inv = 1.0 / np.power(np.float32(10000.0), np.arange(0, dim, 2, dtype=np.float32) / np.float32(dim))
    ang = np.arange(n, dtype=np.float32)[:, None] * inv[None, :].astype(np.float32)
    return np.concatenate([np.cos(ang), np.sin(ang)], axis=1).astype(np.float32)


def _freeze(fn):
    import types
    if fn.__closure__ is None:
        return fn
    cells = []
    for c in fn.__closure__:
        try:
            cells.append(types.CellType(c.cell_contents))
        except ValueError:
            cells.append(c)
    return types.FunctionType(fn.__code__, fn.__globals__, fn.__name__, fn.__defaults__, tuple(cells))


class Op:
    __slots__ = ("eng", "fn", "waits", "signal", "sem", "val", "is_dma")

    def __init__(self, eng, fn, is_dma=False):
        self.eng = eng
        self.fn = fn
        self.waits = []
        self.signal = False
        self.sem = None
        self.val = None
        self.is_dma = is_dma


class KB:
    ENGS = ("pe", "act", "dve", "pool", "sp")

    def __init__(self, nc):
        self.nc = nc
        self.prog = {e: [] for e in self.ENGS}
        self.last_w = {}
        self.readers = {}
        self.dma_rr = 0
        self.dma_last = [None] * N_DMA_SEMS
        self.dma_uses = [0] * N_DMA_SEMS
        self.bar = None
        self.nops = 0
        self.carrier = None

    def _deps(self, op, reads, writes):
        deps = []
        if self.bar is not None:
            deps.append(self.bar)
        for k in reads:
            w = self.last_w.get(k)
            if w is not None:
                deps.append(w)
        for k in writes:
            w = self.last_w.get(k)
            if w is not None:
                deps.append(w)
            deps.extend(self.readers.get(k, ()))
        for k in reads:
            self.readers.setdefault(k, []).append(op)
        for k in writes:
            self.last_w[k] = op
            self.readers[k] = []
        seen = set()
        for d in deps:
            if d is op or id(d) in seen:
                continue
            seen.add(id(d))
            if op.eng == "pe" and d.eng == "pe" and not d.is_dma and not op.is_dma:
                continue
            op.waits.append(d)

    def op(self, eng, fn, reads=(), writes=()):
        o = Op(eng, _freeze(fn))
        self._deps(o, reads, writes)
        self.prog[eng].append(o)
        self.nops += 1
        return o

    def dma(self, out, in_, reads=(), writes=(), q="sp", **kw):
        o = Op(q, None, is_dma=True)
        o.fn = lambda e, out=out, in_=in_, kw=kw: e.dma_start(out=out, in_=in_, **kw)
        self._deps(o, reads, writes)
        i = self.dma_rr
        self.dma_rr = (self.dma_rr + 1) % N_DMA_SEMS
        prev = self.dma_last[i]
        if prev is not None:
            o.waits.append(prev)
        self.dma_last[i] = o
        self.dma_uses[i] += 1
        o.sem = i
        o.val = 16 * self.dma_uses[i]
        o.signal = True
        self.prog[q].append(o)
        self.nops += 1
        return o

    def barrier(self, scratch_ap):
        o = Op("dve", lambda e: e.memset(scratch_ap, 0.0))
        if self.bar is not None:
            o.waits.append(self.bar)
        for e in self.ENGS:
            for p in reversed(self.prog[e]):
                if not p.is_dma:
                    o.waits.append(p)
                    break
        for d in self.dma_last:
            if d is not None:
                o.waits.append(d)
        self.prog["dve"].append(o)
        self.bar = o
        self.last_w = {}
        self.readers = {}
        return o

    def emit(self, final_wait_ops=()):
        nc = self.nc
        for e in self.ENGS:
            for o in self.prog[e]:
                for w in o.waits:
                    w.signal = True
        for o in final_wait_ops:
            o.signal = True
        for e in self.ENGS:
            c = 0
            for o in self.prog[e]:
                if o.is_dma:
                    continue
                if o.signal:
                    c += 1
                    o.sem = e
                    o.val = c
        with contextlib.ExitStack() as st:
            esem = {e: st.enter_context(nc.semaphore("s_" + e)) for e in self.ENGS}
            dsem = [st.enter_context(nc.semaphore("d_%d" % i)) for i in range(N_DMA_SEMS)]
            block = st.enter_context(nc.Block())

            def sem_of(o):
                return dsem[o.sem] if o.is_dma else esem[o.sem]

            def run(e, h):
                waited = {}
                for o in self.prog[e]:
                    need = {}
                    for w in o.waits:
                        key = ("d", w.sem) if w.is_dma else ("e", w.sem)
                        if waited.get(key, 0) >= w.val:
                            continue
                        waited[key] = w.val
                        need[key] = w
                    need = list(need.values())
                    if e == "pe" and not o.is_dma:
                        for w in need[:-1]:
                            h.ldweights(self.carrier)._wait_ge(sem_of(w), w.val)
                        ins = o.fn(h)
                        if need:
                            ins._wait_ge(sem_of(need[-1]), need[-1].val)
                    else:
                        for w in need:
                            h.wait_ge(sem_of(w), w.val)
                        ins = o.fn(h)
                    if o.is_dma:
                        ins.then_inc(dsem[o.sem], 16)
                    elif o.signal:
                        ins.then_inc(esem[e], 1)
                if e == "sp":
                    for o in final_wait_ops:
                        h.wait_ge(sem_of(o), o.val)

            @block.tensor
            def _(t):
                run("pe", t)

            @block.scalar
            def _(s):
                run("act", s)

            @block.vector
            def _(v):
                run("dve", v)

            @block.gpsimd
            def _(g):
                run("pool", g)

            @block.sync
            def _(s):
                run("sp", s)


class Arena:
    def __init__(self, apf, nf, apb, nb):
        self.apf, self.nf, self.apb, self.nb = apf, nf, apb, nb
        self.p = 0
        self.pb = 0
        self.hi = 0
        self.hib = 0

    def reset(self, to=0, tob=0):
        self.p = to
        self.pb = tob

    def f32(self, cols):
        a = self.apf[:, self.p:self.p + cols]
        self.p += (cols + 7) // 8 * 8
        self.hi = max(self.hi, self.p)
        assert self.p <= self.nf, ("f32 arena overflow", self.p, self.nf)
        return a

    def bf16(self, cols):
        a = self.apb[:, self.pb:self.pb + cols]
        self.pb += (cols + 15) // 16 * 16
        self.hib = max(self.hib, self.pb)
        assert self.pb <= self.nb, ("bf16 arena overflow", self.pb, self.nb)
        return a


def build(S, L, stop_after=None):
    assert S % 512 == 0
    NT = S // 128
    TB = min(1024, S)
    NBLK = S // TB
    NTB = TB // 128
    NCH = S // 256

    nc = bass.Bass("TRN2", target_bir_lowering=False)
    dr = lambda name, shape, dt=F32, kind="ExternalInput": nc.dram_tensor(name, list(shape), dt, kind=kind).ap()
    x_d = dr("x", [S, D])
    wg_d = [dr("ffn1_gate", [L, D, DFF]), dr("ffn2_gate", [L, D, DFF])]
    wu_d = [dr("ffn1_up", [L, D, DFF]), dr("ffn2_up", [L, D, DFF])]
    wd_d = [dr("ffn1_down", [L, DFF, D]), dr("ffn2_down", [L, DFF, D])]
    win_d = dr("w_in", [L, D, NPROJ])
    wout_d = dr("w_out", [L, D, D])
    wuq_d = dr("mla_w_uq", [L, 256, 384])
    wukv_d = dr("mla_w_ukv", [L, 128, 512])
    vec_d = dr("vecs", [L, NV])
    convp_d = dr("convp", [L, 128, 48])
    cst_d = dr("cst", [128, NCST])
    cs_swa_d = dr("cs_swa", [S, 64])
    cs_mla_d = dr("cs_mla", [S, 32])
    out_d = dr("out", [S, D], kind="ExternalOutput")
    xbc_d = dr("xbc_scr", [1024, S], BF16, kind="Internal")
    ptok_d = dr("ptok_scr", [S, NTM], F32, kind="Internal")
    y_d = dr("y_scr", [S, D], BF16, kind="Internal")

    ARENA_F = 17 * 1024
    ARENA_B = 58 * 1024
    st = contextlib.ExitStack()
    with st:
        arena_f = st.enter_context(nc.sbuf_tensor("arena_f", [128, ARENA_F], F32))
        arena_b = st.enter_context(nc.sbuf_tensor("arena_b", [128, ARENA_B], BF16))
        cst = st.enter_context(nc.sbuf_tensor("cst_sb", [128, NCST], F32))
        cstb = st.enter_context(nc.sbuf_tensor("cstb_sb", [128, 128 * 5 + 512 + 1024], BF16))
        barsc = st.enter_context(nc.sbuf_tensor("barsc", [128, 8], F32))
        banks = [st.enter_context(nc.psum_tensor("bank%d" % i, [128, 1024] if i == 6 else [128, 512], BF16 if i == 6 else F32))
                 for i in range(8)]
        k = KB(nc)
        A = Arena(arena_f[:], ARENA_F, arena_b[:], ARENA_B)

        k.carrier = cstb[:, 0:1]
        identf = cst[:, C_ID:C_ID + 128]
        identb = cstb[:, 0:128]
        ones_f = cst[:, C_R0F + 128:C_R0F + 256]

        k.dma(cst[:], cst_d, writes=["cst"])
        k.dma(cstb[:, 0:128], cst_d[:, C_ID:C_ID + 128], writes=["cstb"], q="pool")
        for j in range(2):
            k.dma(cstb[:, 128 + j * 128:256 + j * 128], cst_d[:, C_GE:C_GE + 128], writes=["cstb"], q="pool")
            k.dma(cstb[:, 384 + j * 128:512 + j * 128], cst_d[:, C_R0F:C_R0F + 128], writes=["cstb"], q="pool")

        k.dma(cstb[:, 640:896], cst_d[:, C_NEGF:C_NEGF + 256], writes=["cstb"], q="pool")
        k.dma(cstb[:, 896:1152], cst_d[:, C_POSB:C_POSB + 256], writes=["cstb"], q="pool")
        k.dma(cstb[:, 1152:2176], cst_d[:, C_SEL:C_SEL + 1024], writes=["cstb"], q="pool")

        def PS(i):
            return banks[i], "bank%d" % i

        rr = {"n": 0}
        dbg_outs = []
        dbg_names = set()

        def dbg_dump(name, ap, reads, dt=F32, big=False):
            if not DEBUG or ("dbg_" + name) in dbg_names:
                return
            dbg_names.add("dbg_" + name)
            t = nc.dram_tensor("dbg_" + name, list(ap.shape), dt, kind="ExternalOutput").ap()
            if big:
                for r0 in range(0, ap.shape[0], 256):
                    dbg_outs.append(k.dma(t[r0:r0 + 256], ap[r0:r0 + 256], reads=reads))
            else:
                dbg_outs.append(k.dma(t, ap, reads=reads))

        def alt():
            rr["n"] += 1
            return "dve" if (rr["n"] % 2 or ALT_DVE_ONLY) else "act"

        def copy_op(eng, out, in_, reads, writes):
            if eng == "act":
                return k.op("act", lambda e: e.copy(out, in_), reads, writes)
            return k.op(eng, lambda e: e.tensor_copy(out, in_), reads, writes)

        def vrow(vbc, name, lo=0, n=None):
            o, s = VOFF[name]
            n = s - lo if n is None else n
            return vbc[:, o + lo:o + lo + n]

        def block_phase(pidx, l_prev, l_next):
            A.reset()
            vbcA = A.f32(3072) if l_prev is not None else None
            vbcB = A.f32(3072) if l_next is not None else None
            xb = A.f32(NTB * D).rearrange("p (t d) -> p t d", t=NTB)
            hT = A.bf16(8 * TB).rearrange("p (c t) -> p c t", c=8)
            hb = [A.bf16(D) for _ in range(2)]
            junk = A.f32(D)
            ss = A.f32(NTB)
            sq = A.f32(NTB)
            rstd = A.f32(NTB)
            wgu = [A.bf16(2 * 8 * GF * 128).rearrange("p (w c f) -> p w c f", w=2, c=8) for _ in range(2)]
            wdn = [A.bf16(GF * D).rearrange("p (c d) -> p c d", c=GF) for _ in range(2)]
            sg = [A.bf16(512) for _ in range(2)]
            aT = [A.bf16(GF * 512).rearrange("p (c t) -> p c t", c=GF) for _ in range(2)]
            wpj = [A.bf16(8 * 512).rearrange("p (c f) -> p c f", c=8) for _ in range(2)]
            evf = [A.bf16(512) for _ in range(2)]
            evt = [A.f32(512) for _ in range(2)]
            cnt = {"w": 0, "p": 0, "a": 0, "d": 0, "t": 0, "e": 0, "g": 0}
            if l_prev is not None:
                k.dma(vbcA, vec_d[l_prev:l_prev + 1, 0:3072].partition_broadcast(128), writes=["vbcA"])
            if l_next is not None:
                k.dma(vbcB, vec_d[l_next:l_next + 1, 0:3072].partition_broadcast(128), writes=["vbcB"])

            def norm_to_hT(vbc, vkey, nname, tag):
                k.op("dve", lambda e: e.memset(ss, 0.0), writes=["ss"])
                for tt in range(NTB):
                    k.op("act", lambda e, tt=tt: e.activation(out=junk, in_=xb[:, tt, :], func=AF.Square,
                                                              accum_out=ss[:, tt:tt + 1]),
                         reads=[("xb", tt)], writes=["junk", "ss"])
                k.op("act", lambda e: e.activation(out=sq, in_=ss, func=AF.Sqrt, scale=1.0 / D, bias=EPS),
                     reads=["ss"], writes=["sq"])
                k.op("dve", lambda e: e.reciprocal(out=rstd, in_=sq), reads=["sq"], writes=["rstd"])
                wv = vrow(vbc, nname)
                if tag == "f" and not cnt.get("dbg1"):
                    cnt["dbg1"] = 1
                    dbg_dump("ss", ss, ["ss"]); dbg_dump("rstd", rstd, ["rstd"]); dbg_dump("wv", wv, [vkey])
                    dbg_dump("xb0", xb[:, 0, :], [("xb", 0)])
                for tt in range(NTB):
                    hbi = tt % 2
                    k.op("dve", lambda e, tt=tt, hbi=hbi: e.scalar_tensor_tensor(
                        out=hb[hbi], in0=xb[:, tt, :], scalar=rstd[:, tt:tt + 1], in1=wv,
                        op0=ALU.mult, op1=ALU.mult), reads=[("xb", tt), "rstd", vkey], writes=[("hb", hbi)])
                    transpose_rows(hb[hbi], ("hb", hbi), 8, hT, tt, "hT")

            def transpose_rows(src, skey, nchunks, dstT, tt, dkey):
                for h0 in range(0, nchunks, 4):
                    nn = min(4, nchunks - h0)
                    pb, pk = PS(6)
                    pv = pb[:, (cnt["t"] % 2) * 512:(cnt["t"] % 2) * 512 + 512].rearrange("p (c t) -> p c t", c=4)
                    cnt["t"] += 1
                    for j in range(nn):
                        k.op("pe", lambda e, j=j, h0=h0: e.transpose(pv[:, j, :], src[:, (h0 + j) * 128:(h0 + j + 1) * 128], identb),
                             reads=[skey, "cstb"], writes=[pk])
                    copy_op(alt(), dstT[:, h0:h0 + nn, tt * 128:(tt + 1) * 128], pv[:, 0:nn, :],
                            reads=[], writes=[pk, (dkey, tt, h0)])

            def dump_hT(nm):
                dbg_dump(nm, hT[:, :, 0:128], [("hT", 0, 0), ("hT", 0, 4)], BF16)

            def hT_keys(t4, key="hT"):
                return [(key, tt, h0) for tt in range(t4 * 4, t4 * 4 + 4) for h0 in (0, 4)]

            def ffn(l, which, vbc, vkey):
                norm_to_hT(vbc, vkey, "ffn1_norm" if which == 0 else "ffn2_norm", "f")
                if not cnt.get("dbg2"):
                    cnt["dbg2"] = 1
                    dump_hT("hT")
                wg_l = wg_d[which][l].rearrange("(c p) f -> p c f", p=128)
                wu_l = wu_d[which][l].rearrange("(c p) f -> p c f", p=128)
                wd_l = wd_d[which][l]
                groups = []
                f0 = 0
                while f0 < DFF:
                    nf = min(GF, (DFF - f0) // 128)
                    groups.append((f0, nf))
                    f0 += nf * 128
                NG = len(groups)
                NT4 = TB // 512
                units = [(g, t4) for g in range(NG) for t4 in range(NT4)]

                def load_w(g):
                    if g >= NG:
                        return
                    wi = g % 2
                    f0, nf = groups[g]
                    k.dma(wgu[wi][:, 0, :, 0:nf * 128], wg_l[:, :, f0:f0 + nf * 128], writes=[("wgu", wi, 0)], q="pool")
                    k.dma(wgu[wi][:, 1, :, 0:nf * 128], wu_l[:, :, f0:f0 + nf * 128], writes=[("wgu", wi, 1)], q="pool")
                    k.dma(wdn[wi][:, 0:nf, :], wd_l[f0:f0 + nf * 128, :].rearrange("(c p) d -> p c d", p=128), writes=[("wdn", wi)], q="pool")

                ais = {}

                def front_pieces(u):
                    g, t4 = units[u]
                    wi = g % 2
                    nf = groups[g][1]
                    ai = cnt["a"] % 2
                    cnt["a"] += 1
                    ais[u] = ai
                    hk = hT_keys(t4)
                    pieces = []
                    for fc in range(nf):
                        def piece(fc=fc):
                            gi = cnt["g"] % 2
                            cnt["g"] += 1
                            pg, pgk = PS(gi)
                            pu, puk = PS(2 + gi)
                            for w, (pp, ppk) in enumerate(((pg, pgk), (pu, puk))):
                                for dc in range(8):
                                    k.op("pe", lambda e: e.matmul(
                                        pp[:], lhsT=wgu[wi][:, w, dc, fc * 128:(fc + 1) * 128],
                                        rhs=hT[:, dc, t4 * 512:(t4 + 1) * 512], start=(dc == 0), stop=(dc == 7)),
                                        reads=[("wgu", wi, w)] + hk, writes=[ppk])
                            k.op("act", lambda e: e.activation(out=sg[gi], in_=pg[:], func=AF.Silu),
                                 reads=[], writes=[pgk, ("sg", gi)])
                            k.op("dve", lambda e: e.tensor_tensor(
                                out=aT[ai][:, fc, :], in0=pu[:], in1=sg[gi], op=ALU.mult),
                                reads=[("sg", gi)], writes=[puk, ("aT", ai, fc)])
                        pieces.append(piece)
                    return pieces

                def back_pieces(u):
                    g, t4 = units[u]
                    wi = g % 2
                    nf = groups[g][1]
                    ai = ais[u]
                    pieces = []
                    for ts in range(4):
                        def piece(ts=ts):
                            tt = t4 * 4 + ts
                            for dh in range(2):
                                di = (4, 5, 7)[cnt["d"] % 3]
                                cnt["d"] += 1
                                pd, pdk = PS(di)
                                for fc in range(nf):
                                    k.op("pe", lambda e: e.matmul(
                                        pd[:], lhsT=aT[ai][:, fc, ts * 128:(ts + 1) * 128],
                                        rhs=wdn[wi][:, fc, dh * 512:(dh + 1) * 512], start=(fc == 0), stop=(fc == nf - 1)),
                                        reads=[("aT", ai, fc), ("wdn", wi)], writes=[pdk])
                                k.op("dve", lambda e: e.scalar_tensor_tensor(
                                    out=xb[:, tt, dh * 512:(dh + 1) * 512], in0=pd[:], scalar=0.5,
                                    in1=xb[:, tt, dh * 512:(dh + 1) * 512], op0=ALU.mult, op1=ALU.add),
                                    reads=[], writes=[pdk, ("xb", tt)])
                        pieces.append(piece)
                    return pieces

                load_w(0)
                load_w(1)
                for p_ in front_pieces(0):
                    p_()
                for u in range(len(units)):
                    fp = front_pieces(u + 1) if u + 1 < len(units) else []
                    bp = back_pieces(u)
                    n = max(len(fp), len(bp))
                    for i in range(n):
                        if i < len(fp):
                            fp[i]()
                        if i < len(bp):
                            bp[i]()
                    if units[u][1] == NT4 - 1:
                        load_w(units[u][0] + 2)

            def inproj(l, blk, vbc, vkey, after_norm=None):
                norm_to_hT(vbc, vkey, "mix_norm", "m")
                if after_norm is not None:
                    after_norm()
                win_l = win_d[l].rearrange("(c p) f -> p c f", p=128)
                t0 = blk * TB
                jobs = [("f", 512 + cg * 128, 128, cg) for cg in range(8)] + \
                       [("t", c0, ncol, j0) for (c0, ncol, j0) in ((0, 512, 0), (1536, 512, 512), (2048, 432, 1024))]

                def load_p(i):
                    if i >= len(jobs):
                        return
                    kind, c0, ncol, aux = jobs[i]
                    wi = i % 2
                    k.dma(wpj[wi][:, :, 0:ncol], win_l[:, :, c0:c0 + ncol], writes=[("wpj", wi)], q="pool")

                load_p(0)
                for i, (kind, c0, ncol, aux) in enumerate(jobs):
                    load_p(i + 1)
                    wi = i % 2
                    if kind == "f":
                        cg = aux
                        for t4 in range(TB // 512):
                            gi = cnt["g"] % 2
                            cnt["g"] += 1
                            pg, pgk = PS(gi)
                            hk = hT_keys(t4)
                            for dc in range(8):
                                k.op("pe", lambda e: e.matmul(
                                    pg[:], lhsT=wpj[wi][:, dc, 0:128], rhs=hT[:, dc, t4 * 512:(t4 + 1) * 512],
                                    start=(dc == 0), stop=(dc == 7)), reads=[("wpj", wi)] + hk, writes=[pgk])
                            ei = cnt["e"] % 2
                            cnt["e"] += 1
                            copy_op(alt(), evf[ei], pg[:], reads=[], writes=[pgk, ("evf", ei)])
                            k.dma(xbc_d[cg * 128:(cg + 1) * 128, t0 + t4 * 512:t0 + (t4 + 1) * 512], evf[ei],
                                  reads=[("evf", ei)], writes=[("xbc_d", cg)])
                    else:
                        j0 = aux
                        for tt in range(NTB):
                            gi = cnt["g"] % 2
                            cnt["g"] += 1
                            pu, puk = PS(2 + gi)
                            for dc in range(8):
                                k.op("pe", lambda e: e.matmul(
                                    pu[:, 0:ncol], lhsT=hT[:, dc, tt * 128:(tt + 1) * 128], rhs=wpj[wi][:, dc, 0:ncol],
                                    start=(dc == 0), stop=(dc == 7)),
                                    reads=[("wpj", wi), ("hT", tt, 0), ("hT", tt, 4)], writes=[puk])
                            ei = cnt["e"] % 2
                            cnt["e"] += 1
                            copy_op(alt(), evt[ei][:, 0:ncol], pu[:, 0:ncol], reads=[], writes=[puk, ("evt", ei)])
                            k.dma(ptok_d[t0 + tt * 128:t0 + (tt + 1) * 128, j0:j0 + ncol], evt[ei][:, 0:ncol],
                                  reads=[("evt", ei)], writes=[("ptok_d", blk)])

            def outproj(l, blk):
                t0 = blk * TB
                wout_l = wout_d[l].rearrange("(c p) f -> p c f", p=128)
                for tt in range(NTB):
                    hbi = tt % 2
                    k.dma(hb[hbi], y_d[t0 + tt * 128:t0 + (tt + 1) * 128, :], reads=[("y_d",)], writes=[("hb", hbi)])
                    transpose_rows(hb[hbi], ("hb", hbi), 8, hT, tt, "hT")
                for dh in range(2):
                    k.dma(wpj[dh], wout_l[:, :, dh * 512:(dh + 1) * 512], writes=[("wpj", dh)], q="pool")
                for dh in range(2):
                    wi = dh
                    for tt in range(NTB):
                        di = 4 + cnt["d"] % 2
                        cnt["d"] += 1
                        pd, pdk = PS(di)
                        for dc in range(8):
                            k.op("pe", lambda e, pd=pd, dc=dc, wi=wi, tt=tt: e.matmul(
                                pd[:], lhsT=hT[:, dc, tt * 128:(tt + 1) * 128], rhs=wpj[wi][:, dc, :],
                                start=(dc == 0), stop=(dc == 7)),
                                reads=[("wpj", wi), ("hT", tt, 0), ("hT", tt, 4)], writes=[pdk])
                        k.op("dve", lambda e, pd=pd, tt=tt, dh=dh: e.tensor_tensor(
                            out=xb[:, tt, dh * 512:(dh + 1) * 512], in0=pd[:], in1=xb[:, tt, dh * 512:(dh + 1) * 512],
                            op=ALU.add), reads=[], writes=[pdk, ("xb", tt)])

            stores = []
            src = x_d if pidx == 0 else out_d

            def load_x(blk):
                t0 = blk * TB
                for tt in range(NTB):
                    k.dma(xb[:, tt, :], src[t0 + tt * 128:t0 + (tt + 1) * 128, :],
                          reads=[("out_d", blk)], writes=[("xb", tt)])

            def store_x(blk):
                t0 = blk * TB
                for tt in range(NTB):
                    stores.append(k.dma(out_d[t0 + tt * 128:t0 + (tt + 1) * 128, :], xb[:, tt, :],
                                        reads=[("xb", tt)], writes=[("out_d", blk)]))

            load_x(0)
            for blk in range(NBLK):
                if l_prev is not None:
                    outproj(l_prev, blk)
                    ffn(l_prev, 1, vbcA, "vbcA")
                if l_next is not None:
                    ffn(l_next, 0, vbcB, "vbcB")
                    store_x(blk)
                    inproj(l_next, blk, vbcB, "vbcB",
                           after_norm=(lambda blk=blk: load_x(blk + 1)) if blk + 1 < NBLK else None)
                else:
                    store_x(blk)
                    if blk + 1 < NBLK:
                        load_x(blk + 1)
            return stores

        def mk_tmpset(pfx, n, nr):
            return {"pfx": pfx, "sq": A.f32(n), "ss": A.f32(8), "rs": A.f32(8), "t1": A.f32(nr), "t2": A.f32(nr)}

        def run_interleaved(gens):
            gens = list(gens)
            while gens:
                for g_ in list(gens):
                    try:
                        next(g_)
                    except StopIteration:
                        gens.remove(g_)

        def headnorm_rope(src, H, Dh, wrow, rlo, rhalf, cs, dst, tmp, keys_r, keys_w, cskey):
            sqv = tmp["sq"][:, 0:H * Dh].rearrange("p (h d) -> p h d", h=H)
            ssv = tmp["ss"][:, 0:H]
            rs = tmp["rs"][:, 0:H]
            qn = sqv
            t1 = tmp["t1"][:, 0:H * rhalf].rearrange("p (h d) -> p h d", h=H)
            t2 = tmp["t2"][:, 0:H * rhalf].rearrange("p (h d) -> p h d", h=H)
            p_ = tmp["pfx"]
            T = [p_ + "sq", p_ + "ss", p_ + "rs", p_ + "sq", p_ + "t1", p_ + "t2"]
            k.op("dve", lambda e: e.tensor_tensor(out=sqv, in0=src, in1=src, op=ALU.mult), reads=keys_r, writes=[T[0]])
            yield
            k.op("dve", lambda e: e.tensor_reduce(out=ssv, in_=sqv, axis=AX.X, op=ALU.add), reads=[T[0]], writes=[T[1]])
            yield
            k.op("act", lambda e: e.activation(out=rs, in_=ssv, func=AF.Sqrt, scale=1.0 / Dh, bias=EPS), reads=[T[1]], writes=[T[2]])
            yield
            k.op("dve", lambda e: e.reciprocal(out=ssv, in_=rs), reads=[T[2]], writes=[T[1]])
            yield
            k.op("dve", lambda e: e.tensor_tensor(out=qn, in0=src, in1=ssv.unsqueeze(2).to_broadcast([128, H, Dh]), op=ALU.mult),
                 reads=keys_r + [T[1]], writes=[T[3]])
            yield
            k.op("dve", lambda e: e.tensor_tensor(out=qn, in0=qn, in1=wrow.unsqueeze(1).to_broadcast([128, H, Dh]), op=ALU.mult),
                 reads=["vbc"], writes=[T[3]])
            yield
            if rlo > 0:
                k.op("dve", lambda e: e.tensor_copy(dst[:, :, 0:rlo], qn[:, :, 0:rlo]), reads=[T[3]], writes=keys_w)
                yield
            x1 = qn[:, :, rlo:rlo + rhalf]
            x2 = qn[:, :, rlo + rhalf:rlo + 2 * rhalf]
            cb = cs[:, 0:rhalf].unsqueeze(1).to_broadcast([128, H, rhalf])
            sb = cs[:, rhalf:2 * rhalf].unsqueeze(1).to_broadcast([128, H, rhalf])
            k.op("dve", lambda e: e.tensor_tensor(out=t1, in0=x1, in1=cb, op=ALU.mult), reads=[T[3], cskey], writes=[T[4]])
            yield
            k.op("dve", lambda e: e.tensor_tensor(out=t2, in0=x2, in1=sb, op=ALU.mult), reads=[T[3], cskey], writes=[T[5]])
            yield
            k.op("dve", lambda e: e.tensor_tensor(out=dst[:, :, rlo:rlo + rhalf], in0=t1, in1=t2, op=ALU.subtract),
                 reads=[T[4], T[5]], writes=keys_w)
            yield
            k.op("dve", lambda e: e.tensor_tensor(out=t1, in0=x1, in1=sb, op=ALU.mult), reads=[T[3], cskey], writes=[T[4]])
            yield
            k.op("dve", lambda e: e.tensor_tensor(out=t2, in0=x2, in1=cb, op=ALU.mult), reads=[T[3], cskey], writes=[T[5]])
            yield
            k.op("dve", lambda e: e.tensor_tensor(out=dst[:, :, rlo + rhalf:rlo + 2 * rhalf], in0=t1, in1=t2, op=ALU.add),
                 reads=[T[4], T[5]], writes=keys_w)
            yield

        def heads_to_T(src, skey, H, Dh, dstT, tt, dkey, bank):
            pb, pk = PS(6)
            pv = pb[:, 0:512].rearrange("p (c t) -> p c t", c=4)
            for h in range(H):
                k.op("pe", lambda e, h=h: e.transpose(pv[0:Dh, h, :], src[:, h, :], identb), reads=[skey, "cstb"], writes=[pk])
            copy_op(alt(), dstT[0:Dh, 0:H, tt * 128:(tt + 1) * 128], pv[0:Dh, 0:H, :], reads=[], writes=[pk, (dkey, tt)])

        def out_norm_store(osrc, okeys, width, wrow, col0, tmp, tt, ring, tag):
            k.op("dve", lambda e: e.memset(tmp["ss"][:, 0:1], 0.0), writes=["T_ss"])
            k.op("act", lambda e: e.activation(out=tmp["sq"][:, 0:width], in_=osrc, func=AF.Square, accum_out=tmp["ss"][:, 0:1]),
                 reads=okeys, writes=["T_sq", "T_ss"])
            k.op("act", lambda e: e.activation(out=tmp["rs"][:, 0:1], in_=tmp["ss"][:, 0:1], func=AF.Sqrt, scale=1.0 / width, bias=EPS),
                 reads=["T_ss"], writes=["T_rs"])
            k.op("dve", lambda e: e.reciprocal(out=tmp["ss"][:, 1:2], in_=tmp["rs"][:, 0:1]), reads=["T_rs"], writes=["T_ss"])
            yi = ring % 2
            yb = tmp["yb"][yi][:, 0:width]
            k.op("dve", lambda e: e.scalar_tensor_tensor(out=yb, in0=osrc, scalar=tmp["ss"][:, 1:2], in1=wrow,
                                                         op0=ALU.mult, op1=ALU.mult),
                 reads=list(okeys) + ["T_ss", "vbc"], writes=[(tag + "yb", yi)])
            k.dma(y_d[tt * 128:(tt + 1) * 128, col0:col0 + width], yb, reads=[(tag + "yb", yi)], writes=[("y_d",)])

        def attn_finish(acc_bank, ncols_q, nheads, tmp, odst_fn, extra_den, tag):
            pa, pak = PS(acc_bank)
            oT = tmp["oT"]
            k.op("act", lambda e: e.copy(oT[0:65, 0:ncols_q], pa[0:65, 0:ncols_q]), reads=[], writes=[pak, "T_oT"])
            for j in range(ncols_q // 128):
                pb, pk = PS(7)
                k.op("pe", lambda e, j=j: e.transpose(pb[:, 0:65], oT[0:65, j * 128:(j + 1) * 128], identf[0:65, 0:65]),
                     reads=["T_oT", "cst"], writes=[pk])
                den = tmp["den"]
                if extra_den is not None:
                    ed = extra_den(j)
                    k.op("dve", lambda e, ed=ed: e.tensor_tensor(out=den[:, 0:1], in0=pb[:, 64:65], in1=ed, op=ALU.add),
                         reads=["sinkexp"], writes=[pk, "T_den"])
                else:
                    k.op("dve", lambda e: e.tensor_copy(den[:, 0:1], pb[:, 64:65]), reads=[], writes=[pk, "T_den"])
                k.op("dve", lambda e: e.reciprocal(out=den[:, 1:2], in_=den[:, 0:1]), reads=["T_den"], writes=["T_rden"])
                dst, dkey = odst_fn(j)
                k.op("dve", lambda e, dst=dst: e.tensor_scalar_mul(out=dst, in0=pb[:, 0:64], scalar1=den[:, 1:2]), reads=["T_rden"], writes=[pk, dkey])

        def mla_phase(l, vbc):
            fbase, bbase = A.p, A.pb
            qT = A.bf16(4 * S).rearrange("p (h t) -> p h t", h=4)
            kT = A.bf16(4 * S).rearrange("p (h t) -> p h t", h=4)
            vx = A.bf16(NT * 4 * 65).rearrange("p (t h d) -> p t h d", t=NT, h=4)
            oall = A.f32(NT * 256).rearrange("p (t d) -> p t d", t=NT)
            wuq = A.bf16(2 * 384).rearrange("p (c f) -> p c f", c=2)
            wukv = A.bf16(512)
            pin = [A.f32(416) for _ in range(2)]
            csb = [A.f32(32) for _ in range(2)]
            nb = [A.bf16(384) for _ in range(2)]
            nT = A.bf16(3 * 128).rearrange("p (c t) -> p c t", c=3)
            qf = A.f32(384).rearrange("p (h d) -> p h d", h=4)
            kf = A.f32(384).rearrange("p (h d) -> p h d", h=4)
            qb = A.bf16(384).rearrange("p (h d) -> p h d", h=4)
            kb_ = A.bf16(384).rearrange("p (h d) -> p h d", h=4)
            tmp = {"sq": A.f32(384), "ss": A.f32(8), "rs": A.f32(8),
                   "oT": A.f32(512), "den": A.f32(2), "yb": [A.bf16(256) for _ in range(2)]}
            tsets = [mk_tmpset("Ta_", 384, 64), mk_tmpset("Tb_", 384, 64)]
            eb = [A.bf16(512) for _ in range(4)]
            k.dma(wuq, wuq_d[l].rearrange("(c p) f -> p c f", p=128), writes=["wuq"], q="pool")
            k.dma(wukv, wukv_d[l], writes=["wukv"], q="pool")
            k.op("dve", lambda e: e.memset(vx[:, :, :, 64:65], 1.0), writes=["vx1"])
            for tt in range(NT):
                pi = tt % 2
                k.dma(pin[pi], ptok_d[tt * 128:(tt + 1) * 128, 1040:1456], reads=[("ptok_d",)], writes=[("pin", pi)])
                k.dma(csb[pi], cs_mla_d[tt * 128:(tt + 1) * 128, :], writes=[("cs", pi)])
                for (c0, n, wn, tagn) in ((0, 256, "mla_q_lat_norm", "a"), (256, 128, "mla_kv_norm", "b")):
                    k.op("dve", lambda e: e.memset(tmp["ss"][:, 0:1], 0.0), writes=["T_ss"])
                    k.op("act", lambda e, c0=c0, n=n, pi=pi: e.activation(out=tmp["sq"][:, 0:n], in_=pin[pi][:, c0:c0 + n],
                                                                        func=AF.Square, accum_out=tmp["ss"][:, 0:1]),
                         reads=[("pin", pi)], writes=["T_sq", "T_ss"])
                    k.op("act", lambda e, n=n: e.activation(out=tmp["rs"][:, 0:1], in_=tmp["ss"][:, 0:1], func=AF.Sqrt,
                                                            scale=1.0 / n, bias=EPS), reads=["T_ss"], writes=["T_rs"])
                    k.op("dve", lambda e: e.reciprocal(out=tmp["ss"][:, 1:2], in_=tmp["rs"][:, 0:1]), reads=["T_rs"], writes=["T_ss"])
                    k.op("dve", lambda e, c0=c0, n=n, pi=pi, wn=wn: e.scalar_tensor_tensor(
                        out=nb[pi][:, c0:c0 + n], in0=pin[pi][:, c0:c0 + n], scalar=tmp["ss"][:, 1:2], in1=vrow(vbc, wn),
                        op0=ALU.mult, op1=ALU.mult), reads=[("pin", pi), "T_ss", "vbc"], writes=[("nb", pi, tagn)])
                pb, pk = PS(6)
                pv = pb[:, 512:1024].rearrange("p (c t) -> p c t", c=4)
                for c in range(3):
                    k.op("pe", lambda e, c=c, pi=pi: e.transpose(pv[:, c, :], nb[pi][:, c * 128:(c + 1) * 128], identb),
                         reads=[("nb", pi, "a"), ("nb", pi, "b"), "cstb"], writes=[pk])
                copy_op("act", nT, pv[:, 0:3, :], reads=[], writes=[pk, "nT"])
                pq, pqk = PS(0)
                for c in range(2):
                    k.op("pe", lambda e, c=c: e.matmul(pq[:, 0:384], lhsT=nT[:, c, :], rhs=wuq[:, c, :], start=(c == 0), stop=(c == 1)),
                         reads=["nT", "wuq"], writes=[pqk])
                pkv, pkvk = PS(1)
                k.op("pe", lambda e: e.matmul(pkv[:], lhsT=nT[:, 2, :], rhs=wukv, start=True, stop=True), reads=["nT", "wukv"], writes=[pkvk])
                copy_op("act", qf, pq[:, 0:384].rearrange("p (h d) -> p h d", h=4), reads=[], writes=[pqk, "qf"])
                kvv = pkv[:].rearrange("p (h d) -> p h d", h=4)
                copy_op("act", kf[:, :, 0:64], kvv[:, :, 0:64], reads=[], writes=[pkvk, "kf"])
                copy_op("dve", vx[:, tt, :, 0:64], kvv[:, :, 64:128], reads=[], writes=[pkvk, ("vx", tt)])
                k.op("dve", lambda e, pi=pi: e.tensor_copy(kf[:, :, 64:96], pin[pi][:, 384:416].unsqueeze(1).to_broadcast([128, 4, 32])),
                     reads=[("pin", pi)], writes=["kf"])
                run_interleaved([
                    headnorm_rope(qf, 4, 96, vrow(vbc, "mla_q_norm"), 64, 16, csb[pi], qb, tsets[0], ["qf"], ["qb"], ("cs", pi)),
                    headnorm_rope(kf, 4, 96, vrow(vbc, "mla_k_norm"), 64, 16, csb[pi], kb_, tsets[1], ["kf"], ["kb"], ("cs", pi))])
                heads_to_T(qb, "qb", 4, 96, qT, tt, "qT", 6)
                heads_to_T(kb_, "kb", 4, 96, kT, tt, "kT", 6)
            scale = 96.0 ** -0.5
            qkeys = lambda qi: [("qT", tt) for tt in range(qi * 4, qi * 4 + 4)]
            items = [(h, qi, kt) for h in range(4) for qi in range(S // 512) for kt in range(NT)]
            LA = 3

            def stA(i):
                h, qi, kt = items[i]
                psn, psk = PS(i % 4)
                k.op("pe", lambda e: e.matmul(
                    psn[:], lhsT=kT[0:96, h, kt * 128:(kt + 1) * 128], rhs=qT[0:96, h, qi * 512:(qi + 1) * 512],
                    start=True, stop=True), reads=[("kT", kt)] + qkeys(qi), writes=[psk])

            def stBC(i):
                h, qi, kt = items[i]
                psn, psk = PS(i % 4)
                ebi = i % 4
                blk = h * (S // 512) + qi
                pa, pak = PS(4 + (blk % 2))
                k.op("act", lambda e: e.activation(out=eb[ebi], in_=psn[:], func=AF.Exp, scale=scale),
                     reads=[], writes=[psk, ("eb", ebi)])
                k.op("pe", lambda e: e.matmul(
                    pa[0:65, :], lhsT=vx[:, kt, h, :], rhs=eb[ebi], start=(kt == 0), stop=(kt == NT - 1)),
                    reads=[("vx", kt), "vx1", ("eb", ebi)], writes=[pak])
                if kt == NT - 1:
                    attn_finish(4 + (blk % 2), 512, 1, tmp,
                                lambda j, h=h, qi=qi: (oall[:, qi * 4 + j, h * 64:(h + 1) * 64], ("oall", qi * 4 + j, h)),
                                None, "m")

            for i in range(min(LA, len(items))):
                stA(i)
            for i in range(len(items)):
                if i + LA < len(items):
                    stA(i + LA)
                stBC(i)
            for tt in range(NT):
                okeys = [("oall", tt, h) for h in range(4)]
                out_norm_store(oall[:, tt, :], okeys, 256, vrow(vbc, "mla_out_norm"), 768, tmp, tt, tt, "mo")
            A.reset(fbase, bbase)

        def swa_phase(l, vbc):
            fbase, bbase = A.p, A.pb
            qT = A.bf16(4 * S).rearrange("p (h t) -> p h t", h=4)
            kT = A.bf16(2 * S).rearrange("p (h t) -> p h t", h=2)
            vx = A.bf16(NT * 2 * 65).rearrange("p (t h d) -> p t h d", t=NT, h=2)
            oall = A.f32(NT * 256).rearrange("p (t d) -> p t d", t=NT)
            pin = [A.f32(512) for _ in range(2)]
            csb = [A.f32(64) for _ in range(2)]
            qb = [A.bf16(256).rearrange("p (h d) -> p h d", h=4) for _ in range(2)]
            kb_ = [A.bf16(128).rearrange("p (h d) -> p h d", h=2) for _ in range(2)]
            tmp = {"sq": A.f32(256), "ss": A.f32(8), "rs": A.f32(8),
                   "oT": A.f32(512), "den": A.f32(2), "yb": [A.bf16(256) for _ in range(2)]}
            tsets = [mk_tmpset("T%d_" % i, 256, 128) for i in range(4)]
            sinkexp = A.f32(4)
            eb = [A.bf16(256) for _ in range(4)]
            k.op("act", lambda e: e.activation(out=sinkexp, in_=vrow(vbc, "swa_sink"), func=AF.Exp), reads=["vbc"], writes=["sinkexp"])
            k.op("dve", lambda e: e.memset(vx[:, :, :, 64:65], 1.0), writes=["vx1"])
            for tt0 in range(0, NT, 2):
                gens = []
                for u in range(2):
                    tt = tt0 + u
                    pi = tt % 2
                    k.dma(pin[pi], ptok_d[tt * 128:(tt + 1) * 128, 528:1040], reads=[("ptok_d",)], writes=[("pin", pi)])
                    k.dma(csb[pi], cs_swa_d[tt * 128:(tt + 1) * 128, :], writes=[("cs", pi)])
                    qsrc = pin[pi][:, 0:256].rearrange("p (h d) -> p h d", h=4)
                    ksrc = pin[pi][:, 256:384].rearrange("p (h d) -> p h d", h=2)
                    gens.append(headnorm_rope(qsrc, 4, 64, vrow(vbc, "swa_q_norm"), 0, 32, csb[pi], qb[pi], tsets[2 * u],
                                              [("pin", pi)], [("qb", pi)], ("cs", pi)))
                    gens.append(headnorm_rope(ksrc, 2, 64, vrow(vbc, "swa_k_norm"), 0, 32, csb[pi], kb_[pi], tsets[2 * u + 1],
                                              [("pin", pi)], [("kb", pi)], ("cs", pi)))
                run_interleaved(gens)
                for u in range(2):
                    tt = tt0 + u
                    pi = tt % 2
                    k.op("dve", lambda e: e.tensor_copy(vx[:, tt, :, 0:64], pin[pi][:, 384:512].rearrange("p (h d) -> p h d", h=2)),
                         reads=[("pin", pi)], writes=[("vx", tt)])
                    heads_to_T(qb[pi], ("qb", pi), 4, 64, qT, tt, "qT", 6)
                    heads_to_T(kb_[pi], ("kb", pi), 2, 64, kT, tt, "kT", 6)
            scale = 64.0 ** -0.5
            items = []
            for n in range(NT):
                for kvh in range(2):
                    js = [j for j in (n - 1, n, n + 1) if 0 <= j < NT]
                    for ji, j in enumerate(js):
                        items.append((n, kvh, j, ji, len(js)))
            LA = 3
            pend = []

            def stA(i):
                n, kvh, j, ji, nj = items[i]
                psn, psk = PS(i % 4)
                k.op("pe", lambda e: e.matmul(
                    psn[:, 0:256].rearrange("p (a b) -> p a b", a=2), lhsT=kT[0:64, kvh, j * 128:(j + 1) * 128],
                    rhs=qT[0:64, 2 * kvh:2 * kvh + 2, n * 128:(n + 1) * 128], start=True, stop=True),
                    reads=[("kT", j), ("qT", n)], writes=[psk])

            def stBC(i):
                n, kvh, j, ji, nj = items[i]
                psn, psk = PS(i % 4)
                ebi = i % 4
                blk = n * 2 + kvh
                pa, pak = PS(4 + (blk % 2))
                k.op("act", lambda e: e.activation(out=eb[ebi], in_=psn[:, 0:256], func=AF.Exp, scale=scale),
                     reads=[], writes=[psk, ("eb", ebi)])
                if j != n:
                    mk = cstb[:, 128:384] if j < n else cstb[:, 384:640]
                    k.op("dve", lambda e: e.tensor_tensor(out=eb[ebi], in0=eb[ebi], in1=mk, op=ALU.mult),
                         reads=["cstb"], writes=[("eb", ebi)])
                k.op("pe", lambda e: e.matmul(
                    pa[0:65, 0:256], lhsT=vx[:, j, kvh, :], rhs=eb[ebi], start=(ji == 0), stop=(ji == nj - 1)),
                    reads=[("vx", j), "vx1", ("eb", ebi)], writes=[pak])
                if ji == 0 and pend:
                    pend.pop(0)()
                if ji == nj - 1:
                    pend.append(lambda n=n, kvh=kvh, blk=blk: attn_finish(
                        4 + (blk % 2), 256, 2, tmp,
                        lambda jj: (oall[:, n, (2 * kvh + jj) * 64:(2 * kvh + jj + 1) * 64], ("oall", n, 2 * kvh + jj)),
                        lambda jj: sinkexp[:, 2 * kvh + jj:2 * kvh + jj + 1], "s"))

            for i in range(min(LA, len(items))):
                stA(i)
            for i in range(len(items)):
                if i + LA < len(items):
                    stA(i + LA)
                stBC(i)
            while pend:
                pend.pop(0)()
            for tt in range(NT):
                okeys = [("oall", tt, h) for h in range(4)]
                out_norm_store(oall[:, tt, :], okeys, 256, vrow(vbc, "swa_out_norm"), 512, tmp, tt, tt, "so")
            A.reset(fbase, bbase)

        def ssd_phase(l, vbc):
            fbase, bbase = A.p, A.pb
            convw = A.f32(48).rearrange("p (c k) -> p c k", c=8)
            BT = A.bf16(2 * S).rearrange("p (g t) -> p g t", g=2)
            CT = A.bf16(2 * S).rearrange("p (g t) -> p g t", g=2)
            dt = A.f32(NT * 16).rearrange("p (t d) -> p t d", t=NT)
            av = A.f32(NT * 16).rearrange("p (t d) -> p t d", t=NT)
            cum = A.f32(NT * 16).rearrange("p (t d) -> p t d", t=NT)
            ncf = A.f32(NT * 8).rearrange("p (t d) -> p t d", t=NT)
            sc1 = A.f32(NT * 16).rearrange("p (t d) -> p t d", t=NT)
            sc2 = A.f32(NT * 16).rearrange("p (t d) -> p t d", t=NT)
            dw = A.f32(NT * 16).rearrange("p (t d) -> p t d", t=NT)
            tot = A.f32(NCH * 16).rearrange("p (c d) -> p c d", c=NCH)
            etot = A.f32(NCH * 16).rearrange("p (c d) -> p c d", c=NCH)
            Abc = A.f32(16)
            t16 = [A.f32(NT * 16).rearrange("p (t d) -> p t d", t=NT) for _ in range(3)]
            avd = A.f32(NT * 80).rearrange("p (t d m) -> p t d m", t=NT, d=2)
            Hb = [A.bf16(512) for _ in range(2)]
            Lb = A.bf16(512)
            Rst = [A.f32(256) for _ in range(2)]
            Rtmp = A.f32(256)
            Eb = [A.f32(256) for _ in range(4)]
            ya = [A.f32(256) for _ in range(2)]
            yg = [A.f32(256) for _ in range(2)]
            zb = [A.f32(256) for _ in range(2)]
            szb = [A.f32(256) for _ in range(2)]
            tmp = {"sq": A.f32(256), "ss": A.f32(8), "rs": A.f32(8), "yb": [A.bf16(256) for _ in range(2)]}
            xin = [A.bf16(S + 4)] * 2
            dg = A.bf16(5 * 128).rearrange("p (k c) -> p k c", k=5)
            xTf = A.bf16(2 * S).rearrange("p (c t) -> p c t", c=2)
            xs = A.bf16(NT * 256).rearrange("p (t d) -> p t d", t=NT)
            Btok = A.bf16(NT * 128).rearrange("p (t d) -> p t d", t=NT)
            Sinb = [A.bf16((NCH + 1) * 256).rearrange("p (c d) -> p c d", c=NCH + 1) for _ in range(2)]
            Mb = [A.bf16(256) for _ in range(4)]
            GT = [A.bf16(512) for _ in range(2)]
            xdt = [A.bf16(2 * 2 * 256).rearrange("p (r t d) -> p r t d", r=2, t=2) for _ in range(2)]
            xdw = [A.bf16(2 * 256).rearrange("p (t d) -> p t d", t=2) for _ in range(2)]
            k.dma(convw, convp_d[l].rearrange("p (c k) -> p c k", c=8), writes=["convw"])
            ci = [0]

            def conv_chunk(cc, dst_fn):
                xi = 0
                k.op("dve", lambda e, xi=xi: e.memset(xin[xi][:, 0:2], 0.0), writes=[("xin", xi)])
                k.op("dve", lambda e, xi=xi: e.memset(xin[xi][:, S + 2:S + 4], 0.0), writes=[("xin", xi)])
                k.dma(xin[xi][:, 2:S + 2], xbc_d[cc * 128:(cc + 1) * 128, :], reads=[("xbc_d",)], writes=[("xin", xi)])
                for kk in range(5):
                    k.op("dve", lambda e, kk=kk, cc=cc: e.tensor_scalar_mul(out=dg[:, kk, :], in0=identf, scalar1=convw[:, cc, kk:kk + 1]),
                         reads=["cst", "convw"], writes=["dg"])
                for t4 in range(S // 512):
                    pb, pk = PS(ci[0] % 2)
                    ci[0] += 1
                    for kk in range(5):
                        k.op("pe", lambda e, pb=pb, kk=kk, xi=xi, t4=t4: e.matmul(
                            pb[:], lhsT=dg[:, kk, :], rhs=xin[xi][:, t4 * 512 + kk:t4 * 512 + kk + 512],
                            start=(kk == 0), stop=(kk == 4)), reads=["dg", ("xin", xi)], writes=[pk])
                    dst, dkey = dst_fn(t4)
                    k.op("act", lambda e, pb=pb, dst=dst, cc=cc: e.activation(out=dst, in_=pb[:], func=AF.Silu, bias=convw[:, cc, 5:6]),
                         reads=["convw"], writes=[pk, dkey])
                ci[0] += 1

            for g in range(2):
                conv_chunk(4 + g, lambda t4, g=g: (BT[:, g, t4 * 512:(t4 + 1) * 512], ("BT", g, t4)))
                conv_chunk(6 + g, lambda t4, g=g: (CT[:, g, t4 * 512:(t4 + 1) * 512], ("CT", g, t4)))
            for tt in range(NT):
                k.dma(dt[:, tt, :], ptok_d[tt * 128:(tt + 1) * 128, 512:528], reads=[("ptok_d",)], writes=["dt"])
            bias_bc = vrow(vbc, "ssd_dt_bias").unsqueeze(1).to_broadcast([128, NT, 16])
            k.op("dve", lambda e: e.tensor_tensor(out=dt, in0=dt, in1=bias_bc, op=ALU.add), reads=["vbc"], writes=["dt"])
            k.op("dve", lambda e: e.tensor_scalar_mul(out=t16[0], in0=dt, scalar1=-1.0), reads=["dt"], writes=["t16a"])
            k.op("dve", lambda e: e.tensor_tensor(out=t16[0], in0=t16[0], in1=dt, op=ALU.max), reads=["dt"], writes=["t16a"])
            k.op("act", lambda e: e.activation(out=t16[1], in_=t16[0], func=AF.Exp, scale=-1.0), reads=["t16a"], writes=["t16b"])
            k.op("act", lambda e: e.activation(out=t16[0], in_=t16[1], func=AF.Ln, bias=1.0), reads=["t16b"], writes=["t16a"])
            k.op("dve", lambda e: e.tensor_scalar_max(out=t16[1], in0=dt, scalar1=0.0), reads=["dt"], writes=["t16b"])
            k.op("dve", lambda e: e.tensor_tensor(out=dt, in0=t16[0], in1=t16[1], op=ALU.add), reads=["t16a", "t16b"], writes=["dt"])
            k.op("act", lambda e: e.activation(out=Abc, in_=vrow(vbc, "ssd_a_log"), func=AF.Exp), reads=["vbc"], writes=["Abc"])
            k.op("dve", lambda e: e.scalar_tensor_tensor(out=av, in0=dt, scalar=-1.0, in1=Abc.unsqueeze(1).to_broadcast([128, NT, 16]),
                                                         op0=ALU.mult, op1=ALU.mult), reads=["dt", "Abc"], writes=["av"])
            dbg_dump("ssd_dt", dt, ["dt"])
            k.op("dve", lambda e: e.memset(avd, 0.0), writes=["avd"])
            for d in range(2):
                for c0 in (0, 32):
                    k.op("dve", lambda e: e.tensor_copy(avd[:, :, d, c0:c0 + 8], av[:, :, d * 8:d * 8 + 8]), reads=["av"], writes=["avd"])
            TRI_LE = cst[:, C_R0F:C_R0F + 128]
            TRI_LT = cst[:, C_R0B:C_R0B + 128]
            for c in range(NCH):
                t0_, t1_ = 2 * c, 2 * c + 1
                pb, pk = PS(c % 2)
                pv = pb[:, 0:48].rearrange("p (t d) -> p t d", t=3)
                for d, tri in ((0, TRI_LE), (1, TRI_LT)):
                    cs_ = slice(d * 8, d * 8 + 8)
                    k.op("pe", lambda e, pv=pv, tri=tri, cs_=cs_, t0_=t0_: e.matmul(pv[:, 0, cs_], lhsT=tri, rhs=av[:, t0_, cs_], start=True, stop=True),
                         reads=["av", "cst"], writes=[pk])
                    k.op("pe", lambda e, pv=pv, cs_=cs_, t0_=t0_: e.matmul(pv[:, 1, cs_], lhsT=ones_f, rhs=av[:, t0_, cs_], start=False, stop=False,
                                                                          skip_group_check=True), reads=["av", "cst"], writes=[pk])
                    k.op("pe", lambda e, pv=pv, tri=tri, cs_=cs_, t1_=t1_: e.matmul(pv[:, 1, cs_], lhsT=tri, rhs=av[:, t1_, cs_], start=False, stop=True,
                                                                                  skip_group_check=True), reads=["av", "cst"], writes=[pk])
                k.op("pe", lambda e, pv=pv, t0_=t0_: e.matmul(pv[:, 2, :], lhsT=ones_f, rhs=av[:, t0_, :], start=False, stop=False, skip_group_check=True),
                     reads=["av", "cst"], writes=[pk])
                k.op("pe", lambda e, pv=pv, t1_=t1_: e.matmul(pv[:, 2, :], lhsT=ones_f, rhs=av[:, t1_, :], start=False, stop=True, skip_group_check=True),
                     reads=["av", "cst"], writes=[pk])
                copy_op("dve", cum[:, t0_:t0_ + 2, :], pv[:, 0:2, :], reads=[], writes=[pk, "cum"])
                copy_op("dve", tot[:, c, :], pv[:, 2, :], reads=[], writes=[pk, "tot"])
            totb = lambda d: tot[:, :, d * 8:d * 8 + 8].unsqueeze(2).to_broadcast([128, NCH, 2, 8])
            v4 = lambda a, d: a[:, :, d * 8:d * 8 + 8].rearrange("p (c t) h -> p c t h", t=2)
            k.op("dve", lambda e: e.tensor_scalar_mul(out=ncf, in0=cum[:, :, 0:8], scalar1=-1.0), reads=["cum"], writes=["ncf"])
            k.op("act", lambda e: e.activation(out=sc1[:, :, 0:8], in_=cum[:, :, 0:8], func=AF.Exp), reads=["cum"], writes=["sc1a"])
            k.op("act", lambda e: e.activation(out=sc2[:, :, 8:16], in_=cum[:, :, 8:16], func=AF.Exp), reads=["cum"], writes=["sc2b"])
            k.op("dve", lambda e: e.tensor_tensor(out=v4(t16[2], 0), in0=totb(0), in1=v4(cum, 0), op=ALU.subtract), reads=["cum", "tot"], writes=["t16c0"])
            k.op("dve", lambda e: e.tensor_tensor(out=v4(t16[2], 1), in0=totb(1), in1=v4(cum, 1), op=ALU.subtract), reads=["cum", "tot"], writes=["t16c1"])
            k.op("act", lambda e: e.activation(out=sc2[:, :, 0:8], in_=t16[2][:, :, 0:8], func=AF.Exp), reads=["t16c0"], writes=["sc2a"])
            k.op("act", lambda e: e.activation(out=sc1[:, :, 8:16], in_=t16[2][:, :, 8:16], func=AF.Exp), reads=["t16c1"], writes=["sc1b"])
            k.op("act", lambda e: e.activation(out=etot, in_=tot, func=AF.Exp), reads=["tot"], writes=["etot"])
            k.op("dve", lambda e: e.tensor_tensor(out=dw, in0=dt, in1=sc2, op=ALU.mult), reads=["dt", "sc2a", "sc2b"], writes=["dw"])
            dbg_dump("ssd_cum", cum, ["cum"])
            Dbc = vrow(vbc, "ssd_d")
            v3 = lambda a: a.rearrange("p (h d) -> p h d", h=4)
            for g in range(2):
                for j in range(2):
                    conv_chunk(2 * g + j, lambda t4, j=j: (xTf[:, j, t4 * 512:(t4 + 1) * 512], ("xTf", t4)))
                for tt in range(NT):
                    pb, pk = PS(6)
                    pv = pb[:, 0:256].rearrange("p (c t) -> p c t", c=2)
                    for j in range(2):
                        k.op("pe", lambda e, pv=pv, j=j, tt=tt: e.transpose(pv[:, j, :], xTf[:, j, tt * 128:(tt + 1) * 128], identb),
                             reads=[("xTf", tt // 4), "cstb"], writes=[pk])
                    k.op("pe", lambda e, pb=pb, tt=tt: e.transpose(pb[:, 256:384], BT[:, g, tt * 128:(tt + 1) * 128], identb),
                         reads=[("BT", g, tt // 4), "cstb"], writes=[pk])
                    copy_op("act", xs[:, tt, :], pb[:, 0:256], reads=[], writes=[pk, ("xs", tt)])
                    copy_op("dve", Btok[:, tt, :], pb[:, 256:384], reads=[], writes=[pk, ("Btok", tt)])
                if g == 0:
                    dbg_dump("ssd_xs", xs, [("xs", tt) for tt in range(NT)], BF16)
                xs4 = xs.rearrange("p t (h d) -> p t h d", h=4)
                wi = 0
                for d in range(2):
                    k.op("dve", lambda e, d=d: e.memset(Rst[d], 0.0), writes=[("Rst", d)])
                    order = list(range(NCH)) if d == 0 else list(range(NCH - 1, -1, -1))
                    first_slot = 0 if d == 0 else NCH
                    k.op("dve", lambda e, d=d, first_slot=first_slot: e.memset(Sinb[d][:, first_slot, :], 0.0), writes=[("Sinb", d, first_slot)])
                    for c in order:
                        wr = wi % 2
                        wi += 1
                        dwb = dw[:, 2 * c:2 * c + 2, d * 8 + 4 * g:d * 8 + 4 * g + 4].unsqueeze(3).to_broadcast([128, 2, 4, 64])
                        k.op("dve", lambda e, wr=wr, dwb=dwb, c=c: e.tensor_tensor(out=xdw[wr].rearrange("p t (h d) -> p t h d", h=4),
                                                                                in0=xs4[:, 2 * c:2 * c + 2, :, :], in1=dwb, op=ALU.mult),
                             reads=[("xs", 2 * c), ("xs", 2 * c + 1), "dw"], writes=[("xdw", wr)])
                        pb, pk = PS(c % 2)
                        for ti in range(2):
                            tt = 2 * c + ti
                            k.op("pe", lambda e, pb=pb, tt=tt, ti=ti, wr=wr: e.matmul(
                                pb[:, 0:256], lhsT=Btok[:, tt, :], rhs=xdw[wr][:, ti, :], start=(ti == 0), stop=(ti == 1)),
                                reads=[("Btok", tt), ("xdw", wr)], writes=[pk])
                        etb = etot[:, c, d * 8 + 4 * g:d * 8 + 4 * g + 4].unsqueeze(2).to_broadcast([128, 4, 64])
                        k.op("dve", lambda e, d=d, etb=etb: e.tensor_tensor(out=v3(Rtmp), in0=v3(Rst[d]), in1=etb, op=ALU.mult),
                             reads=[("Rst", d), "etot"], writes=["Rtmp"])
                        k.op("dve", lambda e, pb=pb, d=d: e.tensor_tensor(out=Rst[d], in0=pb[:, 0:256], in1=Rtmp, op=ALU.add),
                             reads=["Rtmp"], writes=[pk, ("Rst", d)])
                        slot = c + 1 if d == 0 else c
                        k.op("act", lambda e, d=d, slot=slot: e.copy(Sinb[d][:, slot, :], Rst[d]), reads=[("Rst", d)], writes=[("Sinb", d, slot)])
                def prologue(c):
                    t0_, t1_ = 2 * c, 2 * c + 1
                    cr = c % 2
                    pb2, pk2 = PS(7)
                    for d, (r0, r1) in ((0, (C_R0F, C_R1F)), (1, (C_R0B, C_R1B))):
                        o_ = pb2[0:40, d * 256:(d + 1) * 256]
                        k.op("pe", lambda e: e.matmul(o_, lhsT=avd[:, t0_, d, :], rhs=cst[:, r0:r0 + 256],
                                                      start=(d == 0), stop=False, skip_group_check=True),
                             reads=["avd", "cst"], writes=[pk2])
                        k.op("pe", lambda e: e.matmul(o_, lhsT=avd[:, t1_, d, :], rhs=cst[:, r1:r1 + 256],
                                                      start=False, stop=True, skip_group_check=True),
                             reads=["avd", "cst"], writes=[pk2])
                    k.op("act", lambda e: e.copy(Hb[cr][0:40, :], pb2[0:40, 0:512]), reads=[], writes=[pk2, ("cumT", cr)])
                    k.op("dve", lambda e: e.tensor_tensor(out=Lb[0:40, :], in0=pb2[0:40, 0:512], in1=Hb[cr][0:40, :], op=ALU.subtract),
                         reads=[("cumT", cr)], writes=[pk2, "Lb"])
                    k.op("dve", lambda e: e.tensor_copy(Hb[cr][32:40, :], Lb[32:40, :]), reads=["Lb"], writes=[("cumT", cr)])
                    for d in range(2):
                        dtb = dt[:, 2 * c:2 * c + 2, d * 8 + 4 * g:d * 8 + 4 * g + 4].unsqueeze(3).to_broadcast([128, 2, 4, 64])
                        k.op("dve", lambda e: e.tensor_tensor(out=xdt[d][:, cr].rearrange("p t (h d) -> p t h d", h=4),
                                                              in0=xs4[:, 2 * c:2 * c + 2, :, :], in1=dtb, op=ALU.mult),
                             reads=[("xs", 2 * c), ("xs", 2 * c + 1), "dt"], writes=[("xdt", d, cr)])
                    pb, pk = PS(7)
                    for si in range(2):
                        k.op("pe", lambda e: e.matmul(pb[:, si * 256:(si + 1) * 256], lhsT=BT[:, g, (2 * c + si) * 128:(2 * c + si + 1) * 128],
                                                      rhs=CT[:, g, c * 256:(c + 1) * 256], start=(si == 0), stop=True, skip_group_check=True),
                             reads=[("BT", g, (2 * c + si) // 4), ("CT", g, c // 2)], writes=[pk])
                    copy_op("act", GT[cr], pb[:, 0:512], reads=[], writes=[pk, ("GT", cr)])

                def item_front(c, it):
                    h, d = it // 2, it % 2
                    hh = 4 * g + h
                    cr = c % 2
                    sel = cstb[0:40, 1152 + hh * 128:1152 + (hh + 1) * 128]
                    for si in range(2):
                        pb, pk = PS(2 * (it % 2) + si)
                        if d == 0:
                            lo, hi = (0, 256) if si == 0 else (128, 256)
                            mask = cstb[:, 640:896] if si == 0 else cstb[:, 640:768]
                        else:
                            lo, hi = (0, 128) if si == 0 else (0, 256)
                            mask = cstb[:, 896 + 128:896 + 256] if si == 0 else cstb[:, 896:896 + 256]
                        w = hi - lo
                        k.op("pe", lambda e: e.matmul(pb[:, 0:w], lhsT=sel, rhs=Hb[cr][0:40, d * 256 + lo:d * 256 + hi], start=True, stop=False),
                             reads=[("cumT", cr), "cstb"], writes=[pk])
                        k.op("pe", lambda e: e.matmul(pb[:, 0:w], lhsT=identb, rhs=mask, start=False, stop=True),
                             reads=["cstb"], writes=[pk])

                def item_back(c, it, started, ybank, ykey):
                    h, d = it // 2, it % 2
                    hh = 4 * g + h
                    cr = c % 2
                    mms = []
                    for si in range(2):
                        pb, pk = PS(2 * (it % 2) + si)
                        if d == 0:
                            lo, hi = (0, 256) if si == 0 else (128, 256)
                        else:
                            lo, hi = (0, 128) if si == 0 else (0, 256)
                        w = hi - lo
                        ebi = 2 * d + si
                        tt_s = 2 * c + si
                        if d == 0:
                            k.op("act", lambda e: e.activation(out=Eb[ebi][:, 0:w], in_=pb[:, 0:w], func=AF.Exp, bias=ncf[:, tt_s, hh:hh + 1], scale=1.0),
                                 reads=["ncf"], writes=[pk, ("Eb", ebi)])
                        else:
                            k.op("act", lambda e: e.activation(out=Eb[ebi][:, 0:w], in_=pb[:, 0:w], func=AF.Exp, bias=cum[:, tt_s, 8 + hh:9 + hh], scale=-1.0),
                                 reads=["cum"], writes=[pk, ("Eb", ebi)])
                        k.op("dve", lambda e: e.tensor_tensor(out=Mb[ebi][:, 0:w], in0=Eb[ebi][:, 0:w], in1=GT[cr][:, si * 256 + lo:si * 256 + hi], op=ALU.mult),
                             reads=[("Eb", ebi), ("GT", cr)], writes=[("Mb", ebi)])
                        for li in range(2):
                            if lo <= li * 128 < hi:
                                mms.append((ebi, li * 128 - lo, li, si))
                    for (ebi, off, li, si) in mms:
                        st_ = not started[li]
                        started[li] = True
                        k.op("pe", lambda e: e.matmul(
                            ybank[li][:, h * 64:(h + 1) * 64], lhsT=Mb[ebi][:, off:off + 128], rhs=xdt[d][:, cr, si, h * 64:(h + 1) * 64],
                            start=st_, stop=False, skip_group_check=True),
                            reads=[("Mb", ebi), ("xdt", d, cr)], writes=[ykey[li]])

                def epilogue(c, ybank, ykey):
                    for li in range(2):
                        tt = 2 * c + li
                        pf7, pfk = PS(6 if False else 7)
                        pf = pf7[:, 0:256]
                        pbw = pf7[:, 256:512]
                        k.op("pe", lambda e: e.matmul(pf, lhsT=CT[:, g, tt * 128:(tt + 1) * 128], rhs=Sinb[0][:, c, :],
                                                      start=True, stop=True), reads=[("CT", g, tt // 4), ("Sinb", 0, c)], writes=[pfk])
                        k.op("pe", lambda e: e.matmul(pbw, lhsT=CT[:, g, tt * 128:(tt + 1) * 128], rhs=Sinb[1][:, c + 1, :],
                                                      start=False, stop=True, skip_group_check=True),
                             reads=[("CT", g, tt // 4), ("Sinb", 1, c + 1)], writes=[pfk])
                        ecfb = sc1[:, tt, 4 * g:4 * g + 4].unsqueeze(2).to_broadcast([128, 4, 64])
                        erbb = sc1[:, tt, 8 + 4 * g:12 + 4 * g].unsqueeze(2).to_broadcast([128, 4, 64])
                        dbb = Dbc[:, 4 * g:4 * g + 4].unsqueeze(2).to_broadcast([128, 4, 64])
                        yi = li
                        k.op("dve", lambda e: e.tensor_tensor(out=v3(ya[yi]), in0=v3(pf), in1=ecfb, op=ALU.mult),
                             reads=["sc1a"], writes=[pfk, ("ya", yi)])
                        k.op("dve", lambda e: e.tensor_tensor(out=v3(Rtmp), in0=v3(pbw), in1=erbb, op=ALU.mult),
                             reads=["sc1b"], writes=[pfk, "Rtmp"])
                        k.op("dve", lambda e: e.tensor_tensor(out=ya[yi], in0=ya[yi], in1=Rtmp, op=ALU.add), reads=["Rtmp"], writes=[("ya", yi)])
                        k.op("dve", lambda e: e.tensor_tensor(out=v3(Rtmp), in0=xs4[:, tt, :, :], in1=dbb, op=ALU.mult),
                             reads=[("xs", tt), "vbc"], writes=["Rtmp"])
                        k.op("dve", lambda e: e.tensor_tensor(out=ya[yi], in0=ya[yi], in1=Rtmp, op=ALU.add), reads=["Rtmp"], writes=[("ya", yi)])
                        k.op("dve", lambda e: e.tensor_tensor(out=ya[yi], in0=ybank[li][:, 0:256], in1=ya[yi], op=ALU.add),
                             reads=[], writes=[ykey[li], ("ya", yi)])
                        zi = li
                        k.dma(zb[zi], ptok_d[tt * 128:(tt + 1) * 128, g * 256:(g + 1) * 256], reads=[("ptok_d",)], writes=[("zb", zi)])
                        k.op("act", lambda e: e.activation(out=szb[zi], in_=zb[zi], func=AF.Exp, scale=-1.0), reads=[("zb", zi)], writes=[("szb", zi)])
                        k.op("act", lambda e: e.activation(out=szb[zi], in_=szb[zi], func=AF.Ln, bias=1.0), reads=[], writes=[("szb", zi)])
                        k.op("act", lambda e: e.activation(out=szb[zi], in_=szb[zi], func=AF.Exp, scale=-1.0), reads=[], writes=[("szb", zi)])
                        k.op("dve", lambda e: e.tensor_tensor(out=yg[yi], in0=ya[yi], in1=zb[zi], op=ALU.mult),
                             reads=[("zb", zi), ("ya", yi)], writes=[("yg", yi)])
                        k.op("dve", lambda e: e.tensor_tensor(out=yg[yi], in0=yg[yi], in1=szb[zi], op=ALU.mult),
                             reads=[("szb", zi)], writes=[("yg", yi)])
                        k.op("dve", lambda e: e.tensor_tensor(out=tmp["sq"][:, 0:256], in0=yg[yi], in1=yg[yi], op=ALU.mult),
                             reads=[("yg", yi)], writes=["T_sq"])
                        k.op("dve", lambda e: e.tensor_reduce(out=tmp["ss"][:, 0:1], in_=tmp["sq"][:, 0:256], axis=AX.X, op=ALU.add),
                             reads=["T_sq"], writes=["T_ss"])
                        k.op("act", lambda e: e.activation(out=tmp["rs"][:, 0:1], in_=tmp["ss"][:, 0:1], func=AF.Ln, scale=1.0 / 256, bias=EPS),
                             reads=["T_ss"], writes=["T_rs"])
                        k.op("act", lambda e: e.activation(out=tmp["ss"][:, 1:2], in_=tmp["rs"][:, 0:1], func=AF.Exp, scale=-0.5),
                             reads=["T_rs"], writes=["T_ss"])
                        ybi = tt % 2
                        ybt = tmp["yb"][ybi][:, 0:256]
                        k.op("dve", lambda e: e.scalar_tensor_tensor(out=ybt, in0=yg[yi], scalar=tmp["ss"][:, 1:2],
                                                                     in1=vrow(vbc, "ssd_norm", g * 256, 256), op0=ALU.mult, op1=ALU.mult),
                             reads=[("yg", yi), "T_ss", "vbc"], writes=[("doyb", ybi)])
                        k.dma(y_d[tt * 128:(tt + 1) * 128, g * 256:(g + 1) * 256], ybt, reads=[("doyb", ybi)], writes=[("y_d",)])

                prologue(0)
                for c in range(NCH):
                    yb0, yk0 = PS(4)
                    yb1, yk1 = PS(5)
                    ybank = (yb0, yb1)
                    ykey = (yk0, yk1)
                    started = [False, False]
                    item_front(c, 0)
                    for it in range(8):
                        if it + 1 < 8:
                            item_front(c, it + 1)
                        item_back(c, it, started, ybank, ykey)
                        if it == 3 and c + 1 < NCH:
                            prologue(c + 1)
                    epilogue(c, ybank, ykey)
            A.reset(fbase, bbase)

        def mixer_phase(l):
            A.reset()
            vbc = A.f32(NV)
            k.dma(vbc, vec_d[l:l + 1, :].partition_broadcast(128), writes=["vbc"])
            ssd_phase(l, vbc)
            k.barrier(barsc[:, 1:2])
            swa_phase(l, vbc)
            k.barrier(barsc[:, 2:3])
            mla_phase(l, vbc)
            if l == 0:
                k.barrier(barsc[:, 3:4])
                dbg_dump("ptok", ptok_d, [("ptok_d",)], big=True)
                dbg_dump("xbc", xbc_d, [("xbc_d",)], BF16, big=True)
                dbg_dump("y", y_d, [("y_d",)], BF16, big=True)

        finals = []
        for p in range(L + 1):
            l_prev = p - 1 if p > 0 else None
            l_next = p if p < L else None
            if stop_after is not None and p > stop_after[0]:
                break
            finals = block_phase(p, l_prev, l_next)
            k.barrier(barsc[:, 0:1])
            if l_next is not None:
                if stop_after is not None and stop_after == (p, "block"):
                    break
                mixer_phase(l_next)
                k.barrier(barsc[:, 0:1])
                if stop_after is not None and stop_after == (p, "mix"):
                    break
        k.emit(list(finals) + dbg_outs)
        print("ops recorded:", k.nops, "arena hi (KiB):", A.hi * 4 / 1024, A.hib * 2 / 1024)
    return nc


def prep_common(inp, S):
    L = inp["ffn1_norm"].shape[0]
    vec = np.zeros((L, NV), np.float32)
    for n, (o, s) in VOFF.items():
        vec[:, o:o + s] = np.asarray(inp[n], np.float32).reshape(L, s)
    cw = np.asarray(inp["ssd_conv_w"], np.float32)
    cb = np.asarray(inp["ssd_conv_b"], np.float32)
    convp = np.zeros((L, 128, 8, 6), np.float32)
    for c in range(8):
        convp[:, :, c, 0:5] = cw[:, :, c * 128:(c + 1) * 128].transpose(0, 2, 1)
        convp[:, :, c, 5] = cb[:, c * 128:(c + 1) * 128]
    com = {"vecs": vec, "convp": np.ascontiguousarray(convp.reshape(L, 128, 48)), "cst": make_consts(),
           "cs_swa": rope_tables(S, 64), "cs_mla": rope_tables(S, 32)}
    for n in ("ffn1_gate", "ffn1_up", "ffn1_down", "ffn2_gate", "ffn2_up", "ffn2_down", "w_in", "w_out",
              "mla_w_uq", "mla_w_ukv"):
        com[n] = np.ascontiguousarray(np.asarray(inp[n], np.float32))
    return com, L


_NC_CACHE = {}


def kernel(**inputs):
    x = np.asarray(inputs["x"], np.float32)
    B, S, _ = x.shape
    com, L = prep_common(inputs, S)
    key = (S, L)
    if key not in _NC_CACHE:
        _NC_CACHE[key] = build(S, L)
    nc = _NC_CACHE[key]
    ncores = 8
    place = [0, 1, 4, 5][:B] if B <= 4 else list(range(B))
    zero_map = None
    in_maps = []
    for c in range(ncores):
        if c in place:
            m = dict(com)
            m["x"] = np.ascontiguousarray(x[place.index(c)])
        else:
            if zero_map is None:
                zero_map = {kk: np.zeros_like(v) for kk, v in com.items()}
                zero_map["x"] = np.zeros((S, D), np.float32)
            m = zero_map
        in_maps.append(m)
    res = run_bass_kernel_spmd(nc, in_maps, core_ids=list(range(ncores)))
    out = np.stack([np.asarray(res.results[place[b]]["out"], np.float32) for b in range(B)], axis=0)
    return out
```
